# Optimizing a Trainium2 kernel written in Bass

```python
import math
import jax, jax.numpy as jnp
from jax import lax
import numpy as np

D_MODEL = 1024
BATCH = 8
SEQ = 4096
DEPTH = 4

CHUNK = 64
N_META = 16
Q_BLOCK = 128
NORM_EPS = 1e-6
SUBLN_EPS = 1e-5
SSD_D_INNER = D_MODEL
SSD_HEAD_DIM = 64
SSD_HEADS = SSD_D_INNER // SSD_HEAD_DIM
SSD_GROUPS = 4
SSD_HEADS_PER_GROUP = SSD_HEADS // SSD_GROUPS
SSD_STATE = 128
SSD_CONV = 4
SSD_BLOCK = 64
SSD_CONV_CH = SSD_D_INNER + 2 * SSD_GROUPS * SSD_STATE
ATTN_HEAD_DIM = 64
ATTN_HEADS = D_MODEL // (2 * ATTN_HEAD_DIM)
ATTN_QK_WIDTH = ATTN_HEADS * 2 * ATTN_HEAD_DIM
ATTN_V_WIDTH = ATTN_HEADS * 2 * ATTN_HEAD_DIM
ROT_DIM = ATTN_HEAD_DIM // 4
ROPE_THETA = 500000.0
LAMBDA_STD = 0.1
D_FF = ((8 * D_MODEL // 3 + 255) // 256) * 256
MLP_CONV = 3
IN_WIDTHS = (SSD_D_INNER, SSD_CONV_CH, SSD_HEADS, ATTN_QK_WIDTH, ATTN_QK_WIDTH, ATTN_V_WIDTH, D_MODEL, D_MODEL)
N_IN = sum(IN_WIDTHS)

kernel_name = "hybrid_ssd_diffattn_convglu_meta"


def rmsnorm(x, g, eps):
    xf = x.astype(jnp.float32)
    y = xf * lax.rsqrt(jnp.mean(xf * xf, axis=-1, keepdims=True) + eps)
    return (y * g.astype(jnp.float32)).astype(x.dtype)


def causal_dwconv(x, w, b):
    k_w, ch = w.shape
    y = lax.conv_general_dilated(x, w[:, None, :].astype(x.dtype), window_strides=(1,),
                                 padding=[(k_w - 1, 0)], dimension_numbers=("NWC", "WIO", "NWC"),
                                 feature_group_count=ch)
    return y + b.astype(x.dtype)


def rope_tables(n_pos):
    half = ROT_DIM // 2
    inv = 1.0 / (ROPE_THETA ** (jnp.arange(half, dtype=jnp.float32) * 2.0 / ROT_DIM))
    ang = jnp.arange(n_pos, dtype=jnp.float32)[:, None] * inv[None, :]
    return jnp.cos(ang), jnp.sin(ang)


def apply_partial_rope(t, cos, sin):
    half = ROT_DIM // 2
    c = cos[:, None, None, :].astype(t.dtype)
    s = sin[:, None, None, :].astype(t.dtype)
    r1, r2, rest = t[..., :half], t[..., half:ROT_DIM], t[..., ROT_DIM:]
    return jnp.concatenate([r1 * c - r2 * s, r2 * c + r1 * s, rest], axis=-1)


def ssd_chunked_scan(x, a_dt, b, c):
    bsz, n, G, R, P = x.shape
    Q = SSD_BLOCK
    nc = n // Q
    x = x.reshape(bsz, nc, Q, G, R, P)
    b = b.reshape(bsz, nc, Q, G, -1)
    c = c.reshape(bsz, nc, Q, G, -1)
    a = a_dt.reshape(bsz, nc, Q, G, R).transpose(0, 3, 4, 1, 2)
    a_cs = jnp.cumsum(a, axis=-1)
    causal = jnp.tril(jnp.ones((Q, Q), dtype=bool))
    decay_in = jnp.exp(jnp.where(causal, a_cs[..., :, None] - a_cs[..., None, :], -jnp.inf))
    cb = jnp.einsum("bclgn,bcsgn->bcgls", c, b)
    y_diag = jnp.einsum("bcgls,bgrcls,bcsgrp->bclgrp", cb, decay_in, x)
    decay_to_end = jnp.exp(a_cs[..., -1:] - a_cs)
    states = jnp.einsum("bclgn,bgrcl,bclgrp->bcgrpn", b, decay_to_end, x)
    chunk_decay = jnp.exp(a_cs[..., -1])

    def step(carry, inp):
        s_c, dec_c = inp
        return carry * dec_c[..., None, None] + s_c, carry

    init = jnp.zeros(states.shape[:1] + states.shape[2:], states.dtype)
    _, prev = lax.scan(step, init, (jnp.moveaxis(states, 1, 0), jnp.moveaxis(chunk_decay, -1, 0)))
    prev = jnp.moveaxis(prev, 0, 1)
    y_off = jnp.einsum("bclgn,bcgrpn,bgrcl->bclgrp", c, prev, jnp.exp(a_cs))
    return (y_diag + y_off).reshape(bsz, n, G, R, P)


def ssd_mixer(z, xbc, dt_raw, conv_w, conv_b, dt_bias, a_log, d_skip, norm_g):
    bsz, n, _ = z.shape
    G, R, P, N = SSD_GROUPS, SSD_HEADS_PER_GROUP, SSD_HEAD_DIM, SSD_STATE
    xbc = jax.nn.silu(causal_dwconv(xbc, conv_w, conv_b))
    xs, b_in, c_in = jnp.split(xbc, [SSD_D_INNER, SSD_D_INNER + G * N], axis=-1)
    xs = xs.reshape(bsz, n, G, R, P)
    b_in = b_in.reshape(bsz, n, G, N)
    c_in = c_in.reshape(bsz, n, G, N)
    dt = jax.nn.softplus(dt_raw.astype(jnp.float32) + dt_bias.astype(jnp.float32)).reshape(bsz, n, G, R)
    a = -jnp.exp(a_log.astype(jnp.float32)).reshape(G, R)
    y = ssd_chunked_scan(xs * dt[..., None].astype(xs.dtype), dt * a, b_in, c_in)
    y = y.astype(xs.dtype) + xs * d_skip.reshape(G, R)[..., None].astype(xs.dtype)
    y = y.reshape(bsz, n, SSD_D_INNER) * jax.nn.silu(z)
    y = rmsnorm(y.reshape(bsz, n, G, SSD_D_INNER // G), norm_g.reshape(G, SSD_D_INNER // G), NORM_EPS)
    return y.reshape(bsz, n, SSD_D_INNER)


def diff_attention_mixer(q, k, v, lam_q1, lam_k1, lam_q2, lam_k2, subln_g, lam_init, chunk_id, cos, sin):
    bsz, n, _ = q.shape
    nblk = n // Q_BLOCK
    d = ATTN_HEAD_DIM
    H = ATTN_HEADS
    q = apply_partial_rope(q.reshape(bsz, n, H, 2, d), cos, sin)
    k = apply_partial_rope(k.reshape(bsz, n, H, 2, d), cos, sin)
    lam = (jnp.exp(jnp.sum(lam_q1.astype(jnp.float32) * lam_k1.astype(jnp.float32)))
           - jnp.exp(jnp.sum(lam_q2.astype(jnp.float32) * lam_k2.astype(jnp.float32))) + lam_init)
    qb = q.reshape(bsz, nblk, Q_BLOCK, H, 2, d).transpose(1, 0, 4, 3, 2, 5)
    kt = k.transpose(0, 3, 2, 1, 4)
    vt = v.reshape(bsz, n, H, 2 * d).transpose(0, 2, 1, 3)
    q_cid = chunk_id.reshape(nblk, Q_BLOCK)
    scale = d ** -0.5

    def one_block(args):
        q_blk, cid_blk = args
        s = jnp.einsum("bmhqd,bmhkd->bmhqk", q_blk, kt).astype(jnp.float32) * scale
        visible = chunk_id[None, :] <= cid_blk[:, None]
        p = jax.nn.softmax(jnp.where(visible, s, -jnp.inf), axis=-1)
        attn = p[:, 0] - lam * p[:, 1]
        return jnp.einsum("bhqk,bhkv->bhqv", attn.astype(vt.dtype), vt)

    o = lax.map(one_block, (qb, q_cid))
    o = rmsnorm(o, subln_g, SUBLN_EPS) * (1.0 - lam_init)
    return o.transpose(1, 0, 3, 2, 4).reshape(bsz, n, H * 2 * d)


def conv_glu_mlp(u, w_up, conv_w, conv_b, w_down):
    hid = causal_dwconv(u @ w_up, conv_w, conv_b)
    gate, val = jnp.split(hid, 2, axis=-1)
    return (jax.nn.silu(gate) * val) @ w_down


def setup_inputs(seed: int = 0) -> dict:
    key = jax.random.key(seed)
    ks = jax.random.split(key, 24)
    f32 = jnp.float32
    nrm = lambda k, shape, s: jax.random.normal(k, shape, f32) * s
    dt0 = jnp.exp(jax.random.uniform(ks[6], (DEPTH, SSD_HEADS), f32) * (math.log(0.1) - math.log(0.001)) + math.log(0.001))
    return {
        "x": nrm(ks[0], (BATCH, SEQ, D_MODEL), 1.0),
        "meta_tokens": nrm(ks[1], (N_META, D_MODEL), 1.0),
        "norm1_g": 1.0 + nrm(ks[2], (DEPTH, D_MODEL), 0.02),
        "w_in": nrm(ks[3], (DEPTH, D_MODEL, N_IN), D_MODEL ** -0.5),
        "ssd_conv_w": nrm(ks[4], (DEPTH, SSD_CONV, SSD_CONV_CH), SSD_CONV ** -0.5),
        "ssd_conv_b": nrm(ks[5], (DEPTH, SSD_CONV_CH), 0.01),
        "ssd_dt_bias": dt0 + jnp.log(-jnp.expm1(-dt0)),
        "ssd_a_log": jnp.log(jax.random.uniform(ks[7], (DEPTH, SSD_HEADS), f32, 1.0, 16.0)),
        "ssd_d": 1.0 + nrm(ks[8], (DEPTH, SSD_HEADS), 0.02),
        "ssd_norm_g": 1.0 + nrm(ks[9], (DEPTH, SSD_D_INNER), 0.02),
        "lambda_q1": nrm(ks[10], (DEPTH, ATTN_HEAD_DIM), LAMBDA_STD),
        "lambda_k1": nrm(ks[11], (DEPTH, ATTN_HEAD_DIM), LAMBDA_STD),
        "lambda_q2": nrm(ks[12], (DEPTH, ATTN_HEAD_DIM), LAMBDA_STD),
        "lambda_k2": nrm(ks[13], (DEPTH, ATTN_HEAD_DIM), LAMBDA_STD),
        "attn_subln_g": 1.0 + nrm(ks[14], (DEPTH, 2 * ATTN_HEAD_DIM), 0.02),
        "w_ssd_branch": nrm(ks[15], (DEPTH, SSD_D_INNER, D_MODEL), SSD_D_INNER ** -0.5),
        "w_attn_branch": nrm(ks[16], (DEPTH, ATTN_V_WIDTH, D_MODEL), ATTN_V_WIDTH ** -0.5),
        "w_out": nrm(ks[17], (DEPTH, D_MODEL, D_MODEL), D_MODEL ** -0.5),
        "norm2_g": 1.0 + nrm(ks[18], (DEPTH, D_MODEL), 0.02),
        "w_up": nrm(ks[19], (DEPTH, D_MODEL, 2 * D_FF), D_MODEL ** -0.5),
        "mlp_conv_w": nrm(ks[20], (DEPTH, MLP_CONV, 2 * D_FF), MLP_CONV ** -0.5),
        "mlp_conv_b": nrm(ks[21], (DEPTH, 2 * D_FF), 0.01),
        "w_down": nrm(ks[22], (DEPTH, D_FF, D_MODEL), D_FF ** -0.5),
        "final_norm_g": 1.0 + nrm(ks[23], (D_MODEL,), 0.02),
    }


def reference(x, meta_tokens, norm1_g, w_in, ssd_conv_w, ssd_conv_b, ssd_dt_bias, ssd_a_log, ssd_d,
              ssd_norm_g, lambda_q1, lambda_k1, lambda_q2, lambda_k2, attn_subln_g, w_ssd_branch,
              w_attn_branch, w_out, norm2_g, w_up, mlp_conv_w, mlp_conv_b, w_down, final_norm_g):
    bsz, seq, _ = x.shape
    n_tok = N_META + seq
    n_pad = -(-n_tok // Q_BLOCK) * Q_BLOCK
    meta = jnp.broadcast_to(meta_tokens.astype(x.dtype)[None], (bsz, N_META, D_MODEL))
    h = jnp.concatenate([meta, x], axis=1)
    h = jnp.pad(h, ((0, 0), (0, n_pad - n_tok), (0, 0)))
    pos = jnp.arange(n_pad)
    chunk_id = jnp.where(pos < N_META, 0, 1 + (pos - N_META) // CHUNK)
    cos, sin = rope_tables(n_pad)
    splits = np.cumsum(IN_WIDTHS)[:-1].tolist()
    for l in range(DEPTH):
        lam_init = 0.8 - 0.6 * math.exp(-0.3 * l)
        u = rmsnorm(h, norm1_g[l], NORM_EPS)
        proj = u @ w_in[l]
        z, xbc, dt_raw, q, k, v, g_ssd, g_attn = jnp.split(proj, splits, axis=-1)
        y_ssd = ssd_mixer(z, xbc, dt_raw, ssd_conv_w[l], ssd_conv_b[l], ssd_dt_bias[l], ssd_a_log[l],
                          ssd_d[l], ssd_norm_g[l])
        y_attn = diff_attention_mixer(q, k, v, lambda_q1[l], lambda_k1[l], lambda_q2[l], lambda_k2[l],
                                      attn_subln_g[l], lam_init, chunk_id, cos, sin)
        merged = (jax.nn.sigmoid(g_ssd) * (y_ssd @ w_ssd_branch[l])
                  + jax.nn.sigmoid(g_attn) * (y_attn @ w_attn_branch[l]))
        h = h + merged @ w_out[l]
        h = h + conv_glu_mlp(rmsnorm(h, norm2_g[l], NORM_EPS), w_up[l], mlp_conv_w[l], mlp_conv_b[l], w_down[l])
    return rmsnorm(h, final_norm_g, NORM_EPS)[:, N_META:N_META + seq]
```

```python
import numpy as np
import concourse.bass as bass
import concourse.mybir as mybir
from concourse.ap import AP
from concourse.bass_utils import run_bass_kernel_spmd

F32 = mybir.dt.float32
BF16 = mybir.dt.bfloat16
AF = mybir.ActivationFunctionType
ALU = mybir.AluOpType

D = 1024
KC = 8
NIN = 8208
DFF = 2816
NORM_EPS = 1e-6
SUBLN_EPS = 1e-5
N_META = 16


import types


def _freeze(fn):
    if fn.__closure__ is None:
        return fn
    cells = []
    for c in fn.__closure__:
        try:
            cells.append(types.CellType(c.cell_contents))
        except ValueError:
            cells.append(c)
    return types.FunctionType(fn.__code__, fn.__globals__, fn.__name__, fn.__defaults__, tuple(cells))


class Sem:
    def __init__(self, nc, name):
        self.h = nc.alloc_semaphore(name)
        self.total = 0
        self.waited = 0


class Buf:
    __slots__ = ("w", "r", "name")

    def __init__(self, name=""):
        self.w = []
        self.r = []
        self.name = name


class Op:
    __slots__ = ("eng", "fn", "isdma", "ds", "deps", "succ", "dur", "lat", "nun", "ready", "done", "sigval",
                 "needsig", "batch", "sched", "kw")


class Eng:
    def __init__(self, k, eng, name, is_pe=False, is_queue=False):
        self.k = k
        self.e = eng
        self.name = name
        self.sem = Sem(k.nc, "s_" + name)
        self.seen = {}
        self.is_pe = is_pe
        self.is_queue = is_queue

    def wait(self, toks):
        for s, v in toks.items():
            vv = s.total if v is None else v
            assert vv <= s.total, "waiting for a value that is never produced"
            if vv <= 0 or self.seen.get(s, 0) >= vv:
                continue
            self.e.wait_ge(s.h, vv)
            self.seen[s] = vv
            s.waited = max(s.waited, vv)


import os as _os
_SIG_LAT = 64.0
_WINDOW = int(_os.environ.get("K_WINDOW", "40"))
_WINDOW_Q = int(_os.environ.get("K_WINDOW_Q", "40"))


class K:
    def __init__(self, nc):
        self.nc = nc
        self.pe = Eng(self, nc.tensor, "pe", is_pe=True)
        self.act = Eng(self, nc.scalar, "act")
        self.dve = Eng(self, nc.vector, "dve")
        self.pool = Eng(self, nc.gpsimd, "pool")
        self.sp = Eng(self, nc.sync, "sp", is_queue=True)
        self.engs = [self.pe, self.act, self.dve, self.pool, self.sp]
        self.dsems = []
        self._dsem_cache = {}
        self.n_inst = 0
        self.batch = 0
        self.pending = []
        self.model_ns = 0.0
        self.flush_log = []

    def dsem(self, name):
        if name in self._dsem_cache:
            return self._dsem_cache[name]
        s = Sem(self.nc, "d_" + name)
        self.dsems.append(s)
        self._dsem_cache[name] = s
        return s

    def _record(self, o, reads, writes):
        bt = self.batch
        deps = {}
        for b in reads:
            for d in b.w:
                if d.batch == bt:
                    deps[id(d)] = d
        for b in writes:
            for d in b.w:
                if d.batch == bt:
                    deps[id(d)] = d
            for d in b.r:
                if d.batch == bt:
                    deps[id(d)] = d
        o.deps = list(deps.values())
        o.succ = []
        o.batch = bt
        o.sched = False
        o.sigval = 0
        for b in reads:
            b.r.append(o)
        for b in writes:
            b.w = [o]
            b.r = []
        self.pending.append(o)

    def _probe(self, fn):
        with self.nc.discard():
            inst = fn()
        ins = inst.ins
        n = 1
        ap = ins.outs[0].ap
        for st, cn in ap[1:]:
            n *= cn
        return n, ap[0][1], ins

    def op(self, eng, fn, reads=(), writes=(), signal=True):
        o = Op()
        o.eng = eng
        o.fn = _freeze(fn)
        o.isdma = False
        o.ds = None
        n, _, ins = self._probe(o.fn)
        if eng.is_pe:
            f32 = str(ins.ins[0].dtype).endswith("float32")
            o.dur = max(n, 64) / 2.0 * (4.0 if f32 else 1.0) + 8.0
        elif eng is self.act:
            o.dur = (n + 224) / 1.4
        elif eng is self.dve:
            o.dur = n / 0.96 + 62.0
        else:
            o.dur = n / 0.6 + 160.0
        o.lat = 0.0
        self._record(o, reads, writes)

    def dma(self, q, ds, out, in_, reads=(), writes=(), **kw):
        o = Op()
        o.eng = q
        o.isdma = True
        o.ds = ds
        o.fn = None
        o.kw = (out, in_, kw)
        nbytes = 1
        for st, cn in out.ap:
            nbytes *= cn
        nbytes *= 4 if out.dtype == F32 else 2
        o.dur = 60.0 if q.is_queue else 700.0
        o.lat = 2000.0 + nbytes / 120.0
        self._record(o, reads, writes)

    def flush(self):
        ops = self.pending
        self.pending = []
        if not ops:
            return
        lists = {e: [] for e in self.engs}
        for o in ops:
            o.nun = len(o.deps)
            o.ready = 0.0
            for d in o.deps:
                d.succ.append(o)
            lists[o.eng].append(o)
        head = {e: 0 for e in self.engs}
        free = {e: 0.0 for e in self.engs}
        cand = {e: None for e in self.engs}
        dirty = set(self.engs)
        order = []
        nleft = len(ops)
        while nleft:
            for e in dirty:
                lst = lists[e]
                h = head[e]
                while h < len(lst) and lst[h].sched:
                    h += 1
                head[e] = h
                best = None
                bt_ = 0.0
                fe = free[e]
                cnt = 0
                i = h
                win = _WINDOW_Q if (e.is_queue or e is self.pool) else _WINDOW
                while i < len(lst) and cnt < win:
                    o = lst[i]
                    i += 1
                    if o.sched:
                        continue
                    cnt += 1
                    if o.nun:
                        continue
                    if o.ready <= fe:
                        best, bt_ = o, fe
                        break
                    if best is None or o.ready < bt_:
                        best, bt_ = o, o.ready
                cand[e] = (best, bt_) if best is not None else None
            dirty.clear()
            pe_, pt_ = None, None
            for e in self.engs:
                c = cand[e]
                if c is not None and (pt_ is None or c[1] < pt_):
                    pe_, pt_ = e, c[1]
            o = cand[pe_][0]
            o.sched = True
            free[pe_] = pt_ + o.dur
            o.done = pt_ + o.dur + o.lat
            order.append(o)
            nleft -= 1
            dirty.add(pe_)
            dn = o.done + _SIG_LAT
            for s_ in o.succ:
                s_.nun -= 1
                if dn > s_.ready:
                    s_.ready = dn
                if s_.nun == 0:
                    dirty.add(s_.eng)
        self.model_ns += max(o.done for o in ops)
        self.flush_log.append((len(ops), max(o.done for o in ops), {e.name: sum(o.dur for o in ops if o.eng is e) for e in self.engs}))
        last = {}
        for i_, o in enumerate(order):
            o.nun = i_
            o.needsig = False
            if not o.isdma:
                last[o.eng] = o
        for o in order:
            per = {}
            for d in o.deps:
                if d.isdma or (d.eng is o.eng and o.eng.is_pe):
                    continue
                p_ = per.get(d.eng)
                if p_ is None or d.nun > p_.nun:
                    per[d.eng] = d
            for d in per.values():
                d.needsig = True
        for o in last.values():
            o.needsig = True
        for o in order:
            e = o.eng
            toks = {}
            for d in o.deps:
                if d.isdma:
                    toks[d.ds] = None
                elif d.eng is e and e.is_pe:
                    continue
                else:
                    sm = d.eng.sem
                    if toks.get(sm, 0) is not None and d.sigval > toks.get(sm, 0):
                        toks[sm] = d.sigval
            e.wait(toks)
            if o.isdma:
                ds = o.ds
                if ds.total > 0 and ds.waited >= ds.total:
                    e.wait({ds: ds.total})
                out, in_, kw = o.kw
                inst = e.e.dma_start(out=out, in_=in_, **kw)
                inst.then_inc(ds.h, 16)
                ds.total += 16
            else:
                inst = o.fn()
                if o.needsig:
                    e.sem.total += 1
                    inst.then_inc(e.sem.h, 1)
                    o.sigval = e.sem.total
            o.fn = None
            o.kw = None
            self.n_inst += 1

    def barrier(self):
        self.flush()
        self.batch += 1
        toks = {}
        for e in self.engs:
            if not e.is_queue:
                toks[e.sem] = e.sem.total
        for s in self.dsems:
            toks[s] = s.total
        for e in self.engs:
            t = dict(toks)
            if e.is_pe or e.is_queue:
                t.pop(e.sem, None)
            e.wait(t)


def bcast_free(ap, pos, n):
    a = [list(x) for x in ap.ap]
    a.insert(1 + pos, [0, n])
    return AP(ap.tensor, ap.offset, a)


class Cfg:
    def __init__(self, seq=4096, depth=4, debug_outs=(), stop_after=None):
        self.seq = seq
        self.L = depth
        self.n_tok = N_META + seq
        self.NT = -(-self.n_tok // 128)
        self.T = self.NT * 128
        self.blocks = [(t0, min(512, self.T - t0)) for t0 in range(0, self.T, 512)]
        self.debug_outs = set(debug_outs)
        self.stop_after = stop_after


PP = {}
_o = 0
for _n, _w in (("g1", 8), ("g2", 8), ("sng", 8), ("subln", 1), ("cw", 64), ("cb", 16), ("mcw", 132), ("mcb", 44),
               ("dtb", 16), ("alog", 16), ("dsk", 16), ("lam", 256)):
    PP[_n] = _o
    _o += _w
NP_ = _o


def pack_params(inp, l):
    f = lambda a: np.asarray(a, np.float32)
    pp = np.zeros((128, NP_), np.float32)
    colT = lambda v: f(v).reshape(-1, 128).T
    pp[:, PP["g1"]:PP["g1"] + 8] = colT(inp["norm1_g"][l])
    pp[:, PP["g2"]:PP["g2"] + 8] = colT(inp["norm2_g"][l])
    pp[:, PP["sng"]:PP["sng"] + 8] = colT(inp["ssd_norm_g"][l])
    pp[:, PP["subln"]:PP["subln"] + 1] = f(inp["attn_subln_g"][l]).reshape(128, 1)
    cw = f(inp["ssd_conv_w"][l])
    pp[:, PP["cw"]:PP["cw"] + 64] = cw.reshape(4, 16, 128).transpose(2, 1, 0).reshape(128, 64)
    pp[:, PP["cb"]:PP["cb"] + 16] = colT(inp["ssd_conv_b"][l])
    mw = f(inp["mlp_conv_w"][l])
    pp[:, PP["mcw"]:PP["mcw"] + 132] = mw.reshape(3, 44, 128).transpose(2, 1, 0).reshape(128, 132)
    pp[:, PP["mcb"]:PP["mcb"] + 44] = colT(inp["mlp_conv_b"][l])
    pp[:, PP["dtb"]:PP["dtb"] + 16] = np.broadcast_to(f(inp["ssd_dt_bias"][l])[None], (128, 16))
    pp[:, PP["alog"]:PP["alog"] + 16] = np.broadcast_to(f(inp["ssd_a_log"][l])[None], (128, 16))
    pp[:, PP["dsk"]:PP["dsk"] + 16] = np.broadcast_to(f(inp["ssd_d"][l])[None], (128, 16))
    lam = np.concatenate([f(inp[n][l]) for n in ("lambda_q1", "lambda_k1", "lambda_q2", "lambda_k2")])
    pp[:, PP["lam"]:PP["lam"] + 256] = np.broadcast_to(lam[None], (128, 256))
    return pp


def make_consts(cfg):
    T = cfg.T
    c = {}
    c["ident"] = np.eye(128, dtype=np.float32)
    perm = np.zeros((128, 128), np.float32)
    for m in range(2):
        for dd in range(16):
            src = dd + 8 if dd < 8 else dd - 8
            perm[m * 64 + src, m * 64 + dd] = 1.0
    c["perm"] = perm
    kk = np.arange(128)
    tri = (kk[:, None] <= kk[None, :]).astype(np.float32)
    upp = (kk[:, None] > kk[None, :]).astype(np.float32)
    c["tri3"] = np.stack([tri, upp, np.ones((128, 128), np.float32)], 1)
    cidr = np.where(kk < 16, 0, np.where(kk < 80, 1, 2))
    mdiag = (cidr[:, None] <= cidr[None, :]).astype(np.float32)
    mnext = ((kk[:, None] < 16) & (kk[None, :] >= 80)).astype(np.float32)
    c["amask"] = np.stack([mdiag, mnext], 1)
    pos = np.arange(T, dtype=np.float32)
    inv = (1.0 / (np.float32(500000.0) ** (np.arange(8, dtype=np.float32) * np.float32(2.0) / np.float32(16)))).astype(np.float32)
    ang = (pos[:, None] * inv[None, :]).astype(np.float32)
    cs, sn = np.cos(ang).astype(np.float32), np.sin(ang).astype(np.float32)
    cosT = np.ones((128, T), np.float32)
    sinT = np.zeros((128, T), np.float32)
    for m in range(2):
        for dd in range(16):
            cosT[m * 64 + dd] = cs[:, dd % 8]
            sinT[m * 64 + dd] = -sn[:, dd % 8] if dd < 8 else sn[:, dd % 8]
    c["ropec"] = cosT
    c["ropes"] = sinT
    return c


class Scope:
    _uid = 0

    def __init__(self, k):
        from contextlib import ExitStack
        self.k = k
        self.st = ExitStack()

    def sb(self, name, shape, dt):
        Scope._uid += 1
        return self.st.enter_context(self.k.nc.sbuf_tensor(f"{name}_{Scope._uid}", list(shape), dt))

    def ps(self, name, shape, dt):
        Scope._uid += 1
        return self.st.enter_context(self.k.nc.psum_tensor(f"{name}_{Scope._uid}", list(shape), dt))

    def close(self):
        self.k.barrier()
        self.st.close()


class Prog:
    def __init__(self, cfg):
        self.cfg = cfg
        nc = self.nc = bass.Bass("TRN2", target_bir_lowering=False)
        self.k = K(nc)
        T, L = cfg.T, cfg.L
        ext = lambda n, s, dt=F32: nc.dram_tensor(n, list(s), dt, kind="ExternalInput").ap()
        self.x = ext("x", [cfg.seq, D])
        self.meta = ext("meta", [N_META, D])
        self.w_in = ext("w_in", [L, D, NIN])
        self.w_sb = ext("w_sb", [L, D, D])
        self.w_ab = ext("w_ab", [L, D, D])
        self.w_out = ext("w_out", [L, D, D])
        self.w_up = ext("w_up", [L, D, 2 * DFF])
        self.w_down = ext("w_down", [L, DFF, D])
        self.pp_d = ext("pp", [L, 128, NP_])
        self.fng_d = ext("fng", [128, D])
        self.c_ident = ext("ident", [128, 128])
        self.c_perm = ext("perm", [128, 128])
        self.c_tri3 = ext("tri3", [128, 3, 128])
        self.c_amask = ext("amask", [128, 2, 128])
        self.c_ropec = ext("ropec", [128, T])
        self.c_ropes = ext("ropes", [128, T])
        self.out = nc.dram_tensor("out", [cfg.seq, D], F32, kind="ExternalOutput").ap()

        def scr(n, s, dt):
            kind = "ExternalOutput" if n in cfg.debug_outs else "Internal"
            return nc.dram_tensor(n, list(s), dt, kind=kind).ap()
        self.H = scr("H", [T, D], F32)
        self.ZS = scr("ZS", [T, D], BF16)
        self.XBC = scr("XBC", [2048, T], BF16)
        self.DT = scr("DT", [T, 32], F32)
        self.QT = scr("QT", [D, T], BF16)
        self.KT = scr("KT", [D, T], BF16)
        self.V = scr("V", [T, D], BF16)
        self.GS = scr("GS", [D, T], BF16)
        self.GA = scr("GA", [D, T], BF16)
        self.YS = scr("YS", [D, T], BF16)
        self.YA = scr("YA", [D, T], BF16)
        self.AT = scr("AT", [DFF, T], BF16)

    def build(self):
        cfg, k, nc = self.cfg, self.k, self.nc
        top = Scope(k)
        self.top = top
        self.ident_f = top.sb("identf", [128, 128], F32)
        self.ident = top.sb("ident", [128, 128], BF16)
        self.perm = top.sb("perm", [128, 128], BF16)
        self.tri3 = top.sb("tri3", [128, 3, 128], F32)
        self.amask = top.sb("amask", [128, 2, 128], BF16)
        self.pp = top.sb("pp", [128, NP_], F32)
        self.b_const = Buf("const")
        self.b_pp = Buf("pp")
        ds = k.dsem("const")
        self.ds_pp = k.dsem("pp")
        tmpf = top.sb("ctmp", [128, 3, 128], F32)
        bt = Buf()
        k.dma(k.sp, ds, self.ident_f[:], self.c_ident[:, :], writes=[bt])
        k.op(k.dve, lambda: nc.vector.tensor_copy(out=self.ident[:], in_=self.ident_f[:]), reads=[bt], writes=[self.b_const])
        bt2 = Buf()
        k.dma(k.sp, ds, tmpf[:, 0, :], self.c_perm[:, :], writes=[bt2])
        k.dma(k.sp, ds, tmpf[:, 1:3, :], self.c_amask[:, :, :], writes=[bt2])
        k.op(k.dve, lambda: nc.vector.tensor_copy(out=self.perm[:], in_=tmpf[:, 0, :]), reads=[bt2], writes=[self.b_const])
        k.op(k.dve, lambda: nc.vector.tensor_copy(out=self.amask[:], in_=tmpf[:, 1:3, :]), reads=[bt2], writes=[self.b_const])
        k.dma(k.sp, ds, self.tri3[:], self.c_tri3[:, :, :], writes=[self.b_const])
        self.phase0()
        for l in range(cfg.L):
            self.layer(l)
            if cfg.stop_after is not None and cfg.stop_after[0] == l:
                break
        if cfg.stop_after is None:
            self.final_norm()
        k.barrier()
        top.st.close()
        return nc

    def phase0(self):
        cfg, k, nc = self.cfg, self.k, self.nc
        sc = Scope(k)
        ds = k.dsem("p0")
        b = Buf()
        k.dma(k.sp, ds, self.H[0:N_META, :], self.meta[:, :], writes=[b])
        pdiv = max(p for p in (128, 64, 32, 16, 8, 4, 2, 1) if cfg.seq % p == 0)
        xs = self.x.rearrange("(p r) d -> p (r d)", p=pdiv)
        hs = self.H[N_META:N_META + cfg.seq, :].rearrange("(p r) d -> p (r d)", p=pdiv)
        k.dma(k.sp, ds, hs, xs, writes=[b])
        npad = cfg.T - cfg.n_tok
        if npad > 0:
            z = sc.sb("zero", [128, D], F32)
            bz = Buf()
            k.op(k.dve, lambda: nc.vector.memset(z[:], 0.0), writes=[bz])
            k.dma(k.sp, ds, self.H[cfg.n_tok:cfg.T, :], z[0:npad, :], reads=[bz])
        sc.close()

    def layer(self, l):
        cfg, k = self.cfg, self.k
        k.dma(k.sp, self.ds_pp, self.pp[:], self.pp_d[l, :, :], writes=[self.b_pp])
        stop = cfg.stop_after[1] if (cfg.stop_after is not None and cfg.stop_after[0] == l) else 99
        sc = Scope(k)
        UT = sc.sb("UT", [128, KC, cfg.T], BF16)
        ut_bufs = [Buf(f"ut{t}") for t in range(cfg.NT)]
        self.norm_pass(sc, l, PP["g1"], UT, ut_bufs, npst=3)
        self.phase1(sc, l, UT, ut_bufs)
        sc.close()
        if stop <= 1:
            return
        scw = Scope(k)
        w4 = []
        w4_b = [[Buf() for _ in range(KC)] for _ in range(3)]
        ds_w = k.dsem("p4w")
        for wi, (nm, src) in enumerate((("wsb", self.w_sb), ("wab", self.w_ab), ("wout", self.w_out))):
            w = scw.sb(nm, [128, KC, D], BF16)
            for kk in range(KC):
                k.dma(k.pool, ds_w, w[:, kk, :], src[l, kk * 128:(kk + 1) * 128, :], writes=[w4_b[wi][kk]])
            w4.append(w)
        self.phase2(l)
        if stop > 2:
            self.phase3(l)
        if stop > 3:
            self.phase4(l, w4, w4_b)
        scw.close()
        if stop <= 4:
            return
        scw = Scope(k)
        NC = DFF // 128
        wd = scw.sb("wd", [128, NC, D], BF16)
        wd_b = [Buf() for _ in range(NC)]
        self._wd_prefetch = (wd, wd_b, k.dsem("p6w"))
        sc = Scope(k)
        UT = sc.sb("UT", [128, KC, cfg.T], BF16)
        ut_bufs = [Buf(f"ut{t}") for t in range(cfg.NT)]
        self.norm_pass(sc, l, PP["g2"], UT, ut_bufs)
        self.phase5(sc, l, UT, ut_bufs)
        sc.close()
        if stop > 5:
            self.phase6(l, wd, wd_b)
        scw.close()

    def norm_pass(self, sc, l, goff, UT, ut_bufs, npst=2):
        cfg, k, nc = self.cfg, self.k, self.nc
        NB = 4
        NR = 3
        ht = [sc.sb("nh", [128, D], F32) for _ in range(NB)]
        hb = [Buf() for _ in range(NB)]
        hds = [k.dsem(f"nh{i}") for i in range(NB)]
        junk = sc.sb("njunk", [128, D], BF16)
        bjunk = Buf()
        hn = [sc.sb("nhn", [128, D], BF16) for _ in range(NR)]
        hnb = [Buf() for _ in range(NR)]
        st = [sc.sb("nst", [128, 4], F32) for _ in range(NR)]
        stb = [Buf() for _ in range(NR)]
        mhalf = sc.sb("mhalf", [128, 1], F32)
        bmh = Buf()
        k.op(k.pool, lambda: nc.gpsimd.memset(mhalf[:], -0.5), writes=[bmh])
        pst = [sc.ps("npt", [128, KC, 128], BF16) for _ in range(npst)]
        psb = [Buf() for _ in range(npst)]
        gT = self.pp[:, goff:goff + KC]
        for t in range(cfg.NT):
            i, j, jp = t % NB, t % NR, t % npst
            k.dma(k.sp, hds[i], ht[i][:], self.H[t * 128:(t + 1) * 128, :], writes=[hb[i]])
            k.op(k.dve, lambda: nc.vector.scalar_tensor_tensor(out=junk[:], in0=ht[i][:], scalar=1.0, in1=ht[i][:],
                                                               op0=ALU.mult, op1=ALU.mult, accum_out=st[j][:, 0:1]),
                 reads=[hb[i]], writes=[bjunk, stb[j]])
            k.op(k.dve, lambda: nc.vector.tensor_scalar(out=st[j][:, 1:2], in0=st[j][:, 0:1], scalar1=1.0 / D, scalar2=NORM_EPS,
                                                        op0=ALU.mult, op1=ALU.add), reads=[stb[j]], writes=[stb[j]])
            k.op(k.pool, lambda: nc.gpsimd.tensor_tensor(out=st[j][:, 2:3], in0=st[j][:, 1:2], in1=mhalf[:], op=ALU.pow),
                 reads=[stb[j], bmh], writes=[stb[j]])
            k.op(k.act, lambda: nc.scalar.activation(out=hn[j][:], in_=ht[i][:], func=AF.Copy, scale=st[j][:, 2:3]),
                 reads=[hb[i], stb[j]], writes=[hnb[j]])
            for c in range(KC):
                k.op(k.pe, lambda: nc.tensor.transpose(out=pst[jp][:, c, :], in_=hn[j][:, c * 128:(c + 1) * 128], identity=self.ident[:]),
                     reads=[hnb[j], self.b_const], writes=[psb[jp]], signal=(c == KC - 1))
            k.op(k.dve, lambda: nc.vector.tensor_tensor(out=UT[:, :, t * 128:(t + 1) * 128], in0=pst[jp][:, :, :],
                                                        in1=bcast_free(gT, 1, 128), op=ALU.mult),
                 reads=[psb[jp], self.b_pp], writes=[ut_bufs[t]])

    def phase1(self, sc, l, UT, ut_bufs):
        cfg, k, nc = self.cfg, self.k, self.nc
        T, NT, blocks = cfg.T, cfg.NT, cfg.blocks
        pp = self.pp
        W = self.w_in[l]
        NW = 3
        wfm = [sc.sb("wfm", [128, KC, 128], BF16) for _ in range(NW)]
        wfm_b = [Buf() for _ in range(NW)]
        wfm_ds = [k.dsem(f"wfm{i}") for i in range(NW)]
        wtm = [sc.sb("wtm", [128, KC, 512], BF16) for _ in range(2)]
        wtm_b = [Buf() for _ in range(2)]
        wtm_ds = [k.dsem(f"wtm{i}") for i in range(2)]
        NX = 2
        xc = [sc.sb("xc", [128, T + 4], BF16) for _ in range(NX)]
        xc_b = [[Buf() for _ in blocks] for _ in range(NX)]
        xc_pad = [Buf() for _ in range(NX)]
        ost = [sc.sb("ost", [128, T], BF16) for _ in range(NX)]
        ost_b = [[Buf() for _ in blocks] for _ in range(NX)]
        ost_ds = [k.dsem(f"ost{i}") for i in range(NX)]
        dg = [sc.sb("dg", [128, 4, 128], BF16) for _ in range(NX)]
        dg_b = [Buf() for _ in range(NX)]
        ropec = sc.sb("ropec", [128, T], F32)
        ropes = sc.sb("ropes", [128, T], F32)
        b_rope = Buf()
        b_rope2 = Buf()
        ds_rope = k.dsem("rope")
        k.dma(k.sp, ds_rope, ropec[:], self.c_ropec[:, :], writes=[b_rope])
        k.dma(k.sp, ds_rope, ropes[:], self.c_ropes[:, :], writes=[b_rope2])
        rt1 = [sc.sb("rt1", [128, 512], F32) for _ in range(2)]
        rt1_b = [Buf() for _ in range(2)]
        rt2 = [sc.sb("rt2", [128, 512], F32) for _ in range(2)]
        rt2_b = [Buf() for _ in range(2)]
        NPS = 3
        ps = [sc.ps("p1a", [128, 512], F32) for _ in range(NPS)]
        ps_b = [Buf() for _ in range(NPS)]
        ps2 = [sc.ps("p1b", [128, 512], F32) for _ in range(2)]
        ps2_b = [Buf() for _ in range(2)]
        tst = [sc.sb("tst", [128, 512], BF16) for _ in range(3)]
        tst_b = [Buf() for _ in range(3)]
        tst_ds = [k.dsem(f"tst{i}") for i in range(3)]
        dtall = sc.sb("dtall", [128, NT, 32], F32)
        dtall_b = Buf()
        for i in range(NX):
            k.op(k.pool, lambda: nc.gpsimd.memset(xc[i][:, 0:3], 0.0), writes=[xc_pad[i]])
        cnt = {"w": 0, "ps": 0, "ps2": 0, "x": 0, "r": 0, "t": 0, "wt": 0}

        def load_wfm(col0):
            s = cnt["w"] % NW
            cnt["w"] += 1
            src = W[:, col0:col0 + 128].rearrange("(kk p) c -> p kk c", p=128)
            k.dma(k.pool, wfm_ds[s], wfm[s][:], src, writes=[wfm_b[s]])
            return s

        def proj_block(ws, t0, n):
            pi = cnt["ps"] % NPS
            cnt["ps"] += 1
            for kk in range(KC):
                k.op(k.pe, lambda: nc.tensor.matmul(ps[pi][:, :n], lhsT=wfm[ws][:, kk, :], rhs=UT[:, kk, t0:t0 + n],
                                                    start=(kk == 0), stop=(kk == KC - 1)),
                     reads=[wfm_b[ws]] + ut_bufs[t0 // 128:(t0 + n) // 128], writes=[ps_b[pi]], signal=(kk == KC - 1))
            return pi

        def store(xs, dst, c):
            k.dma(k.sp, ost_ds[xs], dst[c * 128:(c + 1) * 128, :], ost[xs][:, :], reads=ost_b[xs])

        def gate_chunk(col0, dst, c):
            ws = load_wfm(col0)
            xs = cnt["x"] % NX
            cnt["x"] += 1
            for bi, (t0, n) in enumerate(blocks):
                pi = proj_block(ws, t0, n)
                k.op(k.act, lambda: nc.scalar.activation(out=ost[xs][:, t0:t0 + n], in_=ps[pi][:, :n], func=AF.Sigmoid),
                     reads=[ps_b[pi]], writes=[ost_b[xs][bi]])
            store(xs, dst, c)

        def xbc_chunk(c):
            ws = load_wfm(1024 + c * 128)
            xs = cnt["x"] % NX
            cnt["x"] += 1
            cw = pp[:, PP["cw"] + c * 4:PP["cw"] + c * 4 + 4]
            k.op(k.dve, lambda: nc.vector.tensor_tensor(out=dg[xs][:], in0=bcast_free(self.ident[:], 0, 4), in1=bcast_free(cw, 1, 128), op=ALU.mult),
                 reads=[self.b_const, self.b_pp], writes=[dg_b[xs]])
            for bi, (t0, n) in enumerate(blocks):
                pi = proj_block(ws, t0, n)
                k.op(k.dve, lambda: nc.vector.tensor_copy(out=xc[xs][:, 3 + t0:3 + t0 + n], in_=ps[pi][:, :n]),
                     reads=[ps_b[pi]], writes=[xc_b[xs][bi]])
                qi = cnt["ps2"] % 2
                cnt["ps2"] += 1
                rd = [dg_b[xs], xc_b[xs][bi], xc_pad[xs]] + ([xc_b[xs][bi - 1]] if bi > 0 else [])
                for j in range(4):
                    k.op(k.pe, lambda: nc.tensor.matmul(ps2[qi][:, :n], lhsT=dg[xs][:, j, :], rhs=xc[xs][:, t0 + j:t0 + j + n],
                                                        start=(j == 0), stop=(j == 3)),
                         reads=rd, writes=[ps2_b[qi]], signal=(j == 3))
                k.op(k.act, lambda: nc.scalar.activation(out=ost[xs][:, t0:t0 + n], in_=ps2[qi][:, :n], func=AF.Silu,
                                                         bias=pp[:, PP["cb"] + c:PP["cb"] + c + 1]),
                     reads=[ps2_b[qi], self.b_pp], writes=[ost_b[xs][bi]])
            store(xs, self.XBC, c)

        def rope_chunk(col0, dst, c):
            ws = load_wfm(col0)
            xs = cnt["x"] % NX
            cnt["x"] += 1
            for bi, (t0, n) in enumerate(blocks):
                pi = proj_block(ws, t0, n)
                k.op(k.act, lambda: nc.scalar.copy(out=xc[xs][:, t0:t0 + n], in_=ps[pi][:, :n]),
                     reads=[ps_b[pi]], writes=[xc_b[xs][bi]])
                qi = cnt["ps2"] % 2
                cnt["ps2"] += 1
                k.op(k.pe, lambda: nc.tensor.matmul(ps2[qi][:, :n], lhsT=self.perm[:], rhs=xc[xs][:, t0:t0 + n], start=True, stop=True),
                     reads=[self.b_const, xc_b[xs][bi]], writes=[ps2_b[qi]])
                ri = cnt["r"] % 2
                cnt["r"] += 1
                k.op(k.dve, lambda: nc.vector.tensor_tensor(out=rt1[ri][:, :n], in0=ps2[qi][:, :n], in1=ropes[:, t0:t0 + n], op=ALU.mult),
                     reads=[ps2_b[qi], b_rope2], writes=[rt1_b[ri]])
                k.op(k.pool, lambda: nc.gpsimd.tensor_tensor(out=rt2[ri][:, :n], in0=xc[xs][:, t0:t0 + n], in1=ropec[:, t0:t0 + n], op=ALU.mult),
                     reads=[xc_b[xs][bi], b_rope], writes=[rt2_b[ri]])
                k.op(k.dve, lambda: nc.vector.tensor_tensor(out=ost[xs][:, t0:t0 + n], in0=rt1[ri][:, :n], in1=rt2[ri][:, :n], op=ALU.add),
                     reads=[rt1_b[ri], rt2_b[ri]], writes=[ost_b[xs][bi]])
            store(xs, dst, c)

        def tm_group(col0, ncol, kind, dst, dcol0):
            s = cnt["wt"] % 2
            cnt["wt"] += 1
            src = W[:, col0:col0 + ncol].rearrange("(kk p) c -> p kk c", p=128)
            k.dma(k.pool, wtm_ds[s], wtm[s][:, :, :ncol], src, writes=[wtm_b[s]])
            for t in range(NT):
                pi = cnt["ps"] % NPS
                cnt["ps"] += 1
                for kk in range(KC):
                    k.op(k.pe, lambda: nc.tensor.matmul(ps[pi][:, :ncol], lhsT=UT[:, kk, t * 128:(t + 1) * 128], rhs=wtm[s][:, kk, :ncol],
                                                        start=(kk == 0), stop=(kk == KC - 1)),
                         reads=[wtm_b[s], ut_bufs[t]], writes=[ps_b[pi]], signal=(kk == KC - 1))
                if kind == "dt":
                    k.op(k.dve, lambda: nc.vector.tensor_tensor(out=dtall[:, t, 0:16], in0=ps[pi][:, :16], in1=pp[:, PP["dtb"]:PP["dtb"] + 16], op=ALU.add),
                         reads=[ps_b[pi], self.b_pp], writes=[dtall_b])
                    continue
                ti = cnt["t"] % 3
                cnt["t"] += 1
                if kind == "z":
                    k.op(k.act, lambda: nc.scalar.activation(out=tst[ti][:, :ncol], in_=ps[pi][:, :ncol], func=AF.Silu),
                         reads=[ps_b[pi]], writes=[tst_b[ti]])
                else:
                    k.op(k.dve, lambda: nc.vector.tensor_copy(out=tst[ti][:, :ncol], in_=ps[pi][:, :ncol]),
                         reads=[ps_b[pi]], writes=[tst_b[ti]])
                k.dma(k.sp, tst_ds[ti], dst[t * 128:(t + 1) * 128, dcol0:dcol0 + ncol], tst[ti][:, :ncol], reads=[tst_b[ti]])

        for g in range(2):
            tm_group(g * 512, 512, "z", self.ZS, g * 512)
        for c in range(16):
            xbc_chunk(c)
        for c in range(8):
            gate_chunk(6160 + c * 128, self.GS, c)
        for c in range(8):
            gate_chunk(7184 + c * 128, self.GA, c)
        for c in range(8):
            rope_chunk(3088 + c * 128, self.QT, c)
        for c in range(8):
            rope_chunk(4112 + c * 128, self.KT, c)
        for g in range(2):
            tm_group(5136 + g * 512, 512, "v", self.V, g * 512)
        tm_group(3072, 16, "dt", None, 0)
        dtf = dtall[:, :, 0:16]
        ex = sc.sb("dtex", [128, NT, 16], F32)
        bex = Buf()
        na = sc.sb("nega", [128, 16], F32)
        bna = Buf()
        k.op(k.act, lambda: nc.scalar.activation(out=ex[:], in_=dtf, func=AF.Exp), reads=[dtall_b], writes=[bex])
        k.op(k.act, lambda: nc.scalar.activation(out=dtf, in_=ex[:], func=AF.Ln, bias=1.0), reads=[bex], writes=[dtall_b])
        k.op(k.act, lambda: nc.scalar.activation(out=na[:], in_=pp[:, PP["alog"]:PP["alog"] + 16], func=AF.Exp), reads=[self.b_pp], writes=[bna])
        k.op(k.dve, lambda: nc.vector.scalar_tensor_tensor(out=dtall[:, :, 16:32], in0=dtf, scalar=-1.0, in1=bcast_free(na[:], 0, NT),
                                                           op0=ALU.mult, op1=ALU.mult), reads=[dtall_b, bna], writes=[dtall_b])
        ds_dt = k.dsem("dtst")
        k.dma(k.sp, ds_dt, self.DT.rearrange("(t p) c -> p t c", p=128), dtall[:], reads=[dtall_b])


def make_in_map(cfg, inp, b):
    f = lambda a: np.ascontiguousarray(np.asarray(a, np.float32))
    L = cfg.L
    m = {
        "x": f(inp["x"][b]), "meta": f(inp["meta_tokens"]),
        "w_in": f(inp["w_in"][:L]), "w_sb": f(inp["w_ssd_branch"][:L]), "w_ab": f(inp["w_attn_branch"][:L]),
        "w_out": f(inp["w_out"][:L]), "w_up": f(inp["w_up"][:L]), "w_down": f(inp["w_down"][:L]),
        "pp": np.stack([pack_params(inp, l) for l in range(L)]),
        "fng": f(np.broadcast_to(np.asarray(inp["final_norm_g"], np.float32)[None], (128, D))),
    }
    m.update(make_consts(cfg))
    return m


def _phase2(self, l):
    import os
    cfg, k, nc = self.cfg, self.k, self.nc
    NT = cfg.NT
    pp = self.pp
    sc = Scope(k)
    tri = self.tri3[:, 0, :]
    upp = self.tri3[:, 1, :]

    def dbl(name, shape, dt, n=2):
        return [sc.sb(name, shape, dt) for _ in range(n)], [Buf(name) for _ in range(n)]
    xbT, xbT_b = dbl("xbT", [128, 16, 128], BF16)
    zs, zs_b = dbl("zs", [128, D], BF16)
    dta, dta_b = dbl("dta", [128, 32], F32)
    ld_ds = [[k.dsem(f"p2ld{i}_{a}") for a in range(3)] for i in range(2)]
    xtm, xtm_b = dbl("xtm", [128, 16, 64], BF16)
    btm, btm_b = dbl("btm", [128, 4, 128], BF16)
    E, E_b = dbl("E", [128, 48], F32)
    w1, w1_b = dbl("w1", [128, 16], F32)
    Lh, Lh_b = dbl("Lh", [128, 16, 128], F32)
    Lh2_b = [Buf() for _ in range(2)]
    Dg, Dg_b = dbl("Dg", [128, 4, 128], BF16)
    cbm, cbm_b = dbl("cbm", [128, 4, 128], BF16)
    Mg, Mg_b = dbl("Mg", [128, 4, 128], BF16)
    xdt, xdt_b = dbl("xdt", [128, 16, 64], BF16)
    xdd, xdd_b = dbl("xdd", [128, 16, 64], BF16)
    t1, t1_b = dbl("t1", [128, 256], F32)
    ysb, ysb_b = dbl("ysb", [128, D], F32)
    xD, xD_b = dbl("xD", [128, D], F32)
    junk = sc.sb("p2junk", [128, 256], BF16)
    junk_b = Buf()
    st, st_b = dbl("p2st", [128, 12], F32)
    yn, yn_b = dbl("yn", [128, D], BF16)
    yn_gb = [[Buf() for _ in range(4)] for _ in range(2)]
    yT, yT_b = dbl("yT", [128, KC, 128], BF16)
    yT_ds = [k.dsem(f"p2st{i}") for i in range(2)]
    Sf = sc.sb("Sf", [128, D], F32)
    Sf_b = [Buf() for _ in range(4)]
    Sbf, Sbf_b = dbl("Sbf", [128, D], BF16)
    Sbf_gb = [[Buf() for _ in range(4)] for _ in range(2)]
    mhalf = sc.sb("mhalf2", [128, 1], F32)
    bmh = Buf()
    k.op(k.pool, lambda: nc.gpsimd.memset(mhalf[:], -0.5), writes=[bmh])
    k.op(k.pool, lambda: nc.gpsimd.memset(Sf[:], 0.0), writes=Sf_b)
    k.op(k.pool, lambda: nc.gpsimd.memset(Sbf[1][:], 0.0), writes=Sbf_gb[1])
    ptx = sc.ps("ptx", [128, KC, 128], BF16)
    ptx_b = Buf()
    misc = sc.ps("misc", [128, 512], F32)
    ptb = misc.bitcast(BF16)
    ptb_b = Buf()

    cb = sc.ps("cb", [128, 4, 128], F32)
    cb_b = Buf()
    seg = [sc.ps("seg", [128, 4, 128], F32) for _ in range(2)]
    seg_b = [Buf() for _ in range(2)]
    ydo = sc.ps("ydo", [128, 512], F32)
    yd_b, yo_b = Buf(), Buf()
    sne = sc.ps("sne", [128, 512], F32)
    sn = sne[:, 0:256]
    e3 = sne[:, 256:304]
    sn_b = Buf()
    pty = sc.ps("pty", [128, KC, 128], BF16)
    pty_b = Buf()
    segc = 0

    LV = int(os.environ.get("P2LV", "99"))
    for i in range(NT):
        j = i % 2
        t0 = i * 128
        k.dma(k.sp, ld_ds[j][0], xbT[j][:], self.XBC[:, t0:t0 + 128].rearrange("(c p) t -> p c t", p=128), writes=[xbT_b[j]])
        k.dma(k.sp, ld_ds[j][1], zs[j][:], self.ZS[t0:t0 + 128, :], writes=[zs_b[j]])
        k.dma(k.sp, ld_ds[j][2], dta[j][:], self.DT[t0:t0 + 128, :], writes=[dta_b[j]])
        a = dta[j][:, 16:32]
        dt = dta[j][:, 0:16]
        for c in range(KC):
            k.op(k.pe, lambda: nc.tensor.transpose(out=ptx[:, c, :], in_=xbT[j][:, c, :], identity=self.ident[:]),
                 reads=[xbT_b[j], self.b_const], writes=[ptx_b], signal=(c == KC - 1))
        for g in range(4):
            k.op(k.pe, lambda: nc.tensor.transpose(out=ptb[:, g * 128:(g + 1) * 128], in_=xbT[j][:, 8 + g, :], identity=self.ident[:]),
                 reads=[xbT_b[j], self.b_const], writes=[ptb_b], signal=(g == 3))
        k.op(k.act, lambda: nc.scalar.copy(out=xtm[j][:].rearrange("p h d -> p (h d)"), in_=ptx[:].rearrange("p c t -> p (c t)")),
             reads=[ptx_b], writes=[xtm_b[j]])
        if LV <= 0:
            continue
        for q in range(3):
            k.op(k.pe, lambda: nc.tensor.matmul(e3[:, q * 16:(q + 1) * 16], lhsT=self.tri3[:, q, :], rhs=a, start=True, stop=True),
                 reads=[dta_b[j], self.b_const], writes=[sn_b], signal=(q == 2))
        SUB = int(os.environ.get("P2SUB", "9"))
        if SUB <= 0:
            continue
        k.op(k.dve, lambda: nc.vector.tensor_copy(out=btm[j][:].rearrange("p g n -> p (g n)"), in_=ptb[:, 0:512]),
             reads=[ptb_b], writes=[btm_b[j]])
        if SUB <= 1:
            continue
        k.op(k.act, lambda: nc.scalar.activation(out=E[j][:], in_=e3, func=AF.Exp), reads=[sn_b], writes=[E_b[j]])
        if LV <= 1:
            continue
        k.op(k.dve, lambda: nc.vector.tensor_tensor(out=w1[j][:], in0=dt, in1=E[j][:, 16:32], op=ALU.mult),
             reads=[dta_b[j], E_b[j]], writes=[w1_b[j]])
        k.op(k.dve, lambda: nc.vector.tensor_tensor(out=Lh[j][:, 0:8, :], in0=bcast_free(upp, 0, 8), in1=bcast_free(a[:, 0:8], 1, 128), op=ALU.mult),
             reads=[dta_b[j], self.b_const], writes=[Lh_b[j]])
        k.op(k.pool, lambda: nc.gpsimd.tensor_tensor(out=Lh[j][:, 8:16, :], in0=bcast_free(upp, 0, 8), in1=bcast_free(a[:, 8:16], 1, 128), op=ALU.mult),
             reads=[dta_b[j], self.b_const], writes=[Lh2_b[j]])
        k.op(k.pool, lambda: nc.gpsimd.tensor_tensor(out=xdt[j][:], in0=xtm[j][:], in1=bcast_free(dt, 1, 64), op=ALU.mult),
             reads=[xtm_b[j], dta_b[j]], writes=[xdt_b[j]])
        k.op(k.pool, lambda: nc.gpsimd.tensor_tensor(out=xdd[j][:], in0=xtm[j][:], in1=bcast_free(w1[j][:], 1, 64), op=ALU.mult),
             reads=[xtm_b[j], w1_b[j]], writes=[xdd_b[j]])
        k.op(k.pool, lambda: nc.gpsimd.tensor_tensor(out=xD[j][:].rearrange("p (h d) -> p h d", d=64), in0=xtm[j][:],
                                                     in1=bcast_free(pp[:, PP["dsk"]:PP["dsk"] + 16], 1, 64), op=ALU.mult),
             reads=[xtm_b[j], self.b_pp], writes=[xD_b[j]])
        if LV <= 2:
            continue
        for g in range(4):
            k.op(k.pe, lambda: nc.tensor.matmul(cb[:, g, :], lhsT=xbT[j][:, 8 + g, :], rhs=xbT[j][:, 12 + g, :], start=True, stop=True),
                 reads=[xbT_b[j]], writes=[cb_b], signal=(g == 3))
        k.op(k.dve, lambda: nc.vector.tensor_tensor(out=cbm[j][:], in0=cb[:], in1=bcast_free(tri, 0, 4), op=ALU.mult),
             reads=[cb_b, self.b_const], writes=[cbm_b[j]])
        if LV <= 3:
            continue
        sprev, snew = Sbf[(i + 1) % 2], Sbf[i % 2]
        sprev_gb, snew_gb = Sbf_gb[(i + 1) % 2], Sbf_gb[i % 2]
        for g in range(4):
            sg = segc % 2
            segc += 1
            for hh in range(4):
                k.op(k.pe, lambda: nc.tensor.matmul(seg[sg][:, hh, :], lhsT=Lh[j][:, g * 4 + hh, :], rhs=tri, start=True, stop=True),
                     reads=[Lh_b[j] if g < 2 else Lh2_b[j], self.b_const], writes=[seg_b[sg]], signal=(hh == 3))
            k.op(k.act, lambda: nc.scalar.activation(out=Dg[j][:], in_=seg[sg][:], func=AF.Exp), reads=[seg_b[sg]], writes=[Dg_b[j]])
            k.op(k.dve, lambda: nc.vector.tensor_tensor(out=Mg[j][:], in0=Dg[j][:], in1=bcast_free(cbm[j][:, g, :], 0, 4), op=ALU.mult),
                 reads=[Dg_b[j], cbm_b[j]], writes=[Mg_b[j]])
            for hh in range(4):
                k.op(k.pe, lambda: nc.tensor.matmul(ydo[:, hh * 64:(hh + 1) * 64], lhsT=Mg[j][:, hh, :], rhs=xdt[j][:, g * 4 + hh, :], start=True, stop=True),
                     reads=[Mg_b[j], xdt_b[j]], writes=[yd_b], signal=False)
            k.op(k.pe, lambda: nc.tensor.matmul(ydo[:, 256:512], lhsT=xbT[j][:, 12 + g, :], rhs=sprev[:, g * 256:(g + 1) * 256], start=True, stop=True),
                 reads=[xbT_b[j], sprev_gb[g]], writes=[yd_b])
            k.op(k.pe, lambda: nc.tensor.matmul(sn, lhsT=btm[j][:, g, :], rhs=xdd[j][:, g * 4:(g + 1) * 4, :].rearrange("p h d -> p (h d)"), start=True, stop=True),
                 reads=[btm_b[j], xdd_b[j]], writes=[sn_b])
            k.op(k.dve, lambda: nc.vector.tensor_tensor(out=t1[j][:].rearrange("p (h d) -> p h d", d=64), in0=ydo[:, 256:512].rearrange("p (h d) -> p h d", d=64),
                                                        in1=bcast_free(E[j][:, g * 4:(g + 1) * 4], 1, 64), op=ALU.mult),
                 reads=[yd_b, E_b[j]], writes=[t1_b[j]])
            k.op(k.dve, lambda: nc.vector.tensor_tensor(out=ysb[j][:, g * 256:(g + 1) * 256], in0=ydo[:, 0:256], in1=t1[j][:], op=ALU.add),
                 reads=[yd_b, t1_b[j]], writes=[ysb_b[j]])
            sfv = Sf[:, g * 256:(g + 1) * 256]
            k.op(k.dve, lambda: nc.vector.tensor_tensor(out=sfv.rearrange("p (h d) -> p h d", d=64), in0=sfv.rearrange("p (h d) -> p h d", d=64),
                                                        in1=bcast_free(E[j][:, 32 + g * 4:32 + (g + 1) * 4], 1, 64), op=ALU.mult),
                 reads=[E_b[j]], writes=[Sf_b[g]])
            k.op(k.dve, lambda: nc.vector.tensor_tensor(out=sfv, in0=sn, in1=sfv, op=ALU.add), reads=[sn_b], writes=[Sf_b[g]])
            k.op(k.act, lambda: nc.scalar.copy(out=snew[:, g * 256:(g + 1) * 256], in_=sfv), reads=[Sf_b[g]], writes=[snew_gb[g]])
        if LV <= 4:
            continue
        k.op(k.pool, lambda: nc.gpsimd.tensor_tensor(out=ysb[j][:], in0=ysb[j][:], in1=xD[j][:], op=ALU.add), reads=[xD_b[j]], writes=[ysb_b[j]])
        k.op(k.pool, lambda: nc.gpsimd.tensor_tensor(out=ysb[j][:], in0=ysb[j][:], in1=zs[j][:], op=ALU.mult), reads=[zs_b[j]], writes=[ysb_b[j]])
        for g in range(4):
            k.op(k.act, lambda: nc.scalar.activation(out=junk[:], in_=ysb[j][:, g * 256:(g + 1) * 256], func=AF.Square, accum_out=st[j][:, g:g + 1]),
                 reads=[ysb_b[j]], writes=[junk_b, st_b[j]])
        k.op(k.dve, lambda: nc.vector.tensor_scalar(out=st[j][:, 4:8], in0=st[j][:, 0:4], scalar1=1.0 / 256, scalar2=NORM_EPS, op0=ALU.mult, op1=ALU.add),
             reads=[st_b[j]], writes=[st_b[j]])
        k.op(k.pool, lambda: nc.gpsimd.tensor_tensor(out=st[j][:, 8:12], in0=st[j][:, 4:8], in1=bcast_free(mhalf[:, 0:1], 0, 4)[:, :, 0], op=ALU.pow),
             reads=[st_b[j], bmh], writes=[st_b[j]])
        for g in range(4):
            k.op(k.act, lambda: nc.scalar.activation(out=yn[j][:, g * 256:(g + 1) * 256], in_=ysb[j][:, g * 256:(g + 1) * 256], func=AF.Copy, scale=st[j][:, 8 + g:9 + g]),
                 reads=[ysb_b[j], st_b[j]], writes=[yn_gb[j][g]])
        for c in range(KC):
            k.op(k.pe, lambda: nc.tensor.transpose(out=pty[:, c, :], in_=yn[j][:, c * 128:(c + 1) * 128], identity=self.ident[:]),
                 reads=[yn_gb[j][c // 2], self.b_const], writes=[pty_b], signal=(c == KC - 1))
        k.op(k.dve, lambda: nc.vector.tensor_tensor(out=yT[j][:], in0=pty[:], in1=bcast_free(pp[:, PP["sng"]:PP["sng"] + KC], 1, 128), op=ALU.mult),
             reads=[pty_b, self.b_pp], writes=[yT_b[j]])
        k.dma(k.sp, yT_ds[j], self.YS[:, t0:t0 + 128].rearrange("(c p) t -> p c t", p=128), yT[j][:], reads=[yT_b[j]])
    sc.close()


Prog.phase2 = _phase2


def _phase3(self, l):
    import math
    cfg, k, nc = self.cfg, self.k, self.nc
    NT, T = cfg.NT, cfg.T
    pp = self.pp
    lam_init = 0.8 - 0.6 * math.exp(-0.3 * l)
    sc = Scope(k)
    qT = [sc.sb("qT", [128, T], BF16) for _ in range(2)]
    kT = [sc.sb("kT", [128, T], BF16) for _ in range(2)]
    Vh = [sc.sb("Vh", [128, NT, 130], BF16) for _ in range(2)]
    qkv_b = [Buf() for _ in range(2)]
    q_b = [Buf() for _ in range(2)]
    k_b = [Buf() for _ in range(2)]
    ones_b = [Buf() for _ in range(2)]
    ld_ds = [k.dsem(f"p3ld{i}") for i in range(2)]
    for i in range(2):
        k.op(k.pool, lambda: nc.gpsimd.memset(Vh[i][:, :, 128:130], 1.0), writes=[ones_b[i]])
    NPT = 3
    pt = [sc.sb("pt", [128, 2, 512], BF16) for _ in range(NPT)]
    pt_b = [Buf() for _ in range(NPT)]
    yst = [sc.sb("yst", [128, T], BF16) for _ in range(2)]
    yst_b = [[Buf() for _ in range(NT)] for _ in range(2)]
    yst_ds = [k.dsem(f"p3st{i}") for i in range(2)]
    sm = sc.sb("p3sm", [128, 16], F32)
    sm_b = Buf()
    fs = [sc.sb("p3fs", [128, 8], F32) for _ in range(2)]
    fs_b = [Buf() for _ in range(2)]
    ft = [sc.sb("p3ft", [128, 128], F32) for _ in range(2)]
    ft_b = [Buf() for _ in range(2)]
    fo = [sc.sb("p3fo", [128, 128], F32) for _ in range(2)]
    fo_b = [Buf() for _ in range(2)]
    fn = [sc.sb("p3fn", [128, 128], BF16) for _ in range(2)]
    fn_b = [Buf() for _ in range(2)]
    accS = [sc.sb("accS", [128, 3, 396], F32) for _ in range(2)]
    accS_b = [[Buf() for _ in range(3)] for _ in range(2)]
    junk = sc.sb("p3junk", [128, 128], F32)
    junk_b = Buf()
    mhalf = sc.sb("mhalf3", [128, 1], F32)
    bmh = Buf()
    k.op(k.pool, lambda: nc.gpsimd.memset(mhalf[:], -0.5), writes=[bmh])
    spm = [sc.ps("spm", [128, 2, 512], F32) for _ in range(2)]
    spm_b = [Buf() for _ in range(2)]
    accb = [sc.ps("acc", [128, 512], F32) for _ in range(3)]
    acc_b = [Buf() for _ in range(3)]
    ptr = sc.ps("ptr", [128, 128], BF16)
    ptr_b = Buf()
    slots = {}
    idx = 0
    for jq in range(4):
        for m in range(2):
            slots[(jq, m)] = (idx // 3, (idx % 3) * 129)
            idx += 1

    lo = PP["lam"]
    k.op(k.dve, lambda: nc.vector.scalar_tensor_tensor(out=junk[:, 0:64], in0=pp[:, lo:lo + 64], scalar=1.0, in1=pp[:, lo + 64:lo + 128],
                                                       op0=ALU.mult, op1=ALU.mult, accum_out=sm[:, 0:1]), reads=[self.b_pp], writes=[junk_b, sm_b])
    k.op(k.dve, lambda: nc.vector.scalar_tensor_tensor(out=junk[:, 0:64], in0=pp[:, lo + 128:lo + 192], scalar=1.0, in1=pp[:, lo + 192:lo + 256],
                                                       op0=ALU.mult, op1=ALU.mult, accum_out=sm[:, 1:2]), reads=[self.b_pp], writes=[junk_b, sm_b])
    k.op(k.act, lambda: nc.scalar.activation(out=sm[:, 2:4], in_=sm[:, 0:2], func=AF.Exp), reads=[sm_b], writes=[sm_b])
    k.op(k.dve, lambda: nc.vector.tensor_tensor(out=sm[:, 4:5], in0=sm[:, 3:4], in1=sm[:, 2:3], op=ALU.subtract), reads=[sm_b], writes=[sm_b])
    k.op(k.dve, lambda: nc.vector.tensor_scalar(out=sm[:, 5:6], in0=sm[:, 4:5], scalar1=-lam_init, scalar2=None, op0=ALU.add), reads=[sm_b], writes=[sm_b])
    nlam = sm[:, 5:6]

    cnt = {"sp": 0, "pt": 0, "f": 0, "a": 0}
    qblocks = [list(range(q0, min(q0 + 4, NT))) for q0 in range(0, NT, 4)]
    for h in range(8):
        hs = h % 2
        k.dma(k.sp, ld_ds[hs], qT[hs][:], self.QT[h * 128:(h + 1) * 128, :], writes=[q_b[hs]])
        k.dma(k.sp, ld_ds[hs], kT[hs][:], self.KT[h * 128:(h + 1) * 128, :], writes=[k_b[hs]])
        k.dma(k.sp, ld_ds[hs], Vh[hs][:, :, 0:128], self.V[:, h * 128:(h + 1) * 128].rearrange("(t p) c -> p t c", p=128), writes=[qkv_b[hs]])
        for qts in qblocks:
            q0, nq = qts[0], len(qts)
            bank_started = [False, False, False]
            kt_max = min(qts[-1] + 1, NT - 1)
            for kt in range(kt_max + 1):
                jlo = max(0, kt - 1 - q0)
                c0, c1 = jlo * 128, nq * 128
                si = cnt["sp"] % 2
                cnt["sp"] += 1
                def qk_pair():
                    for m in range(2):
                        ins_ = nc.tensor.matmul(spm[si][:, m, c0:c1], lhsT=kT[hs][m * 64:(m + 1) * 64, kt * 128:(kt + 1) * 128],
                                                rhs=qT[hs][m * 64:(m + 1) * 64, q0 * 128 + c0:q0 * 128 + c1], start=True, stop=True)
                    return ins_
                k.op(k.pe, qk_pair, reads=[q_b[hs], k_b[hs]], writes=[spm_b[si]])
                pi = cnt["pt"] % NPT
                cnt["pt"] += 1
                k.op(k.act, lambda: nc.scalar.activation(out=pt[pi][:, :, c0:c1], in_=spm[si][:, :, c0:c1], func=AF.Exp, scale=0.125),
                     reads=[spm_b[si]], writes=[pt_b[pi]])
                for jq in range(jlo, nq):
                    qt = q0 + jq
                    which = 0 if kt == qt else (1 if kt == qt + 1 else None)
                    if which is not None:
                        k.op(k.dve, lambda: nc.vector.tensor_tensor(out=pt[pi][:, :, jq * 128:(jq + 1) * 128], in0=pt[pi][:, :, jq * 128:(jq + 1) * 128],
                                                                    in1=bcast_free(self.amask[:, which, :], 0, 2), op=ALU.mult),
                             reads=[self.b_const], writes=[pt_b[pi]])
                for jq in range(jlo, nq):
                    last_kt = min(q0 + jq + 1, NT - 1)
                    for m in range(2):
                        bk, co = slots[(jq, m)]
                        st_flag = not bank_started[bk]
                        bank_started[bk] = True
                        k.op(k.pe, lambda: nc.tensor.matmul(accb[bk][:, co:co + 129], lhsT=pt[pi][:, m, jq * 128:(jq + 1) * 128], rhs=Vh[hs][:, kt, 0:129],
                                                            start=st_flag, stop=(kt == last_kt), skip_group_check=True),
                             reads=[pt_b[pi], qkv_b[hs], ones_b[hs]], writes=[acc_b[bk]], signal=(kt == last_kt and m == 1))
            ai = cnt["a"] % 2
            cnt["a"] += 1
            nbk = (2 * nq + 2) // 3
            for bk in range(nbk):
                ncol = 129 * min(3, 2 * nq - 3 * bk)
                k.op(k.dve, lambda: nc.vector.tensor_copy(out=accS[ai][:, bk, 0:ncol], in_=accb[bk][:, 0:ncol]), reads=[acc_b[bk]], writes=[accS_b[ai][bk]])
            for jq in range(nq):
                qt = q0 + jq
                fi = cnt["f"] % 2
                cnt["f"] += 1
                b0, o0 = slots[(jq, 0)]
                b1, o1 = slots[(jq, 1)]
                k.op(k.dve, lambda: nc.vector.reciprocal(out=fs[fi][:, 0:1], in_=accS[ai][:, b0, o0 + 128:o0 + 129]), reads=[accS_b[ai][b0]], writes=[fs_b[fi]])
                k.op(k.dve, lambda: nc.vector.reciprocal(out=fs[fi][:, 1:2], in_=accS[ai][:, b1, o1 + 128:o1 + 129]), reads=[accS_b[ai][b1]], writes=[fs_b[fi]])
                k.op(k.dve, lambda: nc.vector.tensor_tensor(out=fs[fi][:, 2:3], in0=fs[fi][:, 1:2], in1=nlam, op=ALU.mult), reads=[sm_b], writes=[fs_b[fi]])
                k.op(k.dve, lambda: nc.vector.tensor_scalar(out=ft[fi][:], in0=accS[ai][:, b1, o1:o1 + 128], scalar1=fs[fi][:, 2:3], scalar2=None, op0=ALU.mult),
                     reads=[accS_b[ai][b1], fs_b[fi]], writes=[ft_b[fi]])
                k.op(k.dve, lambda: nc.vector.scalar_tensor_tensor(out=fo[fi][:], in0=accS[ai][:, b0, o0:o0 + 128], scalar=fs[fi][:, 0:1], in1=ft[fi][:],
                                                                   op0=ALU.mult, op1=ALU.add), reads=[accS_b[ai][b0], fs_b[fi], ft_b[fi]], writes=[fo_b[fi]])
                k.op(k.dve, lambda: nc.vector.scalar_tensor_tensor(out=junk[:], in0=fo[fi][:], scalar=1.0, in1=fo[fi][:], op0=ALU.mult, op1=ALU.mult,
                                                                   accum_out=fs[fi][:, 3:4]), reads=[fo_b[fi]], writes=[junk_b, fs_b[fi]])
                k.op(k.dve, lambda: nc.vector.tensor_scalar(out=fs[fi][:, 4:5], in0=fs[fi][:, 3:4], scalar1=1.0 / 128, scalar2=SUBLN_EPS, op0=ALU.mult, op1=ALU.add),
                     reads=[fs_b[fi]], writes=[fs_b[fi]])
                k.op(k.pool, lambda: nc.gpsimd.tensor_tensor(out=fs[fi][:, 5:6], in0=fs[fi][:, 4:5], in1=mhalf[:], op=ALU.pow), reads=[fs_b[fi], bmh], writes=[fs_b[fi]])
                k.op(k.act, lambda: nc.scalar.activation(out=fn[fi][:], in_=fo[fi][:], func=AF.Copy, scale=fs[fi][:, 5:6]), reads=[fo_b[fi], fs_b[fi]], writes=[fn_b[fi]])
                k.op(k.pe, lambda: nc.tensor.transpose(out=ptr[:], in_=fn[fi][:], identity=self.ident[:]), reads=[fn_b[fi], self.b_const], writes=[ptr_b])
                k.op(k.dve, lambda: nc.vector.tensor_scalar(out=yst[hs][:, qt * 128:(qt + 1) * 128], in0=ptr[:], scalar1=pp[:, PP["subln"]:PP["subln"] + 1],
                                                            scalar2=(1.0 - lam_init), op0=ALU.mult, op1=ALU.mult), reads=[ptr_b, self.b_pp], writes=[yst_b[hs][qt]])
        k.dma(k.sp, yst_ds[hs], self.YA[h * 128:(h + 1) * 128, :], yst[hs][:, :], reads=yst_b[hs])
    sc.close()


Prog.phase3 = _phase3


def _phase4(self, l, wts, wt_b):
    cfg, k, nc = self.cfg, self.k, self.nc
    sc = Scope(k)
    wsb, wab, wout = wts
    ins = [[sc.sb("p4in", [128, KC, 512], BF16) for _ in range(4)] for _ in range(3)]
    ins_b = [[Buf() for _ in range(4)] for _ in range(3)]
    ins_ds = [[k.dsem(f"p4in{i}_{a}") for a in range(4)] for i in range(3)]
    mg = [sc.sb("mg", [128, KC, 512], BF16) for _ in range(2)]
    mg_b = [[Buf() for _ in range(KC)] for _ in range(2)]
    tA = [sc.sb("p4ta", [128, 512], F32) for _ in range(2)]
    tA_b = [Buf() for _ in range(2)]
    tB = [sc.sb("p4tb", [128, 512], F32) for _ in range(2)]
    tB_b = [Buf() for _ in range(2)]
    NH = 6
    ht = [sc.sb("p4h", [128, D], F32) for _ in range(NH)]
    ht_b = [Buf() for _ in range(NH)]
    ht_ds = [k.dsem(f"p4h{i}") for i in range(NH)]
    ps1 = [sc.ps("p4a", [128, 512], F32) for _ in range(2)]
    ps1_b = [Buf() for _ in range(2)]
    ps2 = [sc.ps("p4b", [128, 512], F32) for _ in range(2)]
    ps2_b = [Buf() for _ in range(2)]
    ps3 = [sc.ps("p4c", [128, 512], F32) for _ in range(2)]
    ps3_b = [Buf() for _ in range(2)]
    c1 = c3 = ch = 0
    srcs = (self.YS, self.YA, self.GS, self.GA)
    for bi, (t0, n) in enumerate(cfg.blocks):
        s = bi % 2
        si = bi % 3
        for a in range(4):
            k.dma(k.sp, ins_ds[si][a], ins[si][a][:, :, :n], srcs[a][:, t0:t0 + n].rearrange("(c p) t -> p c t", p=128), writes=[ins_b[si][a]])
        ys, ya, gs, ga = ins[si]
        for c in range(KC):
            pi = c1 % 2
            c1 += 1
            for kk in range(KC):
                k.op(k.pe, lambda: nc.tensor.matmul(ps1[pi][:, :n], lhsT=wsb[:, kk, c * 128:(c + 1) * 128], rhs=ys[:, kk, :n], start=(kk == 0), stop=(kk == KC - 1)),
                     reads=[wt_b[0][kk], ins_b[si][0]], writes=[ps1_b[pi]], signal=(kk == KC - 1))
            for kk in range(KC):
                k.op(k.pe, lambda: nc.tensor.matmul(ps2[pi][:, :n], lhsT=wab[:, kk, c * 128:(c + 1) * 128], rhs=ya[:, kk, :n], start=(kk == 0), stop=(kk == KC - 1)),
                     reads=[wt_b[1][kk], ins_b[si][1]], writes=[ps2_b[pi]], signal=(kk == KC - 1))
            k.op(k.dve, lambda: nc.vector.tensor_tensor(out=tA[pi][:, :n], in0=ps1[pi][:, :n], in1=gs[:, c, :n], op=ALU.mult), reads=[ps1_b[pi], ins_b[si][2]], writes=[tA_b[pi]])
            k.op(k.dve, lambda: nc.vector.tensor_tensor(out=tB[pi][:, :n], in0=ps2[pi][:, :n], in1=ga[:, c, :n], op=ALU.mult), reads=[ps2_b[pi], ins_b[si][3]], writes=[tB_b[pi]])
            k.op(k.pool, lambda: nc.gpsimd.tensor_tensor(out=mg[s][:, c, :n], in0=tA[pi][:, :n], in1=tB[pi][:, :n], op=ALU.add), reads=[tA_b[pi], tB_b[pi]], writes=[mg_b[s][c]])
        for jq in range(n // 128):
            t = t0 // 128 + jq
            hi = ch % NH
            ch += 1
            k.dma(k.sp, ht_ds[hi], ht[hi][:], self.H[t * 128:(t + 1) * 128, :], writes=[ht_b[hi]])
            for half in range(2):
                pi = c3 % 2
                c3 += 1
                for kk in range(KC):
                    k.op(k.pe, lambda: nc.tensor.matmul(ps3[pi][:, :], lhsT=mg[s][:, kk, jq * 128:(jq + 1) * 128], rhs=wout[:, kk, half * 512:(half + 1) * 512],
                                                        start=(kk == 0), stop=(kk == KC - 1)),
                         reads=[wt_b[2][kk]] + mg_b[s], writes=[ps3_b[pi]], signal=(kk == KC - 1))
                k.op(k.dve, lambda: nc.vector.tensor_tensor(out=ht[hi][:, half * 512:(half + 1) * 512], in0=ps3[pi][:, :], in1=ht[hi][:, half * 512:(half + 1) * 512], op=ALU.add),
                     reads=[ps3_b[pi]], writes=[ht_b[hi]])
            k.dma(k.sp, ht_ds[hi], self.H[t * 128:(t + 1) * 128, :], ht[hi][:], reads=[ht_b[hi]])
    sc.close()


def _phase5(self, sc, l, UT, ut_bufs):
    cfg, k, nc = self.cfg, self.k, self.nc
    T, blocks = cfg.T, cfg.blocks
    pp = self.pp
    W = self.w_up[l]
    NW = 2
    wg = [sc.sb("wg", [128, 2, KC, 128], BF16) for _ in range(NW)]
    wg_b = [[Buf(), Buf()] for _ in range(NW)]
    wg_ds = [[k.dsem(f"p5w{i}_{a}") for a in range(2)] for i in range(NW)]
    gv = [sc.sb("gv", [128, 2, T + 4], BF16) for _ in range(2)]
    gv_b = [[[Buf() for _ in blocks] for _ in range(2)] for _ in range(2)]
    gv_pad = [Buf() for _ in range(2)]
    ost = [sc.sb("p5ost", [128, T], BF16) for _ in range(2)]
    ost_b = [[Buf() for _ in blocks] for _ in range(2)]
    ost_ds = [k.dsem(f"p5o{i}") for i in range(2)]
    dg = [sc.sb("p5dg", [128, 2, 3, 128], BF16) for _ in range(2)]
    dg_b = [Buf() for _ in range(2)]
    sg = [sc.sb("p5sg", [128, 512], F32) for _ in range(2)]
    sg_b = [Buf() for _ in range(2)]
    psA = [[sc.ps("p5a", [128, 512], F32) for _ in range(2)] for _ in range(2)]
    psA_b = [[Buf() for _ in range(2)] for _ in range(2)]
    psB = [[sc.ps("p5b", [128, 512], F32)] * 2 for _ in range(2)]
    psB_b = [[Buf()] * 2 for _ in range(2)]
    for i in range(2):
        k.op(k.pool, lambda: nc.gpsimd.memset(gv[i][:, :, 0:2], 0.0), writes=[gv_pad[i]])
    ca = cb_ = 0
    for j in range(DFF // 128):
        s = j % 2
        ws = j % NW
        for a, col0 in enumerate((j * 128, DFF + j * 128)):
            k.dma(k.pool, wg_ds[ws][a], wg[ws][:, a, :, :], W[:, col0:col0 + 128].rearrange("(kk p) c -> p kk c", p=128), writes=[wg_b[ws][a]])
        if j >= 1:
            wd_, wdb_, wds_ = self._wd_prefetch
            k.dma(k.pool, wds_, wd_[:, j - 1, :], self.w_down[l, (j - 1) * 128:j * 128, :], writes=[wdb_[j - 1]])
            if j == DFF // 128 - 1:
                k.dma(k.pool, wds_, wd_[:, j, :], self.w_down[l, j * 128:(j + 1) * 128, :], writes=[wdb_[j]])
        for a, cidx in enumerate((j, DFF // 128 + j)):
            cw = pp[:, PP["mcw"] + cidx * 3:PP["mcw"] + cidx * 3 + 3]
            k.op(k.dve, lambda: nc.vector.tensor_tensor(out=dg[s][:, a, :, :], in0=bcast_free(self.ident[:], 0, 3), in1=bcast_free(cw, 1, 128), op=ALU.mult),
                 reads=[self.b_const, self.b_pp], writes=[dg_b[s]])
        for bi, (t0, n) in enumerate(blocks):
            pa = ca % 2
            ca += 1
            for a in range(2):
                for kk in range(KC):
                    k.op(k.pe, lambda: nc.tensor.matmul(psA[a][pa][:, :n], lhsT=wg[ws][:, a, kk, :], rhs=UT[:, kk, t0:t0 + n], start=(kk == 0), stop=(kk == KC - 1)),
                         reads=[wg_b[ws][a]] + ut_bufs[t0 // 128:(t0 + n) // 128], writes=[psA_b[a][pa]], signal=(kk == KC - 1))
            k.op(k.act, lambda: nc.scalar.copy(out=gv[s][:, 0, 2 + t0:2 + t0 + n], in_=psA[0][pa][:, :n]), reads=[psA_b[0][pa]], writes=[gv_b[s][0][bi]])
            k.op(k.dve, lambda: nc.vector.tensor_copy(out=gv[s][:, 1, 2 + t0:2 + t0 + n], in_=psA[1][pa][:, :n]), reads=[psA_b[1][pa]], writes=[gv_b[s][1][bi]])
            pb = cb_ % 2
            cb_ += 1
            for a in range(2):
                rd = [dg_b[s], gv_b[s][a][bi], gv_pad[s]] + ([gv_b[s][a][bi - 1]] if bi > 0 else [])
                for jj in range(3):
                    k.op(k.pe, lambda: nc.tensor.matmul(psB[a][pb][:, :n], lhsT=dg[s][:, a, jj, :], rhs=gv[s][:, a, t0 + jj:t0 + jj + n], start=(jj == 0), stop=(jj == 2)),
                         reads=rd, writes=[psB_b[a][pb]], signal=(jj == 2))
            k.op(k.act, lambda: nc.scalar.activation(out=sg[pb][:, :n], in_=psB[0][pb][:, :n], func=AF.Silu, bias=pp[:, PP["mcb"] + j:PP["mcb"] + j + 1]),
                 reads=[psB_b[0][pb], self.b_pp], writes=[sg_b[pb]])
            vb = PP["mcb"] + DFF // 128 + j
            k.op(k.dve, lambda: nc.vector.scalar_tensor_tensor(out=ost[s][:, t0:t0 + n], in0=psB[1][pb][:, :n], scalar=pp[:, vb:vb + 1], in1=sg[pb][:, :n],
                                                               op0=ALU.add, op1=ALU.mult), reads=[psB_b[1][pb], sg_b[pb], self.b_pp], writes=[ost_b[s][bi]])
        k.dma(k.sp, ost_ds[s], self.AT[j * 128:(j + 1) * 128, :], ost[s][:, :], reads=ost_b[s])


def _phase6(self, l, wd, wd_b):
    cfg, k, nc = self.cfg, self.k, self.nc
    NC = DFF // 128
    sc = Scope(k)
    NB = 3
    at = [sc.sb("p6at", [128, NC, 128], BF16) for _ in range(NB)]
    at_b = [Buf() for _ in range(NB)]
    ht = [sc.sb("p6h", [128, D], F32) for _ in range(NB)]
    ht_b = [Buf() for _ in range(NB)]
    ds = [k.dsem(f"p6l{i}") for i in range(NB)]
    ds_at = [k.dsem(f"p6a{i}") for i in range(NB)]
    ps = [sc.ps("p6", [128, 512], F32) for _ in range(4)]
    ps_b = [Buf() for _ in range(4)]
    cp = 0
    for t in range(cfg.NT):
        i = t % NB
        k.dma(k.sp, ds_at[i], at[i][:], self.AT[:, t * 128:(t + 1) * 128].rearrange("(c p) t -> p c t", p=128), writes=[at_b[i]])
        k.dma(k.sp, ds[i], ht[i][:], self.H[t * 128:(t + 1) * 128, :], writes=[ht_b[i]])
        for half in range(2):
            pi = cp % 4
            cp += 1
            for c in range(NC):
                k.op(k.pe, lambda: nc.tensor.matmul(ps[pi][:, :], lhsT=at[i][:, c, :], rhs=wd[:, c, half * 512:(half + 1) * 512], start=(c == 0), stop=(c == NC - 1)),
                     reads=[at_b[i], wd_b[c]], writes=[ps_b[pi]], signal=(c == NC - 1))
            k.op(k.dve, lambda: nc.vector.tensor_tensor(out=ht[i][:, half * 512:(half + 1) * 512], in0=ps[pi][:, :], in1=ht[i][:, half * 512:(half + 1) * 512], op=ALU.add),
                 reads=[ps_b[pi]], writes=[ht_b[i]])
        k.dma(k.sp, ds[i], self.H[t * 128:(t + 1) * 128, :], ht[i][:], reads=[ht_b[i]])
    sc.close()


def _final_norm(self):
    cfg, k, nc = self.cfg, self.k, self.nc
    sc = Scope(k)
    fng = sc.sb("fng", [128, D], F32)
    fng_b = Buf()
    k.dma(k.sp, k.dsem("fng"), fng[:], self.fng_d[:, :], writes=[fng_b])
    NB = 3
    ht = [sc.sb("fh", [128, D], F32) for _ in range(NB)]
    hb = [Buf() for _ in range(NB)]
    hds = [k.dsem(f"fh{i}") for i in range(NB)]
    ot = [sc.sb("fo", [128, D], F32) for _ in range(NB)]
    ob = [Buf() for _ in range(NB)]
    ods = [k.dsem(f"fo{i}") for i in range(NB)]
    junk = sc.sb("fjunk", [128, D], BF16)
    bjunk = Buf()
    st = [sc.sb("fst", [128, 4], F32) for _ in range(2)]
    stb = [Buf() for _ in range(2)]
    mhalf = sc.sb("mhalff", [128, 1], F32)
    bmh = Buf()
    k.op(k.pool, lambda: nc.gpsimd.memset(mhalf[:], -0.5), writes=[bmh])
    for t in range(cfg.NT):
        i, j = t % NB, t % 2
        k.dma(k.sp, hds[i], ht[i][:], self.H[t * 128:(t + 1) * 128, :], writes=[hb[i]])
        k.op(k.dve, lambda: nc.vector.scalar_tensor_tensor(out=junk[:], in0=ht[i][:], scalar=1.0, in1=ht[i][:], op0=ALU.mult, op1=ALU.mult, accum_out=st[j][:, 0:1]),
             reads=[hb[i]], writes=[bjunk, stb[j]])
        k.op(k.dve, lambda: nc.vector.tensor_scalar(out=st[j][:, 1:2], in0=st[j][:, 0:1], scalar1=1.0 / D, scalar2=NORM_EPS, op0=ALU.mult, op1=ALU.add),
             reads=[stb[j]], writes=[stb[j]])
        k.op(k.pool, lambda: nc.gpsimd.tensor_tensor(out=st[j][:, 2:3], in0=st[j][:, 1:2], in1=mhalf[:], op=ALU.pow), reads=[stb[j], bmh], writes=[stb[j]])
        k.op(k.dve, lambda: nc.vector.scalar_tensor_tensor(out=ot[i][:], in0=ht[i][:], scalar=st[j][:, 2:3], in1=fng[:], op0=ALU.mult, op1=ALU.mult),
             reads=[hb[i], stb[j], fng_b], writes=[ob[i]])
        lo = max(t * 128, N_META)
        hi = min((t + 1) * 128, N_META + cfg.seq)
        if hi > lo:
            k.dma(k.sp, ods[i], self.out[lo - N_META:hi - N_META, :], ot[i][lo - t * 128:hi - t * 128, :], reads=[ob[i]])
    sc.close()


Prog.phase4 = _phase4
Prog.phase5 = _phase5
Prog.phase6 = _phase6
Prog.final_norm = _final_norm


_PROG_CACHE = {}


def kernel(x, meta_tokens, norm1_g, w_in, ssd_conv_w, ssd_conv_b, ssd_dt_bias, ssd_a_log, ssd_d,
           ssd_norm_g, lambda_q1, lambda_k1, lambda_q2, lambda_k2, attn_subln_g, w_ssd_branch,
           w_attn_branch, w_out, norm2_g, w_up, mlp_conv_w, mlp_conv_b, w_down, final_norm_g):
    inp = dict(x=x, meta_tokens=meta_tokens, norm1_g=norm1_g, w_in=w_in, ssd_conv_w=ssd_conv_w, ssd_conv_b=ssd_conv_b,
               ssd_dt_bias=ssd_dt_bias, ssd_a_log=ssd_a_log, ssd_d=ssd_d, ssd_norm_g=ssd_norm_g, lambda_q1=lambda_q1,
               lambda_k1=lambda_k1, lambda_q2=lambda_q2, lambda_k2=lambda_k2, attn_subln_g=attn_subln_g,
               w_ssd_branch=w_ssd_branch, w_attn_branch=w_attn_branch, w_out=w_out, norm2_g=norm2_g, w_up=w_up,
               mlp_conv_w=mlp_conv_w, mlp_conv_b=mlp_conv_b, w_down=w_down, final_norm_g=final_norm_g)
    inp = {k_: np.asarray(v) for k_, v in inp.items()}
    bsz, seq, _ = inp["x"].shape
    depth = inp["w_in"].shape[0]
    cfg = Cfg(seq=seq, depth=depth)
    key = (seq, depth)
    if key not in _PROG_CACHE:
        _PROG_CACHE[key] = Prog(cfg).build()
    nc = _PROG_CACHE[key]
    shared = make_in_map(cfg, inp, 0)
    in_maps = []
    for b in range(bsz):
        m = dict(shared)
        m["x"] = np.ascontiguousarray(inp["x"][b], dtype=np.float32)
        in_maps.append(m)
    res = run_bass_kernel_spmd(nc, in_maps, core_ids=list(range(bsz)))
    return np.stack([np.asarray(r["out"], dtype=np.float32) for r in res.results], axis=0)


def _p2_setup(self, l, sc):
    import os
    cfg, k, nc = self.cfg, self.k, self.nc
    NT = cfg.NT
    pp = self.pp
    tri = self.tri3[:, 0, :]
    upp = self.tri3[:, 1, :]

    def dbl(name, shape, dt, n=2):
        return [sc.sb(name, shape, dt) for _ in range(n)], [Buf(name) for _ in range(n)]
    def sgl(name, shape, dt):
        t_, b_ = sc.sb(name, shape, dt), Buf(name)
        return [t_, t_], [b_, b_]
    xbT, xbT_b = dbl("xbT", [128, 16, 128], BF16)
    zs, zs_b = dbl("zs", [128, D], BF16)
    dta, dta_b = dbl("dta", [128, 32], F32)
    ld_ds = [[k.dsem(f"p2ld{i}_{a}") for a in range(3)] for i in range(2)]
    xtm, xtm_b = dbl("xtm", [128, 16, 64], BF16)
    btm, btm_b = dbl("btm", [128, 4, 128], BF16)
    E, E_b = dbl("E", [128, 48], F32)
    w1, w1_b = dbl("w1", [128, 16], F32)
    Lh, Lh_b = sgl("Lh", [128, 16, 128], F32)
    Lh2_b = [Buf()] * 2
    Dg, Dg_b = dbl("Dg", [128, 4, 128], BF16)
    cbm, cbm_b = dbl("cbm", [128, 4, 128], BF16)
    Mg, Mg_b = dbl("Mg", [128, 4, 128], BF16)
    xdt, xdt_b = dbl("xdt", [128, 16, 64], BF16)
    xdd, xdd_b = dbl("xdd", [128, 16, 64], BF16)
    t1, t1_b = dbl("t1", [128, 256], F32)
    ysb, ysb_b = dbl("ysb", [128, D], F32)
    xD, xD_b = sgl("xD", [128, D], F32)
    junk = sc.sb("p2junk", [128, 256], BF16)
    junk_b = Buf()
    st, st_b = dbl("p2st", [128, 12], F32)
    yn, yn_b = sgl("yn", [128, D], BF16)
    Sf = sc.sb("Sf", [128, D], F32)
    Sf_b = [Buf() for _ in range(4)]
    Sbf, Sbf_b = dbl("Sbf", [128, D], BF16)
    Sbf_gb = [[Buf() for _ in range(4)] for _ in range(2)]
    mhalf = sc.sb("mhalf2", [128, 1], F32)
    bmh = Buf()
    k.op(k.pool, lambda: nc.gpsimd.memset(mhalf[:], -0.5), writes=[bmh])
    k.op(k.pool, lambda: nc.gpsimd.memset(Sf[:], 0.0), writes=Sf_b)
    k.op(k.pool, lambda: nc.gpsimd.memset(Sbf[1][:], 0.0), writes=Sbf_gb[1])
    pbf = sc.ps("pbf", [128, KC, 128], BF16)
    ptx = pbf
    ptb = pbf[:].rearrange("p c t -> p (c t)")
    ptx_b = ptb_b = Buf()
    pty = sc.ps("pty", [128, KC, 128], BF16)
    pty_b = Buf()

    cb = sc.ps("cbseg", [128, 4, 128], F32)
    cb_b = Buf()
    seg = [cb] * 2
    seg_b = [cb_b] * 2
    ydo = sc.ps("ydo", [128, 512], F32)
    yd_b, yo_b = Buf(), Buf()
    sne = sc.ps("sne", [128, 512], F32)
    sn = sne[:, 0:256]
    e3 = sne[:, 256:304]
    sn_b = Buf()
    segc = 0

    def chunk(i, ys_dst):
        nonlocal segc
        j = i % 2
        t0 = i * 128
        k.dma(k.sp, ld_ds[j][0], xbT[j][:], self.XBC[:, t0:t0 + 128].rearrange("(c p) t -> p c t", p=128), writes=[xbT_b[j]])
        k.dma(k.sp, ld_ds[j][1], zs[j][:], self.ZS[t0:t0 + 128, :], writes=[zs_b[j]])
        k.dma(k.sp, ld_ds[j][2], dta[j][:], self.DT[t0:t0 + 128, :], writes=[dta_b[j]])
        a = dta[j][:, 16:32]
        dt = dta[j][:, 0:16]
        for c in range(KC):
            k.op(k.pe, lambda: nc.tensor.transpose(out=ptx[:, c, :], in_=xbT[j][:, c, :], identity=self.ident[:]),
                 reads=[xbT_b[j], self.b_const], writes=[ptx_b], signal=(c == KC - 1))
        k.op(k.act, lambda: nc.scalar.copy(out=xtm[j][:].rearrange("p h d -> p (h d)"), in_=ptx[:].rearrange("p c t -> p (c t)")),
             reads=[ptx_b], writes=[xtm_b[j]])
        for g in range(4):
            k.op(k.pe, lambda: nc.tensor.transpose(out=ptb[:, g * 128:(g + 1) * 128], in_=xbT[j][:, 8 + g, :], identity=self.ident[:]),
                 reads=[xbT_b[j], self.b_const], writes=[ptb_b], signal=(g == 3))
        k.op(k.dve, lambda: nc.vector.tensor_copy(out=btm[j][:].rearrange("p g n -> p (g n)"), in_=ptb[:, 0:512]),
             reads=[ptb_b], writes=[btm_b[j]])
        for q in range(3):
            k.op(k.pe, lambda: nc.tensor.matmul(e3[:, q * 16:(q + 1) * 16], lhsT=self.tri3[:, q, :], rhs=a, start=True, stop=True),
                 reads=[dta_b[j], self.b_const], writes=[sn_b], signal=(q == 2))
        k.op(k.act, lambda: nc.scalar.activation(out=E[j][:], in_=e3, func=AF.Exp), reads=[sn_b], writes=[E_b[j]])
        k.op(k.dve, lambda: nc.vector.tensor_tensor(out=w1[j][:], in0=dt, in1=E[j][:, 16:32], op=ALU.mult),
             reads=[dta_b[j], E_b[j]], writes=[w1_b[j]])
        k.op(k.dve, lambda: nc.vector.tensor_tensor(out=Lh[j][:, 0:8, :], in0=bcast_free(upp, 0, 8), in1=bcast_free(a[:, 0:8], 1, 128), op=ALU.mult),
             reads=[dta_b[j], self.b_const], writes=[Lh_b[j]])
        k.op(k.pool, lambda: nc.gpsimd.tensor_tensor(out=Lh[j][:, 8:16, :], in0=bcast_free(upp, 0, 8), in1=bcast_free(a[:, 8:16], 1, 128), op=ALU.mult),
             reads=[dta_b[j], self.b_const], writes=[Lh2_b[j]])
        k.op(k.pool, lambda: nc.gpsimd.tensor_tensor(out=xdt[j][:], in0=xtm[j][:], in1=bcast_free(dt, 1, 64), op=ALU.mult),
             reads=[xtm_b[j], dta_b[j]], writes=[xdt_b[j]])
        k.op(k.pool, lambda: nc.gpsimd.tensor_tensor(out=xdd[j][:], in0=xtm[j][:], in1=bcast_free(w1[j][:], 1, 64), op=ALU.mult),
             reads=[xtm_b[j], w1_b[j]], writes=[xdd_b[j]])
        k.op(k.pool, lambda: nc.gpsimd.tensor_tensor(out=xD[j][:].rearrange("p (h d) -> p h d", d=64), in0=xtm[j][:],
                                                     in1=bcast_free(pp[:, PP["dsk"]:PP["dsk"] + 16], 1, 64), op=ALU.mult),
             reads=[xtm_b[j], self.b_pp], writes=[xD_b[j]])
        for g in range(4):
            k.op(k.pe, lambda: nc.tensor.matmul(cb[:, g, :], lhsT=xbT[j][:, 8 + g, :], rhs=xbT[j][:, 12 + g, :], start=True, stop=True),
                 reads=[xbT_b[j]], writes=[cb_b], signal=(g == 3))
        k.op(k.dve, lambda: nc.vector.tensor_tensor(out=cbm[j][:], in0=cb[:], in1=bcast_free(tri, 0, 4), op=ALU.mult),
             reads=[cb_b, self.b_const], writes=[cbm_b[j]])
        sprev, snew = Sbf[(i + 1) % 2], Sbf[i % 2]
        sprev_gb, snew_gb = Sbf_gb[(i + 1) % 2], Sbf_gb[i % 2]
        for g in range(4):
            sg = segc % 2
            segc += 1
            for hh in range(4):
                k.op(k.pe, lambda: nc.tensor.matmul(seg[sg][:, hh, :], lhsT=Lh[j][:, g * 4 + hh, :], rhs=tri, start=True, stop=True),
                     reads=[Lh_b[j] if g < 2 else Lh2_b[j], self.b_const], writes=[seg_b[sg]], signal=(hh == 3))
            k.op(k.act, lambda: nc.scalar.activation(out=Dg[j][:], in_=seg[sg][:], func=AF.Exp), reads=[seg_b[sg]], writes=[Dg_b[j]])
            k.op(k.dve, lambda: nc.vector.tensor_tensor(out=Mg[j][:], in0=Dg[j][:], in1=bcast_free(cbm[j][:, g, :], 0, 4), op=ALU.mult),
                 reads=[Dg_b[j], cbm_b[j]], writes=[Mg_b[j]])
            for hh in range(4):
                k.op(k.pe, lambda: nc.tensor.matmul(ydo[:, hh * 64:(hh + 1) * 64], lhsT=Mg[j][:, hh, :], rhs=xdt[j][:, g * 4 + hh, :], start=True, stop=True),
                     reads=[Mg_b[j], xdt_b[j]], writes=[yd_b], signal=False)
            k.op(k.pe, lambda: nc.tensor.matmul(ydo[:, 256:512], lhsT=xbT[j][:, 12 + g, :], rhs=sprev[:, g * 256:(g + 1) * 256], start=True, stop=True),
                 reads=[xbT_b[j], sprev_gb[g]], writes=[yd_b])
            k.op(k.pe, lambda: nc.tensor.matmul(sn, lhsT=btm[j][:, g, :], rhs=xdd[j][:, g * 4:(g + 1) * 4, :].rearrange("p h d -> p (h d)"), start=True, stop=True),
                 reads=[btm_b[j], xdd_b[j]], writes=[sn_b])
            k.op(k.dve, lambda: nc.vector.tensor_tensor(out=t1[j][:].rearrange("p (h d) -> p h d", d=64), in0=ydo[:, 256:512].rearrange("p (h d) -> p h d", d=64),
                                                        in1=bcast_free(E[j][:, g * 4:(g + 1) * 4], 1, 64), op=ALU.mult),
                 reads=[yd_b, E_b[j]], writes=[t1_b[j]])
            k.op(k.dve, lambda: nc.vector.tensor_tensor(out=ysb[j][:, g * 256:(g + 1) * 256], in0=ydo[:, 0:256], in1=t1[j][:], op=ALU.add),
                 reads=[yd_b, t1_b[j]], writes=[ysb_b[j]])
            sfv = Sf[:, g * 256:(g + 1) * 256]
            k.op(k.dve, lambda: nc.vector.tensor_tensor(out=sfv.rearrange("p (h d) -> p h d", d=64), in0=sfv.rearrange("p (h d) -> p h d", d=64),
                                                        in1=bcast_free(E[j][:, 32 + g * 4:32 + (g + 1) * 4], 1, 64), op=ALU.mult),
                 reads=[E_b[j]], writes=[Sf_b[g]])
            k.op(k.dve, lambda: nc.vector.tensor_tensor(out=sfv, in0=sn, in1=sfv, op=ALU.add), reads=[sn_b], writes=[Sf_b[g]])
            k.op(k.act, lambda: nc.scalar.copy(out=snew[:, g * 256:(g + 1) * 256], in_=sfv), reads=[Sf_b[g]], writes=[snew_gb[g]])
        k.op(k.pool, lambda: nc.gpsimd.tensor_tensor(out=ysb[j][:], in0=ysb[j][:], in1=xD[j][:], op=ALU.add), reads=[xD_b[j]], writes=[ysb_b[j]])
        k.op(k.dve, lambda: nc.vector.tensor_tensor(out=ysb[j][:], in0=ysb[j][:], in1=zs[j][:], op=ALU.mult), reads=[zs_b[j]], writes=[ysb_b[j]])
        for g in range(4):
            k.op(k.act, lambda: nc.scalar.activation(out=junk[:], in_=ysb[j][:, g * 256:(g + 1) * 256], func=AF.Square, accum_out=st[j][:, g:g + 1]),
                 reads=[ysb_b[j]], writes=[junk_b, st_b[j]])
        k.op(k.dve, lambda: nc.vector.tensor_scalar(out=st[j][:, 4:8], in0=st[j][:, 0:4], scalar1=1.0 / 256, scalar2=NORM_EPS, op0=ALU.mult, op1=ALU.add),
             reads=[st_b[j]], writes=[st_b[j]])
        k.op(k.pool, lambda: nc.gpsimd.tensor_tensor(out=st[j][:, 8:12], in0=st[j][:, 4:8], in1=bcast_free(mhalf[:, 0:1], 0, 4)[:, :, 0], op=ALU.pow),
             reads=[st_b[j], bmh], writes=[st_b[j]])
        k.op(k.dve, lambda: nc.vector.tensor_tensor(out=yn[j][:].rearrange("p (g d) -> p g d", d=256), in0=ysb[j][:].rearrange("p (g d) -> p g d", d=256),
                                                    in1=bcast_free(st[j][:, 8:12], 1, 256), op=ALU.mult),
             reads=[ysb_b[j], st_b[j]], writes=[yn_b[j]])
        for c in range(KC):
            k.op(k.pe, lambda: nc.tensor.transpose(out=pty[:, c, :], in_=yn[j][:, c * 128:(c + 1) * 128], identity=self.ident[:]),
                 reads=[yn_b[j], self.b_const], writes=[pty_b], signal=(c == KC - 1))
        ydst, ydst_b = ys_dst
        k.op(k.dve, lambda: nc.vector.tensor_tensor(out=ydst, in0=pty[:], in1=bcast_free(pp[:, PP["sng"]:PP["sng"] + KC], 1, 128), op=ALU.mult),
             reads=[pty_b, self.b_pp], writes=[ydst_b])
    return chunk


def _p4_setup(self, l, sc, wts, wt_b):
    cfg, k, nc = self.cfg, self.k, self.nc
    wsb, wab, wout = wts
    ins = [[sc.sb("p4in", [128, KC, 512], BF16) for _ in range(4)] for _ in range(2)]
    ins_b = [[Buf() for _ in range(4)] for _ in range(2)]
    ys_b = [[Buf() for _ in range(4)] for _ in range(2)]
    ins_ds = [k.dsem(f"p4in{i}") for i in range(2)]
    mg = [sc.sb("mg", [128, KC, 512], BF16)] * 2
    mg_b = [[Buf() for _ in range(KC)]] * 2
    tA = [sc.sb("p4ta", [128, 512], F32)] * 2
    tA_b = [Buf()] * 2
    tB = [sc.sb("p4tb", [128, 512], F32)] * 2
    tB_b = [Buf()] * 2
    NH = 2
    ht = [sc.sb("p4h", [128, D], F32) for _ in range(NH)]
    ht_b = [Buf() for _ in range(NH)]
    ht_ds = [k.dsem(f"p4h{i}") for i in range(NH)]
    ps1 = [sc.ps("p4a", [128, 512], F32)] * 2
    ps1_b = [Buf()] * 2
    ps2 = [sc.ps("p4b", [128, 512], F32)] * 2
    ps2_b = [Buf()] * 2
    ps3 = [sc.ps("p4c", [128, 512], F32)] * 2
    ps3_b = [Buf()] * 2
    c1 = c3 = ch = 0
    srcs = (self.YS, self.YA, self.GS, self.GA)

    def block(bi):
        nonlocal c1, c3, ch
        t0, n = cfg.blocks[bi]
        s = bi % 2
        for a in range(1, 4):
            k.dma(k.sp, ins_ds[s], ins[s][a][:, :, :n], srcs[a][:, t0:t0 + n].rearrange("(c p) t -> p c t", p=128), writes=[ins_b[s][a]])
        ys, ya, gs, ga = ins[s]
        for c in range(KC):
            pi = c1 % 2
            c1 += 1
            for kk in range(KC):
                k.op(k.pe, lambda: nc.tensor.matmul(ps1[pi][:, :n], lhsT=wsb[:, kk, c * 128:(c + 1) * 128], rhs=ys[:, kk, :n], start=(kk == 0), stop=(kk == KC - 1)),
                     reads=[wt_b[0][kk]] + ys_b[s][:n // 128], writes=[ps1_b[pi]], signal=(kk == KC - 1))
            for kk in range(KC):
                k.op(k.pe, lambda: nc.tensor.matmul(ps2[pi][:, :n], lhsT=wab[:, kk, c * 128:(c + 1) * 128], rhs=ya[:, kk, :n], start=(kk == 0), stop=(kk == KC - 1)),
                     reads=[wt_b[1][kk], ins_b[s][1]], writes=[ps2_b[pi]], signal=(kk == KC - 1))
            k.op(k.dve, lambda: nc.vector.tensor_tensor(out=tA[pi][:, :n], in0=ps1[pi][:, :n], in1=gs[:, c, :n], op=ALU.mult), reads=[ps1_b[pi], ins_b[s][2]], writes=[tA_b[pi]])
            k.op(k.dve, lambda: nc.vector.tensor_tensor(out=tB[pi][:, :n], in0=ps2[pi][:, :n], in1=ga[:, c, :n], op=ALU.mult), reads=[ps2_b[pi], ins_b[s][3]], writes=[tB_b[pi]])
            k.op(k.pool, lambda: nc.gpsimd.tensor_tensor(out=mg[s][:, c, :n], in0=tA[pi][:, :n], in1=tB[pi][:, :n], op=ALU.add), reads=[tA_b[pi], tB_b[pi]], writes=[mg_b[s][c]])
        for jq in range(n // 128):
            t = t0 // 128 + jq
            hi = ch % NH
            ch += 1
            k.dma(k.sp, ht_ds[hi], ht[hi][:], self.H[t * 128:(t + 1) * 128, :], writes=[ht_b[hi]])
            for half in range(2):
                pi = c3 % 2
                c3 += 1
                for kk in range(KC):
                    k.op(k.pe, lambda: nc.tensor.matmul(ps3[pi][:, :], lhsT=mg[s][:, kk, jq * 128:(jq + 1) * 128], rhs=wout[:, kk, half * 512:(half + 1) * 512],
                                                        start=(kk == 0), stop=(kk == KC - 1)),
                         reads=[wt_b[2][kk]] + mg_b[s], writes=[ps3_b[pi]], signal=(kk == KC - 1))
                k.op(k.dve, lambda: nc.vector.tensor_tensor(out=ht[hi][:, half * 512:(half + 1) * 512], in0=ps3[pi][:, :], in1=ht[hi][:, half * 512:(half + 1) * 512], op=ALU.add),
                     reads=[ps3_b[pi]], writes=[ht_b[hi]])
            k.dma(k.sp, ht_ds[hi], self.H[t * 128:(t + 1) * 128, :], ht[hi][:], reads=[ht_b[hi]])
    return block, ins, ys_b


def _phase24(self, l, wts, wt_b):
    cfg, k = self.cfg, self.k
    sc = Scope(k)
    chunk = _p2_setup(self, l, sc)
    block, ins, ys_b = _p4_setup(self, l, sc, wts, wt_b)
    for i in range(cfg.NT):
        bi, q = i // 4, i % 4
        s = bi % 2
        chunk(i, (ins[s][0][:, :, q * 128:(q + 1) * 128], ys_b[s][q]))
        if q == 3 or i == cfg.NT - 1:
            block(bi)
            if "YS" in cfg.debug_outs:
                t0, n = cfg.blocks[bi]
                k.dma(k.sp, k.dsem("dbgys"), self.YS[:, t0:t0 + n].rearrange("(c p) t -> p c t", p=128), ins[s][0][:, :, :n], reads=ys_b[s][:n // 128])
    sc.close()


Prog.phase24 = _phase24
```

```python
import numpy as np
import concourse.bass as bass
import concourse.mybir as mybir
from concourse.ap import AP
from concourse.bass_utils import run_bass_kernel_spmd

F32 = mybir.dt.float32
BF16 = mybir.dt.bfloat16
AF = mybir.ActivationFunctionType
ALU = mybir.AluOpType

D = 1024
KC = 8
NIN = 8208
DFF = 2816
NORM_EPS = 1e-6
SUBLN_EPS = 1e-5
N_META = 16


import types


def _freeze(fn):
    if fn.__closure__ is None:
        return fn
    cells = []
    for c in fn.__closure__:
        try:
            cells.append(types.CellType(c.cell_contents))
        except ValueError:
            cells.append(c)
    return types.FunctionType(fn.__code__, fn.__globals__, fn.__name__, fn.__defaults__, tuple(cells))


class Sem:
    def __init__(self, nc, name):
        self.h = nc.alloc_semaphore(name)
        self.total = 0
        self.waited = 0


class Buf:
    __slots__ = ("w", "r", "name")

    def __init__(self, name=""):
        self.w = []
        self.r = []
        self.name = name


class Op:
    __slots__ = ("eng", "fn", "isdma", "ds", "deps", "succ", "dur", "lat", "nun", "ready", "done", "sigval",
                 "needsig", "batch", "sched", "kw")


class Eng:
    def __init__(self, k, eng, name, is_pe=False, is_queue=False):
        self.k = k
        self.e = eng
        self.name = name
        self.sem = Sem(k.nc, "s_" + name)
        self.seen = {}
        self.is_pe = is_pe
        self.is_queue = is_queue

    def wait(self, toks):
        for s, v in toks.items():
            vv = s.total if v is None else v
            assert vv <= s.total, "waiting for a value that is never produced"
            if vv <= 0 or self.seen.get(s, 0) >= vv:
                continue
            self.e.wait_ge(s.h, vv)
            self.seen[s] = vv
            s.waited = max(s.waited, vv)


import os as _os
_SIG_LAT = 64.0
_WINDOW = int(_os.environ.get("K_WINDOW", "64"))
_WINDOW_Q = int(_os.environ.get("K_WINDOW_Q", "40"))


class K:
    def __init__(self, nc):
        self.nc = nc
        self.pe = Eng(self, nc.tensor, "pe", is_pe=True)
        self.act = Eng(self, nc.scalar, "act")
        self.dve = Eng(self, nc.vector, "dve")
        self.pool = Eng(self, nc.gpsimd, "pool")
        self.sp = Eng(self, nc.sync, "sp", is_queue=True)
        self.engs = [self.pe, self.act, self.dve, self.pool, self.sp]
        self.dsems = []
        self._dsem_cache = {}
        self.n_inst = 0
        self.batch = 0
        self.pending = []
        self.model_ns = 0.0
        self.flush_log = []

    def dsem(self, name):
        if name in self._dsem_cache:
            return self._dsem_cache[name]
        s = Sem(self.nc, "d_" + name)
        self.dsems.append(s)
        self._dsem_cache[name] = s
        return s

    def _record(self, o, reads, writes):
        bt = self.batch
        deps = {}
        for b in reads:
            for d in b.w:
                if d.batch == bt:
                    deps[id(d)] = d
        for b in writes:
            for d in b.w:
                if d.batch == bt:
                    deps[id(d)] = d
            for d in b.r:
                if d.batch == bt:
                    deps[id(d)] = d
        o.deps = list(deps.values())
        o.succ = []
        o.batch = bt
        o.sched = False
        o.sigval = 0
        for b in reads:
            b.r.append(o)
        for b in writes:
            b.w = [o]
            b.r = []
        self.pending.append(o)

    def _probe(self, fn):
        with self.nc.discard():
            inst = fn()
        ins = inst.ins
        n = 1
        ap = ins.outs[0].ap
        for st, cn in ap[1:]:
            n *= cn
        return n, ap[0][1], ins

    def op(self, eng, fn, reads=(), writes=(), signal=True):
        o = Op()
        o.eng = eng
        o.fn = _freeze(fn)
        o.isdma = False
        o.ds = None
        n, _, ins = self._probe(o.fn)
        if eng.is_pe:
            f32 = str(ins.ins[0].dtype).endswith("float32")
            o.dur = max(n, 64) / 2.0 * (4.0 if f32 else 1.0) + 8.0
        elif eng is self.act:
            o.dur = (n + 224) / 1.4
        elif eng is self.dve:
            o.dur = n / 0.96 + 62.0
        else:
            o.dur = n / 0.6 + 160.0
        o.lat = 0.0
        self._record(o, reads, writes)

    def dma(self, q, ds, out, in_, reads=(), writes=(), **kw):
        o = Op()
        o.eng = q
        o.isdma = True
        o.ds = ds
        o.fn = None
        o.kw = (out, in_, kw)
        nbytes = 1
        for st, cn in out.ap:
            nbytes *= cn
        nbytes *= 4 if out.dtype == F32 else 2
        o.dur = 60.0 if q.is_queue else 700.0
        o.lat = 2000.0 + nbytes / 120.0
        self._record(o, reads, writes)

    def flush(self):
        ops = self.pending
        self.pending = []
        if not ops:
            return
        lists = {e: [] for e in self.engs}
        for o in ops:
            o.nun = len(o.deps)
            o.ready = 0.0
            for d in o.deps:
                d.succ.append(o)
            lists[o.eng].append(o)
        head = {e: 0 for e in self.engs}
        free = {e: 0.0 for e in self.engs}
        cand = {e: None for e in self.engs}
        dirty = set(self.engs)
        order = []
        nleft = len(ops)
        while nleft:
            for e in dirty:
                lst = lists[e]
                h = head[e]
                while h < len(lst) and lst[h].sched:
                    h += 1
                head[e] = h
                best = None
                bt_ = 0.0
                fe = free[e]
                cnt = 0
                i = h
                win = _WINDOW_Q if (e.is_queue or e is self.pool) else _WINDOW
                while i < len(lst) and cnt < win:
                    o = lst[i]
                    i += 1
                    if o.sched:
                        continue
                    cnt += 1
                    if o.nun:
                        continue
                    if o.ready <= fe:
                        best, bt_ = o, fe
                        break
                    if best is None or o.ready < bt_:
                        best, bt_ = o, o.ready
                cand[e] = (best, bt_) if best is not None else None
            dirty.clear()
            pe_, pt_ = None, None
            for e in self.engs:
                c = cand[e]
                if c is not None and (pt_ is None or c[1] < pt_):
                    pe_, pt_ = e, c[1]
            o = cand[pe_][0]
            o.sched = True
            free[pe_] = pt_ + o.dur
            o.done = pt_ + o.dur + o.lat
            order.append(o)
            nleft -= 1
            dirty.add(pe_)
            dn = o.done + _SIG_LAT
            for s_ in o.succ:
                s_.nun -= 1
                if dn > s_.ready:
                    s_.ready = dn
                if s_.nun == 0:
                    dirty.add(s_.eng)
        self.model_ns += max(o.done for o in ops)
        self.flush_log.append((len(ops), max(o.done for o in ops), {e.name: sum(o.dur for o in ops if o.eng is e) for e in self.engs}))
        last = {}
        for i_, o in enumerate(order):
            o.nun = i_
            o.needsig = False
            if not o.isdma:
                last[o.eng] = o
        for o in order:
            per = {}
            for d in o.deps:
                if d.isdma or (d.eng is o.eng and o.eng.is_pe):
                    continue
                p_ = per.get(d.eng)
                if p_ is None or d.nun > p_.nun:
                    per[d.eng] = d
            for d in per.values():
                d.needsig = True
        for o in last.values():
            o.needsig = True
        for o in order:
            e = o.eng
            toks = {}
            for d in o.deps:
                if d.isdma:
                    toks[d.ds] = None
                elif d.eng is e and e.is_pe:
                    continue
                else:
                    sm = d.eng.sem
                    if toks.get(sm, 0) is not None and d.sigval > toks.get(sm, 0):
                        toks[sm] = d.sigval
            e.wait(toks)
            if o.isdma:
                ds = o.ds
                if ds.total > 0 and ds.waited >= ds.total:
                    e.wait({ds: ds.total})
                out, in_, kw = o.kw
                inst = e.e.dma_start(out=out, in_=in_, **kw)
                inst.then_inc(ds.h, 16)
                ds.total += 16
            else:
                inst = o.fn()
                if o.needsig:
                    e.sem.total += 1
                    inst.then_inc(e.sem.h, 1)
                    o.sigval = e.sem.total
            o.fn = None
            o.kw = None
            self.n_inst += 1

    def barrier(self):
        self.flush()
        self.batch += 1
        toks = {}
        for e in self.engs:
            if not e.is_queue:
                toks[e.sem] = e.sem.total
        for s in self.dsems:
            toks[s] = s.total
        for e in self.engs:
            t = dict(toks)
            if e.is_pe or e.is_queue:
                t.pop(e.sem, None)
            e.wait(t)


def bcast_free(ap, pos, n):
    a = [list(x) for x in ap.ap]
    a.insert(1 + pos, [0, n])
    return AP(ap.tensor, ap.offset, a)


class Cfg:
    def __init__(self, seq=4096, depth=4, debug_outs=(), stop_after=None):
        self.seq = seq
        self.L = depth
        self.n_tok = N_META + seq
        self.NT = -(-self.n_tok // 128)
        self.T = self.NT * 128
        self.blocks = [(t0, min(512, self.T - t0)) for t0 in range(0, self.T, 512)]
        self.debug_outs = set(debug_outs)
        self.stop_after = stop_after


PP = {}
_o = 0
for _n, _w in (("g1", 8), ("g2", 8), ("sng", 8), ("subln", 1), ("cw", 64), ("cb", 16), ("mcw", 132), ("mcb", 44),
               ("dtb", 16), ("alog", 16), ("dsk", 16), ("lam", 256)):
    PP[_n] = _o
    _o += _w
NP_ = _o


def pack_params(inp, l):
    f = lambda a: np.asarray(a, np.float32)
    pp = np.zeros((128, NP_), np.float32)
    colT = lambda v: f(v).reshape(-1, 128).T
    pp[:, PP["g1"]:PP["g1"] + 8] = colT(inp["norm1_g"][l])
    pp[:, PP["g2"]:PP["g2"] + 8] = colT(inp["norm2_g"][l])
    pp[:, PP["sng"]:PP["sng"] + 8] = colT(inp["ssd_norm_g"][l])
    pp[:, PP["subln"]:PP["subln"] + 1] = f(inp["attn_subln_g"][l]).reshape(128, 1)
    cw = f(inp["ssd_conv_w"][l])
    pp[:, PP["cw"]:PP["cw"] + 64] = cw.reshape(4, 16, 128).transpose(2, 1, 0).reshape(128, 64)
    pp[:, PP["cb"]:PP["cb"] + 16] = colT(inp["ssd_conv_b"][l])
    mw = f(inp["mlp_conv_w"][l])
    pp[:, PP["mcw"]:PP["mcw"] + 132] = mw.reshape(3, 44, 128).transpose(2, 1, 0).reshape(128, 132)
    pp[:, PP["mcb"]:PP["mcb"] + 44] = colT(inp["mlp_conv_b"][l])
    pp[:, PP["dtb"]:PP["dtb"] + 16] = np.broadcast_to(f(inp["ssd_dt_bias"][l])[None], (128, 16))
    pp[:, PP["alog"]:PP["alog"] + 16] = np.broadcast_to(f(inp["ssd_a_log"][l])[None], (128, 16))
    pp[:, PP["dsk"]:PP["dsk"] + 16] = np.broadcast_to(f(inp["ssd_d"][l])[None], (128, 16))
    lam = np.concatenate([f(inp[n][l]) for n in ("lambda_q1", "lambda_k1", "lambda_q2", "lambda_k2")])
    pp[:, PP["lam"]:PP["lam"] + 256] = np.broadcast_to(lam[None], (128, 256))
    return pp


def make_consts(cfg):
    T = cfg.T
    c = {}
    c["ident"] = np.eye(128, dtype=np.float32)
    perm = np.zeros((128, 128), np.float32)
    for m in range(2):
        for dd in range(16):
            src = dd + 8 if dd < 8 else dd - 8
            perm[m * 64 + src, m * 64 + dd] = 1.0
    c["perm"] = perm
    kk = np.arange(128)
    tri = (kk[:, None] <= kk[None, :]).astype(np.float32)
    upp = (kk[:, None] > kk[None, :]).astype(np.float32)
    c["tri3"] = np.stack([tri, upp, np.ones((128, 128), np.float32)], 1)
    cidr = np.where(kk < 16, 0, np.where(kk < 80, 1, 2))
    mdiag = (cidr[:, None] <= cidr[None, :]).astype(np.float32)
    mnext = ((kk[:, None] < 16) & (kk[None, :] >= 80)).astype(np.float32)
    c["amask"] = np.stack([mdiag, mnext], 1)
    pos = np.arange(T, dtype=np.float32)
    inv = (1.0 / (np.float32(500000.0) ** (np.arange(8, dtype=np.float32) * np.float32(2.0) / np.float32(16)))).astype(np.float32)
    ang = (pos[:, None] * inv[None, :]).astype(np.float32)
    cs, sn = np.cos(ang).astype(np.float32), np.sin(ang).astype(np.float32)
    cosT = np.ones((128, T), np.float32)
    sinT = np.zeros((128, T), np.float32)
    for m in range(2):
        for dd in range(16):
            cosT[m * 64 + dd] = cs[:, dd % 8]
            sinT[m * 64 + dd] = -sn[:, dd % 8] if dd < 8 else sn[:, dd % 8]
    c["ropec"] = cosT
    c["ropes"] = sinT
    return c


class Scope:
    _uid = 0

    def __init__(self, k):
        from contextlib import ExitStack
        self.k = k
        self.st = ExitStack()

    def sb(self, name, shape, dt):
        Scope._uid += 1
        return self.st.enter_context(self.k.nc.sbuf_tensor(f"{name}_{Scope._uid}", list(shape), dt))

    def ps(self, name, shape, dt):
        Scope._uid += 1
        return self.st.enter_context(self.k.nc.psum_tensor(f"{name}_{Scope._uid}", list(shape), dt))

    def close(self):
        self.k.barrier()
        self.st.close()


class Prog:
    def __init__(self, cfg):
        self.cfg = cfg
        nc = self.nc = bass.Bass("TRN2", target_bir_lowering=False)
        self.k = K(nc)
        T, L = cfg.T, cfg.L
        ext = lambda n, s, dt=F32: nc.dram_tensor(n, list(s), dt, kind="ExternalInput").ap()
        self.x = ext("x", [cfg.seq, D])
        self.meta = ext("meta", [N_META, D])
        self.w_in = ext("w_in", [L, D, NIN])
        self.w_sb = ext("w_sb", [L, D, D])
        self.w_ab = ext("w_ab", [L, D, D])
        self.w_out = ext("w_out", [L, D, D])
        self.w_up = ext("w_up", [L, D, 2 * DFF])
        self.w_down = ext("w_down", [L, DFF, D])
        self.pp_d = ext("pp", [L, 128, NP_])
        self.fng_d = ext("fng", [128, D])
        self.c_ident = ext("ident", [128, 128])
        self.c_perm = ext("perm", [128, 128])
        self.c_tri3 = ext("tri3", [128, 3, 128])
        self.c_amask = ext("amask", [128, 2, 128])
        self.c_ropec = ext("ropec", [128, T])
        self.c_ropes = ext("ropes", [128, T])
        self.out = nc.dram_tensor("out", [cfg.seq, D], F32, kind="ExternalOutput").ap()

        def scr(n, s, dt):
            kind = "ExternalOutput" if n in cfg.debug_outs else "Internal"
            return nc.dram_tensor(n, list(s), dt, kind=kind).ap()
        self.H = scr("H", [T, D], F32)
        self.ZS = scr("ZS", [T, D], BF16)
        self.XBC = scr("XBC", [2048, T], BF16)
        self.DT = scr("DT", [T, 32], F32)
        self.QT = scr("QT", [D, T], BF16)
        self.KT = scr("KT", [D, T], BF16)
        self.V = scr("V", [T, D], BF16)
        self.GS = scr("GS", [D, T], BF16)
        self.GA = scr("GA", [D, T], BF16)
        self.YS = scr("YS", [D, T], BF16)
        self.YA = scr("YA", [D, T], BF16)
        self.AT = scr("AT", [DFF, T], BF16)

    def build(self):
        cfg, k, nc = self.cfg, self.k, self.nc
        top = Scope(k)
        self.top = top
        self.ident_f = top.sb("identf", [128, 128], F32)
        self.ident = top.sb("ident", [128, 128], BF16)
        self.perm = top.sb("perm", [128, 128], BF16)
        self.tri3 = top.sb("tri3", [128, 3, 128], F32)
        self.amask = top.sb("amask", [128, 2, 128], BF16)
        self.pp = top.sb("pp", [128, NP_], F32)
        self.b_const = Buf("const")
        self.b_pp = Buf("pp")
        ds = k.dsem("const")
        self.ds_pp = k.dsem("pp")
        tmpf = top.sb("ctmp", [128, 3, 128], F32)
        bt = Buf()
        k.dma(k.sp, ds, self.ident_f[:], self.c_ident[:, :], writes=[bt])
        k.op(k.dve, lambda: nc.vector.tensor_copy(out=self.ident[:], in_=self.ident_f[:]), reads=[bt], writes=[self.b_const])
        bt2 = Buf()
        k.dma(k.sp, ds, tmpf[:, 0, :], self.c_perm[:, :], writes=[bt2])
        k.dma(k.sp, ds, tmpf[:, 1:3, :], self.c_amask[:, :, :], writes=[bt2])
        k.op(k.dve, lambda: nc.vector.tensor_copy(out=self.perm[:], in_=tmpf[:, 0, :]), reads=[bt2], writes=[self.b_const])
        k.op(k.dve, lambda: nc.vector.tensor_copy(out=self.amask[:], in_=tmpf[:, 1:3, :]), reads=[bt2], writes=[self.b_const])
        k.dma(k.sp, ds, self.tri3[:], self.c_tri3[:, :, :], writes=[self.b_const])
        self.phase0()
        for l in range(cfg.L):
            self.layer(l)
            if cfg.stop_after is not None and cfg.stop_after[0] == l:
                break
        if cfg.stop_after is None:
            self.final_norm()
        k.barrier()
        top.st.close()
        return nc

    def phase0(self):
        cfg, k, nc = self.cfg, self.k, self.nc
        sc = Scope(k)
        ds = k.dsem("p0")
        b = Buf()
        k.dma(k.sp, ds, self.H[0:N_META, :], self.meta[:, :], writes=[b])
        pdiv = max(p for p in (128, 64, 32, 16, 8, 4, 2, 1) if cfg.seq % p == 0)
        xs = self.x.rearrange("(p r) d -> p (r d)", p=pdiv)
        hs = self.H[N_META:N_META + cfg.seq, :].rearrange("(p r) d -> p (r d)", p=pdiv)
        k.dma(k.sp, ds, hs, xs, writes=[b])
        npad = cfg.T - cfg.n_tok
        if npad > 0:
            z = sc.sb("zero", [128, D], F32)
            bz = Buf()
            k.op(k.dve, lambda: nc.vector.memset(z[:], 0.0), writes=[bz])
            k.dma(k.sp, ds, self.H[cfg.n_tok:cfg.T, :], z[0:npad, :], reads=[bz])
        sc.close()

    def layer(self, l):
        cfg, k = self.cfg, self.k
        k.dma(k.sp, self.ds_pp, self.pp[:], self.pp_d[l, :, :], writes=[self.b_pp])
        stop = cfg.stop_after[1] if (cfg.stop_after is not None and cfg.stop_after[0] == l) else 99
        sc = Scope(k)
        UT = sc.sb("UT", [128, KC, cfg.T], BF16)
        ut_bufs = [Buf(f"ut{t}") for t in range(cfg.NT)]
        self.norm_pass(sc, l, PP["g1"], UT, ut_bufs)
        self.phase1(sc, l, UT, ut_bufs)
        sc.close()
        if stop <= 1:
            return
        scw = Scope(k)
        w4 = []
        w4_b = [[Buf() for _ in range(KC)] for _ in range(3)]
        ds_w = k.dsem("p4w")
        for wi, (nm, src) in enumerate((("wsb", self.w_sb), ("wab", self.w_ab), ("wout", self.w_out))):
            w = scw.sb(nm, [128, KC, D], BF16)
            for kk in range(KC):
                k.dma(k.pool, ds_w, w[:, kk, :], src[l, kk * 128:(kk + 1) * 128, :], writes=[w4_b[wi][kk]])
            w4.append(w)
        self.phase2(l)
        if stop > 2:
            self.phase3(l)
        if stop > 3:
            self.phase4(l, w4, w4_b)
        scw.close()
        if stop <= 4:
            return
        scw = Scope(k)
        NC = DFF // 128
        wd = scw.sb("wd", [128, NC, D], BF16)
        wd_b = [Buf() for _ in range(NC)]
        self._wd_prefetch = (wd, wd_b, k.dsem("p6w"))
        sc = Scope(k)
        UT = sc.sb("UT", [128, KC, cfg.T], BF16)
        ut_bufs = [Buf(f"ut{t}") for t in range(cfg.NT)]
        self.norm_pass(sc, l, PP["g2"], UT, ut_bufs)
        self.phase5(sc, l, UT, ut_bufs)
        sc.close()
        if stop > 5:
            self.phase6(l, wd, wd_b)
        scw.close()

    def norm_pass(self, sc, l, goff, UT, ut_bufs):
        cfg, k, nc = self.cfg, self.k, self.nc
        NB = 3
        ht = [sc.sb("nh", [128, D], F32) for _ in range(NB)]
        hb = [Buf() for _ in range(NB)]
        hds = [k.dsem(f"nh{i}") for i in range(NB)]
        junk = sc.sb("njunk", [128, D], BF16)
        bjunk = Buf()
        hn = [sc.sb("nhn", [128, D], BF16) for _ in range(2)]
        hnb = [Buf() for _ in range(2)]
        st = [sc.sb("nst", [128, 4], F32) for _ in range(2)]
        stb = [Buf() for _ in range(2)]
        mhalf = sc.sb("mhalf", [128, 1], F32)
        bmh = Buf()
        k.op(k.pool, lambda: nc.gpsimd.memset(mhalf[:], -0.5), writes=[bmh])
        pst = [sc.ps("npt", [128, KC, 128], BF16) for _ in range(2)]
        psb = [Buf() for _ in range(2)]
        gT = self.pp[:, goff:goff + KC]
        for t in range(cfg.NT):
            i, j = t % NB, t % 2
            k.dma(k.sp, hds[i], ht[i][:], self.H[t * 128:(t + 1) * 128, :], writes=[hb[i]])
            k.op(k.dve, lambda: nc.vector.scalar_tensor_tensor(out=junk[:], in0=ht[i][:], scalar=1.0, in1=ht[i][:],
                                                               op0=ALU.mult, op1=ALU.mult, accum_out=st[j][:, 0:1]),
                 reads=[hb[i]], writes=[bjunk, stb[j]])
            k.op(k.dve, lambda: nc.vector.tensor_scalar(out=st[j][:, 1:2], in0=st[j][:, 0:1], scalar1=1.0 / D, scalar2=NORM_EPS,
                                                        op0=ALU.mult, op1=ALU.add), reads=[stb[j]], writes=[stb[j]])
            k.op(k.pool, lambda: nc.gpsimd.tensor_tensor(out=st[j][:, 2:3], in0=st[j][:, 1:2], in1=mhalf[:], op=ALU.pow),
                 reads=[stb[j], bmh], writes=[stb[j]])
            k.op(k.act, lambda: nc.scalar.activation(out=hn[j][:], in_=ht[i][:], func=AF.Copy, scale=st[j][:, 2:3]),
                 reads=[hb[i], stb[j]], writes=[hnb[j]])
            for c in range(KC):
                k.op(k.pe, lambda: nc.tensor.transpose(out=pst[j][:, c, :], in_=hn[j][:, c * 128:(c + 1) * 128], identity=self.ident[:]),
                     reads=[hnb[j], self.b_const], writes=[psb[j]], signal=(c == KC - 1))
            k.op(k.dve, lambda: nc.vector.tensor_tensor(out=UT[:, :, t * 128:(t + 1) * 128], in0=pst[j][:, :, :],
                                                        in1=bcast_free(gT, 1, 128), op=ALU.mult),
                 reads=[psb[j], self.b_pp], writes=[ut_bufs[t]])

    def phase1(self, sc, l, UT, ut_bufs):
        cfg, k, nc = self.cfg, self.k, self.nc
        T, NT, blocks = cfg.T, cfg.NT, cfg.blocks
        pp = self.pp
        W = self.w_in[l]
        NW = 3
        wfm = [sc.sb("wfm", [128, KC, 128], BF16) for _ in range(NW)]
        wfm_b = [Buf() for _ in range(NW)]
        wfm_ds = [k.dsem(f"wfm{i}") for i in range(NW)]
        wtm = [sc.sb("wtm", [128, KC, 512], BF16) for _ in range(2)]
        wtm_b = [Buf() for _ in range(2)]
        wtm_ds = [k.dsem(f"wtm{i}") for i in range(2)]
        NX = 2
        xc = [sc.sb("xc", [128, T + 4], BF16) for _ in range(NX)]
        xc_b = [[Buf() for _ in blocks] for _ in range(NX)]
        xc_pad = [Buf() for _ in range(NX)]
        ost = [sc.sb("ost", [128, T], BF16) for _ in range(NX)]
        ost_b = [[Buf() for _ in blocks] for _ in range(NX)]
        ost_ds = [k.dsem(f"ost{i}") for i in range(NX)]
        dg = [sc.sb("dg", [128, 4, 128], BF16) for _ in range(NX)]
        dg_b = [Buf() for _ in range(NX)]
        ropec = sc.sb("ropec", [128, T], F32)
        ropes = sc.sb("ropes", [128, T], F32)
        b_rope = Buf()
        b_rope2 = Buf()
        ds_rope = k.dsem("rope")
        k.dma(k.sp, ds_rope, ropec[:], self.c_ropec[:, :], writes=[b_rope])
        k.dma(k.sp, ds_rope, ropes[:], self.c_ropes[:, :], writes=[b_rope2])
        rt1 = [sc.sb("rt1", [128, 512], F32) for _ in range(2)]
        rt1_b = [Buf() for _ in range(2)]
        rt2 = [sc.sb("rt2", [128, 512], F32) for _ in range(2)]
        rt2_b = [Buf() for _ in range(2)]
        NPS = 3
        ps = [sc.ps("p1a", [128, 512], F32) for _ in range(NPS)]
        ps_b = [Buf() for _ in range(NPS)]
        ps2 = [sc.ps("p1b", [128, 512], F32) for _ in range(2)]
        ps2_b = [Buf() for _ in range(2)]
        tst = [sc.sb("tst", [128, 512], BF16) for _ in range(3)]
        tst_b = [Buf() for _ in range(3)]
        tst_ds = [k.dsem(f"tst{i}") for i in range(3)]
        dtall = sc.sb("dtall", [128, NT, 32], F32)
        dtall_b = Buf()
        for i in range(NX):
            k.op(k.pool, lambda: nc.gpsimd.memset(xc[i][:, 0:3], 0.0), writes=[xc_pad[i]])
        cnt = {"w": 0, "ps": 0, "ps2": 0, "x": 0, "r": 0, "t": 0, "wt": 0}

        def load_wfm(col0):
            s = cnt["w"] % NW
            cnt["w"] += 1
            src = W[:, col0:col0 + 128].rearrange("(kk p) c -> p kk c", p=128)
            k.dma(k.pool, wfm_ds[s], wfm[s][:], src, writes=[wfm_b[s]])
            return s

        def proj_block(ws, t0, n):
            pi = cnt["ps"] % NPS
            cnt["ps"] += 1
            for kk in range(KC):
                k.op(k.pe, lambda: nc.tensor.matmul(ps[pi][:, :n], lhsT=wfm[ws][:, kk, :], rhs=UT[:, kk, t0:t0 + n],
                                                    start=(kk == 0), stop=(kk == KC - 1)),
                     reads=[wfm_b[ws]] + ut_bufs[t0 // 128:(t0 + n) // 128], writes=[ps_b[pi]], signal=(kk == KC - 1))
            return pi

        def store(xs, dst, c):
            k.dma(k.sp, ost_ds[xs], dst[c * 128:(c + 1) * 128, :], ost[xs][:, :], reads=ost_b[xs])

        def gate_chunk(col0, dst, c):
            ws = load_wfm(col0)
            xs = cnt["x"] % NX
            cnt["x"] += 1
            for bi, (t0, n) in enumerate(blocks):
                pi = proj_block(ws, t0, n)
                k.op(k.act, lambda: nc.scalar.activation(out=ost[xs][:, t0:t0 + n], in_=ps[pi][:, :n], func=AF.Sigmoid),
                     reads=[ps_b[pi]], writes=[ost_b[xs][bi]])
            store(xs, dst, c)

        def xbc_chunk(c):
            ws = load_wfm(1024 + c * 128)
            xs = cnt["x"] % NX
            cnt["x"] += 1
            cw = pp[:, PP["cw"] + c * 4:PP["cw"] + c * 4 + 4]
            k.op(k.dve, lambda: nc.vector.tensor_tensor(out=dg[xs][:], in0=bcast_free(self.ident[:], 0, 4), in1=bcast_free(cw, 1, 128), op=ALU.mult),
                 reads=[self.b_const, self.b_pp], writes=[dg_b[xs]])
            for bi, (t0, n) in enumerate(blocks):
                pi = proj_block(ws, t0, n)
                k.op(k.dve, lambda: nc.vector.tensor_copy(out=xc[xs][:, 3 + t0:3 + t0 + n], in_=ps[pi][:, :n]),
                     reads=[ps_b[pi]], writes=[xc_b[xs][bi]])
                qi = cnt["ps2"] % 2
                cnt["ps2"] += 1
                rd = [dg_b[xs], xc_b[xs][bi], xc_pad[xs]] + ([xc_b[xs][bi - 1]] if bi > 0 else [])
                for j in range(4):
                    k.op(k.pe, lambda: nc.tensor.matmul(ps2[qi][:, :n], lhsT=dg[xs][:, j, :], rhs=xc[xs][:, t0 + j:t0 + j + n],
                                                        start=(j == 0), stop=(j == 3)),
                         reads=rd, writes=[ps2_b[qi]], signal=(j == 3))
                k.op(k.act, lambda: nc.scalar.activation(out=ost[xs][:, t0:t0 + n], in_=ps2[qi][:, :n], func=AF.Silu,
                                                         bias=pp[:, PP["cb"] + c:PP["cb"] + c + 1]),
                     reads=[ps2_b[qi], self.b_pp], writes=[ost_b[xs][bi]])
            store(xs, self.XBC, c)

        def rope_chunk(col0, dst, c):
            ws = load_wfm(col0)
            xs = cnt["x"] % NX
            cnt["x"] += 1
            for bi, (t0, n) in enumerate(blocks):
                pi = proj_block(ws, t0, n)
                k.op(k.act, lambda: nc.scalar.copy(out=xc[xs][:, t0:t0 + n], in_=ps[pi][:, :n]),
                     reads=[ps_b[pi]], writes=[xc_b[xs][bi]])
                qi = cnt["ps2"] % 2
                cnt["ps2"] += 1
                k.op(k.pe, lambda: nc.tensor.matmul(ps2[qi][:, :n], lhsT=self.perm[:], rhs=xc[xs][:, t0:t0 + n], start=True, stop=True),
                     reads=[self.b_const, xc_b[xs][bi]], writes=[ps2_b[qi]])
                ri = cnt["r"] % 2
                cnt["r"] += 1
                k.op(k.dve, lambda: nc.vector.tensor_tensor(out=rt1[ri][:, :n], in0=ps2[qi][:, :n], in1=ropes[:, t0:t0 + n], op=ALU.mult),
                     reads=[ps2_b[qi], b_rope2], writes=[rt1_b[ri]])
                k.op(k.pool, lambda: nc.gpsimd.tensor_tensor(out=rt2[ri][:, :n], in0=xc[xs][:, t0:t0 + n], in1=ropec[:, t0:t0 + n], op=ALU.mult),
                     reads=[xc_b[xs][bi], b_rope], writes=[rt2_b[ri]])
                k.op(k.dve, lambda: nc.vector.tensor_tensor(out=ost[xs][:, t0:t0 + n], in0=rt1[ri][:, :n], in1=rt2[ri][:, :n], op=ALU.add),
                     reads=[rt1_b[ri], rt2_b[ri]], writes=[ost_b[xs][bi]])
            store(xs, dst, c)

        def tm_group(col0, ncol, kind, dst, dcol0):
            s = cnt["wt"] % 2
            cnt["wt"] += 1
            src = W[:, col0:col0 + ncol].rearrange("(kk p) c -> p kk c", p=128)
            k.dma(k.pool, wtm_ds[s], wtm[s][:, :, :ncol], src, writes=[wtm_b[s]])
            for t in range(NT):
                pi = cnt["ps"] % NPS
                cnt["ps"] += 1
                for kk in range(KC):
                    k.op(k.pe, lambda: nc.tensor.matmul(ps[pi][:, :ncol], lhsT=UT[:, kk, t * 128:(t + 1) * 128], rhs=wtm[s][:, kk, :ncol],
                                                        start=(kk == 0), stop=(kk == KC - 1)),
                         reads=[wtm_b[s], ut_bufs[t]], writes=[ps_b[pi]], signal=(kk == KC - 1))
                if kind == "dt":
                    k.op(k.dve, lambda: nc.vector.tensor_tensor(out=dtall[:, t, 0:16], in0=ps[pi][:, :16], in1=pp[:, PP["dtb"]:PP["dtb"] + 16], op=ALU.add),
                         reads=[ps_b[pi], self.b_pp], writes=[dtall_b])
                    continue
                ti = cnt["t"] % 3
                cnt["t"] += 1
                if kind == "z":
                    k.op(k.act, lambda: nc.scalar.activation(out=tst[ti][:, :ncol], in_=ps[pi][:, :ncol], func=AF.Silu),
                         reads=[ps_b[pi]], writes=[tst_b[ti]])
                else:
                    k.op(k.dve, lambda: nc.vector.tensor_copy(out=tst[ti][:, :ncol], in_=ps[pi][:, :ncol]),
                         reads=[ps_b[pi]], writes=[tst_b[ti]])
                k.dma(k.sp, tst_ds[ti], dst[t * 128:(t + 1) * 128, dcol0:dcol0 + ncol], tst[ti][:, :ncol], reads=[tst_b[ti]])

        for g in range(2):
            tm_group(g * 512, 512, "z", self.ZS, g * 512)
        for c in range(16):
            xbc_chunk(c)
        for c in range(8):
            gate_chunk(6160 + c * 128, self.GS, c)
        for c in range(8):
            gate_chunk(7184 + c * 128, self.GA, c)
        for c in range(8):
            rope_chunk(3088 + c * 128, self.QT, c)
        for c in range(8):
            rope_chunk(4112 + c * 128, self.KT, c)
        for g in range(2):
            tm_group(5136 + g * 512, 512, "v", self.V, g * 512)
        tm_group(3072, 16, "dt", None, 0)
        dtf = dtall[:, :, 0:16]
        ex = sc.sb("dtex", [128, NT, 16], F32)
        bex = Buf()
        na = sc.sb("nega", [128, 16], F32)
        bna = Buf()
        k.op(k.act, lambda: nc.scalar.activation(out=ex[:], in_=dtf, func=AF.Exp), reads=[dtall_b], writes=[bex])
        k.op(k.act, lambda: nc.scalar.activation(out=dtf, in_=ex[:], func=AF.Ln, bias=1.0), reads=[bex], writes=[dtall_b])
        k.op(k.act, lambda: nc.scalar.activation(out=na[:], in_=pp[:, PP["alog"]:PP["alog"] + 16], func=AF.Exp), reads=[self.b_pp], writes=[bna])
        k.op(k.dve, lambda: nc.vector.scalar_tensor_tensor(out=dtall[:, :, 16:32], in0=dtf, scalar=-1.0, in1=bcast_free(na[:], 0, NT),
                                                           op0=ALU.mult, op1=ALU.mult), reads=[dtall_b, bna], writes=[dtall_b])
        ds_dt = k.dsem("dtst")
        k.dma(k.sp, ds_dt, self.DT.rearrange("(t p) c -> p t c", p=128), dtall[:], reads=[dtall_b])


def make_in_map(cfg, inp, b):
    f = lambda a: np.ascontiguousarray(np.asarray(a, np.float32))
    L = cfg.L
    m = {
        "x": f(inp["x"][b]), "meta": f(inp["meta_tokens"]),
        "w_in": f(inp["w_in"][:L]), "w_sb": f(inp["w_ssd_branch"][:L]), "w_ab": f(inp["w_attn_branch"][:L]),
        "w_out": f(inp["w_out"][:L]), "w_up": f(inp["w_up"][:L]), "w_down": f(inp["w_down"][:L]),
        "pp": np.stack([pack_params(inp, l) for l in range(L)]),
        "fng": f(np.broadcast_to(np.asarray(inp["final_norm_g"], np.float32)[None], (128, D))),
    }
    m.update(make_consts(cfg))
    return m


def _phase2(self, l):
    import os
    cfg, k, nc = self.cfg, self.k, self.nc
    NT = cfg.NT
    pp = self.pp
    sc = Scope(k)
    tri = self.tri3[:, 0, :]
    upp = self.tri3[:, 1, :]

    def dbl(name, shape, dt, n=2):
        return [sc.sb(name, shape, dt) for _ in range(n)], [Buf(name) for _ in range(n)]
    xbT, xbT_b = dbl("xbT", [128, 16, 128], BF16)
    zs, zs_b = dbl("zs", [128, D], BF16)
    dta, dta_b = dbl("dta", [128, 32], F32)
    ld_ds = [[k.dsem(f"p2ld{i}_{a}") for a in range(3)] for i in range(2)]
    xtm, xtm_b = dbl("xtm", [128, 16, 64], BF16)
    btm, btm_b = dbl("btm", [128, 4, 128], BF16)
    E, E_b = dbl("E", [128, 48], F32)
    w1, w1_b = dbl("w1", [128, 16], F32)
    Lh, Lh_b = dbl("Lh", [128, 16, 128], F32)
    Lh2_b = [Buf() for _ in range(2)]
    Dg, Dg_b = dbl("Dg", [128, 4, 128], BF16)
    cbm, cbm_b = dbl("cbm", [128, 4, 128], BF16)
    Mg, Mg_b = dbl("Mg", [128, 4, 128], BF16)
    xdt, xdt_b = dbl("xdt", [128, 16, 64], BF16)
    xdd, xdd_b = dbl("xdd", [128, 16, 64], BF16)
    t1, t1_b = dbl("t1", [128, 256], F32)
    ysb, ysb_b = dbl("ysb", [128, D], F32)
    xD, xD_b = dbl("xD", [128, D], F32)
    junk = sc.sb("p2junk", [128, 256], BF16)
    junk_b = Buf()
    st, st_b = dbl("p2st", [128, 12], F32)
    yn, yn_b = dbl("yn", [128, D], BF16)
    yn_gb = [[Buf() for _ in range(4)] for _ in range(2)]
    yT, yT_b = dbl("yT", [128, KC, 128], BF16)
    yT_ds = [k.dsem(f"p2st{i}") for i in range(2)]
    Sf = sc.sb("Sf", [128, D], F32)
    Sf_b = [Buf() for _ in range(4)]
    Sbf, Sbf_b = dbl("Sbf", [128, D], BF16)
    Sbf_gb = [[Buf() for _ in range(4)] for _ in range(2)]
    mhalf = sc.sb("mhalf2", [128, 1], F32)
    bmh = Buf()
    k.op(k.pool, lambda: nc.gpsimd.memset(mhalf[:], -0.5), writes=[bmh])
    k.op(k.pool, lambda: nc.gpsimd.memset(Sf[:], 0.0), writes=Sf_b)
    k.op(k.pool, lambda: nc.gpsimd.memset(Sbf[1][:], 0.0), writes=Sbf_gb[1])
    ptx = sc.ps("ptx", [128, KC, 128], BF16)
    ptx_b = Buf()
    misc = sc.ps("misc", [128, 512], F32)
    ptb = misc.bitcast(BF16)
    ptb_b = Buf()

    cb = sc.ps("cb", [128, 4, 128], F32)
    cb_b = Buf()
    seg = [sc.ps("seg", [128, 4, 128], F32) for _ in range(2)]
    seg_b = [Buf() for _ in range(2)]
    ydo = sc.ps("ydo", [128, 512], F32)
    yd_b, yo_b = Buf(), Buf()
    sne = sc.ps("sne", [128, 512], F32)
    sn = sne[:, 0:256]
    e3 = sne[:, 256:304]
    sn_b = Buf()
    pty = sc.ps("pty", [128, KC, 128], BF16)
    pty_b = Buf()
    segc = 0

    LV = int(os.environ.get("P2LV", "99"))
    for i in range(NT):
        j = i % 2
        t0 = i * 128
        k.dma(k.sp, ld_ds[j][0], xbT[j][:], self.XBC[:, t0:t0 + 128].rearrange("(c p) t -> p c t", p=128), writes=[xbT_b[j]])
        k.dma(k.sp, ld_ds[j][1], zs[j][:], self.ZS[t0:t0 + 128, :], writes=[zs_b[j]])
        k.dma(k.sp, ld_ds[j][2], dta[j][:], self.DT[t0:t0 + 128, :], writes=[dta_b[j]])
        a = dta[j][:, 16:32]
        dt = dta[j][:, 0:16]
        for c in range(KC):
            k.op(k.pe, lambda: nc.tensor.transpose(out=ptx[:, c, :], in_=xbT[j][:, c, :], identity=self.ident[:]),
                 reads=[xbT_b[j], self.b_const], writes=[ptx_b], signal=(c == KC - 1))
        for g in range(4):
            k.op(k.pe, lambda: nc.tensor.transpose(out=ptb[:, g * 128:(g + 1) * 128], in_=xbT[j][:, 8 + g, :], identity=self.ident[:]),
                 reads=[xbT_b[j], self.b_const], writes=[ptb_b], signal=(g == 3))
        k.op(k.act, lambda: nc.scalar.copy(out=xtm[j][:].rearrange("p h d -> p (h d)"), in_=ptx[:].rearrange("p c t -> p (c t)")),
             reads=[ptx_b], writes=[xtm_b[j]])
        if LV <= 0:
            continue
        for q in range(3):
            k.op(k.pe, lambda: nc.tensor.matmul(e3[:, q * 16:(q + 1) * 16], lhsT=self.tri3[:, q, :], rhs=a, start=True, stop=True),
                 reads=[dta_b[j], self.b_const], writes=[sn_b], signal=(q == 2))
        SUB = int(os.environ.get("P2SUB", "9"))
        if SUB <= 0:
            continue
        k.op(k.dve, lambda: nc.vector.tensor_copy(out=btm[j][:].rearrange("p g n -> p (g n)"), in_=ptb[:, 0:512]),
             reads=[ptb_b], writes=[btm_b[j]])
        if SUB <= 1:
            continue
        k.op(k.act, lambda: nc.scalar.activation(out=E[j][:], in_=e3, func=AF.Exp), reads=[sn_b], writes=[E_b[j]])
        if LV <= 1:
            continue
        k.op(k.dve, lambda: nc.vector.tensor_tensor(out=w1[j][:], in0=dt, in1=E[j][:, 16:32], op=ALU.mult),
             reads=[dta_b[j], E_b[j]], writes=[w1_b[j]])
        k.op(k.dve, lambda: nc.vector.tensor_tensor(out=Lh[j][:, 0:8, :], in0=bcast_free(upp, 0, 8), in1=bcast_free(a[:, 0:8], 1, 128), op=ALU.mult),
             reads=[dta_b[j], self.b_const], writes=[Lh_b[j]])
        k.op(k.pool, lambda: nc.gpsimd.tensor_tensor(out=Lh[j][:, 8:16, :], in0=bcast_free(upp, 0, 8), in1=bcast_free(a[:, 8:16], 1, 128), op=ALU.mult),
             reads=[dta_b[j], self.b_const], writes=[Lh2_b[j]])
        k.op(k.pool, lambda: nc.gpsimd.tensor_tensor(out=xdt[j][:], in0=xtm[j][:], in1=bcast_free(dt, 1, 64), op=ALU.mult),
             reads=[xtm_b[j], dta_b[j]], writes=[xdt_b[j]])
        k.op(k.pool, lambda: nc.gpsimd.tensor_tensor(out=xdd[j][:], in0=xtm[j][:], in1=bcast_free(w1[j][:], 1, 64), op=ALU.mult),
             reads=[xtm_b[j], w1_b[j]], writes=[xdd_b[j]])
        k.op(k.pool, lambda: nc.gpsimd.tensor_tensor(out=xD[j][:].rearrange("p (h d) -> p h d", d=64), in0=xtm[j][:],
                                                     in1=bcast_free(pp[:, PP["dsk"]:PP["dsk"] + 16], 1, 64), op=ALU.mult),
             reads=[xtm_b[j], self.b_pp], writes=[xD_b[j]])
        if LV <= 2:
            continue
        for g in range(4):
            k.op(k.pe, lambda: nc.tensor.matmul(cb[:, g, :], lhsT=xbT[j][:, 8 + g, :], rhs=xbT[j][:, 12 + g, :], start=True, stop=True),
                 reads=[xbT_b[j]], writes=[cb_b], signal=(g == 3))
        k.op(k.dve, lambda: nc.vector.tensor_tensor(out=cbm[j][:], in0=cb[:], in1=bcast_free(tri, 0, 4), op=ALU.mult),
             reads=[cb_b, self.b_const], writes=[cbm_b[j]])
        if LV <= 3:
            continue
        sprev, snew = Sbf[(i + 1) % 2], Sbf[i % 2]
        sprev_gb, snew_gb = Sbf_gb[(i + 1) % 2], Sbf_gb[i % 2]
        for g in range(4):
            sg = segc % 2
            segc += 1
            for hh in range(4):
                k.op(k.pe, lambda: nc.tensor.matmul(seg[sg][:, hh, :], lhsT=Lh[j][:, g * 4 + hh, :], rhs=tri, start=True, stop=True),
                     reads=[Lh_b[j] if g < 2 else Lh2_b[j], self.b_const], writes=[seg_b[sg]], signal=(hh == 3))
            k.op(k.act, lambda: nc.scalar.activation(out=Dg[j][:], in_=seg[sg][:], func=AF.Exp), reads=[seg_b[sg]], writes=[Dg_b[j]])
            k.op(k.dve, lambda: nc.vector.tensor_tensor(out=Mg[j][:], in0=Dg[j][:], in1=bcast_free(cbm[j][:, g, :], 0, 4), op=ALU.mult),
                 reads=[Dg_b[j], cbm_b[j]], writes=[Mg_b[j]])
            for hh in range(4):
                k.op(k.pe, lambda: nc.tensor.matmul(ydo[:, hh * 64:(hh + 1) * 64], lhsT=Mg[j][:, hh, :], rhs=xdt[j][:, g * 4 + hh, :], start=True, stop=True),
                     reads=[Mg_b[j], xdt_b[j]], writes=[yd_b], signal=False)
            k.op(k.pe, lambda: nc.tensor.matmul(ydo[:, 256:512], lhsT=xbT[j][:, 12 + g, :], rhs=sprev[:, g * 256:(g + 1) * 256], start=True, stop=True),
                 reads=[xbT_b[j], sprev_gb[g]], writes=[yd_b])
            k.op(k.pe, lambda: nc.tensor.matmul(sn, lhsT=btm[j][:, g, :], rhs=xdd[j][:, g * 4:(g + 1) * 4, :].rearrange("p h d -> p (h d)"), start=True, stop=True),
                 reads=[btm_b[j], xdd_b[j]], writes=[sn_b])
            k.op(k.dve, lambda: nc.vector.tensor_tensor(out=t1[j][:].rearrange("p (h d) -> p h d", d=64), in0=ydo[:, 256:512].rearrange("p (h d) -> p h d", d=64),
                                                        in1=bcast_free(E[j][:, g * 4:(g + 1) * 4], 1, 64), op=ALU.mult),
                 reads=[yd_b, E_b[j]], writes=[t1_b[j]])
            k.op(k.dve, lambda: nc.vector.tensor_tensor(out=ysb[j][:, g * 256:(g + 1) * 256], in0=ydo[:, 0:256], in1=t1[j][:], op=ALU.add),
                 reads=[yd_b, t1_b[j]], writes=[ysb_b[j]])
            sfv = Sf[:, g * 256:(g + 1) * 256]
            k.op(k.dve, lambda: nc.vector.tensor_tensor(out=sfv.rearrange("p (h d) -> p h d", d=64), in0=sfv.rearrange("p (h d) -> p h d", d=64),
                                                        in1=bcast_free(E[j][:, 32 + g * 4:32 + (g + 1) * 4], 1, 64), op=ALU.mult),
                 reads=[E_b[j]], writes=[Sf_b[g]])
            k.op(k.dve, lambda: nc.vector.tensor_tensor(out=sfv, in0=sn, in1=sfv, op=ALU.add), reads=[sn_b], writes=[Sf_b[g]])
            k.op(k.act, lambda: nc.scalar.copy(out=snew[:, g * 256:(g + 1) * 256], in_=sfv), reads=[Sf_b[g]], writes=[snew_gb[g]])
        if LV <= 4:
            continue
        k.op(k.pool, lambda: nc.gpsimd.tensor_tensor(out=ysb[j][:], in0=ysb[j][:], in1=xD[j][:], op=ALU.add), reads=[xD_b[j]], writes=[ysb_b[j]])
        k.op(k.pool, lambda: nc.gpsimd.tensor_tensor(out=ysb[j][:], in0=ysb[j][:], in1=zs[j][:], op=ALU.mult), reads=[zs_b[j]], writes=[ysb_b[j]])
        for g in range(4):
            k.op(k.act, lambda: nc.scalar.activation(out=junk[:], in_=ysb[j][:, g * 256:(g + 1) * 256], func=AF.Square, accum_out=st[j][:, g:g + 1]),
                 reads=[ysb_b[j]], writes=[junk_b, st_b[j]])
        k.op(k.dve, lambda: nc.vector.tensor_scalar(out=st[j][:, 4:8], in0=st[j][:, 0:4], scalar1=1.0 / 256, scalar2=NORM_EPS, op0=ALU.mult, op1=ALU.add),
             reads=[st_b[j]], writes=[st_b[j]])
        k.op(k.pool, lambda: nc.gpsimd.tensor_tensor(out=st[j][:, 8:12], in0=st[j][:, 4:8], in1=bcast_free(mhalf[:, 0:1], 0, 4)[:, :, 0], op=ALU.pow),
             reads=[st_b[j], bmh], writes=[st_b[j]])
        for g in range(4):
            k.op(k.act, lambda: nc.scalar.activation(out=yn[j][:, g * 256:(g + 1) * 256], in_=ysb[j][:, g * 256:(g + 1) * 256], func=AF.Copy, scale=st[j][:, 8 + g:9 + g]),
                 reads=[ysb_b[j], st_b[j]], writes=[yn_gb[j][g]])
        for c in range(KC):
            k.op(k.pe, lambda: nc.tensor.transpose(out=pty[:, c, :], in_=yn[j][:, c * 128:(c + 1) * 128], identity=self.ident[:]),
                 reads=[yn_gb[j][c // 2], self.b_const], writes=[pty_b], signal=(c == KC - 1))
        k.op(k.dve, lambda: nc.vector.tensor_tensor(out=yT[j][:], in0=pty[:], in1=bcast_free(pp[:, PP["sng"]:PP["sng"] + KC], 1, 128), op=ALU.mult),
             reads=[pty_b, self.b_pp], writes=[yT_b[j]])
        k.dma(k.sp, yT_ds[j], self.YS[:, t0:t0 + 128].rearrange("(c p) t -> p c t", p=128), yT[j][:], reads=[yT_b[j]])
    sc.close()


Prog.phase2 = _phase2


def _phase3(self, l):
    import math
    cfg, k, nc = self.cfg, self.k, self.nc
    NT, T = cfg.NT, cfg.T
    pp = self.pp
    lam_init = 0.8 - 0.6 * math.exp(-0.3 * l)
    sc = Scope(k)
    qT = [sc.sb("qT", [128, T], BF16) for _ in range(2)]
    kT = [sc.sb("kT", [128, T], BF16) for _ in range(2)]
    Vh = [sc.sb("Vh", [128, NT, 130], BF16) for _ in range(2)]
    qkv_b = [Buf() for _ in range(2)]
    q_b = [Buf() for _ in range(2)]
    k_b = [Buf() for _ in range(2)]
    ones_b = [Buf() for _ in range(2)]
    ld_ds = [k.dsem(f"p3ld{i}") for i in range(2)]
    for i in range(2):
        k.op(k.pool, lambda: nc.gpsimd.memset(Vh[i][:, :, 128:130], 1.0), writes=[ones_b[i]])
    NPT = 4
    pt = [sc.sb("pt", [128, 2, 512], BF16) for _ in range(NPT)]
    pt_b = [Buf() for _ in range(NPT)]
    yst = [sc.sb("yst", [128, T], BF16) for _ in range(2)]
    yst_b = [[Buf() for _ in range(NT)] for _ in range(2)]
    yst_ds = [k.dsem(f"p3st{i}") for i in range(2)]
    sm = sc.sb("p3sm", [128, 16], F32)
    sm_b = Buf()
    fs = [sc.sb("p3fs", [128, 8], F32) for _ in range(4)]
    fs_b = [Buf() for _ in range(4)]
    ft = [sc.sb("p3ft", [128, 128], F32) for _ in range(4)]
    ft_b = [Buf() for _ in range(4)]
    fo = [sc.sb("p3fo", [128, 128], F32) for _ in range(4)]
    fo_b = [Buf() for _ in range(4)]
    fn = [sc.sb("p3fn", [128, 128], BF16) for _ in range(4)]
    fn_b = [Buf() for _ in range(4)]
    accS = [sc.sb("accS", [128, 3, 396], F32) for _ in range(2)]
    accS_b = [[Buf() for _ in range(3)] for _ in range(2)]
    junk = sc.sb("p3junk", [128, 128], F32)
    junk_b = Buf()
    mhalf = sc.sb("mhalf3", [128, 1], F32)
    bmh = Buf()
    k.op(k.pool, lambda: nc.gpsimd.memset(mhalf[:], -0.5), writes=[bmh])
    spm = [sc.ps("spm", [128, 2, 512], F32) for _ in range(2)]
    spm_b = [Buf() for _ in range(2)]
    accb = [sc.ps("acc", [128, 512], F32) for _ in range(3)]
    acc_b = [Buf() for _ in range(3)]
    ptr = sc.ps("ptr", [128, 128], BF16)
    ptr_b = Buf()
    slots = {}
    idx = 0
    for jq in range(4):
        for m in range(2):
            slots[(jq, m)] = (idx // 3, (idx % 3) * 129)
            idx += 1

    lo = PP["lam"]
    k.op(k.dve, lambda: nc.vector.scalar_tensor_tensor(out=junk[:, 0:64], in0=pp[:, lo:lo + 64], scalar=1.0, in1=pp[:, lo + 64:lo + 128],
                                                       op0=ALU.mult, op1=ALU.mult, accum_out=sm[:, 0:1]), reads=[self.b_pp], writes=[junk_b, sm_b])
    k.op(k.dve, lambda: nc.vector.scalar_tensor_tensor(out=junk[:, 0:64], in0=pp[:, lo + 128:lo + 192], scalar=1.0, in1=pp[:, lo + 192:lo + 256],
                                                       op0=ALU.mult, op1=ALU.mult, accum_out=sm[:, 1:2]), reads=[self.b_pp], writes=[junk_b, sm_b])
    k.op(k.act, lambda: nc.scalar.activation(out=sm[:, 2:4], in_=sm[:, 0:2], func=AF.Exp), reads=[sm_b], writes=[sm_b])
    k.op(k.dve, lambda: nc.vector.tensor_tensor(out=sm[:, 4:5], in0=sm[:, 3:4], in1=sm[:, 2:3], op=ALU.subtract), reads=[sm_b], writes=[sm_b])
    k.op(k.dve, lambda: nc.vector.tensor_scalar(out=sm[:, 5:6], in0=sm[:, 4:5], scalar1=-lam_init, scalar2=None, op0=ALU.add), reads=[sm_b], writes=[sm_b])
    nlam = sm[:, 5:6]

    cnt = {"sp": 0, "pt": 0, "f": 0, "a": 0}
    qblocks = [list(range(q0, min(q0 + 4, NT))) for q0 in range(0, NT, 4)]
    for h in range(8):
        hs = h % 2
        k.dma(k.sp, ld_ds[hs], qT[hs][:], self.QT[h * 128:(h + 1) * 128, :], writes=[q_b[hs]])
        k.dma(k.sp, ld_ds[hs], kT[hs][:], self.KT[h * 128:(h + 1) * 128, :], writes=[k_b[hs]])
        k.dma(k.sp, ld_ds[hs], Vh[hs][:, :, 0:128], self.V[:, h * 128:(h + 1) * 128].rearrange("(t p) c -> p t c", p=128), writes=[qkv_b[hs]])
        for qts in qblocks:
            q0, nq = qts[0], len(qts)
            bank_started = [False, False, False]
            kt_max = min(qts[-1] + 1, NT - 1)
            for kt in range(kt_max + 1):
                jlo = max(0, kt - 1 - q0)
                c0, c1 = jlo * 128, nq * 128
                si = cnt["sp"] % 2
                cnt["sp"] += 1
                def qk_pair():
                    for m in range(2):
                        ins_ = nc.tensor.matmul(spm[si][:, m, c0:c1], lhsT=kT[hs][m * 64:(m + 1) * 64, kt * 128:(kt + 1) * 128],
                                                rhs=qT[hs][m * 64:(m + 1) * 64, q0 * 128 + c0:q0 * 128 + c1], start=True, stop=True)
                    return ins_
                k.op(k.pe, qk_pair, reads=[q_b[hs], k_b[hs]], writes=[spm_b[si]])
                pi = cnt["pt"] % NPT
                cnt["pt"] += 1
                k.op(k.act, lambda: nc.scalar.activation(out=pt[pi][:, :, c0:c1], in_=spm[si][:, :, c0:c1], func=AF.Exp, scale=0.125),
                     reads=[spm_b[si]], writes=[pt_b[pi]])
                for jq in range(jlo, nq):
                    qt = q0 + jq
                    which = 0 if kt == qt else (1 if kt == qt + 1 else None)
                    if which is not None:
                        k.op(k.dve, lambda: nc.vector.tensor_tensor(out=pt[pi][:, :, jq * 128:(jq + 1) * 128], in0=pt[pi][:, :, jq * 128:(jq + 1) * 128],
                                                                    in1=bcast_free(self.amask[:, which, :], 0, 2), op=ALU.mult),
                             reads=[self.b_const], writes=[pt_b[pi]])
                for jq in range(jlo, nq):
                    last_kt = min(q0 + jq + 1, NT - 1)
                    for m in range(2):
                        bk, co = slots[(jq, m)]
                        st_flag = not bank_started[bk]
                        bank_started[bk] = True
                        k.op(k.pe, lambda: nc.tensor.matmul(accb[bk][:, co:co + 129], lhsT=pt[pi][:, m, jq * 128:(jq + 1) * 128], rhs=Vh[hs][:, kt, 0:129],
                                                            start=st_flag, stop=(kt == last_kt), skip_group_check=True),
                             reads=[pt_b[pi], qkv_b[hs], ones_b[hs]], writes=[acc_b[bk]], signal=(kt == last_kt and m == 1))
            ai = cnt["a"] % 2
            cnt["a"] += 1
            nbk = (2 * nq + 2) // 3
            for bk in range(nbk):
                ncol = 129 * min(3, 2 * nq - 3 * bk)
                k.op(k.dve, lambda: nc.vector.tensor_copy(out=accS[ai][:, bk, 0:ncol], in_=accb[bk][:, 0:ncol]), reads=[acc_b[bk]], writes=[accS_b[ai][bk]])
            for jq in range(nq):
                qt = q0 + jq
                fi = cnt["f"] % 4
                cnt["f"] += 1
                b0, o0 = slots[(jq, 0)]
                b1, o1 = slots[(jq, 1)]
                k.op(k.dve, lambda: nc.vector.reciprocal(out=fs[fi][:, 0:1], in_=accS[ai][:, b0, o0 + 128:o0 + 129]), reads=[accS_b[ai][b0]], writes=[fs_b[fi]])
                k.op(k.dve, lambda: nc.vector.reciprocal(out=fs[fi][:, 1:2], in_=accS[ai][:, b1, o1 + 128:o1 + 129]), reads=[accS_b[ai][b1]], writes=[fs_b[fi]])
                k.op(k.dve, lambda: nc.vector.tensor_tensor(out=fs[fi][:, 2:3], in0=fs[fi][:, 1:2], in1=nlam, op=ALU.mult), reads=[sm_b], writes=[fs_b[fi]])
                k.op(k.dve, lambda: nc.vector.tensor_scalar(out=ft[fi][:], in0=accS[ai][:, b1, o1:o1 + 128], scalar1=fs[fi][:, 2:3], scalar2=None, op0=ALU.mult),
                     reads=[accS_b[ai][b1], fs_b[fi]], writes=[ft_b[fi]])
                k.op(k.dve, lambda: nc.vector.scalar_tensor_tensor(out=fo[fi][:], in0=accS[ai][:, b0, o0:o0 + 128], scalar=fs[fi][:, 0:1], in1=ft[fi][:],
                                                                   op0=ALU.mult, op1=ALU.add), reads=[accS_b[ai][b0], fs_b[fi], ft_b[fi]], writes=[fo_b[fi]])
                k.op(k.dve, lambda: nc.vector.scalar_tensor_tensor(out=junk[:], in0=fo[fi][:], scalar=1.0, in1=fo[fi][:], op0=ALU.mult, op1=ALU.mult,
                                                                   accum_out=fs[fi][:, 3:4]), reads=[fo_b[fi]], writes=[junk_b, fs_b[fi]])
                k.op(k.dve, lambda: nc.vector.tensor_scalar(out=fs[fi][:, 4:5], in0=fs[fi][:, 3:4], scalar1=1.0 / 128, scalar2=SUBLN_EPS, op0=ALU.mult, op1=ALU.add),
                     reads=[fs_b[fi]], writes=[fs_b[fi]])
                k.op(k.pool, lambda: nc.gpsimd.tensor_tensor(out=fs[fi][:, 5:6], in0=fs[fi][:, 4:5], in1=mhalf[:], op=ALU.pow), reads=[fs_b[fi], bmh], writes=[fs_b[fi]])
                k.op(k.act, lambda: nc.scalar.activation(out=fn[fi][:], in_=fo[fi][:], func=AF.Copy, scale=fs[fi][:, 5:6]), reads=[fo_b[fi], fs_b[fi]], writes=[fn_b[fi]])
                k.op(k.pe, lambda: nc.tensor.transpose(out=ptr[:], in_=fn[fi][:], identity=self.ident[:]), reads=[fn_b[fi], self.b_const], writes=[ptr_b])
                k.op(k.dve, lambda: nc.vector.tensor_scalar(out=yst[hs][:, qt * 128:(qt + 1) * 128], in0=ptr[:], scalar1=pp[:, PP["subln"]:PP["subln"] + 1],
                                                            scalar2=(1.0 - lam_init), op0=ALU.mult, op1=ALU.mult), reads=[ptr_b, self.b_pp], writes=[yst_b[hs][qt]])
        k.dma(k.sp, yst_ds[hs], self.YA[h * 128:(h + 1) * 128, :], yst[hs][:, :], reads=yst_b[hs])
    sc.close()


Prog.phase3 = _phase3


def _phase4(self, l, wts, wt_b):
    cfg, k, nc = self.cfg, self.k, self.nc
    sc = Scope(k)
    wsb, wab, wout = wts
    ins = [[sc.sb("p4in", [128, KC, 512], BF16) for _ in range(4)] for _ in range(3)]
    ins_b = [[Buf() for _ in range(4)] for _ in range(3)]
    ins_ds = [[k.dsem(f"p4in{i}_{a}") for a in range(4)] for i in range(3)]
    mg = [sc.sb("mg", [128, KC, 512], BF16) for _ in range(2)]
    mg_b = [[Buf() for _ in range(KC)] for _ in range(2)]
    tA = [sc.sb("p4ta", [128, 512], F32) for _ in range(2)]
    tA_b = [Buf() for _ in range(2)]
    tB = [sc.sb("p4tb", [128, 512], F32) for _ in range(2)]
    tB_b = [Buf() for _ in range(2)]
    NH = 6
    ht = [sc.sb("p4h", [128, D], F32) for _ in range(NH)]
    ht_b = [Buf() for _ in range(NH)]
    ht_ds = [k.dsem(f"p4h{i}") for i in range(NH)]
    ps1 = [sc.ps("p4a", [128, 512], F32) for _ in range(2)]
    ps1_b = [Buf() for _ in range(2)]
    ps2 = [sc.ps("p4b", [128, 512], F32) for _ in range(2)]
    ps2_b = [Buf() for _ in range(2)]
    ps3 = [sc.ps("p4c", [128, 512], F32) for _ in range(2)]
    ps3_b = [Buf() for _ in range(2)]
    c1 = c3 = ch = 0
    srcs = (self.YS, self.YA, self.GS, self.GA)
    for bi, (t0, n) in enumerate(cfg.blocks):
        s = bi % 2
        si = bi % 3
        for a in range(4):
            k.dma(k.sp, ins_ds[si][a], ins[si][a][:, :, :n], srcs[a][:, t0:t0 + n].rearrange("(c p) t -> p c t", p=128), writes=[ins_b[si][a]])
        ys, ya, gs, ga = ins[si]
        for c in range(KC):
            pi = c1 % 2
            c1 += 1
            for kk in range(KC):
                k.op(k.pe, lambda: nc.tensor.matmul(ps1[pi][:, :n], lhsT=wsb[:, kk, c * 128:(c + 1) * 128], rhs=ys[:, kk, :n], start=(kk == 0), stop=(kk == KC - 1)),
                     reads=[wt_b[0][kk], ins_b[si][0]], writes=[ps1_b[pi]], signal=(kk == KC - 1))
            for kk in range(KC):
                k.op(k.pe, lambda: nc.tensor.matmul(ps2[pi][:, :n], lhsT=wab[:, kk, c * 128:(c + 1) * 128], rhs=ya[:, kk, :n], start=(kk == 0), stop=(kk == KC - 1)),
                     reads=[wt_b[1][kk], ins_b[si][1]], writes=[ps2_b[pi]], signal=(kk == KC - 1))
            k.op(k.dve, lambda: nc.vector.tensor_tensor(out=tA[pi][:, :n], in0=ps1[pi][:, :n], in1=gs[:, c, :n], op=ALU.mult), reads=[ps1_b[pi], ins_b[si][2]], writes=[tA_b[pi]])
            k.op(k.dve, lambda: nc.vector.tensor_tensor(out=tB[pi][:, :n], in0=ps2[pi][:, :n], in1=ga[:, c, :n], op=ALU.mult), reads=[ps2_b[pi], ins_b[si][3]], writes=[tB_b[pi]])
            k.op(k.pool, lambda: nc.gpsimd.tensor_tensor(out=mg[s][:, c, :n], in0=tA[pi][:, :n], in1=tB[pi][:, :n], op=ALU.add), reads=[tA_b[pi], tB_b[pi]], writes=[mg_b[s][c]])
        for jq in range(n // 128):
            t = t0 // 128 + jq
            hi = ch % NH
            ch += 1
            k.dma(k.sp, ht_ds[hi], ht[hi][:], self.H[t * 128:(t + 1) * 128, :], writes=[ht_b[hi]])
            for half in range(2):
                pi = c3 % 2
                c3 += 1
                for kk in range(KC):
                    k.op(k.pe, lambda: nc.tensor.matmul(ps3[pi][:, :], lhsT=mg[s][:, kk, jq * 128:(jq + 1) * 128], rhs=wout[:, kk, half * 512:(half + 1) * 512],
                                                        start=(kk == 0), stop=(kk == KC - 1)),
                         reads=[wt_b[2][kk]] + mg_b[s], writes=[ps3_b[pi]], signal=(kk == KC - 1))
                k.op(k.dve, lambda: nc.vector.tensor_tensor(out=ht[hi][:, half * 512:(half + 1) * 512], in0=ps3[pi][:, :], in1=ht[hi][:, half * 512:(half + 1) * 512], op=ALU.add),
                     reads=[ps3_b[pi]], writes=[ht_b[hi]])
            k.dma(k.sp, ht_ds[hi], self.H[t * 128:(t + 1) * 128, :], ht[hi][:], reads=[ht_b[hi]])
    sc.close()


def _phase5(self, sc, l, UT, ut_bufs):
    cfg, k, nc = self.cfg, self.k, self.nc
    T, blocks = cfg.T, cfg.blocks
    pp = self.pp
    W = self.w_up[l]
    NW = 3
    wg = [sc.sb("wg", [128, 2, KC, 128], BF16) for _ in range(NW)]
    wg_b = [[Buf(), Buf()] for _ in range(NW)]
    wg_ds = [[k.dsem(f"p5w{i}_{a}") for a in range(2)] for i in range(NW)]
    gv = [sc.sb("gv", [128, 2, T + 4], BF16) for _ in range(2)]
    gv_b = [[[Buf() for _ in blocks] for _ in range(2)] for _ in range(2)]
    gv_pad = [Buf() for _ in range(2)]
    ost = [sc.sb("p5ost", [128, T], BF16) for _ in range(2)]
    ost_b = [[Buf() for _ in blocks] for _ in range(2)]
    ost_ds = [k.dsem(f"p5o{i}") for i in range(2)]
    dg = [sc.sb("p5dg", [128, 2, 3, 128], BF16) for _ in range(2)]
    dg_b = [Buf() for _ in range(2)]
    sg = [sc.sb("p5sg", [128, 512], F32) for _ in range(2)]
    sg_b = [Buf() for _ in range(2)]
    psA = [[sc.ps("p5a", [128, 512], F32) for _ in range(2)] for _ in range(2)]
    psA_b = [[Buf() for _ in range(2)] for _ in range(2)]
    psB = [[sc.ps("p5b", [128, 512], F32)] * 2 for _ in range(2)]
    psB_b = [[Buf()] * 2 for _ in range(2)]
    for i in range(2):
        k.op(k.pool, lambda: nc.gpsimd.memset(gv[i][:, :, 0:2], 0.0), writes=[gv_pad[i]])
    ca = cb_ = 0
    for j in range(DFF // 128):
        s = j % 2
        ws = j % NW
        for a, col0 in enumerate((j * 128, DFF + j * 128)):
            k.dma(k.pool, wg_ds[ws][a], wg[ws][:, a, :, :], W[:, col0:col0 + 128].rearrange("(kk p) c -> p kk c", p=128), writes=[wg_b[ws][a]])
        if j >= 1:
            wd_, wdb_, wds_ = self._wd_prefetch
            k.dma(k.pool, wds_, wd_[:, j - 1, :], self.w_down[l, (j - 1) * 128:j * 128, :], writes=[wdb_[j - 1]])
            if j == DFF // 128 - 1:
                k.dma(k.pool, wds_, wd_[:, j, :], self.w_down[l, j * 128:(j + 1) * 128, :], writes=[wdb_[j]])
        for a, cidx in enumerate((j, DFF // 128 + j)):
            cw = pp[:, PP["mcw"] + cidx * 3:PP["mcw"] + cidx * 3 + 3]
            k.op(k.dve, lambda: nc.vector.tensor_tensor(out=dg[s][:, a, :, :], in0=bcast_free(self.ident[:], 0, 3), in1=bcast_free(cw, 1, 128), op=ALU.mult),
                 reads=[self.b_const, self.b_pp], writes=[dg_b[s]])
        for bi, (t0, n) in enumerate(blocks):
            pa = ca % 2
            ca += 1
            for a in range(2):
                for kk in range(KC):
                    k.op(k.pe, lambda: nc.tensor.matmul(psA[a][pa][:, :n], lhsT=wg[ws][:, a, kk, :], rhs=UT[:, kk, t0:t0 + n], start=(kk == 0), stop=(kk == KC - 1)),
                         reads=[wg_b[ws][a]] + ut_bufs[t0 // 128:(t0 + n) // 128], writes=[psA_b[a][pa]], signal=(kk == KC - 1))
            k.op(k.act, lambda: nc.scalar.copy(out=gv[s][:, 0, 2 + t0:2 + t0 + n], in_=psA[0][pa][:, :n]), reads=[psA_b[0][pa]], writes=[gv_b[s][0][bi]])
            k.op(k.dve, lambda: nc.vector.tensor_copy(out=gv[s][:, 1, 2 + t0:2 + t0 + n], in_=psA[1][pa][:, :n]), reads=[psA_b[1][pa]], writes=[gv_b[s][1][bi]])
            pb = cb_ % 2
            cb_ += 1
            for a in range(2):
                rd = [dg_b[s], gv_b[s][a][bi], gv_pad[s]] + ([gv_b[s][a][bi - 1]] if bi > 0 else [])
                for jj in range(3):
                    k.op(k.pe, lambda: nc.tensor.matmul(psB[a][pb][:, :n], lhsT=dg[s][:, a, jj, :], rhs=gv[s][:, a, t0 + jj:t0 + jj + n], start=(jj == 0), stop=(jj == 2)),
                         reads=rd, writes=[psB_b[a][pb]], signal=(jj == 2))
            k.op(k.act, lambda: nc.scalar.activation(out=sg[pb][:, :n], in_=psB[0][pb][:, :n], func=AF.Silu, bias=pp[:, PP["mcb"] + j:PP["mcb"] + j + 1]),
                 reads=[psB_b[0][pb], self.b_pp], writes=[sg_b[pb]])
            vb = PP["mcb"] + DFF // 128 + j
            k.op(k.dve, lambda: nc.vector.scalar_tensor_tensor(out=ost[s][:, t0:t0 + n], in0=psB[1][pb][:, :n], scalar=pp[:, vb:vb + 1], in1=sg[pb][:, :n],
                                                               op0=ALU.add, op1=ALU.mult), reads=[psB_b[1][pb], sg_b[pb], self.b_pp], writes=[ost_b[s][bi]])
        k.dma(k.sp, ost_ds[s], self.AT[j * 128:(j + 1) * 128, :], ost[s][:, :], reads=ost_b[s])


def _phase6(self, l, wd, wd_b):
    cfg, k, nc = self.cfg, self.k, self.nc
    NC = DFF // 128
    sc = Scope(k)
    NB = 3
    at = [sc.sb("p6at", [128, NC, 128], BF16) for _ in range(NB)]
    at_b = [Buf() for _ in range(NB)]
    ht = [sc.sb("p6h", [128, D], F32) for _ in range(NB)]
    ht_b = [Buf() for _ in range(NB)]
    ds = [k.dsem(f"p6l{i}") for i in range(NB)]
    ds_at = [k.dsem(f"p6a{i}") for i in range(NB)]
    ps = [sc.ps("p6", [128, 512], F32) for _ in range(4)]
    ps_b = [Buf() for _ in range(4)]
    cp = 0
    for t in range(cfg.NT):
        i = t % NB
        k.dma(k.sp, ds_at[i], at[i][:], self.AT[:, t * 128:(t + 1) * 128].rearrange("(c p) t -> p c t", p=128), writes=[at_b[i]])
        k.dma(k.sp, ds[i], ht[i][:], self.H[t * 128:(t + 1) * 128, :], writes=[ht_b[i]])
        for half in range(2):
            pi = cp % 4
            cp += 1
            for c in range(NC):
                k.op(k.pe, lambda: nc.tensor.matmul(ps[pi][:, :], lhsT=at[i][:, c, :], rhs=wd[:, c, half * 512:(half + 1) * 512], start=(c == 0), stop=(c == NC - 1)),
                     reads=[at_b[i], wd_b[c]], writes=[ps_b[pi]], signal=(c == NC - 1))
            k.op(k.dve, lambda: nc.vector.tensor_tensor(out=ht[i][:, half * 512:(half + 1) * 512], in0=ps[pi][:, :], in1=ht[i][:, half * 512:(half + 1) * 512], op=ALU.add),
                 reads=[ps_b[pi]], writes=[ht_b[i]])
        k.dma(k.sp, ds[i], self.H[t * 128:(t + 1) * 128, :], ht[i][:], reads=[ht_b[i]])
    sc.close()


def _final_norm(self):
    cfg, k, nc = self.cfg, self.k, self.nc
    sc = Scope(k)
    fng = sc.sb("fng", [128, D], F32)
    fng_b = Buf()
    k.dma(k.sp, k.dsem("fng"), fng[:], self.fng_d[:, :], writes=[fng_b])
    NB = 3
    ht = [sc.sb("fh", [128, D], F32) for _ in range(NB)]
    hb = [Buf() for _ in range(NB)]
    hds = [k.dsem(f"fh{i}") for i in range(NB)]
    ot = [sc.sb("fo", [128, D], F32) for _ in range(NB)]
    ob = [Buf() for _ in range(NB)]
    ods = [k.dsem(f"fo{i}") for i in range(NB)]
    junk = sc.sb("fjunk", [128, D], BF16)
    bjunk = Buf()
    st = [sc.sb("fst", [128, 4], F32) for _ in range(2)]
    stb = [Buf() for _ in range(2)]
    mhalf = sc.sb("mhalff", [128, 1], F32)
    bmh = Buf()
    k.op(k.pool, lambda: nc.gpsimd.memset(mhalf[:], -0.5), writes=[bmh])
    for t in range(cfg.NT):
        i, j = t % NB, t % 2
        k.dma(k.sp, hds[i], ht[i][:], self.H[t * 128:(t + 1) * 128, :], writes=[hb[i]])
        k.op(k.dve, lambda: nc.vector.scalar_tensor_tensor(out=junk[:], in0=ht[i][:], scalar=1.0, in1=ht[i][:], op0=ALU.mult, op1=ALU.mult, accum_out=st[j][:, 0:1]),
             reads=[hb[i]], writes=[bjunk, stb[j]])
        k.op(k.dve, lambda: nc.vector.tensor_scalar(out=st[j][:, 1:2], in0=st[j][:, 0:1], scalar1=1.0 / D, scalar2=NORM_EPS, op0=ALU.mult, op1=ALU.add),
             reads=[stb[j]], writes=[stb[j]])
        k.op(k.pool, lambda: nc.gpsimd.tensor_tensor(out=st[j][:, 2:3], in0=st[j][:, 1:2], in1=mhalf[:], op=ALU.pow), reads=[stb[j], bmh], writes=[stb[j]])
        k.op(k.dve, lambda: nc.vector.scalar_tensor_tensor(out=ot[i][:], in0=ht[i][:], scalar=st[j][:, 2:3], in1=fng[:], op0=ALU.mult, op1=ALU.mult),
             reads=[hb[i], stb[j], fng_b], writes=[ob[i]])
        lo = max(t * 128, N_META)
        hi = min((t + 1) * 128, N_META + cfg.seq)
        if hi > lo:
            k.dma(k.sp, ods[i], self.out[lo - N_META:hi - N_META, :], ot[i][lo - t * 128:hi - t * 128, :], reads=[ob[i]])
    sc.close()


Prog.phase4 = _phase4
Prog.phase5 = _phase5
Prog.phase6 = _phase6
Prog.final_norm = _final_norm


_PROG_CACHE = {}


def kernel(x, meta_tokens, norm1_g, w_in, ssd_conv_w, ssd_conv_b, ssd_dt_bias, ssd_a_log, ssd_d,
           ssd_norm_g, lambda_q1, lambda_k1, lambda_q2, lambda_k2, attn_subln_g, w_ssd_branch,
           w_attn_branch, w_out, norm2_g, w_up, mlp_conv_w, mlp_conv_b, w_down, final_norm_g):
    inp = dict(x=x, meta_tokens=meta_tokens, norm1_g=norm1_g, w_in=w_in, ssd_conv_w=ssd_conv_w, ssd_conv_b=ssd_conv_b,
               ssd_dt_bias=ssd_dt_bias, ssd_a_log=ssd_a_log, ssd_d=ssd_d, ssd_norm_g=ssd_norm_g, lambda_q1=lambda_q1,
               lambda_k1=lambda_k1, lambda_q2=lambda_q2, lambda_k2=lambda_k2, attn_subln_g=attn_subln_g,
               w_ssd_branch=w_ssd_branch, w_attn_branch=w_attn_branch, w_out=w_out, norm2_g=norm2_g, w_up=w_up,
               mlp_conv_w=mlp_conv_w, mlp_conv_b=mlp_conv_b, w_down=w_down, final_norm_g=final_norm_g)
    inp = {k_: np.asarray(v) for k_, v in inp.items()}
    bsz, seq, _ = inp["x"].shape
    depth = inp["w_in"].shape[0]
    cfg = Cfg(seq=seq, depth=depth)
    key = (seq, depth)
    if key not in _PROG_CACHE:
        _PROG_CACHE[key] = Prog(cfg).build()
    nc = _PROG_CACHE[key]
    shared = make_in_map(cfg, inp, 0)
    in_maps = []
    for b in range(bsz):
        m = dict(shared)
        m["x"] = np.ascontiguousarray(inp["x"][b], dtype=np.float32)
        in_maps.append(m)
    res = run_bass_kernel_spmd(nc, in_maps, core_ids=list(range(bsz)))
    return np.stack([np.asarray(r["out"], dtype=np.float32) for r in res.results], axis=0)


def _p2_setup(self, l, sc):
    import os
    cfg, k, nc = self.cfg, self.k, self.nc
    NT = cfg.NT
    pp = self.pp
    tri = self.tri3[:, 0, :]
    upp = self.tri3[:, 1, :]

    def dbl(name, shape, dt, n=2):
        return [sc.sb(name, shape, dt) for _ in range(n)], [Buf(name) for _ in range(n)]
    def sgl(name, shape, dt):
        t_, b_ = sc.sb(name, shape, dt), Buf(name)
        return [t_, t_], [b_, b_]
    xbT, xbT_b = dbl("xbT", [128, 16, 128], BF16)
    zs, zs_b = dbl("zs", [128, D], BF16)
    dta, dta_b = dbl("dta", [128, 32], F32)
    ld_ds = [[k.dsem(f"p2ld{i}_{a}") for a in range(3)] for i in range(2)]
    xtm, xtm_b = dbl("xtm", [128, 16, 64], BF16)
    btm, btm_b = dbl("btm", [128, 4, 128], BF16)
    E, E_b = dbl("E", [128, 48], F32)
    w1, w1_b = dbl("w1", [128, 16], F32)
    Lh, Lh_b = sgl("Lh", [128, 16, 128], F32)
    Lh2_b = [Buf()] * 2
    Dg, Dg_b = dbl("Dg", [128, 4, 128], BF16)
    cbm, cbm_b = dbl("cbm", [128, 4, 128], BF16)
    Mg, Mg_b = dbl("Mg", [128, 4, 128], BF16)
    xdt, xdt_b = dbl("xdt", [128, 16, 64], BF16)
    xdd, xdd_b = dbl("xdd", [128, 16, 64], BF16)
    t1, t1_b = dbl("t1", [128, 256], F32)
    ysb, ysb_b = dbl("ysb", [128, D], F32)
    xD, xD_b = sgl("xD", [128, D], F32)
    junk = sc.sb("p2junk", [128, 256], BF16)
    junk_b = Buf()
    st, st_b = dbl("p2st", [128, 12], F32)
    yn, yn_b = sgl("yn", [128, D], BF16)
    Sf = sc.sb("Sf", [128, D], F32)
    Sf_b = [Buf() for _ in range(4)]
    Sbf, Sbf_b = dbl("Sbf", [128, D], BF16)
    Sbf_gb = [[Buf() for _ in range(4)] for _ in range(2)]
    mhalf = sc.sb("mhalf2", [128, 1], F32)
    bmh = Buf()
    k.op(k.pool, lambda: nc.gpsimd.memset(mhalf[:], -0.5), writes=[bmh])
    k.op(k.pool, lambda: nc.gpsimd.memset(Sf[:], 0.0), writes=Sf_b)
    k.op(k.pool, lambda: nc.gpsimd.memset(Sbf[1][:], 0.0), writes=Sbf_gb[1])
    pbf = sc.ps("pbf", [128, KC, 128], BF16)
    ptx = pbf
    ptb = pbf[:].rearrange("p c t -> p (c t)")
    ptx_b = ptb_b = Buf()
    pty = sc.ps("pty", [128, KC, 128], BF16)
    pty_b = Buf()

    cb = sc.ps("cbseg", [128, 4, 128], F32)
    cb_b = Buf()
    seg = [cb] * 2
    seg_b = [cb_b] * 2
    ydo = sc.ps("ydo", [128, 512], F32)
    yd_b, yo_b = Buf(), Buf()
    sne = sc.ps("sne", [128, 512], F32)
    sn = sne[:, 0:256]
    e3 = sne[:, 256:304]
    sn_b = Buf()
    segc = 0

    def chunk(i, ys_dst):
        nonlocal segc
        j = i % 2
        t0 = i * 128
        k.dma(k.sp, ld_ds[j][0], xbT[j][:], self.XBC[:, t0:t0 + 128].rearrange("(c p) t -> p c t", p=128), writes=[xbT_b[j]])
        k.dma(k.sp, ld_ds[j][1], zs[j][:], self.ZS[t0:t0 + 128, :], writes=[zs_b[j]])
        k.dma(k.sp, ld_ds[j][2], dta[j][:], self.DT[t0:t0 + 128, :], writes=[dta_b[j]])
        a = dta[j][:, 16:32]
        dt = dta[j][:, 0:16]
        for c in range(KC):
            k.op(k.pe, lambda: nc.tensor.transpose(out=ptx[:, c, :], in_=xbT[j][:, c, :], identity=self.ident[:]),
                 reads=[xbT_b[j], self.b_const], writes=[ptx_b], signal=(c == KC - 1))
        k.op(k.act, lambda: nc.scalar.copy(out=xtm[j][:].rearrange("p h d -> p (h d)"), in_=ptx[:].rearrange("p c t -> p (c t)")),
             reads=[ptx_b], writes=[xtm_b[j]])
        for g in range(4):
            k.op(k.pe, lambda: nc.tensor.transpose(out=ptb[:, g * 128:(g + 1) * 128], in_=xbT[j][:, 8 + g, :], identity=self.ident[:]),
                 reads=[xbT_b[j], self.b_const], writes=[ptb_b], signal=(g == 3))
        k.op(k.dve, lambda: nc.vector.tensor_copy(out=btm[j][:].rearrange("p g n -> p (g n)"), in_=ptb[:, 0:512]),
             reads=[ptb_b], writes=[btm_b[j]])
        for q in range(3):
            k.op(k.pe, lambda: nc.tensor.matmul(e3[:, q * 16:(q + 1) * 16], lhsT=self.tri3[:, q, :], rhs=a, start=True, stop=True),
                 reads=[dta_b[j], self.b_const], writes=[sn_b], signal=(q == 2))
        k.op(k.act, lambda: nc.scalar.activation(out=E[j][:], in_=e3, func=AF.Exp), reads=[sn_b], writes=[E_b[j]])
        k.op(k.dve, lambda: nc.vector.tensor_tensor(out=w1[j][:], in0=dt, in1=E[j][:, 16:32], op=ALU.mult),
             reads=[dta_b[j], E_b[j]], writes=[w1_b[j]])
        k.op(k.dve, lambda: nc.vector.tensor_tensor(out=Lh[j][:, 0:8, :], in0=bcast_free(upp, 0, 8), in1=bcast_free(a[:, 0:8], 1, 128), op=ALU.mult),
             reads=[dta_b[j], self.b_const], writes=[Lh_b[j]])
        k.op(k.pool, lambda: nc.gpsimd.tensor_tensor(out=Lh[j][:, 8:16, :], in0=bcast_free(upp, 0, 8), in1=bcast_free(a[:, 8:16], 1, 128), op=ALU.mult),
             reads=[dta_b[j], self.b_const], writes=[Lh2_b[j]])
        k.op(k.pool, lambda: nc.gpsimd.tensor_tensor(out=xdt[j][:], in0=xtm[j][:], in1=bcast_free(dt, 1, 64), op=ALU.mult),
             reads=[xtm_b[j], dta_b[j]], writes=[xdt_b[j]])
        k.op(k.pool, lambda: nc.gpsimd.tensor_tensor(out=xdd[j][:], in0=xtm[j][:], in1=bcast_free(w1[j][:], 1, 64), op=ALU.mult),
             reads=[xtm_b[j], w1_b[j]], writes=[xdd_b[j]])
        k.op(k.pool, lambda: nc.gpsimd.tensor_tensor(out=xD[j][:].rearrange("p (h d) -> p h d", d=64), in0=xtm[j][:],
                                                     in1=bcast_free(pp[:, PP["dsk"]:PP["dsk"] + 16], 1, 64), op=ALU.mult),
             reads=[xtm_b[j], self.b_pp], writes=[xD_b[j]])
        for g in range(4):
            k.op(k.pe, lambda: nc.tensor.matmul(cb[:, g, :], lhsT=xbT[j][:, 8 + g, :], rhs=xbT[j][:, 12 + g, :], start=True, stop=True),
                 reads=[xbT_b[j]], writes=[cb_b], signal=(g == 3))
        k.op(k.dve, lambda: nc.vector.tensor_tensor(out=cbm[j][:], in0=cb[:], in1=bcast_free(tri, 0, 4), op=ALU.mult),
             reads=[cb_b, self.b_const], writes=[cbm_b[j]])
        sprev, snew = Sbf[(i + 1) % 2], Sbf[i % 2]
        sprev_gb, snew_gb = Sbf_gb[(i + 1) % 2], Sbf_gb[i % 2]
        for g in range(4):
            sg = segc % 2
            segc += 1
            for hh in range(4):
                k.op(k.pe, lambda: nc.tensor.matmul(seg[sg][:, hh, :], lhsT=Lh[j][:, g * 4 + hh, :], rhs=tri, start=True, stop=True),
                     reads=[Lh_b[j] if g < 2 else Lh2_b[j], self.b_const], writes=[seg_b[sg]], signal=(hh == 3))
            k.op(k.act, lambda: nc.scalar.activation(out=Dg[j][:], in_=seg[sg][:], func=AF.Exp), reads=[seg_b[sg]], writes=[Dg_b[j]])
            k.op(k.dve, lambda: nc.vector.tensor_tensor(out=Mg[j][:], in0=Dg[j][:], in1=bcast_free(cbm[j][:, g, :], 0, 4), op=ALU.mult),
                 reads=[Dg_b[j], cbm_b[j]], writes=[Mg_b[j]])
            for hh in range(4):
                k.op(k.pe, lambda: nc.tensor.matmul(ydo[:, hh * 64:(hh + 1) * 64], lhsT=Mg[j][:, hh, :], rhs=xdt[j][:, g * 4 + hh, :], start=True, stop=True),
                     reads=[Mg_b[j], xdt_b[j]], writes=[yd_b], signal=False)
            k.op(k.pe, lambda: nc.tensor.matmul(ydo[:, 256:512], lhsT=xbT[j][:, 12 + g, :], rhs=sprev[:, g * 256:(g + 1) * 256], start=True, stop=True),
                 reads=[xbT_b[j], sprev_gb[g]], writes=[yd_b])
            k.op(k.pe, lambda: nc.tensor.matmul(sn, lhsT=btm[j][:, g, :], rhs=xdd[j][:, g * 4:(g + 1) * 4, :].rearrange("p h d -> p (h d)"), start=True, stop=True),
                 reads=[btm_b[j], xdd_b[j]], writes=[sn_b])
            k.op(k.dve, lambda: nc.vector.tensor_tensor(out=t1[j][:].rearrange("p (h d) -> p h d", d=64), in0=ydo[:, 256:512].rearrange("p (h d) -> p h d", d=64),
                                                        in1=bcast_free(E[j][:, g * 4:(g + 1) * 4], 1, 64), op=ALU.mult),
                 reads=[yd_b, E_b[j]], writes=[t1_b[j]])
            k.op(k.dve, lambda: nc.vector.tensor_tensor(out=ysb[j][:, g * 256:(g + 1) * 256], in0=ydo[:, 0:256], in1=t1[j][:], op=ALU.add),
                 reads=[yd_b, t1_b[j]], writes=[ysb_b[j]])
            sfv = Sf[:, g * 256:(g + 1) * 256]
            k.op(k.dve, lambda: nc.vector.tensor_tensor(out=sfv.rearrange("p (h d) -> p h d", d=64), in0=sfv.rearrange("p (h d) -> p h d", d=64),
                                                        in1=bcast_free(E[j][:, 32 + g * 4:32 + (g + 1) * 4], 1, 64), op=ALU.mult),
                 reads=[E_b[j]], writes=[Sf_b[g]])
            k.op(k.dve, lambda: nc.vector.tensor_tensor(out=sfv, in0=sn, in1=sfv, op=ALU.add), reads=[sn_b], writes=[Sf_b[g]])
            k.op(k.act, lambda: nc.scalar.copy(out=snew[:, g * 256:(g + 1) * 256], in_=sfv), reads=[Sf_b[g]], writes=[snew_gb[g]])
        k.op(k.pool, lambda: nc.gpsimd.tensor_tensor(out=ysb[j][:], in0=ysb[j][:], in1=xD[j][:], op=ALU.add), reads=[xD_b[j]], writes=[ysb_b[j]])
        k.op(k.dve, lambda: nc.vector.tensor_tensor(out=ysb[j][:], in0=ysb[j][:], in1=zs[j][:], op=ALU.mult), reads=[zs_b[j]], writes=[ysb_b[j]])
        for g in range(4):
            k.op(k.act, lambda: nc.scalar.activation(out=junk[:], in_=ysb[j][:, g * 256:(g + 1) * 256], func=AF.Square, accum_out=st[j][:, g:g + 1]),
                 reads=[ysb_b[j]], writes=[junk_b, st_b[j]])
        k.op(k.dve, lambda: nc.vector.tensor_scalar(out=st[j][:, 4:8], in0=st[j][:, 0:4], scalar1=1.0 / 256, scalar2=NORM_EPS, op0=ALU.mult, op1=ALU.add),
             reads=[st_b[j]], writes=[st_b[j]])
        k.op(k.pool, lambda: nc.gpsimd.tensor_tensor(out=st[j][:, 8:12], in0=st[j][:, 4:8], in1=bcast_free(mhalf[:, 0:1], 0, 4)[:, :, 0], op=ALU.pow),
             reads=[st_b[j], bmh], writes=[st_b[j]])
        k.op(k.dve, lambda: nc.vector.tensor_tensor(out=yn[j][:].rearrange("p (g d) -> p g d", d=256), in0=ysb[j][:].rearrange("p (g d) -> p g d", d=256),
                                                    in1=bcast_free(st[j][:, 8:12], 1, 256), op=ALU.mult),
             reads=[ysb_b[j], st_b[j]], writes=[yn_b[j]])
        for c in range(KC):
            k.op(k.pe, lambda: nc.tensor.transpose(out=pty[:, c, :], in_=yn[j][:, c * 128:(c + 1) * 128], identity=self.ident[:]),
                 reads=[yn_b[j], self.b_const], writes=[pty_b], signal=(c == KC - 1))
        ydst, ydst_b = ys_dst
        k.op(k.dve, lambda: nc.vector.tensor_tensor(out=ydst, in0=pty[:], in1=bcast_free(pp[:, PP["sng"]:PP["sng"] + KC], 1, 128), op=ALU.mult),
             reads=[pty_b, self.b_pp], writes=[ydst_b])
    return chunk


def _p4_setup(self, l, sc, wts, wt_b):
    cfg, k, nc = self.cfg, self.k, self.nc
    wsb, wab, wout = wts
    ins = [[sc.sb("p4in", [128, KC, 512], BF16) for _ in range(4)] for _ in range(2)]
    ins_b = [[Buf() for _ in range(4)] for _ in range(2)]
    ys_b = [[Buf() for _ in range(4)] for _ in range(2)]
    ins_ds = [k.dsem(f"p4in{i}") for i in range(2)]
    mg = [sc.sb("mg", [128, KC, 512], BF16)] * 2
    mg_b = [[Buf() for _ in range(KC)]] * 2
    tA = [sc.sb("p4ta", [128, 512], F32)] * 2
    tA_b = [Buf()] * 2
    tB = [sc.sb("p4tb", [128, 512], F32)] * 2
    tB_b = [Buf()] * 2
    NH = 2
    ht = [sc.sb("p4h", [128, D], F32) for _ in range(NH)]
    ht_b = [Buf() for _ in range(NH)]
    ht_ds = [k.dsem(f"p4h{i}") for i in range(NH)]
    ps1 = [sc.ps("p4a", [128, 512], F32)] * 2
    ps1_b = [Buf()] * 2
    ps2 = [sc.ps("p4b", [128, 512], F32)] * 2
    ps2_b = [Buf()] * 2
    ps3 = [sc.ps("p4c", [128, 512], F32)] * 2
    ps3_b = [Buf()] * 2
    c1 = c3 = ch = 0
    srcs = (self.YS, self.YA, self.GS, self.GA)

    def block(bi):
        nonlocal c1, c3, ch
        t0, n = cfg.blocks[bi]
        s = bi % 2
        for a in range(1, 4):
            k.dma(k.sp, ins_ds[s], ins[s][a][:, :, :n], srcs[a][:, t0:t0 + n].rearrange("(c p) t -> p c t", p=128), writes=[ins_b[s][a]])
        ys, ya, gs, ga = ins[s]
        for c in range(KC):
            pi = c1 % 2
            c1 += 1
            for kk in range(KC):
                k.op(k.pe, lambda: nc.tensor.matmul(ps1[pi][:, :n], lhsT=wsb[:, kk, c * 128:(c + 1) * 128], rhs=ys[:, kk, :n], start=(kk == 0), stop=(kk == KC - 1)),
                     reads=[wt_b[0][kk]] + ys_b[s][:n // 128], writes=[ps1_b[pi]], signal=(kk == KC - 1))
            for kk in range(KC):
                k.op(k.pe, lambda: nc.tensor.matmul(ps2[pi][:, :n], lhsT=wab[:, kk, c * 128:(c + 1) * 128], rhs=ya[:, kk, :n], start=(kk == 0), stop=(kk == KC - 1)),
                     reads=[wt_b[1][kk], ins_b[s][1]], writes=[ps2_b[pi]], signal=(kk == KC - 1))
            k.op(k.dve, lambda: nc.vector.tensor_tensor(out=tA[pi][:, :n], in0=ps1[pi][:, :n], in1=gs[:, c, :n], op=ALU.mult), reads=[ps1_b[pi], ins_b[s][2]], writes=[tA_b[pi]])
            k.op(k.dve, lambda: nc.vector.tensor_tensor(out=tB[pi][:, :n], in0=ps2[pi][:, :n], in1=ga[:, c, :n], op=ALU.mult), reads=[ps2_b[pi], ins_b[s][3]], writes=[tB_b[pi]])
            k.op(k.pool, lambda: nc.gpsimd.tensor_tensor(out=mg[s][:, c, :n], in0=tA[pi][:, :n], in1=tB[pi][:, :n], op=ALU.add), reads=[tA_b[pi], tB_b[pi]], writes=[mg_b[s][c]])
        for jq in range(n // 128):
            t = t0 // 128 + jq
            hi = ch % NH
            ch += 1
            k.dma(k.sp, ht_ds[hi], ht[hi][:], self.H[t * 128:(t + 1) * 128, :], writes=[ht_b[hi]])
            for half in range(2):
                pi = c3 % 2
                c3 += 1
                for kk in range(KC):
                    k.op(k.pe, lambda: nc.tensor.matmul(ps3[pi][:, :], lhsT=mg[s][:, kk, jq * 128:(jq + 1) * 128], rhs=wout[:, kk, half * 512:(half + 1) * 512],
                                                        start=(kk == 0), stop=(kk == KC - 1)),
                         reads=[wt_b[2][kk]] + mg_b[s], writes=[ps3_b[pi]], signal=(kk == KC - 1))
                k.op(k.dve, lambda: nc.vector.tensor_tensor(out=ht[hi][:, half * 512:(half + 1) * 512], in0=ps3[pi][:, :], in1=ht[hi][:, half * 512:(half + 1) * 512], op=ALU.add),
                     reads=[ps3_b[pi]], writes=[ht_b[hi]])
            k.dma(k.sp, ht_ds[hi], self.H[t * 128:(t + 1) * 128, :], ht[hi][:], reads=[ht_b[hi]])
    return block, ins, ys_b


def _phase24(self, l, wts, wt_b):
    cfg, k = self.cfg, self.k
    sc = Scope(k)
    chunk = _p2_setup(self, l, sc)
    block, ins, ys_b = _p4_setup(self, l, sc, wts, wt_b)
    for i in range(cfg.NT):
        bi, q = i // 4, i % 4
        s = bi % 2
        chunk(i, (ins[s][0][:, :, q * 128:(q + 1) * 128], ys_b[s][q]))
        if q == 3 or i == cfg.NT - 1:
            block(bi)
            if "YS" in cfg.debug_outs:
                t0, n = cfg.blocks[bi]
                k.dma(k.sp, k.dsem("dbgys"), self.YS[:, t0:t0 + n].rearrange("(c p) t -> p c t", p=128), ins[s][0][:, :, :n], reads=ys_b[s][:n // 128])
    sc.close()


Prog.phase24 = _phase24
```

```python
import numpy as np
import concourse.bass as bass
import concourse.mybir as mybir
from concourse.ap import AP
from concourse.bass_utils import run_bass_kernel_spmd

F32 = mybir.dt.float32
BF16 = mybir.dt.bfloat16
AF = mybir.ActivationFunctionType
ALU = mybir.AluOpType

D = 1024
KC = 8
NIN = 8208
DFF = 2816
NORM_EPS = 1e-6
SUBLN_EPS = 1e-5
N_META = 16


import types


def _freeze(fn):
    if fn.__closure__ is None:
        return fn
    cells = []
    for c in fn.__closure__:
        try:
            cells.append(types.CellType(c.cell_contents))
        except ValueError:
            cells.append(c)
    return types.FunctionType(fn.__code__, fn.__globals__, fn.__name__, fn.__defaults__, tuple(cells))


class Sem:
    def __init__(self, nc, name):
        self.h = nc.alloc_semaphore(name)
        self.total = 0
        self.waited = 0


class Buf:
    __slots__ = ("w", "r", "name")

    def __init__(self, name=""):
        self.w = []
        self.r = []
        self.name = name


class Op:
    __slots__ = ("eng", "fn", "isdma", "ds", "deps", "succ", "dur", "lat", "nun", "ready", "done", "sigval",
                 "needsig", "batch", "sched", "kw")


class Eng:
    def __init__(self, k, eng, name, is_pe=False, is_queue=False):
        self.k = k
        self.e = eng
        self.name = name
        self.sem = Sem(k.nc, "s_" + name)
        self.seen = {}
        self.is_pe = is_pe
        self.is_queue = is_queue

    def wait(self, toks):
        for s, v in toks.items():
            vv = s.total if v is None else v
            assert vv <= s.total, "waiting for a value that is never produced"
            if vv <= 0 or self.seen.get(s, 0) >= vv:
                continue
            self.e.wait_ge(s.h, vv)
            self.seen[s] = vv
            s.waited = max(s.waited, vv)


import os as _os
_SIG_LAT = 64.0
_WINDOW = int(_os.environ.get("K_WINDOW", "64"))
_WINDOW_Q = int(_os.environ.get("K_WINDOW_Q", "40"))


class K:
    def __init__(self, nc):
        self.nc = nc
        self.pe = Eng(self, nc.tensor, "pe", is_pe=True)
        self.act = Eng(self, nc.scalar, "act")
        self.dve = Eng(self, nc.vector, "dve")
        self.pool = Eng(self, nc.gpsimd, "pool")
        self.sp = Eng(self, nc.sync, "sp", is_queue=True)
        self.engs = [self.pe, self.act, self.dve, self.pool, self.sp]
        self.dsems = []
        self._dsem_cache = {}
        self.n_inst = 0
        self.batch = 0
        self.pending = []
        self.model_ns = 0.0
        self.flush_log = []

    def dsem(self, name):
        if name in self._dsem_cache:
            return self._dsem_cache[name]
        s = Sem(self.nc, "d_" + name)
        self.dsems.append(s)
        self._dsem_cache[name] = s
        return s

    def _record(self, o, reads, writes):
        bt = self.batch
        deps = {}
        for b in reads:
            for d in b.w:
                if d.batch == bt:
                    deps[id(d)] = d
        for b in writes:
            for d in b.w:
                if d.batch == bt:
                    deps[id(d)] = d
            for d in b.r:
                if d.batch == bt:
                    deps[id(d)] = d
        o.deps = list(deps.values())
        o.succ = []
        o.batch = bt
        o.sched = False
        o.sigval = 0
        for b in reads:
            b.r.append(o)
        for b in writes:
            b.w = [o]
            b.r = []
        self.pending.append(o)

    def _probe(self, fn):
        with self.nc.discard():
            inst = fn()
        ins = inst.ins
        n = 1
        ap = ins.outs[0].ap
        for st, cn in ap[1:]:
            n *= cn
        return n, ap[0][1], ins

    def op(self, eng, fn, reads=(), writes=(), signal=True):
        o = Op()
        o.eng = eng
        o.fn = _freeze(fn)
        o.isdma = False
        o.ds = None
        n, _, ins = self._probe(o.fn)
        if eng.is_pe:
            f32 = str(ins.ins[0].dtype).endswith("float32")
            o.dur = max(n, 64) / 2.0 * (4.0 if f32 else 1.0) + 8.0
        elif eng is self.act:
            o.dur = (n + 224) / 1.4
        elif eng is self.dve:
            o.dur = n / 0.96 + 62.0
        else:
            o.dur = n / 0.6 + 160.0
        o.lat = 0.0
        self._record(o, reads, writes)

    def dma(self, q, ds, out, in_, reads=(), writes=(), **kw):
        o = Op()
        o.eng = q
        o.isdma = True
        o.ds = ds
        o.fn = None
        o.kw = (out, in_, kw)
        nbytes = 1
        for st, cn in out.ap:
            nbytes *= cn
        nbytes *= 4 if out.dtype == F32 else 2
        o.dur = 60.0 if q.is_queue else 700.0
        o.lat = 2000.0 + nbytes / 120.0
        self._record(o, reads, writes)

    def flush(self):
        ops = self.pending
        self.pending = []
        if not ops:
            return
        lists = {e: [] for e in self.engs}
        for o in ops:
            o.nun = len(o.deps)
            o.ready = 0.0
            for d in o.deps:
                d.succ.append(o)
            lists[o.eng].append(o)
        head = {e: 0 for e in self.engs}
        free = {e: 0.0 for e in self.engs}
        cand = {e: None for e in self.engs}
        dirty = set(self.engs)
        order = []
        nleft = len(ops)
        while nleft:
            for e in dirty:
                lst = lists[e]
                h = head[e]
                while h < len(lst) and lst[h].sched:
                    h += 1
                head[e] = h
                best = None
                bt_ = 0.0
                fe = free[e]
                cnt = 0
                i = h
                win = _WINDOW_Q if (e.is_queue or e is self.pool) else _WINDOW
                while i < len(lst) and cnt < win:
                    o = lst[i]
                    i += 1
                    if o.sched:
                        continue
                    cnt += 1
                    if o.nun:
                        continue
                    if o.ready <= fe:
                        best, bt_ = o, fe
                        break
                    if best is None or o.ready < bt_:
                        best, bt_ = o, o.ready
                cand[e] = (best, bt_) if best is not None else None
            dirty.clear()
            pe_, pt_ = None, None
            for e in self.engs:
                c = cand[e]
                if c is not None and (pt_ is None or c[1] < pt_):
                    pe_, pt_ = e, c[1]
            o = cand[pe_][0]
            o.sched = True
            free[pe_] = pt_ + o.dur
            o.done = pt_ + o.dur + o.lat
            order.append(o)
            nleft -= 1
            dirty.add(pe_)
            dn = o.done + _SIG_LAT
            for s_ in o.succ:
                s_.nun -= 1
                if dn > s_.ready:
                    s_.ready = dn
                if s_.nun == 0:
                    dirty.add(s_.eng)
        self.model_ns += max(o.done for o in ops)
        self.flush_log.append((len(ops), max(o.done for o in ops), {e.name: sum(o.dur for o in ops if o.eng is e) for e in self.engs}))
        last = {}
        for i_, o in enumerate(order):
            o.nun = i_
            o.needsig = False
            if not o.isdma:
                last[o.eng] = o
        for o in order:
            per = {}
            for d in o.deps:
                if d.isdma or (d.eng is o.eng and o.eng.is_pe):
                    continue
                p_ = per.get(d.eng)
                if p_ is None or d.nun > p_.nun:
                    per[d.eng] = d
            for d in per.values():
                d.needsig = True
        for o in last.values():
            o.needsig = True
        for o in order:
            e = o.eng
            toks = {}
            for d in o.deps:
                if d.isdma:
                    toks[d.ds] = None
                elif d.eng is e and e.is_pe:
                    continue
                else:
                    sm = d.eng.sem
                    if toks.get(sm, 0) is not None and d.sigval > toks.get(sm, 0):
                        toks[sm] = d.sigval
            e.wait(toks)
            if o.isdma:
                ds = o.ds
                if ds.total > 0 and ds.waited >= ds.total:
                    e.wait({ds: ds.total})
                out, in_, kw = o.kw
                inst = e.e.dma_start(out=out, in_=in_, **kw)
                inst.then_inc(ds.h, 16)
                ds.total += 16
            else:
                inst = o.fn()
                if o.needsig:
                    e.sem.total += 1
                    inst.then_inc(e.sem.h, 1)
                    o.sigval = e.sem.total
            o.fn = None
            o.kw = None
            self.n_inst += 1

    def barrier(self):
        self.flush()
        self.batch += 1
        toks = {}
        for e in self.engs:
            if not e.is_queue:
                toks[e.sem] = e.sem.total
        for s in self.dsems:
            toks[s] = s.total
        for e in self.engs:
            t = dict(toks)
            if e.is_pe or e.is_queue:
                t.pop(e.sem, None)
            e.wait(t)


def bcast_free(ap, pos, n):
    a = [list(x) for x in ap.ap]
    a.insert(1 + pos, [0, n])
    return AP(ap.tensor, ap.offset, a)


class Cfg:
    def __init__(self, seq=4096, depth=4, debug_outs=(), stop_after=None):
        self.seq = seq
        self.L = depth
        self.n_tok = N_META + seq
        self.NT = -(-self.n_tok // 128)
        self.T = self.NT * 128
        self.blocks = [(t0, min(512, self.T - t0)) for t0 in range(0, self.T, 512)]
        self.debug_outs = set(debug_outs)
        self.stop_after = stop_after


PP = {}
_o = 0
for _n, _w in (("g1", 8), ("g2", 8), ("sng", 8), ("subln", 1), ("cw", 64), ("cb", 16), ("mcw", 132), ("mcb", 44),
               ("dtb", 16), ("alog", 16), ("dsk", 16), ("lam", 256)):
    PP[_n] = _o
    _o += _w
NP_ = _o


def pack_params(inp, l):
    f = lambda a: np.asarray(a, np.float32)
    pp = np.zeros((128, NP_), np.float32)
    colT = lambda v: f(v).reshape(-1, 128).T
    pp[:, PP["g1"]:PP["g1"] + 8] = colT(inp["norm1_g"][l])
    pp[:, PP["g2"]:PP["g2"] + 8] = colT(inp["norm2_g"][l])
    pp[:, PP["sng"]:PP["sng"] + 8] = colT(inp["ssd_norm_g"][l])
    pp[:, PP["subln"]:PP["subln"] + 1] = f(inp["attn_subln_g"][l]).reshape(128, 1)
    cw = f(inp["ssd_conv_w"][l])
    pp[:, PP["cw"]:PP["cw"] + 64] = cw.reshape(4, 16, 128).transpose(2, 1, 0).reshape(128, 64)
    pp[:, PP["cb"]:PP["cb"] + 16] = colT(inp["ssd_conv_b"][l])
    mw = f(inp["mlp_conv_w"][l])
    pp[:, PP["mcw"]:PP["mcw"] + 132] = mw.reshape(3, 44, 128).transpose(2, 1, 0).reshape(128, 132)
    pp[:, PP["mcb"]:PP["mcb"] + 44] = colT(inp["mlp_conv_b"][l])
    pp[:, PP["dtb"]:PP["dtb"] + 16] = np.broadcast_to(f(inp["ssd_dt_bias"][l])[None], (128, 16))
    pp[:, PP["alog"]:PP["alog"] + 16] = np.broadcast_to(f(inp["ssd_a_log"][l])[None], (128, 16))
    pp[:, PP["dsk"]:PP["dsk"] + 16] = np.broadcast_to(f(inp["ssd_d"][l])[None], (128, 16))
    lam = np.concatenate([f(inp[n][l]) for n in ("lambda_q1", "lambda_k1", "lambda_q2", "lambda_k2")])
    pp[:, PP["lam"]:PP["lam"] + 256] = np.broadcast_to(lam[None], (128, 256))
    return pp


def make_consts(cfg):
    T = cfg.T
    c = {}
    c["ident"] = np.eye(128, dtype=np.float32)
    perm = np.zeros((128, 128), np.float32)
    for m in range(2):
        for dd in range(16):
            src = dd + 8 if dd < 8 else dd - 8
            perm[m * 64 + src, m * 64 + dd] = 1.0
    c["perm"] = perm
    kk = np.arange(128)
    tri = (kk[:, None] <= kk[None, :]).astype(np.float32)
    upp = (kk[:, None] > kk[None, :]).astype(np.float32)
    c["tri3"] = np.stack([tri, upp, np.ones((128, 128), np.float32)], 1)
    cidr = np.where(kk < 16, 0, np.where(kk < 80, 1, 2))
    mdiag = (cidr[:, None] <= cidr[None, :]).astype(np.float32)
    mnext = ((kk[:, None] < 16) & (kk[None, :] >= 80)).astype(np.float32)
    c["amask"] = np.stack([mdiag, mnext], 1)
    pos = np.arange(T, dtype=np.float32)
    inv = (1.0 / (np.float32(500000.0) ** (np.arange(8, dtype=np.float32) * np.float32(2.0) / np.float32(16)))).astype(np.float32)
    ang = (pos[:, None] * inv[None, :]).astype(np.float32)
    cs, sn = np.cos(ang).astype(np.float32), np.sin(ang).astype(np.float32)
    cosT = np.ones((128, T), np.float32)
    sinT = np.zeros((128, T), np.float32)
    for m in range(2):
        for dd in range(16):
            cosT[m * 64 + dd] = cs[:, dd % 8]
            sinT[m * 64 + dd] = -sn[:, dd % 8] if dd < 8 else sn[:, dd % 8]
    c["ropec"] = cosT
    c["ropes"] = sinT
    return c


class Scope:
    _uid = 0

    def __init__(self, k):
        from contextlib import ExitStack
        self.k = k
        self.st = ExitStack()

    def sb(self, name, shape, dt):
        Scope._uid += 1
        return self.st.enter_context(self.k.nc.sbuf_tensor(f"{name}_{Scope._uid}", list(shape), dt))

    def ps(self, name, shape, dt):
        Scope._uid += 1
        return self.st.enter_context(self.k.nc.psum_tensor(f"{name}_{Scope._uid}", list(shape), dt))

    def close(self):
        self.k.barrier()
        self.st.close()


class Prog:
    def __init__(self, cfg):
        self.cfg = cfg
        nc = self.nc = bass.Bass("TRN2", target_bir_lowering=False)
        self.k = K(nc)
        T, L = cfg.T, cfg.L
        ext = lambda n, s, dt=F32: nc.dram_tensor(n, list(s), dt, kind="ExternalInput").ap()
        self.x = ext("x", [cfg.seq, D])
        self.meta = ext("meta", [N_META, D])
        self.w_in = ext("w_in", [L, D, NIN])
        self.w_sb = ext("w_sb", [L, D, D])
        self.w_ab = ext("w_ab", [L, D, D])
        self.w_out = ext("w_out", [L, D, D])
        self.w_up = ext("w_up", [L, D, 2 * DFF])
        self.w_down = ext("w_down", [L, DFF, D])
        self.pp_d = ext("pp", [L, 128, NP_])
        self.fng_d = ext("fng", [128, D])
        self.c_ident = ext("ident", [128, 128])
        self.c_perm = ext("perm", [128, 128])
        self.c_tri3 = ext("tri3", [128, 3, 128])
        self.c_amask = ext("amask", [128, 2, 128])
        self.c_ropec = ext("ropec", [128, T])
        self.c_ropes = ext("ropes", [128, T])
        self.out = nc.dram_tensor("out", [cfg.seq, D], F32, kind="ExternalOutput").ap()

        def scr(n, s, dt):
            kind = "ExternalOutput" if n in cfg.debug_outs else "Internal"
            return nc.dram_tensor(n, list(s), dt, kind=kind).ap()
        self.H = scr("H", [T, D], F32)
        self.ZS = scr("ZS", [T, D], BF16)
        self.XBC = scr("XBC", [2048, T], BF16)
        self.DT = scr("DT", [T, 32], F32)
        self.QT = scr("QT", [D, T], BF16)
        self.KT = scr("KT", [D, T], BF16)
        self.V = scr("V", [T, D], BF16)
        self.GS = scr("GS", [D, T], BF16)
        self.GA = scr("GA", [D, T], BF16)
        self.YS = scr("YS", [D, T], BF16)
        self.YA = scr("YA", [D, T], BF16)
        self.AT = scr("AT", [DFF, T], BF16)

    def build(self):
        cfg, k, nc = self.cfg, self.k, self.nc
        top = Scope(k)
        self.top = top
        self.ident_f = top.sb("identf", [128, 128], F32)
        self.ident = top.sb("ident", [128, 128], BF16)
        self.perm = top.sb("perm", [128, 128], BF16)
        self.tri3 = top.sb("tri3", [128, 3, 128], F32)
        self.amask = top.sb("amask", [128, 2, 128], BF16)
        self.pp = top.sb("pp", [128, NP_], F32)
        self.b_const = Buf("const")
        self.b_pp = Buf("pp")
        ds = k.dsem("const")
        self.ds_pp = k.dsem("pp")
        tmpf = top.sb("ctmp", [128, 3, 128], F32)
        bt = Buf()
        k.dma(k.sp, ds, self.ident_f[:], self.c_ident[:, :], writes=[bt])
        k.op(k.dve, lambda: nc.vector.tensor_copy(out=self.ident[:], in_=self.ident_f[:]), reads=[bt], writes=[self.b_const])
        bt2 = Buf()
        k.dma(k.sp, ds, tmpf[:, 0, :], self.c_perm[:, :], writes=[bt2])
        k.dma(k.sp, ds, tmpf[:, 1:3, :], self.c_amask[:, :, :], writes=[bt2])
        k.op(k.dve, lambda: nc.vector.tensor_copy(out=self.perm[:], in_=tmpf[:, 0, :]), reads=[bt2], writes=[self.b_const])
        k.op(k.dve, lambda: nc.vector.tensor_copy(out=self.amask[:], in_=tmpf[:, 1:3, :]), reads=[bt2], writes=[self.b_const])
        k.dma(k.sp, ds, self.tri3[:], self.c_tri3[:, :, :], writes=[self.b_const])
        self.phase0()
        for l in range(cfg.L):
            self.layer(l)
            if cfg.stop_after is not None and cfg.stop_after[0] == l:
                break
        if cfg.stop_after is None:
            self.final_norm()
        k.barrier()
        top.st.close()
        return nc

    def phase0(self):
        cfg, k, nc = self.cfg, self.k, self.nc
        sc = Scope(k)
        ds = k.dsem("p0")
        b = Buf()
        k.dma(k.sp, ds, self.H[0:N_META, :], self.meta[:, :], writes=[b])
        pdiv = max(p for p in (128, 64, 32, 16, 8, 4, 2, 1) if cfg.seq % p == 0)
        xs = self.x.rearrange("(p r) d -> p (r d)", p=pdiv)
        hs = self.H[N_META:N_META + cfg.seq, :].rearrange("(p r) d -> p (r d)", p=pdiv)
        k.dma(k.sp, ds, hs, xs, writes=[b])
        npad = cfg.T - cfg.n_tok
        if npad > 0:
            z = sc.sb("zero", [128, D], F32)
            bz = Buf()
            k.op(k.dve, lambda: nc.vector.memset(z[:], 0.0), writes=[bz])
            k.dma(k.sp, ds, self.H[cfg.n_tok:cfg.T, :], z[0:npad, :], reads=[bz])
        sc.close()

    def layer(self, l):
        cfg, k = self.cfg, self.k
        k.dma(k.sp, self.ds_pp, self.pp[:], self.pp_d[l, :, :], writes=[self.b_pp])
        stop = cfg.stop_after[1] if (cfg.stop_after is not None and cfg.stop_after[0] == l) else 99
        sc = Scope(k)
        UT = sc.sb("UT", [128, KC, cfg.T], BF16)
        ut_bufs = [Buf(f"ut{t}") for t in range(cfg.NT)]
        self.norm_pass(sc, l, PP["g1"], UT, ut_bufs)
        self.phase1(sc, l, UT, ut_bufs)
        sc.close()
        if stop <= 1:
            return
        scw = Scope(k)
        w4 = []
        w4_b = [[Buf() for _ in range(KC)] for _ in range(3)]
        ds_w = k.dsem("p4w")
        for wi, (nm, src) in enumerate((("wsb", self.w_sb), ("wab", self.w_ab), ("wout", self.w_out))):
            w = scw.sb(nm, [128, KC, D], BF16)
            for kk in range(KC):
                k.dma(k.pool, ds_w, w[:, kk, :], src[l, kk * 128:(kk + 1) * 128, :], writes=[w4_b[wi][kk]])
            w4.append(w)
        self.phase2(l)
        if stop > 2:
            self.phase3(l)
        if stop > 3:
            self.phase4(l, w4, w4_b)
        scw.close()
        if stop <= 4:
            return
        scw = Scope(k)
        NC = DFF // 128
        wd = scw.sb("wd", [128, NC, D], BF16)
        wd_b = [Buf() for _ in range(NC)]
        self._wd_prefetch = (wd, wd_b, k.dsem("p6w"))
        sc = Scope(k)
        UT = sc.sb("UT", [128, KC, cfg.T], BF16)
        ut_bufs = [Buf(f"ut{t}") for t in range(cfg.NT)]
        self.norm_pass(sc, l, PP["g2"], UT, ut_bufs)
        self.phase5(sc, l, UT, ut_bufs)
        sc.close()
        if stop > 5:
            self.phase6(l, wd, wd_b)
        scw.close()

    def norm_pass(self, sc, l, goff, UT, ut_bufs):
        cfg, k, nc = self.cfg, self.k, self.nc
        NB = 3
        ht = [sc.sb("nh", [128, D], F32) for _ in range(NB)]
        hb = [Buf() for _ in range(NB)]
        hds = [k.dsem(f"nh{i}") for i in range(NB)]
        junk = sc.sb("njunk", [128, D], BF16)
        bjunk = Buf()
        hn = [sc.sb("nhn", [128, D], BF16) for _ in range(2)]
        hnb = [Buf() for _ in range(2)]
        st = [sc.sb("nst", [128, 4], F32) for _ in range(2)]
        stb = [Buf() for _ in range(2)]
        mhalf = sc.sb("mhalf", [128, 1], F32)
        bmh = Buf()
        k.op(k.pool, lambda: nc.gpsimd.memset(mhalf[:], -0.5), writes=[bmh])
        pst = [sc.ps("npt", [128, KC, 128], BF16) for _ in range(2)]
        psb = [Buf() for _ in range(2)]
        gT = self.pp[:, goff:goff + KC]
        for t in range(cfg.NT):
            i, j = t % NB, t % 2
            k.dma(k.sp, hds[i], ht[i][:], self.H[t * 128:(t + 1) * 128, :], writes=[hb[i]])
            k.op(k.dve, lambda: nc.vector.scalar_tensor_tensor(out=junk[:], in0=ht[i][:], scalar=1.0, in1=ht[i][:],
                                                               op0=ALU.mult, op1=ALU.mult, accum_out=st[j][:, 0:1]),
                 reads=[hb[i]], writes=[bjunk, stb[j]])
            k.op(k.dve, lambda: nc.vector.tensor_scalar(out=st[j][:, 1:2], in0=st[j][:, 0:1], scalar1=1.0 / D, scalar2=NORM_EPS,
                                                        op0=ALU.mult, op1=ALU.add), reads=[stb[j]], writes=[stb[j]])
            k.op(k.pool, lambda: nc.gpsimd.tensor_tensor(out=st[j][:, 2:3], in0=st[j][:, 1:2], in1=mhalf[:], op=ALU.pow),
                 reads=[stb[j], bmh], writes=[stb[j]])
            k.op(k.act, lambda: nc.scalar.activation(out=hn[j][:], in_=ht[i][:], func=AF.Copy, scale=st[j][:, 2:3]),
                 reads=[hb[i], stb[j]], writes=[hnb[j]])
            for c in range(KC):
                k.op(k.pe, lambda: nc.tensor.transpose(out=pst[j][:, c, :], in_=hn[j][:, c * 128:(c + 1) * 128], identity=self.ident[:]),
                     reads=[hnb[j], self.b_const], writes=[psb[j]], signal=(c == KC - 1))
            k.op(k.dve, lambda: nc.vector.tensor_tensor(out=UT[:, :, t * 128:(t + 1) * 128], in0=pst[j][:, :, :],
                                                        in1=bcast_free(gT, 1, 128), op=ALU.mult),
                 reads=[psb[j], self.b_pp], writes=[ut_bufs[t]])

    def phase1(self, sc, l, UT, ut_bufs):
        cfg, k, nc = self.cfg, self.k, self.nc
        T, NT, blocks = cfg.T, cfg.NT, cfg.blocks
        pp = self.pp
        W = self.w_in[l]
        NW = 3
        wfm = [sc.sb("wfm", [128, KC, 128], BF16) for _ in range(NW)]
        wfm_b = [Buf() for _ in range(NW)]
        wfm_ds = [k.dsem(f"wfm{i}") for i in range(NW)]
        wtm = [sc.sb("wtm", [128, KC, 512], BF16) for _ in range(2)]
        wtm_b = [Buf() for _ in range(2)]
        wtm_ds = [k.dsem(f"wtm{i}") for i in range(2)]
        NX = 2
        xc = [sc.sb("xc", [128, T + 4], BF16) for _ in range(NX)]
        xc_b = [[Buf() for _ in blocks] for _ in range(NX)]
        xc_pad = [Buf() for _ in range(NX)]
        ost = [sc.sb("ost", [128, T], BF16) for _ in range(NX)]
        ost_b = [[Buf() for _ in blocks] for _ in range(NX)]
        ost_ds = [k.dsem(f"ost{i}") for i in range(NX)]
        dg = [sc.sb("dg", [128, 4, 128], BF16) for _ in range(NX)]
        dg_b = [Buf() for _ in range(NX)]
        ropec = sc.sb("ropec", [128, T], F32)
        ropes = sc.sb("ropes", [128, T], F32)
        b_rope = Buf()
        b_rope2 = Buf()
        ds_rope = k.dsem("rope")
        k.dma(k.sp, ds_rope, ropec[:], self.c_ropec[:, :], writes=[b_rope])
        k.dma(k.sp, ds_rope, ropes[:], self.c_ropes[:, :], writes=[b_rope2])
        rt1 = [sc.sb("rt1", [128, 512], F32) for _ in range(2)]
        rt1_b = [Buf() for _ in range(2)]
        rt2 = [sc.sb("rt2", [128, 512], F32) for _ in range(2)]
        rt2_b = [Buf() for _ in range(2)]
        NPS = 4
        ps = [sc.ps("p1a", [128, 512], F32) for _ in range(NPS)]
        ps_b = [Buf() for _ in range(NPS)]
        ps2 = [sc.ps("p1b", [128, 512], F32) for _ in range(2)]
        ps2_b = [Buf() for _ in range(2)]
        tst = [sc.sb("tst", [128, 512], BF16) for _ in range(3)]
        tst_b = [Buf() for _ in range(3)]
        tst_ds = [k.dsem(f"tst{i}") for i in range(3)]
        dtall = sc.sb("dtall", [128, NT, 32], F32)
        dtall_b = Buf()
        for i in range(NX):
            k.op(k.pool, lambda: nc.gpsimd.memset(xc[i][:, 0:3], 0.0), writes=[xc_pad[i]])
        cnt = {"w": 0, "ps": 0, "ps2": 0, "x": 0, "r": 0, "t": 0, "wt": 0}

        def load_wfm(col0):
            s = cnt["w"] % NW
            cnt["w"] += 1
            src = W[:, col0:col0 + 128].rearrange("(kk p) c -> p kk c", p=128)
            k.dma(k.pool, wfm_ds[s], wfm[s][:], src, writes=[wfm_b[s]])
            return s

        def proj_block(ws, t0, n):
            pi = cnt["ps"] % NPS
            cnt["ps"] += 1
            for kk in range(KC):
                k.op(k.pe, lambda: nc.tensor.matmul(ps[pi][:, :n], lhsT=wfm[ws][:, kk, :], rhs=UT[:, kk, t0:t0 + n],
                                                    start=(kk == 0), stop=(kk == KC - 1)),
                     reads=[wfm_b[ws]] + ut_bufs[t0 // 128:(t0 + n) // 128], writes=[ps_b[pi]], signal=(kk == KC - 1))
            return pi

        def store(xs, dst, c):
            k.dma(k.sp, ost_ds[xs], dst[c * 128:(c + 1) * 128, :], ost[xs][:, :], reads=ost_b[xs])

        def gate_chunk(col0, dst, c):
            ws = load_wfm(col0)
            xs = cnt["x"] % NX
            cnt["x"] += 1
            for bi, (t0, n) in enumerate(blocks):
                pi = proj_block(ws, t0, n)
                k.op(k.act, lambda: nc.scalar.activation(out=ost[xs][:, t0:t0 + n], in_=ps[pi][:, :n], func=AF.Sigmoid),
                     reads=[ps_b[pi]], writes=[ost_b[xs][bi]])
            store(xs, dst, c)

        def xbc_chunk(c):
            ws = load_wfm(1024 + c * 128)
            xs = cnt["x"] % NX
            cnt["x"] += 1
            cw = pp[:, PP["cw"] + c * 4:PP["cw"] + c * 4 + 4]
            k.op(k.dve, lambda: nc.vector.tensor_tensor(out=dg[xs][:], in0=bcast_free(self.ident[:], 0, 4), in1=bcast_free(cw, 1, 128), op=ALU.mult),
                 reads=[self.b_const, self.b_pp], writes=[dg_b[xs]])
            for bi, (t0, n) in enumerate(blocks):
                pi = proj_block(ws, t0, n)
                k.op(k.dve, lambda: nc.vector.tensor_copy(out=xc[xs][:, 3 + t0:3 + t0 + n], in_=ps[pi][:, :n]),
                     reads=[ps_b[pi]], writes=[xc_b[xs][bi]])
                qi = cnt["ps2"] % 2
                cnt["ps2"] += 1
                rd = [dg_b[xs], xc_b[xs][bi], xc_pad[xs]] + ([xc_b[xs][bi - 1]] if bi > 0 else [])
                for j in range(4):
                    k.op(k.pe, lambda: nc.tensor.matmul(ps2[qi][:, :n], lhsT=dg[xs][:, j, :], rhs=xc[xs][:, t0 + j:t0 + j + n],
                                                        start=(j == 0), stop=(j == 3)),
                         reads=rd, writes=[ps2_b[qi]], signal=(j == 3))
                k.op(k.act, lambda: nc.scalar.activation(out=ost[xs][:, t0:t0 + n], in_=ps2[qi][:, :n], func=AF.Silu,
                                                         bias=pp[:, PP["cb"] + c:PP["cb"] + c + 1]),
                     reads=[ps2_b[qi], self.b_pp], writes=[ost_b[xs][bi]])
            store(xs, self.XBC, c)

        def rope_chunk(col0, dst, c):
            ws = load_wfm(col0)
            xs = cnt["x"] % NX
            cnt["x"] += 1
            for bi, (t0, n) in enumerate(blocks):
                pi = proj_block(ws, t0, n)
                k.op(k.act, lambda: nc.scalar.copy(out=xc[xs][:, t0:t0 + n], in_=ps[pi][:, :n]),
                     reads=[ps_b[pi]], writes=[xc_b[xs][bi]])
                qi = cnt["ps2"] % 2
                cnt["ps2"] += 1
                k.op(k.pe, lambda: nc.tensor.matmul(ps2[qi][:, :n], lhsT=self.perm[:], rhs=xc[xs][:, t0:t0 + n], start=True, stop=True),
                     reads=[self.b_const, xc_b[xs][bi]], writes=[ps2_b[qi]])
                ri = cnt["r"] % 2
                cnt["r"] += 1
                k.op(k.dve, lambda: nc.vector.tensor_tensor(out=rt1[ri][:, :n], in0=ps2[qi][:, :n], in1=ropes[:, t0:t0 + n], op=ALU.mult),
                     reads=[ps2_b[qi], b_rope2], writes=[rt1_b[ri]])
                k.op(k.pool, lambda: nc.gpsimd.tensor_tensor(out=rt2[ri][:, :n], in0=xc[xs][:, t0:t0 + n], in1=ropec[:, t0:t0 + n], op=ALU.mult),
                     reads=[xc_b[xs][bi], b_rope], writes=[rt2_b[ri]])
                k.op(k.dve, lambda: nc.vector.tensor_tensor(out=ost[xs][:, t0:t0 + n], in0=rt1[ri][:, :n], in1=rt2[ri][:, :n], op=ALU.add),
                     reads=[rt1_b[ri], rt2_b[ri]], writes=[ost_b[xs][bi]])
            store(xs, dst, c)

        def tm_group(col0, ncol, kind, dst, dcol0):
            s = cnt["wt"] % 2
            cnt["wt"] += 1
            src = W[:, col0:col0 + ncol].rearrange("(kk p) c -> p kk c", p=128)
            k.dma(k.pool, wtm_ds[s], wtm[s][:, :, :ncol], src, writes=[wtm_b[s]])
            for t in range(NT):
                pi = cnt["ps"] % NPS
                cnt["ps"] += 1
                for kk in range(KC):
                    k.op(k.pe, lambda: nc.tensor.matmul(ps[pi][:, :ncol], lhsT=UT[:, kk, t * 128:(t + 1) * 128], rhs=wtm[s][:, kk, :ncol],
                                                        start=(kk == 0), stop=(kk == KC - 1)),
                         reads=[wtm_b[s], ut_bufs[t]], writes=[ps_b[pi]], signal=(kk == KC - 1))
                if kind == "dt":
                    k.op(k.dve, lambda: nc.vector.tensor_tensor(out=dtall[:, t, 0:16], in0=ps[pi][:, :16], in1=pp[:, PP["dtb"]:PP["dtb"] + 16], op=ALU.add),
                         reads=[ps_b[pi], self.b_pp], writes=[dtall_b])
                    continue
                ti = cnt["t"] % 3
                cnt["t"] += 1
                if kind == "z":
                    k.op(k.act, lambda: nc.scalar.activation(out=tst[ti][:, :ncol], in_=ps[pi][:, :ncol], func=AF.Silu),
                         reads=[ps_b[pi]], writes=[tst_b[ti]])
                else:
                    k.op(k.dve, lambda: nc.vector.tensor_copy(out=tst[ti][:, :ncol], in_=ps[pi][:, :ncol]),
                         reads=[ps_b[pi]], writes=[tst_b[ti]])
                k.dma(k.sp, tst_ds[ti], dst[t * 128:(t + 1) * 128, dcol0:dcol0 + ncol], tst[ti][:, :ncol], reads=[tst_b[ti]])

        for g in range(2):
            tm_group(g * 512, 512, "z", self.ZS, g * 512)
        for c in range(16):
            xbc_chunk(c)
        for c in range(8):
            gate_chunk(6160 + c * 128, self.GS, c)
        for c in range(8):
            gate_chunk(7184 + c * 128, self.GA, c)
        for c in range(8):
            rope_chunk(3088 + c * 128, self.QT, c)
        for c in range(8):
            rope_chunk(4112 + c * 128, self.KT, c)
        for g in range(2):
            tm_group(5136 + g * 512, 512, "v", self.V, g * 512)
        tm_group(3072, 16, "dt", None, 0)
        dtf = dtall[:, :, 0:16]
        ex = sc.sb("dtex", [128, NT, 16], F32)
        bex = Buf()
        na = sc.sb("nega", [128, 16], F32)
        bna = Buf()
        k.op(k.act, lambda: nc.scalar.activation(out=ex[:], in_=dtf, func=AF.Exp), reads=[dtall_b], writes=[bex])
        k.op(k.act, lambda: nc.scalar.activation(out=dtf, in_=ex[:], func=AF.Ln, bias=1.0), reads=[bex], writes=[dtall_b])
        k.op(k.act, lambda: nc.scalar.activation(out=na[:], in_=pp[:, PP["alog"]:PP["alog"] + 16], func=AF.Exp), reads=[self.b_pp], writes=[bna])
        k.op(k.dve, lambda: nc.vector.scalar_tensor_tensor(out=dtall[:, :, 16:32], in0=dtf, scalar=-1.0, in1=bcast_free(na[:], 0, NT),
                                                           op0=ALU.mult, op1=ALU.mult), reads=[dtall_b, bna], writes=[dtall_b])
        ds_dt = k.dsem("dtst")
        k.dma(k.sp, ds_dt, self.DT.rearrange("(t p) c -> p t c", p=128), dtall[:], reads=[dtall_b])


def make_in_map(cfg, inp, b):
    f = lambda a: np.ascontiguousarray(np.asarray(a, np.float32))
    L = cfg.L
    m = {
        "x": f(inp["x"][b]), "meta": f(inp["meta_tokens"]),
        "w_in": f(inp["w_in"][:L]), "w_sb": f(inp["w_ssd_branch"][:L]), "w_ab": f(inp["w_attn_branch"][:L]),
        "w_out": f(inp["w_out"][:L]), "w_up": f(inp["w_up"][:L]), "w_down": f(inp["w_down"][:L]),
        "pp": np.stack([pack_params(inp, l) for l in range(L)]),
        "fng": f(np.broadcast_to(np.asarray(inp["final_norm_g"], np.float32)[None], (128, D))),
    }
    m.update(make_consts(cfg))
    return m


def _phase2(self, l):
    import os
    cfg, k, nc = self.cfg, self.k, self.nc
    NT = cfg.NT
    pp = self.pp
    sc = Scope(k)
    tri = self.tri3[:, 0, :]
    upp = self.tri3[:, 1, :]

    def dbl(name, shape, dt, n=2):
        return [sc.sb(name, shape, dt) for _ in range(n)], [Buf(name) for _ in range(n)]
    xbT, xbT_b = dbl("xbT", [128, 16, 128], BF16)
    zs, zs_b = dbl("zs", [128, D], BF16)
    dta, dta_b = dbl("dta", [128, 32], F32)
    ld_ds = [[k.dsem(f"p2ld{i}_{a}") for a in range(3)] for i in range(2)]
    xtm, xtm_b = dbl("xtm", [128, 16, 64], BF16)
    btm, btm_b = dbl("btm", [128, 4, 128], BF16)
    E, E_b = dbl("E", [128, 48], F32)
    w1, w1_b = dbl("w1", [128, 16], F32)
    Lh, Lh_b = dbl("Lh", [128, 16, 128], F32)
    Lh2_b = [Buf() for _ in range(2)]
    Dg, Dg_b = dbl("Dg", [128, 4, 128], BF16)
    cbm, cbm_b = dbl("cbm", [128, 4, 128], BF16)
    Mg, Mg_b = dbl("Mg", [128, 4, 128], BF16)
    xdt, xdt_b = dbl("xdt", [128, 16, 64], BF16)
    xdd, xdd_b = dbl("xdd", [128, 16, 64], BF16)
    t1, t1_b = dbl("t1", [128, 256], F32)
    ysb, ysb_b = dbl("ysb", [128, D], F32)
    xD, xD_b = dbl("xD", [128, D], F32)
    junk = sc.sb("p2junk", [128, 256], BF16)
    junk_b = Buf()
    st, st_b = dbl("p2st", [128, 12], F32)
    yn, yn_b = dbl("yn", [128, D], BF16)
    yn_gb = [[Buf() for _ in range(4)] for _ in range(2)]
    yT, yT_b = dbl("yT", [128, KC, 128], BF16)
    yT_ds = [k.dsem(f"p2st{i}") for i in range(2)]
    Sf = sc.sb("Sf", [128, D], F32)
    Sf_b = [Buf() for _ in range(4)]
    Sbf, Sbf_b = dbl("Sbf", [128, D], BF16)
    Sbf_gb = [[Buf() for _ in range(4)] for _ in range(2)]
    mhalf = sc.sb("mhalf2", [128, 1], F32)
    bmh = Buf()
    k.op(k.pool, lambda: nc.gpsimd.memset(mhalf[:], -0.5), writes=[bmh])
    k.op(k.pool, lambda: nc.gpsimd.memset(Sf[:], 0.0), writes=Sf_b)
    k.op(k.pool, lambda: nc.gpsimd.memset(Sbf[1][:], 0.0), writes=Sbf_gb[1])
    ptx = sc.ps("ptx", [128, KC, 128], BF16)
    ptx_b = Buf()
    misc = sc.ps("misc", [128, 512], F32)
    ptb = misc.bitcast(BF16)
    ptb_b = Buf()

    cb = sc.ps("cb", [128, 4, 128], F32)
    cb_b = Buf()
    seg = [sc.ps("seg", [128, 4, 128], F32) for _ in range(2)]
    seg_b = [Buf() for _ in range(2)]
    ydo = sc.ps("ydo", [128, 512], F32)
    yd_b, yo_b = Buf(), Buf()
    sne = sc.ps("sne", [128, 512], F32)
    sn = sne[:, 0:256]
    e3 = sne[:, 256:304]
    sn_b = Buf()
    pty = sc.ps("pty", [128, KC, 128], BF16)
    pty_b = Buf()
    segc = 0

    LV = int(os.environ.get("P2LV", "99"))
    for i in range(NT):
        j = i % 2
        t0 = i * 128
        k.dma(k.sp, ld_ds[j][0], xbT[j][:], self.XBC[:, t0:t0 + 128].rearrange("(c p) t -> p c t", p=128), writes=[xbT_b[j]])
        k.dma(k.sp, ld_ds[j][1], zs[j][:], self.ZS[t0:t0 + 128, :], writes=[zs_b[j]])
        k.dma(k.sp, ld_ds[j][2], dta[j][:], self.DT[t0:t0 + 128, :], writes=[dta_b[j]])
        a = dta[j][:, 16:32]
        dt = dta[j][:, 0:16]
        for c in range(KC):
            k.op(k.pe, lambda: nc.tensor.transpose(out=ptx[:, c, :], in_=xbT[j][:, c, :], identity=self.ident[:]),
                 reads=[xbT_b[j], self.b_const], writes=[ptx_b], signal=(c == KC - 1))
        for g in range(4):
            k.op(k.pe, lambda: nc.tensor.transpose(out=ptb[:, g * 128:(g + 1) * 128], in_=xbT[j][:, 8 + g, :], identity=self.ident[:]),
                 reads=[xbT_b[j], self.b_const], writes=[ptb_b], signal=(g == 3))
        k.op(k.act, lambda: nc.scalar.copy(out=xtm[j][:].rearrange("p h d -> p (h d)"), in_=ptx[:].rearrange("p c t -> p (c t)")),
             reads=[ptx_b], writes=[xtm_b[j]])
        if LV <= 0:
            continue
        for q in range(3):
            k.op(k.pe, lambda: nc.tensor.matmul(e3[:, q * 16:(q + 1) * 16], lhsT=self.tri3[:, q, :], rhs=a, start=True, stop=True),
                 reads=[dta_b[j], self.b_const], writes=[sn_b], signal=(q == 2))
        SUB = int(os.environ.get("P2SUB", "9"))
        if SUB <= 0:
            continue
        k.op(k.dve, lambda: nc.vector.tensor_copy(out=btm[j][:].rearrange("p g n -> p (g n)"), in_=ptb[:, 0:512]),
             reads=[ptb_b], writes=[btm_b[j]])
        if SUB <= 1:
            continue
        k.op(k.act, lambda: nc.scalar.activation(out=E[j][:], in_=e3, func=AF.Exp), reads=[sn_b], writes=[E_b[j]])
        if LV <= 1:
            continue
        k.op(k.dve, lambda: nc.vector.tensor_tensor(out=w1[j][:], in0=dt, in1=E[j][:, 16:32], op=ALU.mult),
             reads=[dta_b[j], E_b[j]], writes=[w1_b[j]])
        k.op(k.dve, lambda: nc.vector.tensor_tensor(out=Lh[j][:, 0:8, :], in0=bcast_free(upp, 0, 8), in1=bcast_free(a[:, 0:8], 1, 128), op=ALU.mult),
             reads=[dta_b[j], self.b_const], writes=[Lh_b[j]])
        k.op(k.pool, lambda: nc.gpsimd.tensor_tensor(out=Lh[j][:, 8:16, :], in0=bcast_free(upp, 0, 8), in1=bcast_free(a[:, 8:16], 1, 128), op=ALU.mult),
             reads=[dta_b[j], self.b_const], writes=[Lh2_b[j]])
        k.op(k.pool, lambda: nc.gpsimd.tensor_tensor(out=xdt[j][:], in0=xtm[j][:], in1=bcast_free(dt, 1, 64), op=ALU.mult),
             reads=[xtm_b[j], dta_b[j]], writes=[xdt_b[j]])
        k.op(k.pool, lambda: nc.gpsimd.tensor_tensor(out=xdd[j][:], in0=xtm[j][:], in1=bcast_free(w1[j][:], 1, 64), op=ALU.mult),
             reads=[xtm_b[j], w1_b[j]], writes=[xdd_b[j]])
        k.op(k.pool, lambda: nc.gpsimd.tensor_tensor(out=xD[j][:].rearrange("p (h d) -> p h d", d=64), in0=xtm[j][:],
                                                     in1=bcast_free(pp[:, PP["dsk"]:PP["dsk"] + 16], 1, 64), op=ALU.mult),
             reads=[xtm_b[j], self.b_pp], writes=[xD_b[j]])
        if LV <= 2:
            continue
        for g in range(4):
            k.op(k.pe, lambda: nc.tensor.matmul(cb[:, g, :], lhsT=xbT[j][:, 8 + g, :], rhs=xbT[j][:, 12 + g, :], start=True, stop=True),
                 reads=[xbT_b[j]], writes=[cb_b], signal=(g == 3))
        k.op(k.dve, lambda: nc.vector.tensor_tensor(out=cbm[j][:], in0=cb[:], in1=bcast_free(tri, 0, 4), op=ALU.mult),
             reads=[cb_b, self.b_const], writes=[cbm_b[j]])
        if LV <= 3:
            continue
        sprev, snew = Sbf[(i + 1) % 2], Sbf[i % 2]
        sprev_gb, snew_gb = Sbf_gb[(i + 1) % 2], Sbf_gb[i % 2]
        for g in range(4):
            sg = segc % 2
            segc += 1
            for hh in range(4):
                k.op(k.pe, lambda: nc.tensor.matmul(seg[sg][:, hh, :], lhsT=Lh[j][:, g * 4 + hh, :], rhs=tri, start=True, stop=True),
                     reads=[Lh_b[j] if g < 2 else Lh2_b[j], self.b_const], writes=[seg_b[sg]], signal=(hh == 3))
            k.op(k.act, lambda: nc.scalar.activation(out=Dg[j][:], in_=seg[sg][:], func=AF.Exp), reads=[seg_b[sg]], writes=[Dg_b[j]])
            k.op(k.dve, lambda: nc.vector.tensor_tensor(out=Mg[j][:], in0=Dg[j][:], in1=bcast_free(cbm[j][:, g, :], 0, 4), op=ALU.mult),
                 reads=[Dg_b[j], cbm_b[j]], writes=[Mg_b[j]])
            for hh in range(4):
                k.op(k.pe, lambda: nc.tensor.matmul(ydo[:, hh * 64:(hh + 1) * 64], lhsT=Mg[j][:, hh, :], rhs=xdt[j][:, g * 4 + hh, :], start=True, stop=True),
                     reads=[Mg_b[j], xdt_b[j]], writes=[yd_b], signal=False)
            k.op(k.pe, lambda: nc.tensor.matmul(ydo[:, 256:512], lhsT=xbT[j][:, 12 + g, :], rhs=sprev[:, g * 256:(g + 1) * 256], start=True, stop=True),
                 reads=[xbT_b[j], sprev_gb[g]], writes=[yd_b])
            k.op(k.pe, lambda: nc.tensor.matmul(sn, lhsT=btm[j][:, g, :], rhs=xdd[j][:, g * 4:(g + 1) * 4, :].rearrange("p h d -> p (h d)"), start=True, stop=True),
                 reads=[btm_b[j], xdd_b[j]], writes=[sn_b])
            k.op(k.dve, lambda: nc.vector.tensor_tensor(out=t1[j][:].rearrange("p (h d) -> p h d", d=64), in0=ydo[:, 256:512].rearrange("p (h d) -> p h d", d=64),
                                                        in1=bcast_free(E[j][:, g * 4:(g + 1) * 4], 1, 64), op=ALU.mult),
                 reads=[yd_b, E_b[j]], writes=[t1_b[j]])
            k.op(k.dve, lambda: nc.vector.tensor_tensor(out=ysb[j][:, g * 256:(g + 1) * 256], in0=ydo[:, 0:256], in1=t1[j][:], op=ALU.add),
                 reads=[yd_b, t1_b[j]], writes=[ysb_b[j]])
            sfv = Sf[:, g * 256:(g + 1) * 256]
            k.op(k.dve, lambda: nc.vector.tensor_tensor(out=sfv.rearrange("p (h d) -> p h d", d=64), in0=sfv.rearrange("p (h d) -> p h d", d=64),
                                                        in1=bcast_free(E[j][:, 32 + g * 4:32 + (g + 1) * 4], 1, 64), op=ALU.mult),
                 reads=[E_b[j]], writes=[Sf_b[g]])
            k.op(k.dve, lambda: nc.vector.tensor_tensor(out=sfv, in0=sn, in1=sfv, op=ALU.add), reads=[sn_b], writes=[Sf_b[g]])
            k.op(k.act, lambda: nc.scalar.copy(out=snew[:, g * 256:(g + 1) * 256], in_=sfv), reads=[Sf_b[g]], writes=[snew_gb[g]])
        if LV <= 4:
            continue
        k.op(k.pool, lambda: nc.gpsimd.tensor_tensor(out=ysb[j][:], in0=ysb[j][:], in1=xD[j][:], op=ALU.add), reads=[xD_b[j]], writes=[ysb_b[j]])
        k.op(k.pool, lambda: nc.gpsimd.tensor_tensor(out=ysb[j][:], in0=ysb[j][:], in1=zs[j][:], op=ALU.mult), reads=[zs_b[j]], writes=[ysb_b[j]])
        for g in range(4):
            k.op(k.act, lambda: nc.scalar.activation(out=junk[:], in_=ysb[j][:, g * 256:(g + 1) * 256], func=AF.Square, accum_out=st[j][:, g:g + 1]),
                 reads=[ysb_b[j]], writes=[junk_b, st_b[j]])
        k.op(k.dve, lambda: nc.vector.tensor_scalar(out=st[j][:, 4:8], in0=st[j][:, 0:4], scalar1=1.0 / 256, scalar2=NORM_EPS, op0=ALU.mult, op1=ALU.add),
             reads=[st_b[j]], writes=[st_b[j]])
        k.op(k.pool, lambda: nc.gpsimd.tensor_tensor(out=st[j][:, 8:12], in0=st[j][:, 4:8], in1=bcast_free(mhalf[:, 0:1], 0, 4)[:, :, 0], op=ALU.pow),
             reads=[st_b[j], bmh], writes=[st_b[j]])
        for g in range(4):
            k.op(k.act, lambda: nc.scalar.activation(out=yn[j][:, g * 256:(g + 1) * 256], in_=ysb[j][:, g * 256:(g + 1) * 256], func=AF.Copy, scale=st[j][:, 8 + g:9 + g]),
                 reads=[ysb_b[j], st_b[j]], writes=[yn_gb[j][g]])
        for c in range(KC):
            k.op(k.pe, lambda: nc.tensor.transpose(out=pty[:, c, :], in_=yn[j][:, c * 128:(c + 1) * 128], identity=self.ident[:]),
                 reads=[yn_gb[j][c // 2], self.b_const], writes=[pty_b], signal=(c == KC - 1))
        k.op(k.dve, lambda: nc.vector.tensor_tensor(out=yT[j][:], in0=pty[:], in1=bcast_free(pp[:, PP["sng"]:PP["sng"] + KC], 1, 128), op=ALU.mult),
             reads=[pty_b, self.b_pp], writes=[yT_b[j]])
        k.dma(k.sp, yT_ds[j], self.YS[:, t0:t0 + 128].rearrange("(c p) t -> p c t", p=128), yT[j][:], reads=[yT_b[j]])
    sc.close()


Prog.phase2 = _phase2


def _phase3(self, l):
    import math
    cfg, k, nc = self.cfg, self.k, self.nc
    NT, T = cfg.NT, cfg.T
    pp = self.pp
    lam_init = 0.8 - 0.6 * math.exp(-0.3 * l)
    sc = Scope(k)
    qT = [sc.sb("qT", [128, T], BF16) for _ in range(2)]
    kT = [sc.sb("kT", [128, T], BF16) for _ in range(2)]
    Vh = [sc.sb("Vh", [128, NT, 130], BF16) for _ in range(2)]
    qkv_b = [Buf() for _ in range(2)]
    q_b = [Buf() for _ in range(2)]
    k_b = [Buf() for _ in range(2)]
    ones_b = [Buf() for _ in range(2)]
    ld_ds = [k.dsem(f"p3ld{i}") for i in range(2)]
    for i in range(2):
        k.op(k.pool, lambda: nc.gpsimd.memset(Vh[i][:, :, 128:130], 1.0), writes=[ones_b[i]])
    NPT = 4
    pt = [sc.sb("pt", [128, 2, 512], BF16) for _ in range(NPT)]
    pt_b = [Buf() for _ in range(NPT)]
    yst = [sc.sb("yst", [128, T], BF16) for _ in range(2)]
    yst_b = [[Buf() for _ in range(NT)] for _ in range(2)]
    yst_ds = [k.dsem(f"p3st{i}") for i in range(2)]
    sm = sc.sb("p3sm", [128, 16], F32)
    sm_b = Buf()
    fs = [sc.sb("p3fs", [128, 8], F32) for _ in range(4)]
    fs_b = [Buf() for _ in range(4)]
    ft = [sc.sb("p3ft", [128, 128], F32) for _ in range(4)]
    ft_b = [Buf() for _ in range(4)]
    fo = [sc.sb("p3fo", [128, 128], F32) for _ in range(4)]
    fo_b = [Buf() for _ in range(4)]
    fn = [sc.sb("p3fn", [128, 128], BF16) for _ in range(4)]
    fn_b = [Buf() for _ in range(4)]
    accS = [sc.sb("accS", [128, 3, 396], F32) for _ in range(2)]
    accS_b = [[Buf() for _ in range(3)] for _ in range(2)]
    junk = sc.sb("p3junk", [128, 128], F32)
    junk_b = Buf()
    mhalf = sc.sb("mhalf3", [128, 1], F32)
    bmh = Buf()
    k.op(k.pool, lambda: nc.gpsimd.memset(mhalf[:], -0.5), writes=[bmh])
    spm = [sc.ps("spm", [128, 2, 512], F32) for _ in range(2)]
    spm_b = [Buf() for _ in range(2)]
    accb = [sc.ps("acc", [128, 512], F32) for _ in range(3)]
    acc_b = [Buf() for _ in range(3)]
    ptr = sc.ps("ptr", [128, 128], BF16)
    ptr_b = Buf()
    slots = {}
    idx = 0
    for jq in range(4):
        for m in range(2):
            slots[(jq, m)] = (idx // 3, (idx % 3) * 129)
            idx += 1

    lo = PP["lam"]
    k.op(k.dve, lambda: nc.vector.scalar_tensor_tensor(out=junk[:, 0:64], in0=pp[:, lo:lo + 64], scalar=1.0, in1=pp[:, lo + 64:lo + 128],
                                                       op0=ALU.mult, op1=ALU.mult, accum_out=sm[:, 0:1]), reads=[self.b_pp], writes=[junk_b, sm_b])
    k.op(k.dve, lambda: nc.vector.scalar_tensor_tensor(out=junk[:, 0:64], in0=pp[:, lo + 128:lo + 192], scalar=1.0, in1=pp[:, lo + 192:lo + 256],
                                                       op0=ALU.mult, op1=ALU.mult, accum_out=sm[:, 1:2]), reads=[self.b_pp], writes=[junk_b, sm_b])
    k.op(k.act, lambda: nc.scalar.activation(out=sm[:, 2:4], in_=sm[:, 0:2], func=AF.Exp), reads=[sm_b], writes=[sm_b])
    k.op(k.dve, lambda: nc.vector.tensor_tensor(out=sm[:, 4:5], in0=sm[:, 3:4], in1=sm[:, 2:3], op=ALU.subtract), reads=[sm_b], writes=[sm_b])
    k.op(k.dve, lambda: nc.vector.tensor_scalar(out=sm[:, 5:6], in0=sm[:, 4:5], scalar1=-lam_init, scalar2=None, op0=ALU.add), reads=[sm_b], writes=[sm_b])
    nlam = sm[:, 5:6]

    cnt = {"sp": 0, "pt": 0, "f": 0, "a": 0}
    qblocks = [list(range(q0, min(q0 + 4, NT))) for q0 in range(0, NT, 4)]
    for h in range(8):
        hs = h % 2
        k.dma(k.sp, ld_ds[hs], qT[hs][:], self.QT[h * 128:(h + 1) * 128, :], writes=[q_b[hs]])
        k.dma(k.sp, ld_ds[hs], kT[hs][:], self.KT[h * 128:(h + 1) * 128, :], writes=[k_b[hs]])
        k.dma(k.sp, ld_ds[hs], Vh[hs][:, :, 0:128], self.V[:, h * 128:(h + 1) * 128].rearrange("(t p) c -> p t c", p=128), writes=[qkv_b[hs]])
        for qts in qblocks:
            q0, nq = qts[0], len(qts)
            bank_started = [False, False, False]
            kt_max = min(qts[-1] + 1, NT - 1)
            for kt in range(kt_max + 1):
                jlo = max(0, kt - 1 - q0)
                c0, c1 = jlo * 128, nq * 128
                si = cnt["sp"] % 2
                cnt["sp"] += 1
                def qk_pair():
                    for m in range(2):
                        ins_ = nc.tensor.matmul(spm[si][:, m, c0:c1], lhsT=kT[hs][m * 64:(m + 1) * 64, kt * 128:(kt + 1) * 128],
                                                rhs=qT[hs][m * 64:(m + 1) * 64, q0 * 128 + c0:q0 * 128 + c1], start=True, stop=True)
                    return ins_
                k.op(k.pe, qk_pair, reads=[q_b[hs], k_b[hs]], writes=[spm_b[si]])
                pi = cnt["pt"] % NPT
                cnt["pt"] += 1
                k.op(k.act, lambda: nc.scalar.activation(out=pt[pi][:, :, c0:c1], in_=spm[si][:, :, c0:c1], func=AF.Exp, scale=0.125),
                     reads=[spm_b[si]], writes=[pt_b[pi]])
                for jq in range(jlo, nq):
                    qt = q0 + jq
                    which = 0 if kt == qt else (1 if kt == qt + 1 else None)
                    if which is not None:
                        k.op(k.dve, lambda: nc.vector.tensor_tensor(out=pt[pi][:, :, jq * 128:(jq + 1) * 128], in0=pt[pi][:, :, jq * 128:(jq + 1) * 128],
                                                                    in1=bcast_free(self.amask[:, which, :], 0, 2), op=ALU.mult),
                             reads=[self.b_const], writes=[pt_b[pi]])
                for jq in range(jlo, nq):
                    last_kt = min(q0 + jq + 1, NT - 1)
                    for m in range(2):
                        bk, co = slots[(jq, m)]
                        st_flag = not bank_started[bk]
                        bank_started[bk] = True
                        k.op(k.pe, lambda: nc.tensor.matmul(accb[bk][:, co:co + 129], lhsT=pt[pi][:, m, jq * 128:(jq + 1) * 128], rhs=Vh[hs][:, kt, 0:129],
                                                            start=st_flag, stop=(kt == last_kt), skip_group_check=True),
                             reads=[pt_b[pi], qkv_b[hs], ones_b[hs]], writes=[acc_b[bk]], signal=(kt == last_kt and m == 1))
            ai = cnt["a"] % 2
            cnt["a"] += 1
            nbk = (2 * nq + 2) // 3
            for bk in range(nbk):
                ncol = 129 * min(3, 2 * nq - 3 * bk)
                k.op(k.dve, lambda: nc.vector.tensor_copy(out=accS[ai][:, bk, 0:ncol], in_=accb[bk][:, 0:ncol]), reads=[acc_b[bk]], writes=[accS_b[ai][bk]])
            for jq in range(nq):
                qt = q0 + jq
                fi = cnt["f"] % 4
                cnt["f"] += 1
                b0, o0 = slots[(jq, 0)]
                b1, o1 = slots[(jq, 1)]
                k.op(k.dve, lambda: nc.vector.reciprocal(out=fs[fi][:, 0:1], in_=accS[ai][:, b0, o0 + 128:o0 + 129]), reads=[accS_b[ai][b0]], writes=[fs_b[fi]])
                k.op(k.dve, lambda: nc.vector.reciprocal(out=fs[fi][:, 1:2], in_=accS[ai][:, b1, o1 + 128:o1 + 129]), reads=[accS_b[ai][b1]], writes=[fs_b[fi]])
                k.op(k.dve, lambda: nc.vector.tensor_tensor(out=fs[fi][:, 2:3], in0=fs[fi][:, 1:2], in1=nlam, op=ALU.mult), reads=[sm_b], writes=[fs_b[fi]])
                k.op(k.dve, lambda: nc.vector.tensor_scalar(out=ft[fi][:], in0=accS[ai][:, b1, o1:o1 + 128], scalar1=fs[fi][:, 2:3], scalar2=None, op0=ALU.mult),
                     reads=[accS_b[ai][b1], fs_b[fi]], writes=[ft_b[fi]])
                k.op(k.dve, lambda: nc.vector.scalar_tensor_tensor(out=fo[fi][:], in0=accS[ai][:, b0, o0:o0 + 128], scalar=fs[fi][:, 0:1], in1=ft[fi][:],
                                                                   op0=ALU.mult, op1=ALU.add), reads=[accS_b[ai][b0], fs_b[fi], ft_b[fi]], writes=[fo_b[fi]])
                k.op(k.dve, lambda: nc.vector.scalar_tensor_tensor(out=junk[:], in0=fo[fi][:], scalar=1.0, in1=fo[fi][:], op0=ALU.mult, op1=ALU.mult,
                                                                   accum_out=fs[fi][:, 3:4]), reads=[fo_b[fi]], writes=[junk_b, fs_b[fi]])
                k.op(k.dve, lambda: nc.vector.tensor_scalar(out=fs[fi][:, 4:5], in0=fs[fi][:, 3:4], scalar1=1.0 / 128, scalar2=SUBLN_EPS, op0=ALU.mult, op1=ALU.add),
                     reads=[fs_b[fi]], writes=[fs_b[fi]])
                k.op(k.pool, lambda: nc.gpsimd.tensor_tensor(out=fs[fi][:, 5:6], in0=fs[fi][:, 4:5], in1=mhalf[:], op=ALU.pow), reads=[fs_b[fi], bmh], writes=[fs_b[fi]])
                k.op(k.act, lambda: nc.scalar.activation(out=fn[fi][:], in_=fo[fi][:], func=AF.Copy, scale=fs[fi][:, 5:6]), reads=[fo_b[fi], fs_b[fi]], writes=[fn_b[fi]])
                k.op(k.pe, lambda: nc.tensor.transpose(out=ptr[:], in_=fn[fi][:], identity=self.ident[:]), reads=[fn_b[fi], self.b_const], writes=[ptr_b])
                k.op(k.dve, lambda: nc.vector.tensor_scalar(out=yst[hs][:, qt * 128:(qt + 1) * 128], in0=ptr[:], scalar1=pp[:, PP["subln"]:PP["subln"] + 1],
                                                            scalar2=(1.0 - lam_init), op0=ALU.mult, op1=ALU.mult), reads=[ptr_b, self.b_pp], writes=[yst_b[hs][qt]])
        k.dma(k.sp, yst_ds[hs], self.YA[h * 128:(h + 1) * 128, :], yst[hs][:, :], reads=yst_b[hs])
    sc.close()


Prog.phase3 = _phase3


def _phase4(self, l, wts, wt_b):
    cfg, k, nc = self.cfg, self.k, self.nc
    sc = Scope(k)
    wsb, wab, wout = wts
    ins = [[sc.sb("p4in", [128, KC, 512], BF16) for _ in range(4)] for _ in range(3)]
    ins_b = [[Buf() for _ in range(4)] for _ in range(3)]
    ins_ds = [[k.dsem(f"p4in{i}_{a}") for a in range(4)] for i in range(3)]
    mg = [sc.sb("mg", [128, KC, 512], BF16) for _ in range(2)]
    mg_b = [[Buf() for _ in range(KC)] for _ in range(2)]
    tA = [sc.sb("p4ta", [128, 512], F32) for _ in range(3)]
    tA_b = [Buf() for _ in range(3)]
    tB = [sc.sb("p4tb", [128, 512], F32) for _ in range(3)]
    tB_b = [Buf() for _ in range(3)]
    NH = 6
    ht = [sc.sb("p4h", [128, D], F32) for _ in range(NH)]
    ht_b = [Buf() for _ in range(NH)]
    ht_ds = [k.dsem(f"p4h{i}") for i in range(NH)]
    ps1 = [sc.ps("p4a", [128, 512], F32) for _ in range(3)]
    ps1_b = [Buf() for _ in range(3)]
    ps2 = [sc.ps("p4b", [128, 512], F32) for _ in range(3)]
    ps2_b = [Buf() for _ in range(3)]
    ps3 = [sc.ps("p4c", [128, 512], F32) for _ in range(2)]
    ps3_b = [Buf() for _ in range(2)]
    c1 = c3 = ch = 0
    srcs = (self.YS, self.YA, self.GS, self.GA)
    for bi, (t0, n) in enumerate(cfg.blocks):
        s = bi % 2
        si = bi % 3
        for a in range(4):
            k.dma(k.sp, ins_ds[si][a], ins[si][a][:, :, :n], srcs[a][:, t0:t0 + n].rearrange("(c p) t -> p c t", p=128), writes=[ins_b[si][a]])
        ys, ya, gs, ga = ins[si]
        for c in range(KC):
            pi = c1 % 3
            c1 += 1
            for kk in range(KC):
                k.op(k.pe, lambda: nc.tensor.matmul(ps1[pi][:, :n], lhsT=wsb[:, kk, c * 128:(c + 1) * 128], rhs=ys[:, kk, :n], start=(kk == 0), stop=(kk == KC - 1)),
                     reads=[wt_b[0][kk], ins_b[si][0]], writes=[ps1_b[pi]], signal=(kk == KC - 1))
            for kk in range(KC):
                k.op(k.pe, lambda: nc.tensor.matmul(ps2[pi][:, :n], lhsT=wab[:, kk, c * 128:(c + 1) * 128], rhs=ya[:, kk, :n], start=(kk == 0), stop=(kk == KC - 1)),
                     reads=[wt_b[1][kk], ins_b[si][1]], writes=[ps2_b[pi]], signal=(kk == KC - 1))
            k.op(k.dve, lambda: nc.vector.tensor_tensor(out=tA[pi][:, :n], in0=ps1[pi][:, :n], in1=gs[:, c, :n], op=ALU.mult), reads=[ps1_b[pi], ins_b[si][2]], writes=[tA_b[pi]])
            k.op(k.dve, lambda: nc.vector.tensor_tensor(out=tB[pi][:, :n], in0=ps2[pi][:, :n], in1=ga[:, c, :n], op=ALU.mult), reads=[ps2_b[pi], ins_b[si][3]], writes=[tB_b[pi]])
            k.op(k.pool, lambda: nc.gpsimd.tensor_tensor(out=mg[s][:, c, :n], in0=tA[pi][:, :n], in1=tB[pi][:, :n], op=ALU.add), reads=[tA_b[pi], tB_b[pi]], writes=[mg_b[s][c]])
        for jq in range(n // 128):
            t = t0 // 128 + jq
            hi = ch % NH
            ch += 1
            k.dma(k.sp, ht_ds[hi], ht[hi][:], self.H[t * 128:(t + 1) * 128, :], writes=[ht_b[hi]])
            for half in range(2):
                pi = c3 % 2
                c3 += 1
                for kk in range(KC):
                    k.op(k.pe, lambda: nc.tensor.matmul(ps3[pi][:, :], lhsT=mg[s][:, kk, jq * 128:(jq + 1) * 128], rhs=wout[:, kk, half * 512:(half + 1) * 512],
                                                        start=(kk == 0), stop=(kk == KC - 1)),
                         reads=[wt_b[2][kk]] + mg_b[s], writes=[ps3_b[pi]], signal=(kk == KC - 1))
                k.op(k.dve, lambda: nc.vector.tensor_tensor(out=ht[hi][:, half * 512:(half + 1) * 512], in0=ps3[pi][:, :], in1=ht[hi][:, half * 512:(half + 1) * 512], op=ALU.add),
                     reads=[ps3_b[pi]], writes=[ht_b[hi]])
            k.dma(k.sp, ht_ds[hi], self.H[t * 128:(t + 1) * 128, :], ht[hi][:], reads=[ht_b[hi]])
    sc.close()


def _phase5(self, sc, l, UT, ut_bufs):
    cfg, k, nc = self.cfg, self.k, self.nc
    T, blocks = cfg.T, cfg.blocks
    pp = self.pp
    W = self.w_up[l]
    NW = 3
    wg = [sc.sb("wg", [128, 2, KC, 128], BF16) for _ in range(NW)]
    wg_b = [[Buf(), Buf()] for _ in range(NW)]
    wg_ds = [[k.dsem(f"p5w{i}_{a}") for a in range(2)] for i in range(NW)]
    gv = [sc.sb("gv", [128, 2, T + 4], BF16) for _ in range(2)]
    gv_b = [[[Buf() for _ in blocks] for _ in range(2)] for _ in range(2)]
    gv_pad = [Buf() for _ in range(2)]
    ost = [sc.sb("p5ost", [128, T], BF16) for _ in range(2)]
    ost_b = [[Buf() for _ in blocks] for _ in range(2)]
    ost_ds = [k.dsem(f"p5o{i}") for i in range(2)]
    dg = [sc.sb("p5dg", [128, 2, 3, 128], BF16) for _ in range(2)]
    dg_b = [Buf() for _ in range(2)]
    sg = [sc.sb("p5sg", [128, 512], F32) for _ in range(2)]
    sg_b = [Buf() for _ in range(2)]
    psA = [[sc.ps("p5a", [128, 512], F32) for _ in range(2)] for _ in range(2)]
    psA_b = [[Buf() for _ in range(2)] for _ in range(2)]
    psB = [[sc.ps("p5b", [128, 512], F32)] * 2 for _ in range(2)]
    psB_b = [[Buf()] * 2 for _ in range(2)]
    for i in range(2):
        k.op(k.pool, lambda: nc.gpsimd.memset(gv[i][:, :, 0:2], 0.0), writes=[gv_pad[i]])
    ca = cb_ = 0
    for j in range(DFF // 128):
        s = j % 2
        ws = j % NW
        for a, col0 in enumerate((j * 128, DFF + j * 128)):
            k.dma(k.pool, wg_ds[ws][a], wg[ws][:, a, :, :], W[:, col0:col0 + 128].rearrange("(kk p) c -> p kk c", p=128), writes=[wg_b[ws][a]])
        if j >= 1:
            wd_, wdb_, wds_ = self._wd_prefetch
            k.dma(k.pool, wds_, wd_[:, j - 1, :], self.w_down[l, (j - 1) * 128:j * 128, :], writes=[wdb_[j - 1]])
            if j == DFF // 128 - 1:
                k.dma(k.pool, wds_, wd_[:, j, :], self.w_down[l, j * 128:(j + 1) * 128, :], writes=[wdb_[j]])
        for a, cidx in enumerate((j, DFF // 128 + j)):
            cw = pp[:, PP["mcw"] + cidx * 3:PP["mcw"] + cidx * 3 + 3]
            k.op(k.dve, lambda: nc.vector.tensor_tensor(out=dg[s][:, a, :, :], in0=bcast_free(self.ident[:], 0, 3), in1=bcast_free(cw, 1, 128), op=ALU.mult),
                 reads=[self.b_const, self.b_pp], writes=[dg_b[s]])
        for bi, (t0, n) in enumerate(blocks):
            pa = ca % 2
            ca += 1
            for a in range(2):
                for kk in range(KC):
                    k.op(k.pe, lambda: nc.tensor.matmul(psA[a][pa][:, :n], lhsT=wg[ws][:, a, kk, :], rhs=UT[:, kk, t0:t0 + n], start=(kk == 0), stop=(kk == KC - 1)),
                         reads=[wg_b[ws][a]] + ut_bufs[t0 // 128:(t0 + n) // 128], writes=[psA_b[a][pa]], signal=(kk == KC - 1))
            k.op(k.act, lambda: nc.scalar.copy(out=gv[s][:, 0, 2 + t0:2 + t0 + n], in_=psA[0][pa][:, :n]), reads=[psA_b[0][pa]], writes=[gv_b[s][0][bi]])
            k.op(k.dve, lambda: nc.vector.tensor_copy(out=gv[s][:, 1, 2 + t0:2 + t0 + n], in_=psA[1][pa][:, :n]), reads=[psA_b[1][pa]], writes=[gv_b[s][1][bi]])
            pb = cb_ % 2
            cb_ += 1
            for a in range(2):
                rd = [dg_b[s], gv_b[s][a][bi], gv_pad[s]] + ([gv_b[s][a][bi - 1]] if bi > 0 else [])
                for jj in range(3):
                    k.op(k.pe, lambda: nc.tensor.matmul(psB[a][pb][:, :n], lhsT=dg[s][:, a, jj, :], rhs=gv[s][:, a, t0 + jj:t0 + jj + n], start=(jj == 0), stop=(jj == 2)),
                         reads=rd, writes=[psB_b[a][pb]], signal=(jj == 2))
            k.op(k.act, lambda: nc.scalar.activation(out=sg[pb][:, :n], in_=psB[0][pb][:, :n], func=AF.Silu, bias=pp[:, PP["mcb"] + j:PP["mcb"] + j + 1]),
                 reads=[psB_b[0][pb], self.b_pp], writes=[sg_b[pb]])
            vb = PP["mcb"] + DFF // 128 + j
            k.op(k.dve, lambda: nc.vector.scalar_tensor_tensor(out=ost[s][:, t0:t0 + n], in0=psB[1][pb][:, :n], scalar=pp[:, vb:vb + 1], in1=sg[pb][:, :n],
                                                               op0=ALU.add, op1=ALU.mult), reads=[psB_b[1][pb], sg_b[pb], self.b_pp], writes=[ost_b[s][bi]])
        k.dma(k.sp, ost_ds[s], self.AT[j * 128:(j + 1) * 128, :], ost[s][:, :], reads=ost_b[s])


def _phase6(self, l, wd, wd_b):
    cfg, k, nc = self.cfg, self.k, self.nc
    NC = DFF // 128
    sc = Scope(k)
    NB = 4
    at = [sc.sb("p6at", [128, NC, 128], BF16) for _ in range(NB)]
    at_b = [Buf() for _ in range(NB)]
    ht = [sc.sb("p6h", [128, D], F32) for _ in range(NB)]
    ht_b = [Buf() for _ in range(NB)]
    ds = [k.dsem(f"p6l{i}") for i in range(NB)]
    ds_at = [k.dsem(f"p6a{i}") for i in range(NB)]
    ps = [sc.ps("p6", [128, 512], F32) for _ in range(4)]
    ps_b = [Buf() for _ in range(4)]
    cp = 0
    for t in range(cfg.NT):
        i = t % NB
        k.dma(k.sp, ds_at[i], at[i][:], self.AT[:, t * 128:(t + 1) * 128].rearrange("(c p) t -> p c t", p=128), writes=[at_b[i]])
        k.dma(k.sp, ds[i], ht[i][:], self.H[t * 128:(t + 1) * 128, :], writes=[ht_b[i]])
        for half in range(2):
            pi = cp % 4
            cp += 1
            for c in range(NC):
                k.op(k.pe, lambda: nc.tensor.matmul(ps[pi][:, :], lhsT=at[i][:, c, :], rhs=wd[:, c, half * 512:(half + 1) * 512], start=(c == 0), stop=(c == NC - 1)),
                     reads=[at_b[i], wd_b[c]], writes=[ps_b[pi]], signal=(c == NC - 1))
            k.op(k.dve, lambda: nc.vector.tensor_tensor(out=ht[i][:, half * 512:(half + 1) * 512], in0=ps[pi][:, :], in1=ht[i][:, half * 512:(half + 1) * 512], op=ALU.add),
                 reads=[ps_b[pi]], writes=[ht_b[i]])
        k.dma(k.sp, ds[i], self.H[t * 128:(t + 1) * 128, :], ht[i][:], reads=[ht_b[i]])
    sc.close()


def _final_norm(self):
    cfg, k, nc = self.cfg, self.k, self.nc
    sc = Scope(k)
    fng = sc.sb("fng", [128, D], F32)
    fng_b = Buf()
    k.dma(k.sp, k.dsem("fng"), fng[:], self.fng_d[:, :], writes=[fng_b])
    NB = 3
    ht = [sc.sb("fh", [128, D], F32) for _ in range(NB)]
    hb = [Buf() for _ in range(NB)]
    hds = [k.dsem(f"fh{i}") for i in range(NB)]
    ot = [sc.sb("fo", [128, D], F32) for _ in range(NB)]
    ob = [Buf() for _ in range(NB)]
    ods = [k.dsem(f"fo{i}") for i in range(NB)]
    junk = sc.sb("fjunk", [128, D], BF16)
    bjunk = Buf()
    st = [sc.sb("fst", [128, 4], F32) for _ in range(2)]
    stb = [Buf() for _ in range(2)]
    mhalf = sc.sb("mhalff", [128, 1], F32)
    bmh = Buf()
    k.op(k.pool, lambda: nc.gpsimd.memset(mhalf[:], -0.5), writes=[bmh])
    for t in range(cfg.NT):
        i, j = t % NB, t % 2
        k.dma(k.sp, hds[i], ht[i][:], self.H[t * 128:(t + 1) * 128, :], writes=[hb[i]])
        k.op(k.dve, lambda: nc.vector.scalar_tensor_tensor(out=junk[:], in0=ht[i][:], scalar=1.0, in1=ht[i][:], op0=ALU.mult, op1=ALU.mult, accum_out=st[j][:, 0:1]),
             reads=[hb[i]], writes=[bjunk, stb[j]])
        k.op(k.dve, lambda: nc.vector.tensor_scalar(out=st[j][:, 1:2], in0=st[j][:, 0:1], scalar1=1.0 / D, scalar2=NORM_EPS, op0=ALU.mult, op1=ALU.add),
             reads=[stb[j]], writes=[stb[j]])
        k.op(k.pool, lambda: nc.gpsimd.tensor_tensor(out=st[j][:, 2:3], in0=st[j][:, 1:2], in1=mhalf[:], op=ALU.pow), reads=[stb[j], bmh], writes=[stb[j]])
        k.op(k.dve, lambda: nc.vector.scalar_tensor_tensor(out=ot[i][:], in0=ht[i][:], scalar=st[j][:, 2:3], in1=fng[:], op0=ALU.mult, op1=ALU.mult),
             reads=[hb[i], stb[j], fng_b], writes=[ob[i]])
        lo = max(t * 128, N_META)
        hi = min((t + 1) * 128, N_META + cfg.seq)
        if hi > lo:
            k.dma(k.sp, ods[i], self.out[lo - N_META:hi - N_META, :], ot[i][lo - t * 128:hi - t * 128, :], reads=[ob[i]])
    sc.close()


Prog.phase4 = _phase4
Prog.phase5 = _phase5
Prog.phase6 = _phase6
Prog.final_norm = _final_norm


_PROG_CACHE = {}


def kernel(x, meta_tokens, norm1_g, w_in, ssd_conv_w, ssd_conv_b, ssd_dt_bias, ssd_a_log, ssd_d,
           ssd_norm_g, lambda_q1, lambda_k1, lambda_q2, lambda_k2, attn_subln_g, w_ssd_branch,
           w_attn_branch, w_out, norm2_g, w_up, mlp_conv_w, mlp_conv_b, w_down, final_norm_g):
    inp = dict(x=x, meta_tokens=meta_tokens, norm1_g=norm1_g, w_in=w_in, ssd_conv_w=ssd_conv_w, ssd_conv_b=ssd_conv_b,
               ssd_dt_bias=ssd_dt_bias, ssd_a_log=ssd_a_log, ssd_d=ssd_d, ssd_norm_g=ssd_norm_g, lambda_q1=lambda_q1,
               lambda_k1=lambda_k1, lambda_q2=lambda_q2, lambda_k2=lambda_k2, attn_subln_g=attn_subln_g,
               w_ssd_branch=w_ssd_branch, w_attn_branch=w_attn_branch, w_out=w_out, norm2_g=norm2_g, w_up=w_up,
               mlp_conv_w=mlp_conv_w, mlp_conv_b=mlp_conv_b, w_down=w_down, final_norm_g=final_norm_g)
    inp = {k_: np.asarray(v) for k_, v in inp.items()}
    bsz, seq, _ = inp["x"].shape
    depth = inp["w_in"].shape[0]
    cfg = Cfg(seq=seq, depth=depth)
    key = (seq, depth)
    if key not in _PROG_CACHE:
        _PROG_CACHE[key] = Prog(cfg).build()
    nc = _PROG_CACHE[key]
    shared = make_in_map(cfg, inp, 0)
    in_maps = []
    for b in range(bsz):
        m = dict(shared)
        m["x"] = np.ascontiguousarray(inp["x"][b], dtype=np.float32)
        in_maps.append(m)
    res = run_bass_kernel_spmd(nc, in_maps, core_ids=list(range(bsz)))
    return np.stack([np.asarray(r["out"], dtype=np.float32) for r in res.results], axis=0)


def _p2_setup(self, l, sc):
    import os
    cfg, k, nc = self.cfg, self.k, self.nc
    NT = cfg.NT
    pp = self.pp
    tri = self.tri3[:, 0, :]
    upp = self.tri3[:, 1, :]

    def dbl(name, shape, dt, n=2):
        return [sc.sb(name, shape, dt) for _ in range(n)], [Buf(name) for _ in range(n)]
    def sgl(name, shape, dt):
        t_, b_ = sc.sb(name, shape, dt), Buf(name)
        return [t_, t_], [b_, b_]
    xbT, xbT_b = dbl("xbT", [128, 16, 128], BF16)
    zs, zs_b = dbl("zs", [128, D], BF16)
    dta, dta_b = dbl("dta", [128, 32], F32)
    ld_ds = [[k.dsem(f"p2ld{i}_{a}") for a in range(3)] for i in range(2)]
    xtm, xtm_b = dbl("xtm", [128, 16, 64], BF16)
    btm, btm_b = dbl("btm", [128, 4, 128], BF16)
    E, E_b = dbl("E", [128, 48], F32)
    w1, w1_b = dbl("w1", [128, 16], F32)
    Lh, Lh_b = sgl("Lh", [128, 16, 128], F32)
    Lh2_b = [Buf()] * 2
    Dg, Dg_b = dbl("Dg", [128, 4, 128], BF16)
    cbm, cbm_b = dbl("cbm", [128, 4, 128], BF16)
    Mg, Mg_b = dbl("Mg", [128, 4, 128], BF16)
    xdt, xdt_b = dbl("xdt", [128, 16, 64], BF16)
    xdd, xdd_b = dbl("xdd", [128, 16, 64], BF16)
    t1, t1_b = dbl("t1", [128, 256], F32)
    ysb, ysb_b = dbl("ysb", [128, D], F32)
    xD, xD_b = sgl("xD", [128, D], F32)
    junk = sc.sb("p2junk", [128, 256], BF16)
    junk_b = Buf()
    st, st_b = dbl("p2st", [128, 12], F32)
    yn, yn_b = sgl("yn", [128, D], BF16)
    Sf = sc.sb("Sf", [128, D], F32)
    Sf_b = [Buf() for _ in range(4)]
    Sbf, Sbf_b = dbl("Sbf", [128, D], BF16)
    Sbf_gb = [[Buf() for _ in range(4)] for _ in range(2)]
    mhalf = sc.sb("mhalf2", [128, 1], F32)
    bmh = Buf()
    k.op(k.pool, lambda: nc.gpsimd.memset(mhalf[:], -0.5), writes=[bmh])
    k.op(k.pool, lambda: nc.gpsimd.memset(Sf[:], 0.0), writes=Sf_b)
    k.op(k.pool, lambda: nc.gpsimd.memset(Sbf[1][:], 0.0), writes=Sbf_gb[1])
    pbf = sc.ps("pbf", [128, KC, 128], BF16)
    ptx = pbf
    ptb = pbf[:].rearrange("p c t -> p (c t)")
    ptx_b = ptb_b = Buf()
    pty = sc.ps("pty", [128, KC, 128], BF16)
    pty_b = Buf()

    cb = sc.ps("cbseg", [128, 4, 128], F32)
    cb_b = Buf()
    seg = [cb] * 2
    seg_b = [cb_b] * 2
    ydo = sc.ps("ydo", [128, 512], F32)
    yd_b, yo_b = Buf(), Buf()
    sne = sc.ps("sne", [128, 512], F32)
    sn = sne[:, 0:256]
    e3 = sne[:, 256:304]
    sn_b = Buf()
    segc = 0

    def chunk(i, ys_dst):
        nonlocal segc
        j = i % 2
        t0 = i * 128
        k.dma(k.sp, ld_ds[j][0], xbT[j][:], self.XBC[:, t0:t0 + 128].rearrange("(c p) t -> p c t", p=128), writes=[xbT_b[j]])
        k.dma(k.sp, ld_ds[j][1], zs[j][:], self.ZS[t0:t0 + 128, :], writes=[zs_b[j]])
        k.dma(k.sp, ld_ds[j][2], dta[j][:], self.DT[t0:t0 + 128, :], writes=[dta_b[j]])
        a = dta[j][:, 16:32]
        dt = dta[j][:, 0:16]
        for c in range(KC):
            k.op(k.pe, lambda: nc.tensor.transpose(out=ptx[:, c, :], in_=xbT[j][:, c, :], identity=self.ident[:]),
                 reads=[xbT_b[j], self.b_const], writes=[ptx_b], signal=(c == KC - 1))
        k.op(k.act, lambda: nc.scalar.copy(out=xtm[j][:].rearrange("p h d -> p (h d)"), in_=ptx[:].rearrange("p c t -> p (c t)")),
             reads=[ptx_b], writes=[xtm_b[j]])
        for g in range(4):
            k.op(k.pe, lambda: nc.tensor.transpose(out=ptb[:, g * 128:(g + 1) * 128], in_=xbT[j][:, 8 + g, :], identity=self.ident[:]),
                 reads=[xbT_b[j], self.b_const], writes=[ptb_b], signal=(g == 3))
        k.op(k.dve, lambda: nc.vector.tensor_copy(out=btm[j][:].rearrange("p g n -> p (g n)"), in_=ptb[:, 0:512]),
             reads=[ptb_b], writes=[btm_b[j]])
        for q in range(3):
            k.op(k.pe, lambda: nc.tensor.matmul(e3[:, q * 16:(q + 1) * 16], lhsT=self.tri3[:, q, :], rhs=a, start=True, stop=True),
                 reads=[dta_b[j], self.b_const], writes=[sn_b], signal=(q == 2))
        k.op(k.act, lambda: nc.scalar.activation(out=E[j][:], in_=e3, func=AF.Exp), reads=[sn_b], writes=[E_b[j]])
        k.op(k.dve, lambda: nc.vector.tensor_tensor(out=w1[j][:], in0=dt, in1=E[j][:, 16:32], op=ALU.mult),
             reads=[dta_b[j], E_b[j]], writes=[w1_b[j]])
        k.op(k.dve, lambda: nc.vector.tensor_tensor(out=Lh[j][:, 0:8, :], in0=bcast_free(upp, 0, 8), in1=bcast_free(a[:, 0:8], 1, 128), op=ALU.mult),
             reads=[dta_b[j], self.b_const], writes=[Lh_b[j]])
        k.op(k.pool, lambda: nc.gpsimd.tensor_tensor(out=Lh[j][:, 8:16, :], in0=bcast_free(upp, 0, 8), in1=bcast_free(a[:, 8:16], 1, 128), op=ALU.mult),
             reads=[dta_b[j], self.b_const], writes=[Lh2_b[j]])
        k.op(k.pool, lambda: nc.gpsimd.tensor_tensor(out=xdt[j][:], in0=xtm[j][:], in1=bcast_free(dt, 1, 64), op=ALU.mult),
             reads=[xtm_b[j], dta_b[j]], writes=[xdt_b[j]])
        k.op(k.pool, lambda: nc.gpsimd.tensor_tensor(out=xdd[j][:], in0=xtm[j][:], in1=bcast_free(w1[j][:], 1, 64), op=ALU.mult),
             reads=[xtm_b[j], w1_b[j]], writes=[xdd_b[j]])
        k.op(k.pool, lambda: nc.gpsimd.tensor_tensor(out=xD[j][:].rearrange("p (h d) -> p h d", d=64), in0=xtm[j][:],
                                                     in1=bcast_free(pp[:, PP["dsk"]:PP["dsk"] + 16], 1, 64), op=ALU.mult),
             reads=[xtm_b[j], self.b_pp], writes=[xD_b[j]])
        for g in range(4):
            k.op(k.pe, lambda: nc.tensor.matmul(cb[:, g, :], lhsT=xbT[j][:, 8 + g, :], rhs=xbT[j][:, 12 + g, :], start=True, stop=True),
                 reads=[xbT_b[j]], writes=[cb_b], signal=(g == 3))
        k.op(k.dve, lambda: nc.vector.tensor_tensor(out=cbm[j][:], in0=cb[:], in1=bcast_free(tri, 0, 4), op=ALU.mult),
             reads=[cb_b, self.b_const], writes=[cbm_b[j]])
        sprev, snew = Sbf[(i + 1) % 2], Sbf[i % 2]
        sprev_gb, snew_gb = Sbf_gb[(i + 1) % 2], Sbf_gb[i % 2]
        for g in range(4):
            sg = segc % 2
            segc += 1
            for hh in range(4):
                k.op(k.pe, lambda: nc.tensor.matmul(seg[sg][:, hh, :], lhsT=Lh[j][:, g * 4 + hh, :], rhs=tri, start=True, stop=True),
                     reads=[Lh_b[j] if g < 2 else Lh2_b[j], self.b_const], writes=[seg_b[sg]], signal=(hh == 3))
            k.op(k.act, lambda: nc.scalar.activation(out=Dg[j][:], in_=seg[sg][:], func=AF.Exp), reads=[seg_b[sg]], writes=[Dg_b[j]])
            k.op(k.dve, lambda: nc.vector.tensor_tensor(out=Mg[j][:], in0=Dg[j][:], in1=bcast_free(cbm[j][:, g, :], 0, 4), op=ALU.mult),
                 reads=[Dg_b[j], cbm_b[j]], writes=[Mg_b[j]])
            for hh in range(4):
                k.op(k.pe, lambda: nc.tensor.matmul(ydo[:, hh * 64:(hh + 1) * 64], lhsT=Mg[j][:, hh, :], rhs=xdt[j][:, g * 4 + hh, :], start=True, stop=True),
                     reads=[Mg_b[j], xdt_b[j]], writes=[yd_b], signal=False)
            k.op(k.pe, lambda: nc.tensor.matmul(ydo[:, 256:512], lhsT=xbT[j][:, 12 + g, :], rhs=sprev[:, g * 256:(g + 1) * 256], start=True, stop=True),
                 reads=[xbT_b[j], sprev_gb[g]], writes=[yd_b])
            k.op(k.pe, lambda: nc.tensor.matmul(sn, lhsT=btm[j][:, g, :], rhs=xdd[j][:, g * 4:(g + 1) * 4, :].rearrange("p h d -> p (h d)"), start=True, stop=True),
                 reads=[btm_b[j], xdd_b[j]], writes=[sn_b])
            k.op(k.dve, lambda: nc.vector.tensor_tensor(out=t1[j][:].rearrange("p (h d) -> p h d", d=64), in0=ydo[:, 256:512].rearrange("p (h d) -> p h d", d=64),
                                                        in1=bcast_free(E[j][:, g * 4:(g + 1) * 4], 1, 64), op=ALU.mult),
                 reads=[yd_b, E_b[j]], writes=[t1_b[j]])
            k.op(k.dve, lambda: nc.vector.tensor_tensor(out=ysb[j][:, g * 256:(g + 1) * 256], in0=ydo[:, 0:256], in1=t1[j][:], op=ALU.add),
                 reads=[yd_b, t1_b[j]], writes=[ysb_b[j]])
            sfv = Sf[:, g * 256:(g + 1) * 256]
            k.op(k.dve, lambda: nc.vector.tensor_tensor(out=sfv.rearrange("p (h d) -> p h d", d=64), in0=sfv.rearrange("p (h d) -> p h d", d=64),
                                                        in1=bcast_free(E[j][:, 32 + g * 4:32 + (g + 1) * 4], 1, 64), op=ALU.mult),
                 reads=[E_b[j]], writes=[Sf_b[g]])
            k.op(k.dve, lambda: nc.vector.tensor_tensor(out=sfv, in0=sn, in1=sfv, op=ALU.add), reads=[sn_b], writes=[Sf_b[g]])
            k.op(k.act, lambda: nc.scalar.copy(out=snew[:, g * 256:(g + 1) * 256], in_=sfv), reads=[Sf_b[g]], writes=[snew_gb[g]])
        k.op(k.pool, lambda: nc.gpsimd.tensor_tensor(out=ysb[j][:], in0=ysb[j][:], in1=xD[j][:], op=ALU.add), reads=[xD_b[j]], writes=[ysb_b[j]])
        k.op(k.dve, lambda: nc.vector.tensor_tensor(out=ysb[j][:], in0=ysb[j][:], in1=zs[j][:], op=ALU.mult), reads=[zs_b[j]], writes=[ysb_b[j]])
        for g in range(4):
            k.op(k.act, lambda: nc.scalar.activation(out=junk[:], in_=ysb[j][:, g * 256:(g + 1) * 256], func=AF.Square, accum_out=st[j][:, g:g + 1]),
                 reads=[ysb_b[j]], writes=[junk_b, st_b[j]])
        k.op(k.dve, lambda: nc.vector.tensor_scalar(out=st[j][:, 4:8], in0=st[j][:, 0:4], scalar1=1.0 / 256, scalar2=NORM_EPS, op0=ALU.mult, op1=ALU.add),
             reads=[st_b[j]], writes=[st_b[j]])
        k.op(k.pool, lambda: nc.gpsimd.tensor_tensor(out=st[j][:, 8:12], in0=st[j][:, 4:8], in1=bcast_free(mhalf[:, 0:1], 0, 4)[:, :, 0], op=ALU.pow),
             reads=[st_b[j], bmh], writes=[st_b[j]])
        k.op(k.dve, lambda: nc.vector.tensor_tensor(out=yn[j][:].rearrange("p (g d) -> p g d", d=256), in0=ysb[j][:].rearrange("p (g d) -> p g d", d=256),
                                                    in1=bcast_free(st[j][:, 8:12], 1, 256), op=ALU.mult),
             reads=[ysb_b[j], st_b[j]], writes=[yn_b[j]])
        for c in range(KC):
            k.op(k.pe, lambda: nc.tensor.transpose(out=pty[:, c, :], in_=yn[j][:, c * 128:(c + 1) * 128], identity=self.ident[:]),
                 reads=[yn_b[j], self.b_const], writes=[pty_b], signal=(c == KC - 1))
        ydst, ydst_b = ys_dst
        k.op(k.dve, lambda: nc.vector.tensor_tensor(out=ydst, in0=pty[:], in1=bcast_free(pp[:, PP["sng"]:PP["sng"] + KC], 1, 128), op=ALU.mult),
             reads=[pty_b, self.b_pp], writes=[ydst_b])
    return chunk


def _p4_setup(self, l, sc, wts, wt_b):
    cfg, k, nc = self.cfg, self.k, self.nc
    wsb, wab, wout = wts
    ins = [[sc.sb("p4in", [128, KC, 512], BF16) for _ in range(4)] for _ in range(2)]
    ins_b = [[Buf() for _ in range(4)] for _ in range(2)]
    ys_b = [[Buf() for _ in range(4)] for _ in range(2)]
    ins_ds = [k.dsem(f"p4in{i}") for i in range(2)]
    mg = [sc.sb("mg", [128, KC, 512], BF16)] * 2
    mg_b = [[Buf() for _ in range(KC)]] * 2
    tA = [sc.sb("p4ta", [128, 512], F32)] * 2
    tA_b = [Buf()] * 2
    tB = [sc.sb("p4tb", [128, 512], F32)] * 2
    tB_b = [Buf()] * 2
    NH = 2
    ht = [sc.sb("p4h", [128, D], F32) for _ in range(NH)]
    ht_b = [Buf() for _ in range(NH)]
    ht_ds = [k.dsem(f"p4h{i}") for i in range(NH)]
    ps1 = [sc.ps("p4a", [128, 512], F32)] * 2
    ps1_b = [Buf()] * 2
    ps2 = [sc.ps("p4b", [128, 512], F32)] * 2
    ps2_b = [Buf()] * 2
    ps3 = [sc.ps("p4c", [128, 512], F32)] * 2
    ps3_b = [Buf()] * 2
    c1 = c3 = ch = 0
    srcs = (self.YS, self.YA, self.GS, self.GA)

    def block(bi):
        nonlocal c1, c3, ch
        t0, n = cfg.blocks[bi]
        s = bi % 2
        for a in range(1, 4):
            k.dma(k.sp, ins_ds[s], ins[s][a][:, :, :n], srcs[a][:, t0:t0 + n].rearrange("(c p) t -> p c t", p=128), writes=[ins_b[s][a]])
        ys, ya, gs, ga = ins[s]
        for c in range(KC):
            pi = c1 % 2
            c1 += 1
            for kk in range(KC):
                k.op(k.pe, lambda: nc.tensor.matmul(ps1[pi][:, :n], lhsT=wsb[:, kk, c * 128:(c + 1) * 128], rhs=ys[:, kk, :n], start=(kk == 0), stop=(kk == KC - 1)),
                     reads=[wt_b[0][kk]] + ys_b[s][:n // 128], writes=[ps1_b[pi]], signal=(kk == KC - 1))
            for kk in range(KC):
                k.op(k.pe, lambda: nc.tensor.matmul(ps2[pi][:, :n], lhsT=wab[:, kk, c * 128:(c + 1) * 128], rhs=ya[:, kk, :n], start=(kk == 0), stop=(kk == KC - 1)),
                     reads=[wt_b[1][kk], ins_b[s][1]], writes=[ps2_b[pi]], signal=(kk == KC - 1))
            k.op(k.dve, lambda: nc.vector.tensor_tensor(out=tA[pi][:, :n], in0=ps1[pi][:, :n], in1=gs[:, c, :n], op=ALU.mult), reads=[ps1_b[pi], ins_b[s][2]], writes=[tA_b[pi]])
            k.op(k.dve, lambda: nc.vector.tensor_tensor(out=tB[pi][:, :n], in0=ps2[pi][:, :n], in1=ga[:, c, :n], op=ALU.mult), reads=[ps2_b[pi], ins_b[s][3]], writes=[tB_b[pi]])
            k.op(k.pool, lambda: nc.gpsimd.tensor_tensor(out=mg[s][:, c, :n], in0=tA[pi][:, :n], in1=tB[pi][:, :n], op=ALU.add), reads=[tA_b[pi], tB_b[pi]], writes=[mg_b[s][c]])
        for jq in range(n // 128):
            t = t0 // 128 + jq
            hi = ch % NH
            ch += 1
            k.dma(k.sp, ht_ds[hi], ht[hi][:], self.H[t * 128:(t + 1) * 128, :], writes=[ht_b[hi]])
            for half in range(2):
                pi = c3 % 2
                c3 += 1
                for kk in range(KC):
                    k.op(k.pe, lambda: nc.tensor.matmul(ps3[pi][:, :], lhsT=mg[s][:, kk, jq * 128:(jq + 1) * 128], rhs=wout[:, kk, half * 512:(half + 1) * 512],
                                                        start=(kk == 0), stop=(kk == KC - 1)),
                         reads=[wt_b[2][kk]] + mg_b[s], writes=[ps3_b[pi]], signal=(kk == KC - 1))
                k.op(k.dve, lambda: nc.vector.tensor_tensor(out=ht[hi][:, half * 512:(half + 1) * 512], in0=ps3[pi][:, :], in1=ht[hi][:, half * 512:(half + 1) * 512], op=ALU.add),
                     reads=[ps3_b[pi]], writes=[ht_b[hi]])
            k.dma(k.sp, ht_ds[hi], self.H[t * 128:(t + 1) * 128, :], ht[hi][:], reads=[ht_b[hi]])
    return block, ins, ys_b


def _phase24(self, l, wts, wt_b):
    cfg, k = self.cfg, self.k
    sc = Scope(k)
    chunk = _p2_setup(self, l, sc)
    block, ins, ys_b = _p4_setup(self, l, sc, wts, wt_b)
    for i in range(cfg.NT):
        bi, q = i // 4, i % 4
        s = bi % 2
        chunk(i, (ins[s][0][:, :, q * 128:(q + 1) * 128], ys_b[s][q]))
        if q == 3 or i == cfg.NT - 1:
            block(bi)
            if "YS" in cfg.debug_outs:
                t0, n = cfg.blocks[bi]
                k.dma(k.sp, k.dsem("dbgys"), self.YS[:, t0:t0 + n].rearrange("(c p) t -> p c t", p=128), ins[s][0][:, :, :n], reads=ys_b[s][:n // 128])
    sc.close()


Prog.phase24 = _phase24
```

```python
import numpy as np
import concourse.bass as bass
import concourse.mybir as mybir
from concourse.ap import AP
from concourse.bass_utils import run_bass_kernel_spmd

F32 = mybir.dt.float32
BF16 = mybir.dt.bfloat16
AF = mybir.ActivationFunctionType
ALU = mybir.AluOpType

D = 1024
KC = 8
NIN = 8208
DFF = 2816
NORM_EPS = 1e-6
SUBLN_EPS = 1e-5
N_META = 16


import types


def _freeze(fn):
    if fn.__closure__ is None:
        return fn
    cells = []
    for c in fn.__closure__:
        try:
            cells.append(types.CellType(c.cell_contents))
        except ValueError:
            cells.append(c)
    return types.FunctionType(fn.__code__, fn.__globals__, fn.__name__, fn.__defaults__, tuple(cells))


class Sem:
    def __init__(self, nc, name):
        self.h = nc.alloc_semaphore(name)
        self.total = 0
        self.waited = 0


class Buf:
    __slots__ = ("w", "r", "name")

    def __init__(self, name=""):
        self.w = []
        self.r = []
        self.name = name


class Op:
    __slots__ = ("eng", "fn", "isdma", "ds", "deps", "succ", "dur", "lat", "nun", "ready", "done", "sigval",
                 "needsig", "batch", "sched", "kw")


class Eng:
    def __init__(self, k, eng, name, is_pe=False, is_queue=False):
        self.k = k
        self.e = eng
        self.name = name
        self.sem = Sem(k.nc, "s_" + name)
        self.seen = {}
        self.is_pe = is_pe
        self.is_queue = is_queue

    def wait(self, toks):
        for s, v in toks.items():
            vv = s.total if v is None else v
            assert vv <= s.total, "waiting for a value that is never produced"
            if vv <= 0 or self.seen.get(s, 0) >= vv:
                continue
            self.e.wait_ge(s.h, vv)
            self.seen[s] = vv
            s.waited = max(s.waited, vv)


import os as _os
_SIG_LAT = 64.0
_WINDOW = int(_os.environ.get("K_WINDOW", "96"))
_WINDOW_Q = int(_os.environ.get("K_WINDOW_Q", "40"))


class K:
    def __init__(self, nc):
        self.nc = nc
        self.pe = Eng(self, nc.tensor, "pe", is_pe=True)
        self.act = Eng(self, nc.scalar, "act")
        self.dve = Eng(self, nc.vector, "dve")
        self.pool = Eng(self, nc.gpsimd, "pool")
        self.sp = Eng(self, nc.sync, "sp", is_queue=True)
        self.engs = [self.pe, self.act, self.dve, self.pool, self.sp]
        self.dsems = []
        self._dsem_cache = {}
        self.n_inst = 0
        self.batch = 0
        self.pending = []
        self.model_ns = 0.0
        self.flush_log = []

    def dsem(self, name):
        if name in self._dsem_cache:
            return self._dsem_cache[name]
        s = Sem(self.nc, "d_" + name)
        self.dsems.append(s)
        self._dsem_cache[name] = s
        return s

    def _record(self, o, reads, writes):
        bt = self.batch
        deps = {}
        for b in reads:
            for d in b.w:
                if d.batch == bt:
                    deps[id(d)] = d
        for b in writes:
            for d in b.w:
                if d.batch == bt:
                    deps[id(d)] = d
            for d in b.r:
                if d.batch == bt:
                    deps[id(d)] = d
        o.deps = list(deps.values())
        o.succ = []
        o.batch = bt
        o.sched = False
        o.sigval = 0
        for b in reads:
            b.r.append(o)
        for b in writes:
            b.w = [o]
            b.r = []
        self.pending.append(o)

    def _probe(self, fn):
        with self.nc.discard():
            inst = fn()
        ins = inst.ins
        n = 1
        ap = ins.outs[0].ap
        for st, cn in ap[1:]:
            n *= cn
        return n, ap[0][1], ins

    def op(self, eng, fn, reads=(), writes=(), signal=True):
        o = Op()
        o.eng = eng
        o.fn = _freeze(fn)
        o.isdma = False
        o.ds = None
        n, _, ins = self._probe(o.fn)
        if eng.is_pe:
            f32 = str(ins.ins[0].dtype).endswith("float32")
            o.dur = max(n, 64) / 2.0 * (4.0 if f32 else 1.0) + 8.0
        elif eng is self.act:
            o.dur = (n + 224) / 1.4
        elif eng is self.dve:
            o.dur = n / 0.96 + 62.0
        else:
            o.dur = n / 0.6 + 160.0
        o.lat = 0.0
        self._record(o, reads, writes)

    def dma(self, q, ds, out, in_, reads=(), writes=(), **kw):
        o = Op()
        o.eng = q
        o.isdma = True
        o.ds = ds
        o.fn = None
        o.kw = (out, in_, kw)
        nbytes = 1
        for st, cn in out.ap:
            nbytes *= cn
        nbytes *= 4 if out.dtype == F32 else 2
        o.dur = 60.0 if q.is_queue else 700.0
        o.lat = 2000.0 + nbytes / 120.0
        self._record(o, reads, writes)

    def flush(self):
        ops = self.pending
        self.pending = []
        if not ops:
            return
        lists = {e: [] for e in self.engs}
        for o in ops:
            o.nun = len(o.deps)
            o.ready = 0.0
            for d in o.deps:
                d.succ.append(o)
            lists[o.eng].append(o)
        head = {e: 0 for e in self.engs}
        free = {e: 0.0 for e in self.engs}
        cand = {e: None for e in self.engs}
        dirty = set(self.engs)
        order = []
        nleft = len(ops)
        while nleft:
            for e in dirty:
                lst = lists[e]
                h = head[e]
                while h < len(lst) and lst[h].sched:
                    h += 1
                head[e] = h
                best = None
                bt_ = 0.0
                fe = free[e]
                cnt = 0
                i = h
                win = _WINDOW_Q if (e.is_queue or e is self.pool) else _WINDOW
                while i < len(lst) and cnt < win:
                    o = lst[i]
                    i += 1
                    if o.sched:
                        continue
                    cnt += 1
                    if o.nun:
                        continue
                    if o.ready <= fe:
                        best, bt_ = o, fe
                        break
                    if best is None or o.ready < bt_:
                        best, bt_ = o, o.ready
                cand[e] = (best, bt_) if best is not None else None
            dirty.clear()
            pe_, pt_ = None, None
            for e in self.engs:
                c = cand[e]
                if c is not None and (pt_ is None or c[1] < pt_):
                    pe_, pt_ = e, c[1]
            o = cand[pe_][0]
            o.sched = True
            free[pe_] = pt_ + o.dur
            o.done = pt_ + o.dur + o.lat
            order.append(o)
            nleft -= 1
            dirty.add(pe_)
            dn = o.done + _SIG_LAT
            for s_ in o.succ:
                s_.nun -= 1
                if dn > s_.ready:
                    s_.ready = dn
                if s_.nun == 0:
                    dirty.add(s_.eng)
        self.model_ns += max(o.done for o in ops)
        self.flush_log.append((len(ops), max(o.done for o in ops), {e.name: sum(o.dur for o in ops if o.eng is e) for e in self.engs}))
        last = {}
        for i_, o in enumerate(order):
            o.nun = i_
            o.needsig = False
            if not o.isdma:
                last[o.eng] = o
        for o in order:
            per = {}
            for d in o.deps:
                if d.isdma or (d.eng is o.eng and o.eng.is_pe):
                    continue
                p_ = per.get(d.eng)
                if p_ is None or d.nun > p_.nun:
                    per[d.eng] = d
            for d in per.values():
                d.needsig = True
        for o in last.values():
            o.needsig = True
        for o in order:
            e = o.eng
            toks = {}
            for d in o.deps:
                if d.isdma:
                    toks[d.ds] = None
                elif d.eng is e and e.is_pe:
                    continue
                else:
                    sm = d.eng.sem
                    if toks.get(sm, 0) is not None and d.sigval > toks.get(sm, 0):
                        toks[sm] = d.sigval
            e.wait(toks)
            if o.isdma:
                ds = o.ds
                if ds.total > 0 and ds.waited >= ds.total:
                    e.wait({ds: ds.total})
                out, in_, kw = o.kw
                inst = e.e.dma_start(out=out, in_=in_, **kw)
                inst.then_inc(ds.h, 16)
                ds.total += 16
            else:
                inst = o.fn()
                if o.needsig:
                    e.sem.total += 1
                    inst.then_inc(e.sem.h, 1)
                    o.sigval = e.sem.total
            o.fn = None
            o.kw = None
            self.n_inst += 1

    def barrier(self):
        self.flush()
        self.batch += 1
        toks = {}
        for e in self.engs:
            if not e.is_queue:
                toks[e.sem] = e.sem.total
        for s in self.dsems:
            toks[s] = s.total
        for e in self.engs:
            t = dict(toks)
            if e.is_pe or e.is_queue:
                t.pop(e.sem, None)
            e.wait(t)


def bcast_free(ap, pos, n):
    a = [list(x) for x in ap.ap]
    a.insert(1 + pos, [0, n])
    return AP(ap.tensor, ap.offset, a)


class Cfg:
    def __init__(self, seq=4096, depth=4, debug_outs=(), stop_after=None):
        self.seq = seq
        self.L = depth
        self.n_tok = N_META + seq
        self.NT = -(-self.n_tok // 128)
        self.T = self.NT * 128
        self.blocks = [(t0, min(512, self.T - t0)) for t0 in range(0, self.T, 512)]
        self.debug_outs = set(debug_outs)
        self.stop_after = stop_after


PP = {}
_o = 0
for _n, _w in (("g1", 8), ("g2", 8), ("sng", 8), ("subln", 1), ("cw", 64), ("cb", 16), ("mcw", 132), ("mcb", 44),
               ("dtb", 16), ("alog", 16), ("dsk", 16), ("lam", 256)):
    PP[_n] = _o
    _o += _w
NP_ = _o


def pack_params(inp, l):
    f = lambda a: np.asarray(a, np.float32)
    pp = np.zeros((128, NP_), np.float32)
    colT = lambda v: f(v).reshape(-1, 128).T
    pp[:, PP["g1"]:PP["g1"] + 8] = colT(inp["norm1_g"][l])
    pp[:, PP["g2"]:PP["g2"] + 8] = colT(inp["norm2_g"][l])
    pp[:, PP["sng"]:PP["sng"] + 8] = colT(inp["ssd_norm_g"][l])
    pp[:, PP["subln"]:PP["subln"] + 1] = f(inp["attn_subln_g"][l]).reshape(128, 1)
    cw = f(inp["ssd_conv_w"][l])
    pp[:, PP["cw"]:PP["cw"] + 64] = cw.reshape(4, 16, 128).transpose(2, 1, 0).reshape(128, 64)
    pp[:, PP["cb"]:PP["cb"] + 16] = colT(inp["ssd_conv_b"][l])
    mw = f(inp["mlp_conv_w"][l])
    pp[:, PP["mcw"]:PP["mcw"] + 132] = mw.reshape(3, 44, 128).transpose(2, 1, 0).reshape(128, 132)
    pp[:, PP["mcb"]:PP["mcb"] + 44] = colT(inp["mlp_conv_b"][l])
    pp[:, PP["dtb"]:PP["dtb"] + 16] = np.broadcast_to(f(inp["ssd_dt_bias"][l])[None], (128, 16))
    pp[:, PP["alog"]:PP["alog"] + 16] = np.broadcast_to(f(inp["ssd_a_log"][l])[None], (128, 16))
    pp[:, PP["dsk"]:PP["dsk"] + 16] = np.broadcast_to(f(inp["ssd_d"][l])[None], (128, 16))
    lam = np.concatenate([f(inp[n][l]) for n in ("lambda_q1", "lambda_k1", "lambda_q2", "lambda_k2")])
    pp[:, PP["lam"]:PP["lam"] + 256] = np.broadcast_to(lam[None], (128, 256))
    return pp


def make_consts(cfg):
    T = cfg.T
    c = {}
    c["ident"] = np.eye(128, dtype=np.float32)
    perm = np.zeros((128, 128), np.float32)
    for m in range(2):
        for dd in range(16):
            src = dd + 8 if dd < 8 else dd - 8
            perm[m * 64 + src, m * 64 + dd] = 1.0
    c["perm"] = perm
    kk = np.arange(128)
    tri = (kk[:, None] <= kk[None, :]).astype(np.float32)
    upp = (kk[:, None] > kk[None, :]).astype(np.float32)
    c["tri3"] = np.stack([tri, upp, np.ones((128, 128), np.float32)], 1)
    cidr = np.where(kk < 16, 0, np.where(kk < 80, 1, 2))
    mdiag = (cidr[:, None] <= cidr[None, :]).astype(np.float32)
    mnext = ((kk[:, None] < 16) & (kk[None, :] >= 80)).astype(np.float32)
    c["amask"] = np.stack([mdiag, mnext], 1)
    pos = np.arange(T, dtype=np.float32)
    inv = (1.0 / (np.float32(500000.0) ** (np.arange(8, dtype=np.float32) * np.float32(2.0) / np.float32(16)))).astype(np.float32)
    ang = (pos[:, None] * inv[None, :]).astype(np.float32)
    cs, sn = np.cos(ang).astype(np.float32), np.sin(ang).astype(np.float32)
    cosT = np.ones((128, T), np.float32)
    sinT = np.zeros((128, T), np.float32)
    for m in range(2):
        for dd in range(16):
            cosT[m * 64 + dd] = cs[:, dd % 8]
            sinT[m * 64 + dd] = -sn[:, dd % 8] if dd < 8 else sn[:, dd % 8]
    c["ropec"] = cosT
    c["ropes"] = sinT
    return c


class Scope:
    _uid = 0

    def __init__(self, k):
        from contextlib import ExitStack
        self.k = k
        self.st = ExitStack()

    def sb(self, name, shape, dt):
        Scope._uid += 1
        return self.st.enter_context(self.k.nc.sbuf_tensor(f"{name}_{Scope._uid}", list(shape), dt))

    def ps(self, name, shape, dt):
        Scope._uid += 1
        return self.st.enter_context(self.k.nc.psum_tensor(f"{name}_{Scope._uid}", list(shape), dt))

    def close(self):
        self.k.barrier()
        self.st.close()


class Prog:
    def __init__(self, cfg):
        self.cfg = cfg
        nc = self.nc = bass.Bass("TRN2", target_bir_lowering=False)
        self.k = K(nc)
        T, L = cfg.T, cfg.L
        ext = lambda n, s, dt=F32: nc.dram_tensor(n, list(s), dt, kind="ExternalInput").ap()
        self.x = ext("x", [cfg.seq, D])
        self.meta = ext("meta", [N_META, D])
        self.w_in = ext("w_in", [L, D, NIN])
        self.w_sb = ext("w_sb", [L, D, D])
        self.w_ab = ext("w_ab", [L, D, D])
        self.w_out = ext("w_out", [L, D, D])
        self.w_up = ext("w_up", [L, D, 2 * DFF])
        self.w_down = ext("w_down", [L, DFF, D])
        self.pp_d = ext("pp", [L, 128, NP_])
        self.fng_d = ext("fng", [128, D])
        self.c_ident = ext("ident", [128, 128])
        self.c_perm = ext("perm", [128, 128])
        self.c_tri3 = ext("tri3", [128, 3, 128])
        self.c_amask = ext("amask", [128, 2, 128])
        self.c_ropec = ext("ropec", [128, T])
        self.c_ropes = ext("ropes", [128, T])
        self.out = nc.dram_tensor("out", [cfg.seq, D], F32, kind="ExternalOutput").ap()

        def scr(n, s, dt):
            kind = "ExternalOutput" if n in cfg.debug_outs else "Internal"
            return nc.dram_tensor(n, list(s), dt, kind=kind).ap()
        self.H = scr("H", [T, D], F32)
        self.ZS = scr("ZS", [T, D], BF16)
        self.XBC = scr("XBC", [2048, T], BF16)
        self.DT = scr("DT", [T, 32], F32)
        self.QT = scr("QT", [D, T], BF16)
        self.KT = scr("KT", [D, T], BF16)
        self.V = scr("V", [T, D], BF16)
        self.GS = scr("GS", [D, T], BF16)
        self.GA = scr("GA", [D, T], BF16)
        self.YS = scr("YS", [D, T], BF16)
        self.YA = scr("YA", [D, T], BF16)
        self.AT = scr("AT", [DFF, T], BF16)

    def build(self):
        cfg, k, nc = self.cfg, self.k, self.nc
        top = Scope(k)
        self.top = top
        self.ident_f = top.sb("identf", [128, 128], F32)
        self.ident = top.sb("ident", [128, 128], BF16)
        self.perm = top.sb("perm", [128, 128], BF16)
        self.tri3 = top.sb("tri3", [128, 3, 128], F32)
        self.amask = top.sb("amask", [128, 2, 128], BF16)
        self.pp = top.sb("pp", [128, NP_], F32)
        self.b_const = Buf("const")
        self.b_pp = Buf("pp")
        ds = k.dsem("const")
        self.ds_pp = k.dsem("pp")
        tmpf = top.sb("ctmp", [128, 3, 128], F32)
        bt = Buf()
        k.dma(k.sp, ds, self.ident_f[:], self.c_ident[:, :], writes=[bt])
        k.op(k.dve, lambda: nc.vector.tensor_copy(out=self.ident[:], in_=self.ident_f[:]), reads=[bt], writes=[self.b_const])
        bt2 = Buf()
        k.dma(k.sp, ds, tmpf[:, 0, :], self.c_perm[:, :], writes=[bt2])
        k.dma(k.sp, ds, tmpf[:, 1:3, :], self.c_amask[:, :, :], writes=[bt2])
        k.op(k.dve, lambda: nc.vector.tensor_copy(out=self.perm[:], in_=tmpf[:, 0, :]), reads=[bt2], writes=[self.b_const])
        k.op(k.dve, lambda: nc.vector.tensor_copy(out=self.amask[:], in_=tmpf[:, 1:3, :]), reads=[bt2], writes=[self.b_const])
        k.dma(k.sp, ds, self.tri3[:], self.c_tri3[:, :, :], writes=[self.b_const])
        self.phase0()
        for l in range(cfg.L):
            self.layer(l)
            if cfg.stop_after is not None and cfg.stop_after[0] == l:
                break
        if cfg.stop_after is None:
            self.final_norm()
        k.barrier()
        top.st.close()
        return nc

    def phase0(self):
        cfg, k, nc = self.cfg, self.k, self.nc
        sc = Scope(k)
        ds = k.dsem("p0")
        b = Buf()
        k.dma(k.sp, ds, self.H[0:N_META, :], self.meta[:, :], writes=[b])
        pdiv = max(p for p in (128, 64, 32, 16, 8, 4, 2, 1) if cfg.seq % p == 0)
        xs = self.x.rearrange("(p r) d -> p (r d)", p=pdiv)
        hs = self.H[N_META:N_META + cfg.seq, :].rearrange("(p r) d -> p (r d)", p=pdiv)
        k.dma(k.sp, ds, hs, xs, writes=[b])
        npad = cfg.T - cfg.n_tok
        if npad > 0:
            z = sc.sb("zero", [128, D], F32)
            bz = Buf()
            k.op(k.dve, lambda: nc.vector.memset(z[:], 0.0), writes=[bz])
            k.dma(k.sp, ds, self.H[cfg.n_tok:cfg.T, :], z[0:npad, :], reads=[bz])
        sc.close()

    def layer(self, l):
        cfg, k = self.cfg, self.k
        k.dma(k.sp, self.ds_pp, self.pp[:], self.pp_d[l, :, :], writes=[self.b_pp])
        stop = cfg.stop_after[1] if (cfg.stop_after is not None and cfg.stop_after[0] == l) else 99
        sc = Scope(k)
        UT = sc.sb("UT", [128, KC, cfg.T], BF16)
        ut_bufs = [Buf(f"ut{t}") for t in range(cfg.NT)]
        self.norm_pass(sc, l, PP["g1"], UT, ut_bufs)
        self.phase1(sc, l, UT, ut_bufs)
        sc.close()
        if stop <= 1:
            return
        scw = Scope(k)
        w4 = []
        w4_b = [[Buf() for _ in range(KC)] for _ in range(3)]
        ds_w = k.dsem("p4w")
        for wi, (nm, src) in enumerate((("wsb", self.w_sb), ("wab", self.w_ab), ("wout", self.w_out))):
            w = scw.sb(nm, [128, KC, D], BF16)
            for kk in range(KC):
                k.dma(k.pool, ds_w, w[:, kk, :], src[l, kk * 128:(kk + 1) * 128, :], writes=[w4_b[wi][kk]])
            w4.append(w)
        self.phase2(l)
        if stop > 2:
            self.phase3(l)
        if stop > 3:
            self.phase4(l, w4, w4_b)
        scw.close()
        if stop <= 4:
            return
        scw = Scope(k)
        NC = DFF // 128
        wd = scw.sb("wd", [128, NC, D], BF16)
        wd_b = [Buf() for _ in range(NC)]
        self._wd_prefetch = (wd, wd_b, k.dsem("p6w"))
        sc = Scope(k)
        UT = sc.sb("UT", [128, KC, cfg.T], BF16)
        ut_bufs = [Buf(f"ut{t}") for t in range(cfg.NT)]
        self.norm_pass(sc, l, PP["g2"], UT, ut_bufs)
        self.phase5(sc, l, UT, ut_bufs)
        sc.close()
        if stop > 5:
            self.phase6(l, wd, wd_b)
        scw.close()

    def norm_pass(self, sc, l, goff, UT, ut_bufs):
        cfg, k, nc = self.cfg, self.k, self.nc
        NB = 3
        ht = [sc.sb("nh", [128, D], F32) for _ in range(NB)]
        hb = [Buf() for _ in range(NB)]
        hds = [k.dsem(f"nh{i}") for i in range(NB)]
        junk = sc.sb("njunk", [128, D], BF16)
        bjunk = Buf()
        hn = [sc.sb("nhn", [128, D], BF16) for _ in range(2)]
        hnb = [Buf() for _ in range(2)]
        st = [sc.sb("nst", [128, 4], F32) for _ in range(2)]
        stb = [Buf() for _ in range(2)]
        mhalf = sc.sb("mhalf", [128, 1], F32)
        bmh = Buf()
        k.op(k.pool, lambda: nc.gpsimd.memset(mhalf[:], -0.5), writes=[bmh])
        pst = [sc.ps("npt", [128, KC, 128], BF16) for _ in range(2)]
        psb = [Buf() for _ in range(2)]
        gT = self.pp[:, goff:goff + KC]
        for t in range(cfg.NT):
            i, j = t % NB, t % 2
            k.dma(k.sp, hds[i], ht[i][:], self.H[t * 128:(t + 1) * 128, :], writes=[hb[i]])
            k.op(k.dve, lambda: nc.vector.scalar_tensor_tensor(out=junk[:], in0=ht[i][:], scalar=1.0, in1=ht[i][:],
                                                               op0=ALU.mult, op1=ALU.mult, accum_out=st[j][:, 0:1]),
                 reads=[hb[i]], writes=[bjunk, stb[j]])
            k.op(k.dve, lambda: nc.vector.tensor_scalar(out=st[j][:, 1:2], in0=st[j][:, 0:1], scalar1=1.0 / D, scalar2=NORM_EPS,
                                                        op0=ALU.mult, op1=ALU.add), reads=[stb[j]], writes=[stb[j]])
            k.op(k.pool, lambda: nc.gpsimd.tensor_tensor(out=st[j][:, 2:3], in0=st[j][:, 1:2], in1=mhalf[:], op=ALU.pow),
                 reads=[stb[j], bmh], writes=[stb[j]])
            k.op(k.act, lambda: nc.scalar.activation(out=hn[j][:], in_=ht[i][:], func=AF.Copy, scale=st[j][:, 2:3]),
                 reads=[hb[i], stb[j]], writes=[hnb[j]])
            for c in range(KC):
                k.op(k.pe, lambda: nc.tensor.transpose(out=pst[j][:, c, :], in_=hn[j][:, c * 128:(c + 1) * 128], identity=self.ident[:]),
                     reads=[hnb[j], self.b_const], writes=[psb[j]], signal=(c == KC - 1))
            k.op(k.dve, lambda: nc.vector.tensor_tensor(out=UT[:, :, t * 128:(t + 1) * 128], in0=pst[j][:, :, :],
                                                        in1=bcast_free(gT, 1, 128), op=ALU.mult),
                 reads=[psb[j], self.b_pp], writes=[ut_bufs[t]])

    def phase1(self, sc, l, UT, ut_bufs):
        cfg, k, nc = self.cfg, self.k, self.nc
        T, NT, blocks = cfg.T, cfg.NT, cfg.blocks
        pp = self.pp
        W = self.w_in[l]
        NW = 3
        wfm = [sc.sb("wfm", [128, KC, 128], BF16) for _ in range(NW)]
        wfm_b = [Buf() for _ in range(NW)]
        wfm_ds = [k.dsem(f"wfm{i}") for i in range(NW)]
        wtm = [sc.sb("wtm", [128, KC, 512], BF16) for _ in range(2)]
        wtm_b = [Buf() for _ in range(2)]
        wtm_ds = [k.dsem(f"wtm{i}") for i in range(2)]
        NX = 2
        xc = [sc.sb("xc", [128, T + 4], BF16) for _ in range(NX)]
        xc_b = [[Buf() for _ in blocks] for _ in range(NX)]
        xc_pad = [Buf() for _ in range(NX)]
        ost = [sc.sb("ost", [128, T], BF16) for _ in range(NX)]
        ost_b = [[Buf() for _ in blocks] for _ in range(NX)]
        ost_ds = [k.dsem(f"ost{i}") for i in range(NX)]
        dg = [sc.sb("dg", [128, 4, 128], BF16) for _ in range(NX)]
        dg_b = [Buf() for _ in range(NX)]
        ropec = sc.sb("ropec", [128, T], F32)
        ropes = sc.sb("ropes", [128, T], F32)
        b_rope = Buf()
        b_rope2 = Buf()
        ds_rope = k.dsem("rope")
        k.dma(k.sp, ds_rope, ropec[:], self.c_ropec[:, :], writes=[b_rope])
        k.dma(k.sp, ds_rope, ropes[:], self.c_ropes[:, :], writes=[b_rope2])
        rt1 = [sc.sb("rt1", [128, 512], F32) for _ in range(2)]
        rt1_b = [Buf() for _ in range(2)]
        rt2 = [sc.sb("rt2", [128, 512], F32) for _ in range(2)]
        rt2_b = [Buf() for _ in range(2)]
        NPS = 4
        ps = [sc.ps("p1a", [128, 512], F32) for _ in range(NPS)]
        ps_b = [Buf() for _ in range(NPS)]
        ps2 = [sc.ps("p1b", [128, 512], F32) for _ in range(2)]
        ps2_b = [Buf() for _ in range(2)]
        tst = [sc.sb("tst", [128, 512], BF16) for _ in range(3)]
        tst_b = [Buf() for _ in range(3)]
        tst_ds = [k.dsem(f"tst{i}") for i in range(3)]
        dtall = sc.sb("dtall", [128, NT, 32], F32)
        dtall_b = Buf()
        for i in range(NX):
            k.op(k.pool, lambda: nc.gpsimd.memset(xc[i][:, 0:3], 0.0), writes=[xc_pad[i]])
        cnt = {"w": 0, "ps": 0, "ps2": 0, "x": 0, "r": 0, "t": 0, "wt": 0}

        def load_wfm(col0):
            s = cnt["w"] % NW
            cnt["w"] += 1
            src = W[:, col0:col0 + 128].rearrange("(kk p) c -> p kk c", p=128)
            k.dma(k.pool, wfm_ds[s], wfm[s][:], src, writes=[wfm_b[s]])
            return s

        def proj_block(ws, t0, n):
            pi = cnt["ps"] % NPS
            cnt["ps"] += 1
            for kk in range(KC):
                k.op(k.pe, lambda: nc.tensor.matmul(ps[pi][:, :n], lhsT=wfm[ws][:, kk, :], rhs=UT[:, kk, t0:t0 + n],
                                                    start=(kk == 0), stop=(kk == KC - 1)),
                     reads=[wfm_b[ws]] + ut_bufs[t0 // 128:(t0 + n) // 128], writes=[ps_b[pi]], signal=(kk == KC - 1))
            return pi

        def store(xs, dst, c):
            k.dma(k.sp, ost_ds[xs], dst[c * 128:(c + 1) * 128, :], ost[xs][:, :], reads=ost_b[xs])

        def gate_chunk(col0, dst, c):
            ws = load_wfm(col0)
            xs = cnt["x"] % NX
            cnt["x"] += 1
            for bi, (t0, n) in enumerate(blocks):
                pi = proj_block(ws, t0, n)
                k.op(k.act, lambda: nc.scalar.activation(out=ost[xs][:, t0:t0 + n], in_=ps[pi][:, :n], func=AF.Sigmoid),
                     reads=[ps_b[pi]], writes=[ost_b[xs][bi]])
            store(xs, dst, c)

        def xbc_chunk(c):
            ws = load_wfm(1024 + c * 128)
            xs = cnt["x"] % NX
            cnt["x"] += 1
            cw = pp[:, PP["cw"] + c * 4:PP["cw"] + c * 4 + 4]
            k.op(k.dve, lambda: nc.vector.tensor_tensor(out=dg[xs][:], in0=bcast_free(self.ident[:], 0, 4), in1=bcast_free(cw, 1, 128), op=ALU.mult),
                 reads=[self.b_const, self.b_pp], writes=[dg_b[xs]])
            for bi, (t0, n) in enumerate(blocks):
                pi = proj_block(ws, t0, n)
                k.op(k.dve, lambda: nc.vector.tensor_copy(out=xc[xs][:, 3 + t0:3 + t0 + n], in_=ps[pi][:, :n]),
                     reads=[ps_b[pi]], writes=[xc_b[xs][bi]])
                qi = cnt["ps2"] % 2
                cnt["ps2"] += 1
                rd = [dg_b[xs], xc_b[xs][bi], xc_pad[xs]] + ([xc_b[xs][bi - 1]] if bi > 0 else [])
                for j in range(4):
                    k.op(k.pe, lambda: nc.tensor.matmul(ps2[qi][:, :n], lhsT=dg[xs][:, j, :], rhs=xc[xs][:, t0 + j:t0 + j + n],
                                                        start=(j == 0), stop=(j == 3)),
                         reads=rd, writes=[ps2_b[qi]], signal=(j == 3))
                k.op(k.act, lambda: nc.scalar.activation(out=ost[xs][:, t0:t0 + n], in_=ps2[qi][:, :n], func=AF.Silu,
                                                         bias=pp[:, PP["cb"] + c:PP["cb"] + c + 1]),
                     reads=[ps2_b[qi], self.b_pp], writes=[ost_b[xs][bi]])
            store(xs, self.XBC, c)

        def rope_chunk(col0, dst, c):
            ws = load_wfm(col0)
            xs = cnt["x"] % NX
            cnt["x"] += 1
            for bi, (t0, n) in enumerate(blocks):
                pi = proj_block(ws, t0, n)
                k.op(k.act, lambda: nc.scalar.copy(out=xc[xs][:, t0:t0 + n], in_=ps[pi][:, :n]),
                     reads=[ps_b[pi]], writes=[xc_b[xs][bi]])
                qi = cnt["ps2"] % 2
                cnt["ps2"] += 1
                k.op(k.pe, lambda: nc.tensor.matmul(ps2[qi][:, :n], lhsT=self.perm[:], rhs=xc[xs][:, t0:t0 + n], start=True, stop=True),
                     reads=[self.b_const, xc_b[xs][bi]], writes=[ps2_b[qi]])
                ri = cnt["r"] % 2
                cnt["r"] += 1
                k.op(k.dve, lambda: nc.vector.tensor_tensor(out=rt1[ri][:, :n], in0=ps2[qi][:, :n], in1=ropes[:, t0:t0 + n], op=ALU.mult),
                     reads=[ps2_b[qi], b_rope2], writes=[rt1_b[ri]])
                k.op(k.pool, lambda: nc.gpsimd.tensor_tensor(out=rt2[ri][:, :n], in0=xc[xs][:, t0:t0 + n], in1=ropec[:, t0:t0 + n], op=ALU.mult),
                     reads=[xc_b[xs][bi], b_rope], writes=[rt2_b[ri]])
                k.op(k.dve, lambda: nc.vector.tensor_tensor(out=ost[xs][:, t0:t0 + n], in0=rt1[ri][:, :n], in1=rt2[ri][:, :n], op=ALU.add),
                     reads=[rt1_b[ri], rt2_b[ri]], writes=[ost_b[xs][bi]])
            store(xs, dst, c)

        def tm_group(col0, ncol, kind, dst, dcol0):
            s = cnt["wt"] % 2
            cnt["wt"] += 1
            src = W[:, col0:col0 + ncol].rearrange("(kk p) c -> p kk c", p=128)
            k.dma(k.pool, wtm_ds[s], wtm[s][:, :, :ncol], src, writes=[wtm_b[s]])
            for t in range(NT):
                pi = cnt["ps"] % NPS
                cnt["ps"] += 1
                for kk in range(KC):
                    k.op(k.pe, lambda: nc.tensor.matmul(ps[pi][:, :ncol], lhsT=UT[:, kk, t * 128:(t + 1) * 128], rhs=wtm[s][:, kk, :ncol],
                                                        start=(kk == 0), stop=(kk == KC - 1)),
                         reads=[wtm_b[s], ut_bufs[t]], writes=[ps_b[pi]], signal=(kk == KC - 1))
                if kind == "dt":
                    k.op(k.dve, lambda: nc.vector.tensor_tensor(out=dtall[:, t, 0:16], in0=ps[pi][:, :16], in1=pp[:, PP["dtb"]:PP["dtb"] + 16], op=ALU.add),
                         reads=[ps_b[pi], self.b_pp], writes=[dtall_b])
                    continue
                ti = cnt["t"] % 3
                cnt["t"] += 1
                if kind == "z":
                    k.op(k.act, lambda: nc.scalar.activation(out=tst[ti][:, :ncol], in_=ps[pi][:, :ncol], func=AF.Silu),
                         reads=[ps_b[pi]], writes=[tst_b[ti]])
                else:
                    k.op(k.dve, lambda: nc.vector.tensor_copy(out=tst[ti][:, :ncol], in_=ps[pi][:, :ncol]),
                         reads=[ps_b[pi]], writes=[tst_b[ti]])
                k.dma(k.sp, tst_ds[ti], dst[t * 128:(t + 1) * 128, dcol0:dcol0 + ncol], tst[ti][:, :ncol], reads=[tst_b[ti]])

        for g in range(2):
            tm_group(g * 512, 512, "z", self.ZS, g * 512)
        for c in range(16):
            xbc_chunk(c)
        for c in range(8):
            gate_chunk(6160 + c * 128, self.GS, c)
        for c in range(8):
            gate_chunk(7184 + c * 128, self.GA, c)
        for c in range(8):
            rope_chunk(3088 + c * 128, self.QT, c)
        for c in range(8):
            rope_chunk(4112 + c * 128, self.KT, c)
        for g in range(2):
            tm_group(5136 + g * 512, 512, "v", self.V, g * 512)
        tm_group(3072, 16, "dt", None, 0)
        dtf = dtall[:, :, 0:16]
        ex = sc.sb("dtex", [128, NT, 16], F32)
        bex = Buf()
        na = sc.sb("nega", [128, 16], F32)
        bna = Buf()
        k.op(k.act, lambda: nc.scalar.activation(out=ex[:], in_=dtf, func=AF.Exp), reads=[dtall_b], writes=[bex])
        k.op(k.act, lambda: nc.scalar.activation(out=dtf, in_=ex[:], func=AF.Ln, bias=1.0), reads=[bex], writes=[dtall_b])
        k.op(k.act, lambda: nc.scalar.activation(out=na[:], in_=pp[:, PP["alog"]:PP["alog"] + 16], func=AF.Exp), reads=[self.b_pp], writes=[bna])
        k.op(k.dve, lambda: nc.vector.scalar_tensor_tensor(out=dtall[:, :, 16:32], in0=dtf, scalar=-1.0, in1=bcast_free(na[:], 0, NT),
                                                           op0=ALU.mult, op1=ALU.mult), reads=[dtall_b, bna], writes=[dtall_b])
        ds_dt = k.dsem("dtst")
        k.dma(k.sp, ds_dt, self.DT.rearrange("(t p) c -> p t c", p=128), dtall[:], reads=[dtall_b])


def make_in_map(cfg, inp, b):
    f = lambda a: np.ascontiguousarray(np.asarray(a, np.float32))
    L = cfg.L
    m = {
        "x": f(inp["x"][b]), "meta": f(inp["meta_tokens"]),
        "w_in": f(inp["w_in"][:L]), "w_sb": f(inp["w_ssd_branch"][:L]), "w_ab": f(inp["w_attn_branch"][:L]),
        "w_out": f(inp["w_out"][:L]), "w_up": f(inp["w_up"][:L]), "w_down": f(inp["w_down"][:L]),
        "pp": np.stack([pack_params(inp, l) for l in range(L)]),
        "fng": f(np.broadcast_to(np.asarray(inp["final_norm_g"], np.float32)[None], (128, D))),
    }
    m.update(make_consts(cfg))
    return m


def _phase2(self, l):
    import os
    cfg, k, nc = self.cfg, self.k, self.nc
    NT = cfg.NT
    pp = self.pp
    sc = Scope(k)
    tri = self.tri3[:, 0, :]
    upp = self.tri3[:, 1, :]

    def dbl(name, shape, dt, n=2):
        return [sc.sb(name, shape, dt) for _ in range(n)], [Buf(name) for _ in range(n)]
    xbT, xbT_b = dbl("xbT", [128, 16, 128], BF16)
    zs, zs_b = dbl("zs", [128, D], BF16)
    dta, dta_b = dbl("dta", [128, 32], F32)
    ld_ds = [[k.dsem(f"p2ld{i}_{a}") for a in range(3)] for i in range(2)]
    xtm, xtm_b = dbl("xtm", [128, 16, 64], BF16)
    btm, btm_b = dbl("btm", [128, 4, 128], BF16)
    E, E_b = dbl("E", [128, 48], F32)
    w1, w1_b = dbl("w1", [128, 16], F32)
    Lh, Lh_b = dbl("Lh", [128, 16, 128], F32)
    Lh2_b = [Buf() for _ in range(2)]
    Dg, Dg_b = dbl("Dg", [128, 4, 128], BF16)
    cbm, cbm_b = dbl("cbm", [128, 4, 128], BF16)
    Mg, Mg_b = dbl("Mg", [128, 4, 128], BF16)
    xdt, xdt_b = dbl("xdt", [128, 16, 64], BF16)
    xdd, xdd_b = dbl("xdd", [128, 16, 64], BF16)
    t1, t1_b = dbl("t1", [128, 256], F32)
    ysb, ysb_b = dbl("ysb", [128, D], F32)
    xD, xD_b = dbl("xD", [128, D], F32)
    junk = sc.sb("p2junk", [128, 256], BF16)
    junk_b = Buf()
    st, st_b = dbl("p2st", [128, 12], F32)
    yn, yn_b = dbl("yn", [128, D], BF16)
    yn_gb = [[Buf() for _ in range(4)] for _ in range(2)]
    yT, yT_b = dbl("yT", [128, KC, 128], BF16)
    yT_ds = [k.dsem(f"p2st{i}") for i in range(2)]
    Sf = sc.sb("Sf", [128, D], F32)
    Sf_b = [Buf() for _ in range(4)]
    Sbf, Sbf_b = dbl("Sbf", [128, D], BF16)
    Sbf_gb = [[Buf() for _ in range(4)] for _ in range(2)]
    mhalf = sc.sb("mhalf2", [128, 1], F32)
    bmh = Buf()
    k.op(k.pool, lambda: nc.gpsimd.memset(mhalf[:], -0.5), writes=[bmh])
    k.op(k.pool, lambda: nc.gpsimd.memset(Sf[:], 0.0), writes=Sf_b)
    k.op(k.pool, lambda: nc.gpsimd.memset(Sbf[1][:], 0.0), writes=Sbf_gb[1])
    ptx = sc.ps("ptx", [128, KC, 128], BF16)
    ptx_b = Buf()
    misc = sc.ps("misc", [128, 512], F32)
    ptb = misc.bitcast(BF16)
    ptb_b = Buf()

    cb = sc.ps("cb", [128, 4, 128], F32)
    cb_b = Buf()
    seg = [sc.ps("seg", [128, 4, 128], F32) for _ in range(2)]
    seg_b = [Buf() for _ in range(2)]
    ydo = sc.ps("ydo", [128, 512], F32)
    yd_b, yo_b = Buf(), Buf()
    sne = sc.ps("sne", [128, 512], F32)
    sn = sne[:, 0:256]
    e3 = sne[:, 256:304]
    sn_b = Buf()
    pty = sc.ps("pty", [128, KC, 128], BF16)
    pty_b = Buf()
    segc = 0

    LV = int(os.environ.get("P2LV", "99"))
    for i in range(NT):
        j = i % 2
        t0 = i * 128
        k.dma(k.sp, ld_ds[j][0], xbT[j][:], self.XBC[:, t0:t0 + 128].rearrange("(c p) t -> p c t", p=128), writes=[xbT_b[j]])
        k.dma(k.sp, ld_ds[j][1], zs[j][:], self.ZS[t0:t0 + 128, :], writes=[zs_b[j]])
        k.dma(k.sp, ld_ds[j][2], dta[j][:], self.DT[t0:t0 + 128, :], writes=[dta_b[j]])
        a = dta[j][:, 16:32]
        dt = dta[j][:, 0:16]
        for c in range(KC):
            k.op(k.pe, lambda: nc.tensor.transpose(out=ptx[:, c, :], in_=xbT[j][:, c, :], identity=self.ident[:]),
                 reads=[xbT_b[j], self.b_const], writes=[ptx_b], signal=(c == KC - 1))
        for g in range(4):
            k.op(k.pe, lambda: nc.tensor.transpose(out=ptb[:, g * 128:(g + 1) * 128], in_=xbT[j][:, 8 + g, :], identity=self.ident[:]),
                 reads=[xbT_b[j], self.b_const], writes=[ptb_b], signal=(g == 3))
        k.op(k.act, lambda: nc.scalar.copy(out=xtm[j][:].rearrange("p h d -> p (h d)"), in_=ptx[:].rearrange("p c t -> p (c t)")),
             reads=[ptx_b], writes=[xtm_b[j]])
        if LV <= 0:
            continue
        for q in range(3):
            k.op(k.pe, lambda: nc.tensor.matmul(e3[:, q * 16:(q + 1) * 16], lhsT=self.tri3[:, q, :], rhs=a, start=True, stop=True),
                 reads=[dta_b[j], self.b_const], writes=[sn_b], signal=(q == 2))
        SUB = int(os.environ.get("P2SUB", "9"))
        if SUB <= 0:
            continue
        k.op(k.act, lambda: nc.scalar.copy(out=btm[j][:].rearrange("p g n -> p (g n)"), in_=ptb[:, 0:512]),
             reads=[ptb_b], writes=[btm_b[j]])
        if SUB <= 1:
            continue
        k.op(k.act, lambda: nc.scalar.activation(out=E[j][:], in_=e3, func=AF.Exp), reads=[sn_b], writes=[E_b[j]])
        if LV <= 1:
            continue
        k.op(k.dve, lambda: nc.vector.tensor_tensor(out=w1[j][:], in0=dt, in1=E[j][:, 16:32], op=ALU.mult),
             reads=[dta_b[j], E_b[j]], writes=[w1_b[j]])
        k.op(k.dve, lambda: nc.vector.tensor_tensor(out=Lh[j][:, 0:8, :], in0=bcast_free(upp, 0, 8), in1=bcast_free(a[:, 0:8], 1, 128), op=ALU.mult),
             reads=[dta_b[j], self.b_const], writes=[Lh_b[j]])
        k.op(k.pool, lambda: nc.gpsimd.tensor_tensor(out=Lh[j][:, 8:16, :], in0=bcast_free(upp, 0, 8), in1=bcast_free(a[:, 8:16], 1, 128), op=ALU.mult),
             reads=[dta_b[j], self.b_const], writes=[Lh2_b[j]])
        k.op(k.pool, lambda: nc.gpsimd.tensor_tensor(out=xdt[j][:], in0=xtm[j][:], in1=bcast_free(dt, 1, 64), op=ALU.mult),
             reads=[xtm_b[j], dta_b[j]], writes=[xdt_b[j]])
        k.op(k.pool, lambda: nc.gpsimd.tensor_tensor(out=xdd[j][:], in0=xtm[j][:], in1=bcast_free(w1[j][:], 1, 64), op=ALU.mult),
             reads=[xtm_b[j], w1_b[j]], writes=[xdd_b[j]])
        k.op(k.pool, lambda: nc.gpsimd.tensor_tensor(out=xD[j][:].rearrange("p (h d) -> p h d", d=64), in0=xtm[j][:],
                                                     in1=bcast_free(pp[:, PP["dsk"]:PP["dsk"] + 16], 1, 64), op=ALU.mult),
             reads=[xtm_b[j], self.b_pp], writes=[xD_b[j]])
        if LV <= 2:
            continue
        for g in range(4):
            k.op(k.pe, lambda: nc.tensor.matmul(cb[:, g, :], lhsT=xbT[j][:, 8 + g, :], rhs=xbT[j][:, 12 + g, :], start=True, stop=True),
                 reads=[xbT_b[j]], writes=[cb_b], signal=(g == 3))
        k.op(k.dve, lambda: nc.vector.tensor_tensor(out=cbm[j][:], in0=cb[:], in1=bcast_free(tri, 0, 4), op=ALU.mult),
             reads=[cb_b, self.b_const], writes=[cbm_b[j]])
        if LV <= 3:
            continue
        sprev, snew = Sbf[(i + 1) % 2], Sbf[i % 2]
        sprev_gb, snew_gb = Sbf_gb[(i + 1) % 2], Sbf_gb[i % 2]
        for g in range(4):
            sg = segc % 2
            segc += 1
            for hh in range(4):
                k.op(k.pe, lambda: nc.tensor.matmul(seg[sg][:, hh, :], lhsT=Lh[j][:, g * 4 + hh, :], rhs=tri, start=True, stop=True),
                     reads=[Lh_b[j] if g < 2 else Lh2_b[j], self.b_const], writes=[seg_b[sg]], signal=(hh == 3))
            k.op(k.act, lambda: nc.scalar.activation(out=Dg[j][:], in_=seg[sg][:], func=AF.Exp), reads=[seg_b[sg]], writes=[Dg_b[j]])
            k.op(k.dve, lambda: nc.vector.tensor_tensor(out=Mg[j][:], in0=Dg[j][:], in1=bcast_free(cbm[j][:, g, :], 0, 4), op=ALU.mult),
                 reads=[Dg_b[j], cbm_b[j]], writes=[Mg_b[j]])
            for hh in range(4):
                k.op(k.pe, lambda: nc.tensor.matmul(ydo[:, hh * 64:(hh + 1) * 64], lhsT=Mg[j][:, hh, :], rhs=xdt[j][:, g * 4 + hh, :], start=True, stop=True),
                     reads=[Mg_b[j], xdt_b[j]], writes=[yd_b], signal=False)
            k.op(k.pe, lambda: nc.tensor.matmul(ydo[:, 256:512], lhsT=xbT[j][:, 12 + g, :], rhs=sprev[:, g * 256:(g + 1) * 256], start=True, stop=True),
                 reads=[xbT_b[j], sprev_gb[g]], writes=[yd_b])
            k.op(k.pe, lambda: nc.tensor.matmul(sn, lhsT=btm[j][:, g, :], rhs=xdd[j][:, g * 4:(g + 1) * 4, :].rearrange("p h d -> p (h d)"), start=True, stop=True),
                 reads=[btm_b[j], xdd_b[j]], writes=[sn_b])
            k.op(k.dve, lambda: nc.vector.tensor_tensor(out=t1[j][:].rearrange("p (h d) -> p h d", d=64), in0=ydo[:, 256:512].rearrange("p (h d) -> p h d", d=64),
                                                        in1=bcast_free(E[j][:, g * 4:(g + 1) * 4], 1, 64), op=ALU.mult),
                 reads=[yd_b, E_b[j]], writes=[t1_b[j]])
            k.op(k.dve, lambda: nc.vector.tensor_tensor(out=ysb[j][:, g * 256:(g + 1) * 256], in0=ydo[:, 0:256], in1=t1[j][:], op=ALU.add),
                 reads=[yd_b, t1_b[j]], writes=[ysb_b[j]])
            sfv = Sf[:, g * 256:(g + 1) * 256]
            for hh in range(4):
                sh = Sf[:, g * 256 + hh * 64:g * 256 + (hh + 1) * 64]
                k.op(k.dve, lambda: nc.vector.scalar_tensor_tensor(out=sh, in0=sh, scalar=E[j][:, 32 + g * 4 + hh:33 + g * 4 + hh], in1=sn[:, hh * 64:(hh + 1) * 64],
                                                                   op0=ALU.mult, op1=ALU.add),
                     reads=[E_b[j], sn_b], writes=[Sf_b[g]])
            k.op(k.act, lambda: nc.scalar.copy(out=snew[:, g * 256:(g + 1) * 256], in_=sfv), reads=[Sf_b[g]], writes=[snew_gb[g]])
        if LV <= 4:
            continue
        k.op(k.pool, lambda: nc.gpsimd.tensor_tensor(out=ysb[j][:], in0=ysb[j][:], in1=xD[j][:], op=ALU.add), reads=[xD_b[j]], writes=[ysb_b[j]])
        k.op(k.pool, lambda: nc.gpsimd.tensor_tensor(out=ysb[j][:], in0=ysb[j][:], in1=zs[j][:], op=ALU.mult), reads=[zs_b[j]], writes=[ysb_b[j]])
        for g in range(4):
            k.op(k.act, lambda: nc.scalar.activation(out=junk[:], in_=ysb[j][:, g * 256:(g + 1) * 256], func=AF.Square, accum_out=st[j][:, g:g + 1]),
                 reads=[ysb_b[j]], writes=[junk_b, st_b[j]])
        k.op(k.dve, lambda: nc.vector.tensor_scalar(out=st[j][:, 4:8], in0=st[j][:, 0:4], scalar1=1.0 / 256, scalar2=NORM_EPS, op0=ALU.mult, op1=ALU.add),
             reads=[st_b[j]], writes=[st_b[j]])
        k.op(k.pool, lambda: nc.gpsimd.tensor_tensor(out=st[j][:, 8:12], in0=st[j][:, 4:8], in1=bcast_free(mhalf[:, 0:1], 0, 4)[:, :, 0], op=ALU.pow),
             reads=[st_b[j], bmh], writes=[st_b[j]])
        for g in range(4):
            k.op(k.act, lambda: nc.scalar.activation(out=yn[j][:, g * 256:(g + 1) * 256], in_=ysb[j][:, g * 256:(g + 1) * 256], func=AF.Copy, scale=st[j][:, 8 + g:9 + g]),
                 reads=[ysb_b[j], st_b[j]], writes=[yn_gb[j][g]])
        for c in range(KC):
            k.op(k.pe, lambda: nc.tensor.transpose(out=pty[:, c, :], in_=yn[j][:, c * 128:(c + 1) * 128], identity=self.ident[:]),
                 reads=[yn_gb[j][c // 2], self.b_const], writes=[pty_b], signal=(c == KC - 1))
        k.op(k.dve, lambda: nc.vector.tensor_tensor(out=yT[j][:], in0=pty[:], in1=bcast_free(pp[:, PP["sng"]:PP["sng"] + KC], 1, 128), op=ALU.mult),
             reads=[pty_b, self.b_pp], writes=[yT_b[j]])
        k.dma(k.sp, yT_ds[j], self.YS[:, t0:t0 + 128].rearrange("(c p) t -> p c t", p=128), yT[j][:], reads=[yT_b[j]])
    sc.close()


Prog.phase2 = _phase2


def _phase3(self, l):
    import math
    cfg, k, nc = self.cfg, self.k, self.nc
    NT, T = cfg.NT, cfg.T
    pp = self.pp
    lam_init = 0.8 - 0.6 * math.exp(-0.3 * l)
    sc = Scope(k)
    qT = [sc.sb("qT", [128, T], BF16) for _ in range(2)]
    kT = [sc.sb("kT", [128, T], BF16) for _ in range(2)]
    Vh = [sc.sb("Vh", [128, NT, 130], BF16) for _ in range(2)]
    qkv_b = [Buf() for _ in range(2)]
    q_b = [Buf() for _ in range(2)]
    k_b = [Buf() for _ in range(2)]
    ones_b = [Buf() for _ in range(2)]
    ld_ds = [k.dsem(f"p3ld{i}") for i in range(2)]
    for i in range(2):
        k.op(k.pool, lambda: nc.gpsimd.memset(Vh[i][:, :, 128:130], 1.0), writes=[ones_b[i]])
    NPT = 4
    pt = [sc.sb("pt", [128, 2, 512], BF16) for _ in range(NPT)]
    pt_b = [Buf() for _ in range(NPT)]
    yst = [sc.sb("yst", [128, T], BF16) for _ in range(2)]
    yst_b = [[Buf() for _ in range(NT)] for _ in range(2)]
    yst_ds = [k.dsem(f"p3st{i}") for i in range(2)]
    sm = sc.sb("p3sm", [128, 16], F32)
    sm_b = Buf()
    fs = [sc.sb("p3fs", [128, 8], F32) for _ in range(4)]
    fs_b = [Buf() for _ in range(4)]
    ft = [sc.sb("p3ft", [128, 128], F32) for _ in range(4)]
    ft_b = [Buf() for _ in range(4)]
    fo = [sc.sb("p3fo", [128, 128], F32) for _ in range(4)]
    fo_b = [Buf() for _ in range(4)]
    fn = [sc.sb("p3fn", [128, 128], BF16) for _ in range(4)]
    fn_b = [Buf() for _ in range(4)]
    accS = [sc.sb("accS", [128, 3, 396], F32) for _ in range(2)]
    accS_b = [[Buf() for _ in range(3)] for _ in range(2)]
    junk = sc.sb("p3junk", [128, 128], F32)
    junk_b = Buf()
    mhalf = sc.sb("mhalf3", [128, 1], F32)
    bmh = Buf()
    k.op(k.pool, lambda: nc.gpsimd.memset(mhalf[:], -0.5), writes=[bmh])
    spm = [sc.ps("spm", [128, 2, 512], F32) for _ in range(2)]
    spm_b = [Buf() for _ in range(2)]
    accb = [sc.ps("acc", [128, 512], F32) for _ in range(3)]
    acc_b = [Buf() for _ in range(3)]
    ptr = sc.ps("ptr", [128, 128], BF16)
    ptr_b = Buf()
    slots = {}
    idx = 0
    for jq in range(4):
        for m in range(2):
            slots[(jq, m)] = (idx // 3, (idx % 3) * 129)
            idx += 1

    lo = PP["lam"]
    k.op(k.dve, lambda: nc.vector.scalar_tensor_tensor(out=junk[:, 0:64], in0=pp[:, lo:lo + 64], scalar=1.0, in1=pp[:, lo + 64:lo + 128],
                                                       op0=ALU.mult, op1=ALU.mult, accum_out=sm[:, 0:1]), reads=[self.b_pp], writes=[junk_b, sm_b])
    k.op(k.dve, lambda: nc.vector.scalar_tensor_tensor(out=junk[:, 0:64], in0=pp[:, lo + 128:lo + 192], scalar=1.0, in1=pp[:, lo + 192:lo + 256],
                                                       op0=ALU.mult, op1=ALU.mult, accum_out=sm[:, 1:2]), reads=[self.b_pp], writes=[junk_b, sm_b])
    k.op(k.act, lambda: nc.scalar.activation(out=sm[:, 2:4], in_=sm[:, 0:2], func=AF.Exp), reads=[sm_b], writes=[sm_b])
    k.op(k.dve, lambda: nc.vector.tensor_tensor(out=sm[:, 4:5], in0=sm[:, 3:4], in1=sm[:, 2:3], op=ALU.subtract), reads=[sm_b], writes=[sm_b])
    k.op(k.dve, lambda: nc.vector.tensor_scalar(out=sm[:, 5:6], in0=sm[:, 4:5], scalar1=-lam_init, scalar2=None, op0=ALU.add), reads=[sm_b], writes=[sm_b])
    nlam = sm[:, 5:6]

    cnt = {"sp": 0, "pt": 0, "f": 0, "a": 0}
    qblocks = [list(range(q0, min(q0 + 4, NT))) for q0 in range(0, NT, 4)]
    for h in range(8):
        hs = h % 2
        k.dma(k.sp, ld_ds[hs], qT[hs][:], self.QT[h * 128:(h + 1) * 128, :], writes=[q_b[hs]])
        k.dma(k.sp, ld_ds[hs], kT[hs][:], self.KT[h * 128:(h + 1) * 128, :], writes=[k_b[hs]])
        k.dma(k.sp, ld_ds[hs], Vh[hs][:, :, 0:128], self.V[:, h * 128:(h + 1) * 128].rearrange("(t p) c -> p t c", p=128), writes=[qkv_b[hs]])
        for qts in qblocks:
            q0, nq = qts[0], len(qts)
            bank_started = [False, False, False]
            kt_max = min(qts[-1] + 1, NT - 1)
            for kt in range(kt_max + 1):
                jlo = max(0, kt - 1 - q0)
                c0, c1 = jlo * 128, nq * 128
                si = cnt["sp"] % 2
                cnt["sp"] += 1
                def qk_pair():
                    for m in range(2):
                        ins_ = nc.tensor.matmul(spm[si][:, m, c0:c1], lhsT=kT[hs][m * 64:(m + 1) * 64, kt * 128:(kt + 1) * 128],
                                                rhs=qT[hs][m * 64:(m + 1) * 64, q0 * 128 + c0:q0 * 128 + c1], start=True, stop=True)
                    return ins_
                k.op(k.pe, qk_pair, reads=[q_b[hs], k_b[hs]], writes=[spm_b[si]])
                pi = cnt["pt"] % NPT
                cnt["pt"] += 1
                k.op(k.act, lambda: nc.scalar.activation(out=pt[pi][:, :, c0:c1], in_=spm[si][:, :, c0:c1], func=AF.Exp, scale=0.125),
                     reads=[spm_b[si]], writes=[pt_b[pi]])
                for jq in range(jlo, nq):
                    qt = q0 + jq
                    which = 0 if kt == qt else (1 if kt == qt + 1 else None)
                    if which is not None:
                        k.op(k.dve, lambda: nc.vector.tensor_tensor(out=pt[pi][:, :, jq * 128:(jq + 1) * 128], in0=pt[pi][:, :, jq * 128:(jq + 1) * 128],
                                                                    in1=bcast_free(self.amask[:, which, :], 0, 2), op=ALU.mult),
                             reads=[self.b_const], writes=[pt_b[pi]])
                for jq in range(jlo, nq):
                    last_kt = min(q0 + jq + 1, NT - 1)
                    for m in range(2):
                        bk, co = slots[(jq, m)]
                        st_flag = not bank_started[bk]
                        bank_started[bk] = True
                        k.op(k.pe, lambda: nc.tensor.matmul(accb[bk][:, co:co + 129], lhsT=pt[pi][:, m, jq * 128:(jq + 1) * 128], rhs=Vh[hs][:, kt, 0:129],
                                                            start=st_flag, stop=(kt == last_kt), skip_group_check=True),
                             reads=[pt_b[pi], qkv_b[hs], ones_b[hs]], writes=[acc_b[bk]], signal=(kt == last_kt and m == 1))
            ai = cnt["a"] % 2
            cnt["a"] += 1
            nbk = (2 * nq + 2) // 3
            for bk in range(nbk):
                ncol = 129 * min(3, 2 * nq - 3 * bk)
                k.op(k.dve, lambda: nc.vector.tensor_copy(out=accS[ai][:, bk, 0:ncol], in_=accb[bk][:, 0:ncol]), reads=[acc_b[bk]], writes=[accS_b[ai][bk]])
            for jq in range(nq):
                qt = q0 + jq
                fi = cnt["f"] % 4
                cnt["f"] += 1
                b0, o0 = slots[(jq, 0)]
                b1, o1 = slots[(jq, 1)]
                k.op(k.dve, lambda: nc.vector.reciprocal(out=fs[fi][:, 0:1], in_=accS[ai][:, b0, o0 + 128:o0 + 129]), reads=[accS_b[ai][b0]], writes=[fs_b[fi]])
                k.op(k.dve, lambda: nc.vector.reciprocal(out=fs[fi][:, 1:2], in_=accS[ai][:, b1, o1 + 128:o1 + 129]), reads=[accS_b[ai][b1]], writes=[fs_b[fi]])
                k.op(k.dve, lambda: nc.vector.tensor_tensor(out=fs[fi][:, 2:3], in0=fs[fi][:, 1:2], in1=nlam, op=ALU.mult), reads=[sm_b], writes=[fs_b[fi]])
                k.op(k.dve, lambda: nc.vector.tensor_scalar(out=ft[fi][:], in0=accS[ai][:, b1, o1:o1 + 128], scalar1=fs[fi][:, 2:3], scalar2=None, op0=ALU.mult),
                     reads=[accS_b[ai][b1], fs_b[fi]], writes=[ft_b[fi]])
                k.op(k.dve, lambda: nc.vector.scalar_tensor_tensor(out=fo[fi][:], in0=accS[ai][:, b0, o0:o0 + 128], scalar=fs[fi][:, 0:1], in1=ft[fi][:],
                                                                   op0=ALU.mult, op1=ALU.add), reads=[accS_b[ai][b0], fs_b[fi], ft_b[fi]], writes=[fo_b[fi]])
                k.op(k.dve, lambda: nc.vector.scalar_tensor_tensor(out=junk[:], in0=fo[fi][:], scalar=1.0, in1=fo[fi][:], op0=ALU.mult, op1=ALU.mult,
                                                                   accum_out=fs[fi][:, 3:4]), reads=[fo_b[fi]], writes=[junk_b, fs_b[fi]])
                k.op(k.dve, lambda: nc.vector.tensor_scalar(out=fs[fi][:, 4:5], in0=fs[fi][:, 3:4], scalar1=1.0 / 128, scalar2=SUBLN_EPS, op0=ALU.mult, op1=ALU.add),
                     reads=[fs_b[fi]], writes=[fs_b[fi]])
                k.op(k.pool, lambda: nc.gpsimd.tensor_tensor(out=fs[fi][:, 5:6], in0=fs[fi][:, 4:5], in1=mhalf[:], op=ALU.pow), reads=[fs_b[fi], bmh], writes=[fs_b[fi]])
                k.op(k.act, lambda: nc.scalar.activation(out=fn[fi][:], in_=fo[fi][:], func=AF.Copy, scale=fs[fi][:, 5:6]), reads=[fo_b[fi], fs_b[fi]], writes=[fn_b[fi]])
                k.op(k.pe, lambda: nc.tensor.transpose(out=ptr[:], in_=fn[fi][:], identity=self.ident[:]), reads=[fn_b[fi], self.b_const], writes=[ptr_b])
                k.op(k.dve, lambda: nc.vector.tensor_scalar(out=yst[hs][:, qt * 128:(qt + 1) * 128], in0=ptr[:], scalar1=pp[:, PP["subln"]:PP["subln"] + 1],
                                                            scalar2=(1.0 - lam_init), op0=ALU.mult, op1=ALU.mult), reads=[ptr_b, self.b_pp], writes=[yst_b[hs][qt]])
        k.dma(k.sp, yst_ds[hs], self.YA[h * 128:(h + 1) * 128, :], yst[hs][:, :], reads=yst_b[hs])
    sc.close()


Prog.phase3 = _phase3


def _phase4(self, l, wts, wt_b):
    cfg, k, nc = self.cfg, self.k, self.nc
    sc = Scope(k)
    wsb, wab, wout = wts
    ins = [[sc.sb("p4in", [128, KC, 512], BF16) for _ in range(4)] for _ in range(3)]
    ins_b = [[Buf() for _ in range(4)] for _ in range(3)]
    ins_ds = [[k.dsem(f"p4in{i}_{a}") for a in range(4)] for i in range(3)]
    mg = [sc.sb("mg", [128, KC, 512], BF16) for _ in range(2)]
    mg_b = [[Buf() for _ in range(KC)] for _ in range(2)]
    tA = [sc.sb("p4ta", [128, 512], F32) for _ in range(3)]
    tA_b = [Buf() for _ in range(3)]
    tB = [sc.sb("p4tb", [128, 512], F32) for _ in range(3)]
    tB_b = [Buf() for _ in range(3)]
    NH = 6
    ht = [sc.sb("p4h", [128, D], F32) for _ in range(NH)]
    ht_b = [Buf() for _ in range(NH)]
    ht_ds = [k.dsem(f"p4h{i}") for i in range(NH)]
    ps1 = [sc.ps("p4a", [128, 512], F32) for _ in range(3)]
    ps1_b = [Buf() for _ in range(3)]
    ps2 = [sc.ps("p4b", [128, 512], F32) for _ in range(3)]
    ps2_b = [Buf() for _ in range(3)]
    ps3 = [sc.ps("p4c", [128, 512], F32) for _ in range(2)]
    ps3_b = [Buf() for _ in range(2)]
    c1 = c3 = ch = 0
    srcs = (self.YS, self.YA, self.GS, self.GA)
    for bi, (t0, n) in enumerate(cfg.blocks):
        s = bi % 2
        si = bi % 3
        for a in range(4):
            k.dma(k.sp, ins_ds[si][a], ins[si][a][:, :, :n], srcs[a][:, t0:t0 + n].rearrange("(c p) t -> p c t", p=128), writes=[ins_b[si][a]])
        ys, ya, gs, ga = ins[si]
        for c in range(KC):
            pi = c1 % 3
            c1 += 1
            for kk in range(KC):
                k.op(k.pe, lambda: nc.tensor.matmul(ps1[pi][:, :n], lhsT=wsb[:, kk, c * 128:(c + 1) * 128], rhs=ys[:, kk, :n], start=(kk == 0), stop=(kk == KC - 1)),
                     reads=[wt_b[0][kk], ins_b[si][0]], writes=[ps1_b[pi]], signal=(kk == KC - 1))
            for kk in range(KC):
                k.op(k.pe, lambda: nc.tensor.matmul(ps2[pi][:, :n], lhsT=wab[:, kk, c * 128:(c + 1) * 128], rhs=ya[:, kk, :n], start=(kk == 0), stop=(kk == KC - 1)),
                     reads=[wt_b[1][kk], ins_b[si][1]], writes=[ps2_b[pi]], signal=(kk == KC - 1))
            k.op(k.dve, lambda: nc.vector.tensor_tensor(out=tA[pi][:, :n], in0=ps1[pi][:, :n], in1=gs[:, c, :n], op=ALU.mult), reads=[ps1_b[pi], ins_b[si][2]], writes=[tA_b[pi]])
            k.op(k.dve, lambda: nc.vector.tensor_tensor(out=tB[pi][:, :n], in0=ps2[pi][:, :n], in1=ga[:, c, :n], op=ALU.mult), reads=[ps2_b[pi], ins_b[si][3]], writes=[tB_b[pi]])
            k.op(k.pool, lambda: nc.gpsimd.tensor_tensor(out=mg[s][:, c, :n], in0=tA[pi][:, :n], in1=tB[pi][:, :n], op=ALU.add), reads=[tA_b[pi], tB_b[pi]], writes=[mg_b[s][c]])
        for jq in range(n // 128):
            t = t0 // 128 + jq
            hi = ch % NH
            ch += 1
            k.dma(k.sp, ht_ds[hi], ht[hi][:], self.H[t * 128:(t + 1) * 128, :], writes=[ht_b[hi]])
            for half in range(2):
                pi = c3 % 2
                c3 += 1
                for kk in range(KC):
                    k.op(k.pe, lambda: nc.tensor.matmul(ps3[pi][:, :], lhsT=mg[s][:, kk, jq * 128:(jq + 1) * 128], rhs=wout[:, kk, half * 512:(half + 1) * 512],
                                                        start=(kk == 0), stop=(kk == KC - 1)),
                         reads=[wt_b[2][kk]] + mg_b[s], writes=[ps3_b[pi]], signal=(kk == KC - 1))
                k.op(k.dve, lambda: nc.vector.tensor_tensor(out=ht[hi][:, half * 512:(half + 1) * 512], in0=ps3[pi][:, :], in1=ht[hi][:, half * 512:(half + 1) * 512], op=ALU.add),
                     reads=[ps3_b[pi]], writes=[ht_b[hi]])
            k.dma(k.sp, ht_ds[hi], self.H[t * 128:(t + 1) * 128, :], ht[hi][:], reads=[ht_b[hi]])
    sc.close()


def _phase5(self, sc, l, UT, ut_bufs):
    cfg, k, nc = self.cfg, self.k, self.nc
    T, blocks = cfg.T, cfg.blocks
    pp = self.pp
    W = self.w_up[l]
    NW = 3
    wg = [sc.sb("wg", [128, 2, KC, 128], BF16) for _ in range(NW)]
    wg_b = [[Buf(), Buf()] for _ in range(NW)]
    wg_ds = [[k.dsem(f"p5w{i}_{a}") for a in range(2)] for i in range(NW)]
    gv = [sc.sb("gv", [128, 2, T + 4], BF16) for _ in range(2)]
    gv_b = [[[Buf() for _ in blocks] for _ in range(2)] for _ in range(2)]
    gv_pad = [Buf() for _ in range(2)]
    ost = [sc.sb("p5ost", [128, T], BF16) for _ in range(2)]
    ost_b = [[Buf() for _ in blocks] for _ in range(2)]
    ost_ds = [k.dsem(f"p5o{i}") for i in range(2)]
    dg = [sc.sb("p5dg", [128, 2, 3, 128], BF16) for _ in range(2)]
    dg_b = [Buf() for _ in range(2)]
    sg = [sc.sb("p5sg", [128, 512], F32) for _ in range(2)]
    sg_b = [Buf() for _ in range(2)]
    psA = [[sc.ps("p5a", [128, 512], F32) for _ in range(2)] for _ in range(2)]
    psA_b = [[Buf() for _ in range(2)] for _ in range(2)]
    psB = [[sc.ps("p5b", [128, 512], F32)] * 2 for _ in range(2)]
    psB_b = [[Buf()] * 2 for _ in range(2)]
    for i in range(2):
        k.op(k.pool, lambda: nc.gpsimd.memset(gv[i][:, :, 0:2], 0.0), writes=[gv_pad[i]])
    ca = cb_ = 0
    for j in range(DFF // 128):
        s = j % 2
        ws = j % NW
        for a, col0 in enumerate((j * 128, DFF + j * 128)):
            k.dma(k.pool, wg_ds[ws][a], wg[ws][:, a, :, :], W[:, col0:col0 + 128].rearrange("(kk p) c -> p kk c", p=128), writes=[wg_b[ws][a]])
        if j >= 1:
            wd_, wdb_, wds_ = self._wd_prefetch
            k.dma(k.pool, wds_, wd_[:, j - 1, :], self.w_down[l, (j - 1) * 128:j * 128, :], writes=[wdb_[j - 1]])
            if j == DFF // 128 - 1:
                k.dma(k.pool, wds_, wd_[:, j, :], self.w_down[l, j * 128:(j + 1) * 128, :], writes=[wdb_[j]])
        for a, cidx in enumerate((j, DFF // 128 + j)):
            cw = pp[:, PP["mcw"] + cidx * 3:PP["mcw"] + cidx * 3 + 3]
            k.op(k.dve, lambda: nc.vector.tensor_tensor(out=dg[s][:, a, :, :], in0=bcast_free(self.ident[:], 0, 3), in1=bcast_free(cw, 1, 128), op=ALU.mult),
                 reads=[self.b_const, self.b_pp], writes=[dg_b[s]])
        for bi, (t0, n) in enumerate(blocks):
            pa = ca % 2
            ca += 1
            for a in range(2):
                for kk in range(KC):
                    k.op(k.pe, lambda: nc.tensor.matmul(psA[a][pa][:, :n], lhsT=wg[ws][:, a, kk, :], rhs=UT[:, kk, t0:t0 + n], start=(kk == 0), stop=(kk == KC - 1)),
                         reads=[wg_b[ws][a]] + ut_bufs[t0 // 128:(t0 + n) // 128], writes=[psA_b[a][pa]], signal=(kk == KC - 1))
            k.op(k.act, lambda: nc.scalar.copy(out=gv[s][:, 0, 2 + t0:2 + t0 + n], in_=psA[0][pa][:, :n]), reads=[psA_b[0][pa]], writes=[gv_b[s][0][bi]])
            k.op(k.dve, lambda: nc.vector.tensor_copy(out=gv[s][:, 1, 2 + t0:2 + t0 + n], in_=psA[1][pa][:, :n]), reads=[psA_b[1][pa]], writes=[gv_b[s][1][bi]])
            pb = cb_ % 2
            cb_ += 1
            for a in range(2):
                rd = [dg_b[s], gv_b[s][a][bi], gv_pad[s]] + ([gv_b[s][a][bi - 1]] if bi > 0 else [])
                for jj in range(3):
                    k.op(k.pe, lambda: nc.tensor.matmul(psB[a][pb][:, :n], lhsT=dg[s][:, a, jj, :], rhs=gv[s][:, a, t0 + jj:t0 + jj + n], start=(jj == 0), stop=(jj == 2)),
                         reads=rd, writes=[psB_b[a][pb]], signal=(jj == 2))
            k.op(k.act, lambda: nc.scalar.activation(out=sg[pb][:, :n], in_=psB[0][pb][:, :n], func=AF.Silu, bias=pp[:, PP["mcb"] + j:PP["mcb"] + j + 1]),
                 reads=[psB_b[0][pb], self.b_pp], writes=[sg_b[pb]])
            vb = PP["mcb"] + DFF // 128 + j
            k.op(k.dve, lambda: nc.vector.scalar_tensor_tensor(out=ost[s][:, t0:t0 + n], in0=psB[1][pb][:, :n], scalar=pp[:, vb:vb + 1], in1=sg[pb][:, :n],
                                                               op0=ALU.add, op1=ALU.mult), reads=[psB_b[1][pb], sg_b[pb], self.b_pp], writes=[ost_b[s][bi]])
        k.dma(k.sp, ost_ds[s], self.AT[j * 128:(j + 1) * 128, :], ost[s][:, :], reads=ost_b[s])


def _phase6(self, l, wd, wd_b):
    cfg, k, nc = self.cfg, self.k, self.nc
    NC = DFF // 128
    sc = Scope(k)
    NB = 4
    at = [sc.sb("p6at", [128, NC, 128], BF16) for _ in range(NB)]
    at_b = [Buf() for _ in range(NB)]
    ht = [sc.sb("p6h", [128, D], F32) for _ in range(NB)]
    ht_b = [Buf() for _ in range(NB)]
    ds = [k.dsem(f"p6l{i}") for i in range(NB)]
    ds_at = [k.dsem(f"p6a{i}") for i in range(NB)]
    ps = [sc.ps("p6", [128, 512], F32) for _ in range(4)]
    ps_b = [Buf() for _ in range(4)]
    cp = 0
    for t in range(cfg.NT):
        i = t % NB
        k.dma(k.sp, ds_at[i], at[i][:], self.AT[:, t * 128:(t + 1) * 128].rearrange("(c p) t -> p c t", p=128), writes=[at_b[i]])
        k.dma(k.sp, ds[i], ht[i][:], self.H[t * 128:(t + 1) * 128, :], writes=[ht_b[i]])
        for half in range(2):
            pi = cp % 4
            cp += 1
            for c in range(NC):
                k.op(k.pe, lambda: nc.tensor.matmul(ps[pi][:, :], lhsT=at[i][:, c, :], rhs=wd[:, c, half * 512:(half + 1) * 512], start=(c == 0), stop=(c == NC - 1)),
                     reads=[at_b[i], wd_b[c]], writes=[ps_b[pi]], signal=(c == NC - 1))
            k.op(k.dve, lambda: nc.vector.tensor_tensor(out=ht[i][:, half * 512:(half + 1) * 512], in0=ps[pi][:, :], in1=ht[i][:, half * 512:(half + 1) * 512], op=ALU.add),
                 reads=[ps_b[pi]], writes=[ht_b[i]])
        k.dma(k.sp, ds[i], self.H[t * 128:(t + 1) * 128, :], ht[i][:], reads=[ht_b[i]])
    sc.close()


def _final_norm(self):
    cfg, k, nc = self.cfg, self.k, self.nc
    sc = Scope(k)
    fng = sc.sb("fng", [128, D], F32)
    fng_b = Buf()
    k.dma(k.sp, k.dsem("fng"), fng[:], self.fng_d[:, :], writes=[fng_b])
    NB = 3
    ht = [sc.sb("fh", [128, D], F32) for _ in range(NB)]
    hb = [Buf() for _ in range(NB)]
    hds = [k.dsem(f"fh{i}") for i in range(NB)]
    ot = [sc.sb("fo", [128, D], F32) for _ in range(NB)]
    ob = [Buf() for _ in range(NB)]
    ods = [k.dsem(f"fo{i}") for i in range(NB)]
    junk = sc.sb("fjunk", [128, D], BF16)
    bjunk = Buf()
    st = [sc.sb("fst", [128, 4], F32) for _ in range(2)]
    stb = [Buf() for _ in range(2)]
    mhalf = sc.sb("mhalff", [128, 1], F32)
    bmh = Buf()
    k.op(k.pool, lambda: nc.gpsimd.memset(mhalf[:], -0.5), writes=[bmh])
    for t in range(cfg.NT):
        i, j = t % NB, t % 2
        k.dma(k.sp, hds[i], ht[i][:], self.H[t * 128:(t + 1) * 128, :], writes=[hb[i]])
        k.op(k.dve, lambda: nc.vector.scalar_tensor_tensor(out=junk[:], in0=ht[i][:], scalar=1.0, in1=ht[i][:], op0=ALU.mult, op1=ALU.mult, accum_out=st[j][:, 0:1]),
             reads=[hb[i]], writes=[bjunk, stb[j]])
        k.op(k.dve, lambda: nc.vector.tensor_scalar(out=st[j][:, 1:2], in0=st[j][:, 0:1], scalar1=1.0 / D, scalar2=NORM_EPS, op0=ALU.mult, op1=ALU.add),
             reads=[stb[j]], writes=[stb[j]])
        k.op(k.pool, lambda: nc.gpsimd.tensor_tensor(out=st[j][:, 2:3], in0=st[j][:, 1:2], in1=mhalf[:], op=ALU.pow), reads=[stb[j], bmh], writes=[stb[j]])
        k.op(k.dve, lambda: nc.vector.scalar_tensor_tensor(out=ot[i][:], in0=ht[i][:], scalar=st[j][:, 2:3], in1=fng[:], op0=ALU.mult, op1=ALU.mult),
             reads=[hb[i], stb[j], fng_b], writes=[ob[i]])
        lo = max(t * 128, N_META)
        hi = min((t + 1) * 128, N_META + cfg.seq)
        if hi > lo:
            k.dma(k.sp, ods[i], self.out[lo - N_META:hi - N_META, :], ot[i][lo - t * 128:hi - t * 128, :], reads=[ob[i]])
    sc.close()


Prog.phase4 = _phase4
Prog.phase5 = _phase5
Prog.phase6 = _phase6
Prog.final_norm = _final_norm


_PROG_CACHE = {}


def kernel(x, meta_tokens, norm1_g, w_in, ssd_conv_w, ssd_conv_b, ssd_dt_bias, ssd_a_log, ssd_d,
           ssd_norm_g, lambda_q1, lambda_k1, lambda_q2, lambda_k2, attn_subln_g, w_ssd_branch,
           w_attn_branch, w_out, norm2_g, w_up, mlp_conv_w, mlp_conv_b, w_down, final_norm_g):
    inp = dict(x=x, meta_tokens=meta_tokens, norm1_g=norm1_g, w_in=w_in, ssd_conv_w=ssd_conv_w, ssd_conv_b=ssd_conv_b,
               ssd_dt_bias=ssd_dt_bias, ssd_a_log=ssd_a_log, ssd_d=ssd_d, ssd_norm_g=ssd_norm_g, lambda_q1=lambda_q1,
               lambda_k1=lambda_k1, lambda_q2=lambda_q2, lambda_k2=lambda_k2, attn_subln_g=attn_subln_g,
               w_ssd_branch=w_ssd_branch, w_attn_branch=w_attn_branch, w_out=w_out, norm2_g=norm2_g, w_up=w_up,
               mlp_conv_w=mlp_conv_w, mlp_conv_b=mlp_conv_b, w_down=w_down, final_norm_g=final_norm_g)
    inp = {k_: np.asarray(v) for k_, v in inp.items()}
    bsz, seq, _ = inp["x"].shape
    depth = inp["w_in"].shape[0]
    cfg = Cfg(seq=seq, depth=depth)
    key = (seq, depth)
    if key not in _PROG_CACHE:
        _PROG_CACHE[key] = Prog(cfg).build()
    nc = _PROG_CACHE[key]
    shared = make_in_map(cfg, inp, 0)
    in_maps = []
    for b in range(bsz):
        m = dict(shared)
        m["x"] = np.ascontiguousarray(inp["x"][b], dtype=np.float32)
        in_maps.append(m)
    res = run_bass_kernel_spmd(nc, in_maps, core_ids=list(range(bsz)))
    return np.stack([np.asarray(r["out"], dtype=np.float32) for r in res.results], axis=0)


def _p2_setup(self, l, sc):
    import os
    cfg, k, nc = self.cfg, self.k, self.nc
    NT = cfg.NT
    pp = self.pp
    tri = self.tri3[:, 0, :]
    upp = self.tri3[:, 1, :]

    def dbl(name, shape, dt, n=2):
        return [sc.sb(name, shape, dt) for _ in range(n)], [Buf(name) for _ in range(n)]
    def sgl(name, shape, dt):
        t_, b_ = sc.sb(name, shape, dt), Buf(name)
        return [t_, t_], [b_, b_]
    xbT, xbT_b = dbl("xbT", [128, 16, 128], BF16)
    zs, zs_b = dbl("zs", [128, D], BF16)
    dta, dta_b = dbl("dta", [128, 32], F32)
    ld_ds = [[k.dsem(f"p2ld{i}_{a}") for a in range(3)] for i in range(2)]
    xtm, xtm_b = dbl("xtm", [128, 16, 64], BF16)
    btm, btm_b = dbl("btm", [128, 4, 128], BF16)
    E, E_b = dbl("E", [128, 48], F32)
    w1, w1_b = dbl("w1", [128, 16], F32)
    Lh, Lh_b = sgl("Lh", [128, 16, 128], F32)
    Lh2_b = [Buf()] * 2
    Dg, Dg_b = dbl("Dg", [128, 4, 128], BF16)
    cbm, cbm_b = dbl("cbm", [128, 4, 128], BF16)
    Mg, Mg_b = dbl("Mg", [128, 4, 128], BF16)
    xdt, xdt_b = dbl("xdt", [128, 16, 64], BF16)
    xdd, xdd_b = dbl("xdd", [128, 16, 64], BF16)
    t1, t1_b = dbl("t1", [128, 256], F32)
    ysb, ysb_b = dbl("ysb", [128, D], F32)
    xD, xD_b = sgl("xD", [128, D], F32)
    junk = sc.sb("p2junk", [128, 256], BF16)
    junk_b = Buf()
    st, st_b = dbl("p2st", [128, 12], F32)
    yn, yn_b = sgl("yn", [128, D], BF16)
    Sf = sc.sb("Sf", [128, D], F32)
    Sf_b = [Buf() for _ in range(4)]
    Sbf, Sbf_b = dbl("Sbf", [128, D], BF16)
    Sbf_gb = [[Buf() for _ in range(4)] for _ in range(2)]
    mhalf = sc.sb("mhalf2", [128, 1], F32)
    bmh = Buf()
    k.op(k.pool, lambda: nc.gpsimd.memset(mhalf[:], -0.5), writes=[bmh])
    k.op(k.pool, lambda: nc.gpsimd.memset(Sf[:], 0.0), writes=Sf_b)
    k.op(k.pool, lambda: nc.gpsimd.memset(Sbf[1][:], 0.0), writes=Sbf_gb[1])
    pbf = sc.ps("pbf", [128, KC, 128], BF16)
    ptx = pbf
    ptb = pbf[:].rearrange("p c t -> p (c t)")
    ptx_b = ptb_b = Buf()
    pty = sc.ps("pty", [128, KC, 128], BF16)
    pty_b = Buf()

    cb = sc.ps("cbseg", [128, 4, 128], F32)
    cb_b = Buf()
    seg = [cb] * 2
    seg_b = [cb_b] * 2
    ydo = sc.ps("ydo", [128, 512], F32)
    yd_b, yo_b = Buf(), Buf()
    sne = sc.ps("sne", [128, 512], F32)
    sn = sne[:, 0:256]
    e3 = sne[:, 256:304]
    sn_b = Buf()
    segc = 0

    def chunk(i, ys_dst):
        nonlocal segc
        j = i % 2
        t0 = i * 128
        k.dma(k.sp, ld_ds[j][0], xbT[j][:], self.XBC[:, t0:t0 + 128].rearrange("(c p) t -> p c t", p=128), writes=[xbT_b[j]])
        k.dma(k.sp, ld_ds[j][1], zs[j][:], self.ZS[t0:t0 + 128, :], writes=[zs_b[j]])
        k.dma(k.sp, ld_ds[j][2], dta[j][:], self.DT[t0:t0 + 128, :], writes=[dta_b[j]])
        a = dta[j][:, 16:32]
        dt = dta[j][:, 0:16]
        for c in range(KC):
            k.op(k.pe, lambda: nc.tensor.transpose(out=ptx[:, c, :], in_=xbT[j][:, c, :], identity=self.ident[:]),
                 reads=[xbT_b[j], self.b_const], writes=[ptx_b], signal=(c == KC - 1))
        k.op(k.act, lambda: nc.scalar.copy(out=xtm[j][:].rearrange("p h d -> p (h d)"), in_=ptx[:].rearrange("p c t -> p (c t)")),
             reads=[ptx_b], writes=[xtm_b[j]])
        for g in range(4):
            k.op(k.pe, lambda: nc.tensor.transpose(out=ptb[:, g * 128:(g + 1) * 128], in_=xbT[j][:, 8 + g, :], identity=self.ident[:]),
                 reads=[xbT_b[j], self.b_const], writes=[ptb_b], signal=(g == 3))
        k.op(k.dve, lambda: nc.vector.tensor_copy(out=btm[j][:].rearrange("p g n -> p (g n)"), in_=ptb[:, 0:512]),
             reads=[ptb_b], writes=[btm_b[j]])
        for q in range(3):
            k.op(k.pe, lambda: nc.tensor.matmul(e3[:, q * 16:(q + 1) * 16], lhsT=self.tri3[:, q, :], rhs=a, start=True, stop=True),
                 reads=[dta_b[j], self.b_const], writes=[sn_b], signal=(q == 2))
        k.op(k.act, lambda: nc.scalar.activation(out=E[j][:], in_=e3, func=AF.Exp), reads=[sn_b], writes=[E_b[j]])
        k.op(k.dve, lambda: nc.vector.tensor_tensor(out=w1[j][:], in0=dt, in1=E[j][:, 16:32], op=ALU.mult),
             reads=[dta_b[j], E_b[j]], writes=[w1_b[j]])
        k.op(k.dve, lambda: nc.vector.tensor_tensor(out=Lh[j][:, 0:8, :], in0=bcast_free(upp, 0, 8), in1=bcast_free(a[:, 0:8], 1, 128), op=ALU.mult),
             reads=[dta_b[j], self.b_const], writes=[Lh_b[j]])
        k.op(k.pool, lambda: nc.gpsimd.tensor_tensor(out=Lh[j][:, 8:16, :], in0=bcast_free(upp, 0, 8), in1=bcast_free(a[:, 8:16], 1, 128), op=ALU.mult),
             reads=[dta_b[j], self.b_const], writes=[Lh2_b[j]])
        k.op(k.pool, lambda: nc.gpsimd.tensor_tensor(out=xdt[j][:], in0=xtm[j][:], in1=bcast_free(dt, 1, 64), op=ALU.mult),
             reads=[xtm_b[j], dta_b[j]], writes=[xdt_b[j]])
        k.op(k.pool, lambda: nc.gpsimd.tensor_tensor(out=xdd[j][:], in0=xtm[j][:], in1=bcast_free(w1[j][:], 1, 64), op=ALU.mult),
             reads=[xtm_b[j], w1_b[j]], writes=[xdd_b[j]])
        k.op(k.pool, lambda: nc.gpsimd.tensor_tensor(out=xD[j][:].rearrange("p (h d) -> p h d", d=64), in0=xtm[j][:],
                                                     in1=bcast_free(pp[:, PP["dsk"]:PP["dsk"] + 16], 1, 64), op=ALU.mult),
             reads=[xtm_b[j], self.b_pp], writes=[xD_b[j]])
        for g in range(4):
            k.op(k.pe, lambda: nc.tensor.matmul(cb[:, g, :], lhsT=xbT[j][:, 8 + g, :], rhs=xbT[j][:, 12 + g, :], start=True, stop=True),
                 reads=[xbT_b[j]], writes=[cb_b], signal=(g == 3))
        k.op(k.dve, lambda: nc.vector.tensor_tensor(out=cbm[j][:], in0=cb[:], in1=bcast_free(tri, 0, 4), op=ALU.mult),
             reads=[cb_b, self.b_const], writes=[cbm_b[j]])
        sprev, snew = Sbf[(i + 1) % 2], Sbf[i % 2]
        sprev_gb, snew_gb = Sbf_gb[(i + 1) % 2], Sbf_gb[i % 2]
        for g in range(4):
            sg = segc % 2
            segc += 1
            for hh in range(4):
                k.op(k.pe, lambda: nc.tensor.matmul(seg[sg][:, hh, :], lhsT=Lh[j][:, g * 4 + hh, :], rhs=tri, start=True, stop=True),
                     reads=[Lh_b[j] if g < 2 else Lh2_b[j], self.b_const], writes=[seg_b[sg]], signal=(hh == 3))
            k.op(k.act, lambda: nc.scalar.activation(out=Dg[j][:], in_=seg[sg][:], func=AF.Exp), reads=[seg_b[sg]], writes=[Dg_b[j]])
            k.op(k.dve, lambda: nc.vector.tensor_tensor(out=Mg[j][:], in0=Dg[j][:], in1=bcast_free(cbm[j][:, g, :], 0, 4), op=ALU.mult),
                 reads=[Dg_b[j], cbm_b[j]], writes=[Mg_b[j]])
            for hh in range(4):
                k.op(k.pe, lambda: nc.tensor.matmul(ydo[:, hh * 64:(hh + 1) * 64], lhsT=Mg[j][:, hh, :], rhs=xdt[j][:, g * 4 + hh, :], start=True, stop=True),
                     reads=[Mg_b[j], xdt_b[j]], writes=[yd_b], signal=False)
            k.op(k.pe, lambda: nc.tensor.matmul(ydo[:, 256:512], lhsT=xbT[j][:, 12 + g, :], rhs=sprev[:, g * 256:(g + 1) * 256], start=True, stop=True),
                 reads=[xbT_b[j], sprev_gb[g]], writes=[yd_b])
            k.op(k.pe, lambda: nc.tensor.matmul(sn, lhsT=btm[j][:, g, :], rhs=xdd[j][:, g * 4:(g + 1) * 4, :].rearrange("p h d -> p (h d)"), start=True, stop=True),
                 reads=[btm_b[j], xdd_b[j]], writes=[sn_b])
            k.op(k.dve, lambda: nc.vector.tensor_tensor(out=t1[j][:].rearrange("p (h d) -> p h d", d=64), in0=ydo[:, 256:512].rearrange("p (h d) -> p h d", d=64),
                                                        in1=bcast_free(E[j][:, g * 4:(g + 1) * 4], 1, 64), op=ALU.mult),
                 reads=[yd_b, E_b[j]], writes=[t1_b[j]])
            k.op(k.dve, lambda: nc.vector.tensor_tensor(out=ysb[j][:, g * 256:(g + 1) * 256], in0=ydo[:, 0:256], in1=t1[j][:], op=ALU.add),
                 reads=[yd_b, t1_b[j]], writes=[ysb_b[j]])
            sfv = Sf[:, g * 256:(g + 1) * 256]
            k.op(k.dve, lambda: nc.vector.tensor_tensor(out=sfv.rearrange("p (h d) -> p h d", d=64), in0=sfv.rearrange("p (h d) -> p h d", d=64),
                                                        in1=bcast_free(E[j][:, 32 + g * 4:32 + (g + 1) * 4], 1, 64), op=ALU.mult),
                 reads=[E_b[j]], writes=[Sf_b[g]])
            k.op(k.dve, lambda: nc.vector.tensor_tensor(out=sfv, in0=sn, in1=sfv, op=ALU.add), reads=[sn_b], writes=[Sf_b[g]])
            k.op(k.act, lambda: nc.scalar.copy(out=snew[:, g * 256:(g + 1) * 256], in_=sfv), reads=[Sf_b[g]], writes=[snew_gb[g]])
        k.op(k.pool, lambda: nc.gpsimd.tensor_tensor(out=ysb[j][:], in0=ysb[j][:], in1=xD[j][:], op=ALU.add), reads=[xD_b[j]], writes=[ysb_b[j]])
        k.op(k.dve, lambda: nc.vector.tensor_tensor(out=ysb[j][:], in0=ysb[j][:], in1=zs[j][:], op=ALU.mult), reads=[zs_b[j]], writes=[ysb_b[j]])
        for g in range(4):
            k.op(k.act, lambda: nc.scalar.activation(out=junk[:], in_=ysb[j][:, g * 256:(g + 1) * 256], func=AF.Square, accum_out=st[j][:, g:g + 1]),
                 reads=[ysb_b[j]], writes=[junk_b, st_b[j]])
        k.op(k.dve, lambda: nc.vector.tensor_scalar(out=st[j][:, 4:8], in0=st[j][:, 0:4], scalar1=1.0 / 256, scalar2=NORM_EPS, op0=ALU.mult, op1=ALU.add),
             reads=[st_b[j]], writes=[st_b[j]])
        k.op(k.pool, lambda: nc.gpsimd.tensor_tensor(out=st[j][:, 8:12], in0=st[j][:, 4:8], in1=bcast_free(mhalf[:, 0:1], 0, 4)[:, :, 0], op=ALU.pow),
             reads=[st_b[j], bmh], writes=[st_b[j]])
        k.op(k.dve, lambda: nc.vector.tensor_tensor(out=yn[j][:].rearrange("p (g d) -> p g d", d=256), in0=ysb[j][:].rearrange("p (g d) -> p g d", d=256),
                                                    in1=bcast_free(st[j][:, 8:12], 1, 256), op=ALU.mult),
             reads=[ysb_b[j], st_b[j]], writes=[yn_b[j]])
        for c in range(KC):
            k.op(k.pe, lambda: nc.tensor.transpose(out=pty[:, c, :], in_=yn[j][:, c * 128:(c + 1) * 128], identity=self.ident[:]),
                 reads=[yn_b[j], self.b_const], writes=[pty_b], signal=(c == KC - 1))
        ydst, ydst_b = ys_dst
        k.op(k.dve, lambda: nc.vector.tensor_tensor(out=ydst, in0=pty[:], in1=bcast_free(pp[:, PP["sng"]:PP["sng"] + KC], 1, 128), op=ALU.mult),
             reads=[pty_b, self.b_pp], writes=[ydst_b])
    return chunk


def _p4_setup(self, l, sc, wts, wt_b):
    cfg, k, nc = self.cfg, self.k, self.nc
    wsb, wab, wout = wts
    ins = [[sc.sb("p4in", [128, KC, 512], BF16) for _ in range(4)] for _ in range(2)]
    ins_b = [[Buf() for _ in range(4)] for _ in range(2)]
    ys_b = [[Buf() for _ in range(4)] for _ in range(2)]
    ins_ds = [k.dsem(f"p4in{i}") for i in range(2)]
    mg = [sc.sb("mg", [128, KC, 512], BF16)] * 2
    mg_b = [[Buf() for _ in range(KC)]] * 2
    tA = [sc.sb("p4ta", [128, 512], F32)] * 2
    tA_b = [Buf()] * 2
    tB = [sc.sb("p4tb", [128, 512], F32)] * 2
    tB_b = [Buf()] * 2
    NH = 2
    ht = [sc.sb("p4h", [128, D], F32) for _ in range(NH)]
    ht_b = [Buf() for _ in range(NH)]
    ht_ds = [k.dsem(f"p4h{i}") for i in range(NH)]
    ps1 = [sc.ps("p4a", [128, 512], F32)] * 2
    ps1_b = [Buf()] * 2
    ps2 = [sc.ps("p4b", [128, 512], F32)] * 2
    ps2_b = [Buf()] * 2
    ps3 = [sc.ps("p4c", [128, 512], F32)] * 2
    ps3_b = [Buf()] * 2
    c1 = c3 = ch = 0
    srcs = (self.YS, self.YA, self.GS, self.GA)

    def block(bi):
        nonlocal c1, c3, ch
        t0, n = cfg.blocks[bi]
        s = bi % 2
        for a in range(1, 4):
            k.dma(k.sp, ins_ds[s], ins[s][a][:, :, :n], srcs[a][:, t0:t0 + n].rearrange("(c p) t -> p c t", p=128), writes=[ins_b[s][a]])
        ys, ya, gs, ga = ins[s]
        for c in range(KC):
            pi = c1 % 2
            c1 += 1
            for kk in range(KC):
                k.op(k.pe, lambda: nc.tensor.matmul(ps1[pi][:, :n], lhsT=wsb[:, kk, c * 128:(c + 1) * 128], rhs=ys[:, kk, :n], start=(kk == 0), stop=(kk == KC - 1)),
                     reads=[wt_b[0][kk]] + ys_b[s][:n // 128], writes=[ps1_b[pi]], signal=(kk == KC - 1))
            for kk in range(KC):
                k.op(k.pe, lambda: nc.tensor.matmul(ps2[pi][:, :n], lhsT=wab[:, kk, c * 128:(c + 1) * 128], rhs=ya[:, kk, :n], start=(kk == 0), stop=(kk == KC - 1)),
                     reads=[wt_b[1][kk], ins_b[s][1]], writes=[ps2_b[pi]], signal=(kk == KC - 1))
            k.op(k.dve, lambda: nc.vector.tensor_tensor(out=tA[pi][:, :n], in0=ps1[pi][:, :n], in1=gs[:, c, :n], op=ALU.mult), reads=[ps1_b[pi], ins_b[s][2]], writes=[tA_b[pi]])
            k.op(k.dve, lambda: nc.vector.tensor_tensor(out=tB[pi][:, :n], in0=ps2[pi][:, :n], in1=ga[:, c, :n], op=ALU.mult), reads=[ps2_b[pi], ins_b[s][3]], writes=[tB_b[pi]])
            k.op(k.pool, lambda: nc.gpsimd.tensor_tensor(out=mg[s][:, c, :n], in0=tA[pi][:, :n], in1=tB[pi][:, :n], op=ALU.add), reads=[tA_b[pi], tB_b[pi]], writes=[mg_b[s][c]])
        for jq in range(n // 128):
            t = t0 // 128 + jq
            hi = ch % NH
            ch += 1
            k.dma(k.sp, ht_ds[hi], ht[hi][:], self.H[t * 128:(t + 1) * 128, :], writes=[ht_b[hi]])
            for half in range(2):
                pi = c3 % 2
                c3 += 1
                for kk in range(KC):
                    k.op(k.pe, lambda: nc.tensor.matmul(ps3[pi][:, :], lhsT=mg[s][:, kk, jq * 128:(jq + 1) * 128], rhs=wout[:, kk, half * 512:(half + 1) * 512],
                                                        start=(kk == 0), stop=(kk == KC - 1)),
                         reads=[wt_b[2][kk]] + mg_b[s], writes=[ps3_b[pi]], signal=(kk == KC - 1))
                k.op(k.dve, lambda: nc.vector.tensor_tensor(out=ht[hi][:, half * 512:(half + 1) * 512], in0=ps3[pi][:, :], in1=ht[hi][:, half * 512:(half + 1) * 512], op=ALU.add),
                     reads=[ps3_b[pi]], writes=[ht_b[hi]])
            k.dma(k.sp, ht_ds[hi], self.H[t * 128:(t + 1) * 128, :], ht[hi][:], reads=[ht_b[hi]])
    return block, ins, ys_b


def _phase24(self, l, wts, wt_b):
    cfg, k = self.cfg, self.k
    sc = Scope(k)
    chunk = _p2_setup(self, l, sc)
    block, ins, ys_b = _p4_setup(self, l, sc, wts, wt_b)
    for i in range(cfg.NT):
        bi, q = i // 4, i % 4
        s = bi % 2
        chunk(i, (ins[s][0][:, :, q * 128:(q + 1) * 128], ys_b[s][q]))
        if q == 3 or i == cfg.NT - 1:
            block(bi)
            if "YS" in cfg.debug_outs:
                t0, n = cfg.blocks[bi]
                k.dma(k.sp, k.dsem("dbgys"), self.YS[:, t0:t0 + n].rearrange("(c p) t -> p c t", p=128), ins[s][0][:, :, :n], reads=ys_b[s][:n // 128])
    sc.close()


Prog.phase24 = _phase24
```
